# Optimizing a Trainium2 kernel written in Bass

```python
import jax
import jax.numpy as jnp
from jax import lax
import numpy as np

D_MODEL = 1024
BATCH = 16
SEQ = 256
DEPTH = 4
DEC_BATCH = 2
DEC_SEQ = 2048
PAST_LEN = 512

GRID_W = 64
HEAD_DIM = 64
A_HEADS = 8
A_KV = 2
A_GROUP = A_HEADS // A_KV
WINDOW = 128
Q_BLOCK = 128
B_HEADS = 8
B_WIDTH = B_HEADS * HEAD_DIM
LORA_W = 64
LORA_A = 64
LORA_G = 128
GN_EPS = 64e-5
C_HEADS = 8
NA_ROWS = 8
NA_COLS = 16
N_BRANCH = 3
A_COLS = (A_HEADS + 2 * A_KV) * HEAD_DIM
B_COLS = 3 * B_WIDTH + 2 * LORA_W + 2 * LORA_A + LORA_G
C_COLS = 3 * C_HEADS * HEAD_DIM
IN_COLS = A_COLS + B_COLS + C_COLS + N_BRANCH * D_MODEL
PEER_HEADS = 8
N_KEYS = 128
N_EXPERTS = N_KEYS * N_KEYS
PEER_KEY_DIM = 256
PEER_TOPK = 16
PEER_CHUNK = 128
ROPE_BASE = 10000.0
RMS_EPS = 1e-6
N_MOD = 6

kernel_name = "hybrid_diffusion_prefix_trunk_step"


def _rmsnorm(x, g):
    xf = x.astype(jnp.float32)
    y = xf * lax.rsqrt(jnp.mean(xf * xf, axis=-1, keepdims=True) + RMS_EPS)
    return (y * g.astype(jnp.float32)).astype(x.dtype)


def _modulate(x, shift, scale):
    return x * (1 + scale) + shift


def _axial_rope(x):
    T = x.shape[1]
    quarter = x.shape[-1] // 4
    t = jnp.arange(T)
    pos = jnp.stack([t // GRID_W, t % GRID_W], axis=-1).astype(jnp.float32)
    freqs = ROPE_BASE ** (-jnp.arange(quarter, dtype=jnp.float32) / quarter)
    ang = pos[:, :, None] * freqs
    bshape = (T,) + (1,) * (x.ndim - 3) + (2, quarter)
    cos = jnp.cos(ang).reshape(bshape).astype(x.dtype)
    sin = jnp.sin(ang).reshape(bshape).astype(x.dtype)
    xr = x.reshape(x.shape[:-1] + (2, 2, quarter))
    x1 = xr[..., 0, :]
    x2 = xr[..., 1, :]
    out = jnp.stack([x1 * cos - x2 * sin, x2 * cos + x1 * sin], axis=-2)
    return out.reshape(x.shape)


def _dense_attention(q, k, v, sink):
    B, S, KV, G, D = q.shape
    nb = S // Q_BLOCK
    scale = D ** -0.5
    vf = v.astype(jnp.float32)
    qb = jnp.moveaxis(q.reshape(B, nb, Q_BLOCK, KV, G, D), 1, 0)

    def block(qi):
        s = jnp.einsum("bqkgd,bskd->bkgqs", qi, k).astype(jnp.float32) * scale
        m = jnp.max(s, axis=-1)
        if sink is not None:
            sk = sink.astype(jnp.float32)[None, :, :, None]
            m = jnp.maximum(m, sk)
        p = jnp.exp(s - m[..., None])
        den = jnp.sum(p, axis=-1)
        if sink is not None:
            den = den + jnp.exp(sk - m)
        o = jnp.einsum("bkgqs,bskd->bqkgd", p, vf)
        return o / jnp.moveaxis(den, -1, 1)[..., None]

    o = lax.map(block, qb)
    return jnp.moveaxis(o, 0, 1).reshape(B, S, KV * G * D).astype(q.dtype)


def _window_attention(q, k, v, ck, cv, sink):
    B, T, KV, G, D = q.shape
    nb = T // Q_BLOCK
    span = Q_BLOCK + 2 * WINDOW
    scale = D ** -0.5
    pad = ((0, 0), (WINDOW, WINDOW), (0, 0), (0, 0))
    kp = jnp.pad(k, pad)
    vp = jnp.pad(v, pad)
    idx = jnp.arange(nb)[:, None] * Q_BLOCK + jnp.arange(span)[None, :]
    kb = kp[:, idx]
    vb = vp[:, idx].astype(jnp.float32)
    qb = q.reshape(B, nb, Q_BLOCK, KV, G, D)
    s = jnp.einsum("bnqkgd,bnskd->bnkgqs", qb, kb).astype(jnp.float32) * scale
    qpos = jnp.arange(nb)[:, None] * Q_BLOCK + jnp.arange(Q_BLOCK)[None, :]
    kpos = idx - WINDOW
    valid = ((jnp.abs(qpos[:, :, None] - kpos[:, None, :]) <= WINDOW)
             & (kpos >= 0)[:, None, :] & (kpos < T)[:, None, :])
    s = jnp.where(valid[None, :, None, None], s, -jnp.inf)
    sc = jnp.einsum("bnqkgd,bpkd->bnkgqp", qb, ck).astype(jnp.float32) * scale
    sk = sink.astype(jnp.float32)[None, None, :, :, None]
    m = jnp.maximum(jnp.maximum(jnp.max(s, axis=-1), jnp.max(sc, axis=-1)), sk)
    ps = jnp.exp(s - m[..., None])
    pc = jnp.exp(sc - m[..., None])
    den = jnp.sum(ps, axis=-1) + jnp.sum(pc, axis=-1) + jnp.exp(sk - m)
    o = (jnp.einsum("bnkgqs,bnskd->bnqkgd", ps, vb)
         + jnp.einsum("bnkgqp,bpkd->bnqkgd", pc, cv.astype(jnp.float32)))
    o = o / jnp.moveaxis(den, -1, 2)[..., None]
    return o.reshape(B, T, KV * G * D).astype(q.dtype)


def _neighbourhood_attention(q, k, v, ck, cv, rpb):
    B, T, H, D = q.shape
    rows = T // GRID_W
    wr = min(NA_ROWS, rows)
    scale = D ** -0.5
    qg = q.reshape(B, rows, GRID_W, H, D)
    kg = k.reshape(B, rows, GRID_W, H, D)
    vg = v.reshape(B, rows, GRID_W, H, D)
    r = jnp.arange(rows)
    rs = jnp.clip(r - wr // 2, 0, rows - wr)
    row_idx = rs[:, None] + jnp.arange(wr)[None, :]
    k_rows = kg[:, row_idx]
    v_rows = vg[:, row_idx].astype(jnp.float32)
    col = jnp.arange(GRID_W)
    cs = jnp.clip(col - NA_COLS // 2, 0, GRID_W - NA_COLS)
    col_ok = (col[None, :] >= cs[:, None]) & (col[None, :] < cs[:, None] + NA_COLS)
    dr = row_idx - r[:, None] + NA_ROWS - 1
    dc = jnp.clip(col[None, :] - col[:, None], -(NA_COLS - 1), NA_COLS - 1) + NA_COLS - 1
    bias = rpb[:, dr[:, None, :, None], dc[None, :, None, :]].astype(jnp.float32)
    s = jnp.einsum("brchd,brawhd->bhrcaw", qg, k_rows).astype(jnp.float32) * scale + bias[None]
    s = jnp.where(col_ok[None, None, None, :, None, :], s, -jnp.inf)
    sc = jnp.einsum("brchd,bphd->bhrcp", qg, ck).astype(jnp.float32) * scale
    m = jnp.maximum(jnp.max(s, axis=(-2, -1)), jnp.max(sc, axis=-1))
    pn = jnp.exp(s - m[..., None, None])
    pc = jnp.exp(sc - m[..., None])
    den = jnp.sum(pn, axis=(-2, -1)) + jnp.sum(pc, axis=-1)
    o = (jnp.einsum("bhrcaw,brawhd->brchd", pn, v_rows)
         + jnp.einsum("bhrcp,bphd->brchd", pc, cv.astype(jnp.float32)))
    o = o / jnp.transpose(den, (0, 2, 3, 1))[..., None]
    return o.reshape(B, T, H * D).astype(q.dtype)


def _heads(z):
    return z.reshape(z.shape[:2] + (B_HEADS, HEAD_DIM))


def _rwkv7_scan(s0, r, w, k, v, kk, a, reverse):
    xs = tuple(jnp.moveaxis(z, 1, 0) for z in (r, w, k, v, kk, a))

    def step(S, xt):
        rt, wt, kt, vt, kkt, at = xt
        sa = jnp.einsum("bhij,bhj->bhi", S, -kkt)
        S = (S * wt[:, :, None, :] + sa[..., None] * (kkt * at)[:, :, None, :]
             + vt[..., None] * kt[:, :, None, :])
        return S, jnp.einsum("bhij,bhj->bhi", S, rt)

    S, ys = lax.scan(step, s0.astype(jnp.float32), xs, reverse=reverse)
    return jnp.moveaxis(ys, 0, 1), S


def _rwkv7_bidir(pb, lp, s0_f, s0_b):
    B, T, _ = pb.shape
    prev = jnp.pad(pb[:, :-1], ((0, 0), (1, 0), (0, 0)))
    nxt = jnp.pad(pb[:, 1:], ((0, 0), (0, 1), (0, 0)))
    xb = pb + lp["rw_mu"] * (0.5 * (prev + nxt) - pb)
    r, k, v, wd, ad, gd = jnp.split(
        xb, [B_WIDTH, 2 * B_WIDTH, 3 * B_WIDTH, 3 * B_WIDTH + 2 * LORA_W,
             3 * B_WIDTH + 2 * LORA_W + 2 * LORA_A], axis=-1)
    wd = wd.reshape(B, T, 2, LORA_W)
    ad = ad.reshape(B, T, 2, LORA_A)
    wlog = -jax.nn.softplus(-(lp["rw_w0"] + jnp.einsum("btzl,zlc->btzc", jnp.tanh(wd), lp["rw_w2"]))) - 0.5
    decay = jnp.exp(-jnp.exp(wlog.astype(jnp.float32)))
    a = jax.nn.sigmoid(lp["rw_a0"] + jnp.einsum("btzl,zlc->btzc", ad, lp["rw_a2"])).astype(jnp.float32)
    g = (jax.nn.sigmoid(gd) @ lp["rw_g2"]).astype(jnp.float32)
    rf = r.astype(jnp.float32)
    kf = k.astype(jnp.float32)
    vf = v.astype(jnp.float32)
    kk = _heads(kf * lp["rw_kk"])
    kk = kk * lax.rsqrt(jnp.sum(kk * kk, axis=-1, keepdims=True) + 1e-12)
    kd = kf[:, :, None, :] * (1 + (a - 1) * lp["rw_ka"])
    y_f, s_f = _rwkv7_scan(s0_f, _heads(rf), _heads(decay[:, :, 0]), _heads(kd[:, :, 0]),
                           _heads(vf), kk, _heads(a[:, :, 0]), False)
    y_b, s_b = _rwkv7_scan(s0_b, _heads(rf), _heads(decay[:, :, 1]), _heads(kd[:, :, 1]),
                           _heads(vf), kk, _heads(a[:, :, 1]), True)
    y = y_f + y_b
    mu = jnp.mean(y, axis=-1, keepdims=True)
    var = jnp.mean(jnp.square(y - mu), axis=-1, keepdims=True)
    y = ((y - mu) * lax.rsqrt(var + GN_EPS)).reshape(B, T, B_WIDTH) * lp["rw_lnx_g"] + lp["rw_lnx_b"]
    bonus = jnp.sum(_heads(rf * (kd[:, :, 0] + kd[:, :, 1])) * lp["rw_rk"], axis=-1, keepdims=True) * _heads(vf)
    y = (y + bonus.reshape(B, T, B_WIDTH)) * g
    return y.astype(pb.dtype), s_f, s_b


def _mixer(h, lp, ctx):
    B, T, _ = h.shape
    p = h @ lp["w_in"]
    pa, pb, pc, pg = jnp.split(p, [A_COLS, A_COLS + B_COLS, A_COLS + B_COLS + C_COLS], axis=-1)
    qa, ka, va = jnp.split(pa, [A_HEADS * HEAD_DIM, (A_HEADS + A_KV) * HEAD_DIM], axis=-1)
    qa = qa.reshape(B, T, A_KV, A_GROUP, HEAD_DIM)
    ka = ka.reshape(B, T, A_KV, HEAD_DIM)
    va = va.reshape(B, T, A_KV, HEAD_DIM)
    qc, kc, vc = jnp.split(pc, 3, axis=-1)
    qc = qc.reshape(B, T, C_HEADS, HEAD_DIM)
    kc = kc.reshape(B, T, C_HEADS, HEAD_DIM)
    vc = vc.reshape(B, T, C_HEADS, HEAD_DIM)
    sink = lp["a_sink"].reshape(A_KV, A_GROUP)
    if ctx is None:
        ya = _dense_attention(qa, ka, va, sink)
        yc = _dense_attention(qc[:, :, :, None, :], kc, vc, None)
        s0_f = jnp.zeros((B, B_HEADS, HEAD_DIM, HEAD_DIM), jnp.float32)
        s0_b = s0_f
    else:
        ctx_ak, ctx_av, ctx_ck, ctx_cv, s0 = ctx
        ya = _window_attention(_axial_rope(qa), _axial_rope(ka), va, ctx_ak, ctx_av, sink)
        yc = _neighbourhood_attention(qc, kc, vc, ctx_ck, ctx_cv, lp["na_rpb"])
        s0_f = s0[:, 0]
        s0_b = s0[:, 1]
    yb, s_f, s_b = _rwkv7_bidir(pb, lp, s0_f, s0_b)
    gates = jax.nn.sigmoid(pg.reshape(B, T, N_BRANCH, D_MODEL))
    merged = (gates[:, :, 0] * (ya @ lp["a_out"]) + gates[:, :, 1] * (yb @ lp["rw_out"])
              + gates[:, :, 2] * (yc @ lp["na_out"]))
    out = merged @ lp["w_o"]
    if ctx is None:
        return out, (ka, va, kc, vc, jnp.stack([s_f, s_b], axis=1).astype(h.dtype))
    return out, None


def _peer(h, q_w, subkeys, u_tab, v_tab):
    B, T, D = h.shape
    hc_all = h.reshape((B * T) // PEER_CHUNK, PEER_CHUNK, D)

    def chunk(hc):
        q = (hc @ q_w).reshape(PEER_CHUNK, PEER_HEADS, 2, PEER_KEY_DIM // 2)
        s = jnp.einsum("chzk,hznk->chzn", q, subkeys).astype(jnp.float32)
        v1, i1 = lax.top_k(s[:, :, 0], PEER_TOPK)
        v2, i2 = lax.top_k(s[:, :, 1], PEER_TOPK)
        cand = (v1[..., :, None] + v2[..., None, :]).reshape(PEER_CHUNK, PEER_HEADS, PEER_TOPK * PEER_TOPK)
        sv, si = lax.top_k(cand, PEER_TOPK)
        e = (jnp.take_along_axis(i1, si // PEER_TOPK, axis=-1) * N_KEYS
             + jnp.take_along_axis(i2, si % PEER_TOPK, axis=-1))
        gate = jax.nn.softmax(sv, axis=-1)
        u = jnp.take(u_tab, e, axis=0)
        act = jax.nn.gelu(jnp.einsum("cd,chkd->chk", hc, u).astype(jnp.float32))
        vv = jnp.take(v_tab, e, axis=0)
        return jnp.einsum("chk,chkd->cd", (gate * act).astype(hc.dtype), vv)

    return lax.map(chunk, hc_all).reshape(B, T, D)


def setup_inputs(seed: int = 0) -> dict:
    key = jax.random.key(seed)
    ks = iter(jax.random.split(key, 48))

    def nrm(shape, s):
        return jax.random.normal(next(ks), shape, jnp.float32) * s

    L, D = DEPTH, D_MODEL
    return {
        "x_prompt": nrm((BATCH, SEQ, D), 1.0),
        "x_sample": nrm((DEC_BATCH, DEC_SEQ, D), 1.0),
        "cache_a_k": nrm((DEC_BATCH, L, PAST_LEN, A_KV, HEAD_DIM), 1.0),
        "cache_a_v": nrm((DEC_BATCH, L, PAST_LEN, A_KV, HEAD_DIM), 1.0),
        "cache_c_k": nrm((DEC_BATCH, L, PAST_LEN, C_HEADS, HEAD_DIM), 1.0),
        "cache_c_v": nrm((DEC_BATCH, L, PAST_LEN, C_HEADS, HEAD_DIM), 1.0),
        "state_rwkv": nrm((DEC_BATCH, L, 2, B_HEADS, HEAD_DIM, HEAD_DIM), 0.5),
        "c": nrm((DEC_BATCH, D), 1.0),
        "c_ctx": nrm((D,), 1.0),
        "ln1_g": 1.0 + nrm((L, D), 0.02),
        "ln2_g": 1.0 + nrm((L, D), 0.02),
        "lnf_g": 1.0 + nrm((D,), 0.02),
        "ada_w": nrm((L, D, N_MOD * D), 0.5 * D ** -0.5),
        "ada_b": nrm((L, N_MOD * D), 0.02),
        "w_in": nrm((L, D, IN_COLS), D ** -0.5),
        "a_sink": nrm((L, A_HEADS), 1.0),
        "a_out": nrm((L, A_HEADS * HEAD_DIM, D), (A_HEADS * HEAD_DIM) ** -0.5),
        "rw_mu": jax.random.uniform(next(ks), (L, B_COLS), jnp.float32),
        "rw_w0": nrm((L, 2, B_WIDTH), 0.5),
        "rw_w2": nrm((L, 2, LORA_W, B_WIDTH), 0.5 * LORA_W ** -0.5),
        "rw_a0": nrm((L, 2, B_WIDTH), 0.5),
        "rw_a2": nrm((L, 2, LORA_A, B_WIDTH), 0.5 * LORA_A ** -0.5),
        "rw_g2": nrm((L, LORA_G, B_WIDTH), LORA_G ** -0.5),
        "rw_kk": 1.0 + nrm((L, B_WIDTH), 0.1),
        "rw_ka": 1.0 + nrm((L, B_WIDTH), 0.1),
        "rw_rk": nrm((L, B_HEADS, HEAD_DIM), 0.1),
        "rw_lnx_g": 1.0 + nrm((L, B_WIDTH), 0.02),
        "rw_lnx_b": nrm((L, B_WIDTH), 0.02),
        "rw_out": nrm((L, B_WIDTH, D), B_WIDTH ** -0.5),
        "na_rpb": nrm((L, C_HEADS, 2 * NA_ROWS - 1, 2 * NA_COLS - 1), 0.5),
        "na_out": nrm((L, C_HEADS * HEAD_DIM, D), (C_HEADS * HEAD_DIM) ** -0.5),
        "w_o": nrm((L, D, D), D ** -0.5),
        "pe_q": nrm((L, D, PEER_HEADS * PEER_KEY_DIM), D ** -0.5),
        "pe_subkeys": nrm((L, PEER_HEADS, 2, N_KEYS, PEER_KEY_DIM // 2), (PEER_KEY_DIM // 2) ** -0.5),
        "pe_u": nrm((L, N_EXPERTS, D), D ** -0.5),
        "pe_v": nrm((L, N_EXPERTS, D), (PEER_HEADS * PEER_TOPK) ** -0.5),
    }


def reference(x_prompt, x_sample, cache_a_k, cache_a_v, cache_c_k, cache_c_v, state_rwkv,
              c, c_ctx, ln1_g, ln2_g, lnf_g, ada_w, ada_b, w_in, a_sink, a_out,
              rw_mu, rw_w0, rw_w2, rw_a0, rw_a2, rw_g2, rw_kk, rw_ka, rw_rk,
              rw_lnx_g, rw_lnx_b, rw_out, na_rpb, na_out, w_o,
              pe_q, pe_subkeys, pe_u, pe_v):
    xp = x_prompt
    xs = x_sample
    silu_ctx = jax.nn.silu(c_ctx)[None, :]
    silu_lat = jax.nn.silu(c)
    list_ak, list_av, list_ck, list_cv, list_st = [], [], [], [], []
    for l in range(DEPTH):
        lp = {
            "w_in": w_in[l], "a_sink": a_sink[l], "a_out": a_out[l],
            "rw_mu": rw_mu[l], "rw_w0": rw_w0[l], "rw_w2": rw_w2[l], "rw_a0": rw_a0[l],
            "rw_a2": rw_a2[l], "rw_g2": rw_g2[l], "rw_kk": rw_kk[l], "rw_ka": rw_ka[l],
            "rw_rk": rw_rk[l], "rw_lnx_g": rw_lnx_g[l], "rw_lnx_b": rw_lnx_b[l],
            "rw_out": rw_out[l], "na_rpb": na_rpb[l], "na_out": na_out[l], "w_o": w_o[l],
        }
        mod_p = (silu_ctx @ ada_w[l] + ada_b[l]).reshape(1, 1, N_MOD, D_MODEL)
        mod_s = (silu_lat @ ada_w[l] + ada_b[l]).reshape(-1, 1, N_MOD, D_MODEL)

        h = _modulate(_rmsnorm(xp, ln1_g[l]), mod_p[:, :, 0], mod_p[:, :, 1])
        o, (ak, av, ck, cv, st) = _mixer(h, lp, None)
        xp = xp + mod_p[:, :, 2] * o
        h = _modulate(_rmsnorm(xp, ln2_g[l]), mod_p[:, :, 3], mod_p[:, :, 4])
        xp = xp + mod_p[:, :, 5] * _peer(h, pe_q[l], pe_subkeys[l], pe_u[l], pe_v[l])
        list_ak.append(ak)
        list_av.append(av)
        list_ck.append(ck)
        list_cv.append(cv)
        list_st.append(st)

        ctx = (cache_a_k[:, l], cache_a_v[:, l], cache_c_k[:, l], cache_c_v[:, l], state_rwkv[:, l])
        h = _modulate(_rmsnorm(xs, ln1_g[l]), mod_s[:, :, 0], mod_s[:, :, 1])
        o, _ = _mixer(h, lp, ctx)
        xs = xs + mod_s[:, :, 2] * o
        h = _modulate(_rmsnorm(xs, ln2_g[l]), mod_s[:, :, 3], mod_s[:, :, 4])
        xs = xs + mod_s[:, :, 5] * _peer(h, pe_q[l], pe_subkeys[l], pe_u[l], pe_v[l])

    y_prompt = _rmsnorm(xp, lnf_g)
    y_sample = _rmsnorm(xs, lnf_g)
    new_a_k = jnp.stack(list_ak, axis=1)
    new_a_v = jnp.stack(list_av, axis=1)
    new_c_k = jnp.stack(list_ck, axis=1)
    new_c_v = jnp.stack(list_cv, axis=1)
    new_state_rwkv = jnp.stack(list_st, axis=1)
    return (y_prompt, y_sample, new_a_k, new_a_v, new_c_k, new_c_v, new_state_rwkv)
```

```python
import numpy as np
from contextlib import ExitStack
import concourse.bass as bass
import concourse.mybir as mybir
from concourse.bass_utils import run_bass_kernel_spmd

F32 = mybir.dt.float32
I32 = mybir.dt.int32
U32 = mybir.dt.uint32
AF = mybir.ActivationFunctionType
ALU = mybir.AluOpType
AX = mybir.AxisListType

NDS = 40
SAME_ENGINE_SYNC = True

D = 1024
L = 4
TOK = 2560
NPT = 512
TS = 2048
IN_COLS = 7296
PT_COLS = 3200
PF_ROWS = 4736
SCALE = 0.125
NEG = -30000.0


class Res:
    __slots__ = ("w", "r", "ap", "name")

    def __init__(self, ap=None, name=None):
        self.w = None
        self.r = []
        self.ap = ap
        self.name = name


class KB:
    def __init__(self, nc):
        self.nc = nc
        self.eng = {"pe": nc.tensor, "dve": nc.vector, "act": nc.scalar, "pool": nc.gpsimd, "sp": nc.sync}
        self.esem = {k: nc.alloc_semaphore("es_" + k) for k in self.eng}
        self.ecnt = {k: 0 for k in self.eng}
        self.seen = {k: {} for k in self.eng}
        self.dsems = [nc.alloc_semaphore("ds%d" % i) for i in range(NDS)]
        self.dcnt = [0] * NDS
        self.dnext = 0
        self.nins = 0
        self.uid = 0
        self.rr = 0

    def sb(self, st, name, shape, dt=F32):
        self.uid += 1
        t = st.enter_context(self.nc.sbuf_tensor("%s_%d" % (name, self.uid), list(shape), dt))
        return Res(t.ap(), name)

    def ps(self, name, shape, dt=F32):
        t = self.nc.alloc_psum_tensor(name, list(shape), dt)
        return Res(t.ap(), name)

    def _wait(self, e, ev):
        sem, val = ev
        own = sem is self.esem[e]
        if own and (e == "pe" or not SAME_ENGINE_SYNC):
            return
        key = id(sem)
        if self.seen[e].get(key, 0) >= val:
            return
        self.eng[e].wait_ge(sem, val)
        self.seen[e][key] = val
        self.nins += 1

    def deps(self, e, reads, writes):
        for r in reads:
            if r.w is not None:
                self._wait(e, r.w)
        for w in writes:
            if w.w is not None:
                self._wait(e, w.w)
            for ev in w.r:
                self._wait(e, ev)

    def _record(self, ev, reads, writes):
        for r in reads:
            r.r.append(ev)
            if len(r.r) > 16:
                d = {}
                for s, v in r.r:
                    if id(s) not in d or d[id(s)][1] < v:
                        d[id(s)] = (s, v)
                r.r = list(d.values())
        for w in writes:
            w.w = ev
            w.r = []

    def op(self, e, fn, reads, writes):
        self.deps(e, reads, writes)
        ins = fn(self.eng[e])
        self.ecnt[e] += 1
        ins.then_inc(self.esem[e], 1)
        self.nins += 1
        self._record((self.esem[e], self.ecnt[e]), reads, writes)

    def _dma_common(self, q, reads, writes, emit):
        kk = self.dnext
        self.dnext = (kk + 1) % NDS
        sem = self.dsems[kk]
        if self.dcnt[kk] > 0:
            self._wait(q, (sem, self.dcnt[kk]))
        self.deps(q, reads, writes)
        ins = emit()
        self.dcnt[kk] += 16
        ins.then_inc(sem, 16)
        self.nins += 1
        self._record((sem, self.dcnt[kk]), reads, writes)

    def dma(self, q, out, in_, reads, writes, **kw):
        if q is None:
            q = ("sp", "act")[self.rr % 2]
            self.rr += 1
        self._dma_common(q, reads, writes, lambda: self.eng[q].dma_start(out=out, in_=in_, **kw))

    def idma(self, out, in_, off_ap, reads, writes):
        self._dma_common("pool", reads, writes, lambda: self.nc.gpsimd.indirect_dma_start(
            out=out, out_offset=None, in_=in_,
            in_offset=bass.IndirectOffsetOnAxis(ap=off_ap, axis=0)))

    def barrier(self):
        engs = ("pe", "dve", "act", "pool", "sp")
        for e in engs:
            for kk in range(NDS):
                if self.dcnt[kk] > 0:
                    self._wait(e, (self.dsems[kk], self.dcnt[kk]))
            for e2 in engs:
                if e2 != e and self.ecnt[e2] > 0:
                    self._wait(e, (self.esem[e2], self.ecnt[e2]))

    def finish(self):
        self.barrier()


class Ctx:
    pass


def bc(ap, shape):
    return ap.to_broadcast(list(shape))


def build(nlayers=L, debug=False, stages=("all",)):
    nc = bass.Bass("TRN2", target_bir_lowering=False)
    k = KB(nc)
    C = Ctx()
    C.k = k
    C.nc = nc
    C.debug = debug

    def din(name, shape, dt=F32):
        return nc.dram_tensor(name, list(shape), dt, kind="ExternalInput").ap()

    def dout(name, shape, dt=F32):
        return nc.dram_tensor(name, list(shape), dt, kind="ExternalOutput").ap()

    def scr(name, shape, dt=F32):
        if debug:
            return nc.dram_tensor(name, list(shape), dt, kind="ExternalOutput").ap()
        return nc.dram_tensor(name, list(shape), dt).ap()

    I = Ctx()
    C.I = I
    I.xall = din("xall", [TOK, D])
    I.cvec = din("cvec", [2, D])
    I.cak = din("cak", [L, 512, 128])
    I.cav = din("cav", [L, 512, 128])
    I.cck = din("cck", [L, 512, 512])
    I.ccv = din("ccv", [L, 512, 512])
    I.st0 = din("st0", [L, 128, 512])
    I.ln1 = din("ln1_g", [L, D])
    I.ln2 = din("ln2_g", [L, D])
    I.lnf = din("lnf_g", [D])
    I.ada_w = din("ada_w", [L, D, 6 * D])
    I.ada_b = din("ada_b", [L, 6 * D])
    I.w_in = din("w_in", [L, D, IN_COLS])
    I.a_sink = din("a_sink", [L, 8])
    I.a_out = din("a_out", [L, 512, D])
    I.rw_mu = din("rw_mu", [L, 1920])
    I.rw_w0 = din("rw_w0", [L, 1024])
    I.rw_w2 = din("rw_w2", [L, 128, 512])
    I.rw_a0 = din("rw_a0", [L, 1024])
    I.rw_a2 = din("rw_a2", [L, 128, 512])
    I.rw_g2 = din("rw_g2", [L, 128, 512])
    I.rw_kk = din("rw_kk", [L, 512])
    I.rw_ka = din("rw_ka", [L, 512])
    I.rw_rk = din("rw_rk", [L, 512])
    I.rw_lnx_g = din("rw_lnx_g", [L, 512])
    I.rw_lnx_b = din("rw_lnx_b", [L, 512])
    I.rw_out = din("rw_out", [L, 512, D])
    I.na_rpb = din("na_rpb", [L, 8, 15, 31])
    I.na_out = din("na_out", [L, 512, D])
    I.w_o = din("w_o", [L, D, D])
    I.pe_q = din("pe_q", [L, D, 2048])
    I.pe_sk = din("pe_subkeys", [L, 16, 128, 128])
    I.pe_u = din("pe_u", [L * 16384, D])
    I.pe_v = din("pe_v", [L * 16384, D])
    I.ident = din("c_ident", [128, 128])
    I.antiid = din("c_antiid", [128, 128])
    I.rot = din("c_rot", [128, 128])
    I.cos = din("c_cos", [128, TS])
    I.sin = din("c_sin", [128, TS])
    I.mleft = din("c_mleft", [128, 128])
    I.mright = din("c_mright", [128, 128])
    I.colmask = din("c_colmask", [64, 64])
    I.blk64 = din("c_blk64", [128, 128])
    I.iota256 = din("c_iota", [128, 256])

    O = Ctx()
    C.O = O
    O.y = dout("y", [TOK, D])
    O.nak = dout("nak", [2, L, 256, 128])
    O.nav = dout("nav", [2, L, 256, 128])
    O.nck = dout("nck", [2, L, 256, 512])
    O.ncv = dout("ncv", [2, L, 256, 512])
    O.nst = dout("nst", [2, L, 64, 1024])

    S = Ctx()
    C.S = S
    S.PT = scr("s_PT", [TOK, PT_COLS])
    S.PF = scr("s_PF", [PF_ROWS, TOK])
    S.YA = scr("s_YA", [512, TOK])
    S.YB = scr("s_YB", [512, TOK])
    S.YC = scr("s_YC", [512, TOK])
    S.QR = scr("s_QR", [640, TS])
    S.E = scr("s_E", [1, 120, 127])
    S.OPS = scr("s_OPS", [2, TOK, 3072])
    S.YS = scr("s_YS", [2, TOK, 512])
    S.BG = scr("s_BG", [TOK, 1024])
    S.rOPS = Res(); S.rYS = Res(); S.rBG = Res()
    S.rPT = Res(); S.rPF = Res(); S.rYA = Res(); S.rYB = Res(); S.rYC = Res(); S.rQR = Res(); S.rE = Res()

    C.PS = [k.ps("psb%d" % i, [128, 512]) for i in range(8)]

    with ExitStack() as gst:
        C.xT = k.sb(gst, "xT", [128, 8, TOK])
        C.ident = k.sb(gst, "ident", [128, 128])
        C.ones = k.sb(gst, "ones", [128, 128])
        C.eps = k.sb(gst, "eps", [128, 1])
        C.scT = k.sb(gst, "scT", [128, 8, 2])
        C.modT = k.sb(gst, "modT", [128, 48, 2])
        C.gs1 = k.sb(gst, "gs1", [128, 8, 2])
        C.gs2 = k.sb(gst, "gs2", [128, 8, 2])
        k.dma("sp", C.ident.ap[:], I.ident[:, :], [], [C.ident])
        k.op("dve", lambda e: e.memset(C.ones.ap[:], 1.0), [], [C.ones])
        k.op("dve", lambda e: e.memset(C.eps.ap[:], 1e-6), [], [C.eps])

        phase_init(C)
        for l in range(nlayers):
            phase_mod(C, l)
            for tg in range(5):
                phase_norm_win(C, l, tg)
            k.barrier()
            phase_kv_out(C, l)
            if "attn" in stages or "all" in stages:
                phase_attn(C, l)
            if "rwkv" in stages or "all" in stages:
                phase_rwkv(C, l)
            if "merge" in stages or "all" in stages:
                phase_merge(C, l)
            if "peer" in stages or "all" in stages:
                phase_peer(C, l)
        phase_final(C)
        k.finish()
    C.nins = k.nins
    return nc, C


def phase_init(C):
    k, I = C.k, C.I
    with ExitStack() as st:
        xt = [k.sb(st, "xin%d" % i, [128, D]) for i in range(2)]
        for t in range(TOK // 128):
            x = xt[t % 2]
            k.dma(None, x.ap[:], I.xall[t * 128:(t + 1) * 128, :], [], [x])
            for half in range(2):
                ps = C.PS[(t * 2 + half) % 8]
                for j in range(4):
                    kc = half * 4 + j
                    k.op("pe", lambda e: e.transpose(ps.ap[:, j * 128:(j + 1) * 128], x.ap[:, kc * 128:(kc + 1) * 128], C.ident.ap[:]), [x, C.ident], [ps])
                eng = ("dve", "act")[half]
                if eng == "dve":
                    k.op("dve", lambda e: e.tensor_copy(C.xT.ap[:, half * 4:half * 4 + 4, t * 128:(t + 1) * 128], ps.ap[:].rearrange("p (a b) -> p a b", a=4)), [ps], [C.xT])
                else:
                    k.op("act", lambda e: e.copy(C.xT.ap[:, half * 4:half * 4 + 4, t * 128:(t + 1) * 128], ps.ap[:].rearrange("p (a b) -> p a b", a=4)), [ps], [C.xT])
        for g in range(2):
            k.dma("sp", C.scT.ap[:, :, g], I.cvec[g].rearrange("(c p) -> p c", p=128), [], [C.scT], allow_slow_non_contiguous=True)
        k.op("act", lambda e: e.activation(out=C.scT.ap[:], in_=C.scT.ap[:], func=AF.Silu), [C.scT], [C.scT])
        k.barrier()


def phase_mod(C, l):
    k, I = C.k, C.I
    with ExitStack() as st:
        wb = [k.sb(st, "adaw%d" % i, [128, 8, 512]) for i in range(2)]
        abT = k.sb(st, "abT", [128, 48])
        lnT = k.sb(st, "lnT", [128, 2, 8])
        k.dma("sp", abT.ap[:], I.ada_b[l].rearrange("(c p) -> p c", p=128), [], [abT], allow_slow_non_contiguous=True)
        k.dma("sp", lnT.ap[:, 0, :], I.ln1[l].rearrange("(c p) -> p c", p=128), [], [lnT], allow_slow_non_contiguous=True)
        k.dma("sp", lnT.ap[:, 1, :], I.ln2[l].rearrange("(c p) -> p c", p=128), [], [lnT], allow_slow_non_contiguous=True)
        ps = C.PS[0]
        for b in range(12):
            w = wb[b % 2]
            k.dma(None, w.ap[:], I.ada_w[l][:, b * 512:(b + 1) * 512].rearrange("(c p) n -> p c n", p=128), [], [w])
            for sub in range(4):
                fc = b * 4 + sub
                for kc in range(8):
                    k.op("pe", lambda e: e.matmul(ps.ap[:, fc * 2:fc * 2 + 2], lhsT=w.ap[:, kc, sub * 128:(sub + 1) * 128], rhs=C.scT.ap[:, kc, :], start=(kc == 0), stop=(kc == 7)), [w, C.scT], [ps])
        k.op("dve", lambda e: e.tensor_tensor(out=C.modT.ap[:], in0=ps.ap[:, 0:96].rearrange("p (a b) -> p a b", b=2), in1=bc(abT.ap[:].unsqueeze(2), [128, 48, 2]), op=ALU.add), [ps, abT], [C.modT])
        for (gs, mi, li) in ((C.gs1, 1, 0), (C.gs2, 4, 1)):
            k.op("dve", lambda e: e.tensor_scalar(gs.ap[:], C.modT.ap[:, mi * 8:mi * 8 + 8, :], 1.0, None, op0=ALU.add), [C.modT], [gs])
            k.op("dve", lambda e: e.tensor_tensor(out=gs.ap[:], in0=gs.ap[:], in1=bc(lnT.ap[:, li, :].unsqueeze(2), [128, 8, 2]), op=ALU.mult), [gs, lnT], [gs])
        k.barrier()


JOBS = [
    (0, 512, "F", 0), (512, 128, "F", 512), (2688, 512, "F", 640), (3200, 512, "F", 1152),
    (4224, 512, "F", 1664), (4736, 512, "F", 2176), (5248, 512, "F", 2688), (5760, 512, "F", 3200),
    (6272, 512, "F", 3712), (6784, 512, "F", 4224),
    (512, 256, "T", 0), (768, 512, "T", 256), (1280, 512, "T", 768), (1792, 512, "T", 1280), (2304, 384, "T", 1792),
    (3200, 512, "T", 2176), (3712, 512, "T", 2688),
]


def norm_group(C, st, tg, gs, shift_idx, hT):
    k = C.k
    g = 0 if tg == 0 else 1
    cols = slice(tg * 512, (tg + 1) * 512)
    sq = [k.sb(st, "sq%d" % i, [128, 512]) for i in range(2)]
    rstd = k.sb(st, "rstd", [128, 512])
    ps = C.PS[7]
    for kc in range(8):
        s = sq[kc % 2]
        k.op("act", lambda e: e.activation(out=s.ap[:], in_=C.xT.ap[:, kc, cols], func=AF.Square), [C.xT], [s])
        k.op("pe", lambda e: e.matmul(ps.ap[:], lhsT=C.ones.ap[:], rhs=s.ap[:], start=(kc == 0), stop=(kc == 7)), [s, C.ones], [ps])
    k.op("act", lambda e: e.activation(out=rstd.ap[:], in_=ps.ap[:], func=AF.Sqrt, scale=1.0 / D, bias=C.eps.ap[:, 0:1]), [ps, C.eps], [rstd])
    k.op("dve", lambda e: e.reciprocal(rstd.ap[:], rstd.ap[:]), [rstd], [rstd])
    for kc in range(8):
        eng = "dve" if kc % 2 == 0 else "pool"
        k.op(eng, lambda e: e.tensor_tensor(out=hT.ap[:, kc, :], in0=C.xT.ap[:, kc, cols], in1=rstd.ap[:], op=ALU.mult), [C.xT, rstd], [hT])
        k.op(eng, lambda e: e.tensor_scalar(hT.ap[:, kc, :], hT.ap[:, kc, :], gs.ap[:, kc, g:g + 1], C.modT.ap[:, shift_idx * 8 + kc, g:g + 1], op0=ALU.mult, op1=ALU.add), [hT, gs, C.modT], [hT])


def phase_norm_win(C, l, tg):
    k, I, S = C.k, C.I, C.S
    with ExitStack() as st:
        hT = k.sb(st, "hT", [128, 8, 512])
        norm_group(C, st, tg, C.gs1, 0, hT)
        wb = [k.sb(st, "winw%d" % i, [128, 8, 512]) for i in range(2)]
        ev = [k.sb(st, "winev%d" % i, [128, 512]) for i in range(4)]
        ei = 0
        pi = 0
        for ji, (c0, n, lay, d0) in enumerate(JOBS):
            w = wb[ji % 2]
            k.dma(None, w.ap[:, :, 0:n], I.w_in[l][:, c0:c0 + n].rearrange("(c p) n -> p c n", p=128), [], [w])
            if lay == "F":
                for sub in range(n // 128):
                    ps = C.PS[pi % 6]; pi += 1
                    for kc in range(8):
                        k.op("pe", lambda e: e.matmul(ps.ap[:], lhsT=w.ap[:, kc, sub * 128:(sub + 1) * 128], rhs=hT.ap[:, kc, :], start=(kc == 0), stop=(kc == 7)), [w, hT], [ps])
                    o = ev[ei % 4]; ei += 1
                    if d0 >= 1664:
                        k.op("act", lambda e: e.activation(out=o.ap[:], in_=ps.ap[:], func=AF.Sigmoid), [ps], [o])
                    elif ei % 2 == 0:
                        k.op("act", lambda e: e.copy(o.ap[:], ps.ap[:]), [ps], [o])
                    else:
                        k.op("dve", lambda e: e.tensor_copy(o.ap[:], ps.ap[:]), [ps], [o])
                    r0 = d0 + sub * 128
                    k.dma(None, S.PF[r0:r0 + 128, tg * 512:(tg + 1) * 512], o.ap[:], [o], [S.rPF])
            else:
                for tt in range(4):
                    ps = C.PS[pi % 6]; pi += 1
                    for kc in range(8):
                        k.op("pe", lambda e: e.matmul(ps.ap[:, 0:n], lhsT=hT.ap[:, kc, tt * 128:(tt + 1) * 128], rhs=w.ap[:, kc, 0:n], start=(kc == 0), stop=(kc == 7)), [w, hT], [ps])
                    o = ev[ei % 4]; ei += 1
                    if ei % 2 == 0:
                        k.op("act", lambda e: e.copy(o.ap[:, 0:n], ps.ap[:, 0:n]), [ps], [o])
                    else:
                        k.op("dve", lambda e: e.tensor_copy(o.ap[:, 0:n], ps.ap[:, 0:n]), [ps], [o])
                    t0 = tg * 512 + tt * 128
                    k.dma(None, S.PT[t0:t0 + 128, d0:d0 + n], o.ap[:, 0:n], [o], [S.rPT])
        k.barrier()


def phase_kv_out(C, l):
    k, S, O = C.k, C.S, C.O
    for s in range(2):
        rows = slice(s * 256, (s + 1) * 256)
        k.dma(None, O.nak[s, l], S.PT[rows, 0:128], [S.rPT], [])
        k.dma(None, O.nav[s, l], S.PT[rows, 128:256], [S.rPT], [])
        k.dma(None, O.nck[s, l], S.PT[rows, 2176:2688], [S.rPT], [])
        k.dma(None, O.ncv[s, l], S.PT[rows, 2688:3200], [S.rPT], [])


def attn_unit(C, W, ui, qT, qres, Mq, segs, vch, sink_h, out_ap, out_res):
    k = C.k
    par = ui % 2
    S_sb = W.S[par]
    sm = W.sm[par]
    tot = sum(s[2] for s in segs)
    off = 0
    k.op("pool", lambda e: e.memset(sm.ap[:], 0.0), [], [sm])
    for si, (kres, kT, n, masks) in enumerate(segs):
        ps = C.PS[par * 2 + si]
        k.op("pe", lambda e: e.matmul(ps.ap[:Mq, 0:n], lhsT=qT, rhs=kT, start=True, stop=True), [qres, kres], [ps])
        covered = []
        for (c0, ncol, mres, map_) in masks:
            k.op("dve", lambda e: e.tensor_tensor(out=S_sb.ap[:Mq, off + c0:off + c0 + ncol], in0=ps.ap[:Mq, c0:c0 + ncol], in1=map_, op=ALU.add), [ps, mres], [S_sb])
            covered.append((c0, c0 + ncol))
        covered.sort()
        pos = 0
        gaps = []
        for (a, b) in covered:
            if a > pos:
                gaps.append((pos, a))
            pos = max(pos, b)
        if pos < n:
            gaps.append((pos, n))
        for gi, (a, b) in enumerate(gaps):
            if (ui + gi + si) % 2 == 0:
                k.op("act", lambda e: e.copy(S_sb.ap[:Mq, off + a:off + b], ps.ap[:Mq, a:b]), [ps], [S_sb])
            else:
                k.op("dve", lambda e: e.tensor_copy(S_sb.ap[:Mq, off + a:off + b], ps.ap[:Mq, a:b]), [ps], [S_sb])
        off += n
    k.op("dve", lambda e: e.reduce_max(out=sm.ap[:Mq, 0:1], in_=S_sb.ap[:Mq, 0:tot], axis=AX.X), [S_sb], [sm])
    if sink_h is not None:
        k.op("dve", lambda e: e.tensor_scalar(sm.ap[:Mq, 1:2], sm.ap[:Mq, 0:1], -SCALE, W.nsink.ap[:Mq, sink_h:sink_h + 1], op0=ALU.mult, op1=ALU.min), [sm, W.nsink], [sm])
    else:
        k.op("dve", lambda e: e.tensor_scalar(sm.ap[:Mq, 1:2], sm.ap[:Mq, 0:1], -SCALE, None, op0=ALU.mult), [sm], [sm])
    k.op("act", lambda e: e.activation(out=S_sb.ap[:Mq, 0:tot], in_=S_sb.ap[:Mq, 0:tot], func=AF.Exp, bias=sm.ap[:Mq, 1:2], scale=SCALE, accum_out=sm.ap[:Mq, 2:3]), [S_sb, sm], [S_sb, sm])
    if sink_h is not None:
        k.op("act", lambda e: e.activation(out=sm.ap[:Mq, 3:4], in_=W.sink.ap[:Mq, sink_h:sink_h + 1], func=AF.Exp, bias=sm.ap[:Mq, 1:2], scale=1.0), [sm, W.sink], [sm])
        k.op("dve", lambda e: e.tensor_tensor(out=sm.ap[:Mq, 2:3], in0=sm.ap[:Mq, 2:3], in1=sm.ap[:Mq, 3:4], op=ALU.add), [sm], [sm])
    k.op("dve", lambda e: e.reciprocal(sm.ap[:Mq, 4:5], sm.ap[:Mq, 2:3]), [sm], [sm])
    k.op("dve", lambda e: e.tensor_scalar(S_sb.ap[:Mq, 0:tot], S_sb.ap[:Mq, 0:tot], sm.ap[:Mq, 4:5], None, op0=ALU.mult), [S_sb, sm], [S_sb])
    pso = C.PS[6 + par]
    nch = len(vch)
    for g0 in range(0, nch, 4):
        grp = vch[g0:g0 + 4]
        gi = W.tcount
        W.tcount += 1
        pst = C.PS[4 + gi % 2]
        ptsb = W.PTs[gi % 2]
        maxnk = max(v[2] for v in grp)
        for jj, (vres, vap, nk, c0) in enumerate(grp):
            k.op("pe", lambda e: e.transpose(pst.ap[:nk, jj * 128:jj * 128 + Mq], S_sb.ap[:Mq, c0:c0 + nk], C.ident.ap[:Mq, :Mq]), [S_sb, C.ident], [pst])
        w = len(grp) * 128
        if gi % 2 == 0:
            k.op("dve", lambda e: e.tensor_copy(ptsb.ap[:maxnk, 0:w], pst.ap[:maxnk, 0:w]), [pst], [ptsb])
        else:
            k.op("act", lambda e: e.copy(ptsb.ap[:maxnk, 0:w], pst.ap[:maxnk, 0:w]), [pst], [ptsb])
        for jj, (vres, vap, nk, c0) in enumerate(grp):
            ci = g0 + jj
            k.op("pe", lambda e: e.matmul(pso.ap[:64, 0:Mq], lhsT=vap, rhs=ptsb.ap[:nk, jj * 128:jj * 128 + Mq], start=(ci == 0), stop=(ci == nch - 1)), [vres, ptsb], [pso])
    osb = W.o[par]
    k.op("act", lambda e: e.copy(osb.ap[:64, 0:Mq], pso.ap[:64, 0:Mq]), [pso], [osb])
    k.dma(None, out_ap, osb.ap[:64, 0:Mq], [osb], [out_res])


def attn_work(C, st, l, need_sink):
    k, I = C.k, C.I
    W = Ctx()
    W.S = [k.sb(st, "S_sb%d" % i, [128, 1024]) for i in range(2)]
    W.sm = [k.sb(st, "sm%d" % i, [128, 8]) for i in range(2)]
    W.PTs = [k.sb(st, "PTs%d" % i, [128, 512]) for i in range(2)]
    W.o = [k.sb(st, "osb%d" % i, [64, 128]) for i in range(2)]
    W.tcount = 0
    if need_sink:
        W.sink = k.sb(st, "sink", [128, 8])
        W.nsink = k.sb(st, "nsink", [128, 8])
        k.dma("sp", W.sink.ap[:], I.a_sink[l:l + 1, :].partition_broadcast(128) if False else bass.AP(tensor=I.a_sink.tensor, offset=l * 8, ap=[[0, 128], [1, 8]]), [], [W.sink])
        k.op("dve", lambda e: e.tensor_scalar(W.nsink.ap[:], W.sink.ap[:], -1.0, None, op0=ALU.mult), [W.sink], [W.nsink])
    return W


def phase_attn_prompt(C, l):
    k, I, S = C.k, C.I, C.S
    with ExitStack() as st:
        W = attn_work(C, st, l, True)
        Q = k.sb(st, "pQ", [64, 8, 256])
        K_ = k.sb(st, "pK", [64, 8, 256])
        V = k.sb(st, "pV", [128, 2, 512])
        ui = 0
        for mixer in ("A", "C"):
            for s in range(2):
                base = s * 256
                if mixer == "A":
                    q0, k0, nkv, v0, vw, Y, rY = 0, 512, 2, 128, 128, S.YA, S.rYA
                else:
                    q0, k0, nkv, v0, vw, Y, rY = 640, 1152, 8, 2688, 512, S.YC, S.rYC
                k.dma(None, Q.ap[:], S.PF[q0:q0 + 512, base:base + 256].rearrange("(h d) t -> d h t", d=64), [S.rPF], [Q])
                k.dma(None, K_.ap[:, 0:nkv, :], S.PF[k0:k0 + nkv * 64, base:base + 256].rearrange("(h d) t -> d h t", d=64), [S.rPF], [K_])
                k.dma(None, V.ap[:, :, 0:vw], S.PT[base:base + 256, v0:v0 + vw].rearrange("(t p) f -> p t f", p=128), [S.rPT], [V])
                for h in range(8):
                    kv = h // 4 if mixer == "A" else h
                    for qb in range(2):
                        segs = [(K_, K_.ap[:, kv, :], 256, [])]
                        vch = [(V, V.ap[:, t, kv * 64:(kv + 1) * 64], 128, t * 128) for t in range(2)]
                        attn_unit(C, W, ui, Q.ap[:, h, qb * 128:(qb + 1) * 128], Q, 128, segs, vch,
                                  h if mixer == "A" else None,
                                  Y[h * 64:(h + 1) * 64, base + qb * 128:base + (qb + 1) * 128], rY)
                        ui += 1
        k.barrier()


def load_cache_kT(C, st, src, nh, name):
    k = C.k
    ct = k.sb(st, name + "_tm", [128, 4, nh * 64])
    CK = k.sb(st, name, [64, nh, 512])
    k.dma(None, ct.ap[:], src.rearrange("(j p) f -> p j f", p=128), [], [ct])
    for h in range(nh):
        ps = C.PS[h % 4]
        for j in range(4):
            k.op("pe", lambda e: e.transpose(ps.ap[:64, j * 128:(j + 1) * 128], ct.ap[:, j, h * 64:(h + 1) * 64], C.ident.ap[:]), [ct, C.ident], [ps])
        k.op("dve", lambda e: e.tensor_copy(CK.ap[:, h, :], ps.ap[:64, :]), [ps], [CK])
    return CK


def phase_attn_sample_A(C, l):
    k, I, S = C.k, C.I, C.S
    B0 = NPT
    with ExitStack() as st:
        cos = k.sb(st, "cos", [128, TS]); sin = k.sb(st, "sin", [128, TS]); rot = k.sb(st, "rot", [128, 128])
        k.dma("sp", cos.ap[:], I.cos[:, :], [], [cos])
        k.dma("act", sin.ap[:], I.sin[:, :], [], [sin])
        k.dma("sp", rot.ap[:], I.rot[:, :], [], [rot])
        X = [k.sb(st, "ropeX%d" % i, [128, TS]) for i in range(2)]
        T1 = [k.sb(st, "ropeT%d" % i, [128, 512]) for i in range(2)]
        XR = [k.sb(st, "ropeR%d" % i, [128, 512]) for i in range(2)]
        cnt = 0
        for c in range(5):
            x = X[c % 2]
            k.dma(None, x.ap[:], S.PF[c * 128:(c + 1) * 128, B0:B0 + TS], [S.rPF], [x])
            for tg in range(4):
                cs = slice(tg * 512, (tg + 1) * 512)
                ps = C.PS[cnt % 4]; t1 = T1[cnt % 2]; xr = XR[cnt % 2]; cnt += 1
                k.op("pe", lambda e: e.matmul(ps.ap[:], lhsT=rot.ap[:], rhs=x.ap[:, cs], start=True, stop=True), [rot, x], [ps])
                k.op("pool", lambda e: e.tensor_tensor(out=t1.ap[:], in0=x.ap[:, cs], in1=cos.ap[:, cs], op=ALU.mult), [x, cos], [t1])
                k.op("dve", lambda e: e.tensor_tensor(out=xr.ap[:], in0=ps.ap[:], in1=sin.ap[:, cs], op=ALU.mult), [ps, sin], [xr])
                k.op("dve", lambda e: e.tensor_tensor(out=xr.ap[:], in0=xr.ap[:], in1=t1.ap[:], op=ALU.add), [xr, t1], [xr])
                k.dma(None, S.QR[c * 128:(c + 1) * 128, cs], xr.ap[:], [xr], [S.rQR])
        k.barrier()
    with ExitStack() as st:
        W = attn_work(C, st, l, True)
        ml = k.sb(st, "mleft", [128, 128]); mr = k.sb(st, "mright", [128, 128])
        k.dma("sp", ml.ap[:], I.mleft[:, :], [], [ml])
        k.dma("sp", mr.ap[:], I.mright[:, :], [], [mr])
        CK = load_cache_kT(C, st, I.cak[l], 2, "CKa")
        CV = k.sb(st, "CVa", [128, 4, 128])
        k.dma(None, CV.ap[:], I.cav[l].rearrange("(j p) f -> p j f", p=128), [], [CV])
        Q = k.sb(st, "sQ", [64, TS]); K_ = k.sb(st, "sK", [64, TS]); V = k.sb(st, "sV", [128, 16, 64])
        ui = 0
        for h in range(8):
            kv = h // 4
            if h % 4 == 0:
                k.dma(None, K_.ap[:], S.QR[512 + kv * 64:512 + (kv + 1) * 64, :], [S.rQR], [K_])
                k.dma(None, V.ap[:], S.PT[B0:B0 + TS, 128 + kv * 64:128 + (kv + 1) * 64].rearrange("(t p) f -> p t f", p=128), [S.rPT], [V])
            k.dma(None, Q.ap[:], S.QR[h * 64:(h + 1) * 64, :], [S.rQR], [Q])
            for n in range(16):
                ta = max(0, n - 1); tb = min(16, n + 2)
                nlat = (tb - ta) * 128
                masks = []
                if n > 0:
                    masks.append((0, 128, ml, ml.ap[:, :]))
                if n < 15:
                    masks.append((nlat - 128, 128, mr, mr.ap[:, :]))
                segs = [(K_, K_.ap[:, ta * 128:tb * 128], nlat, masks), (CK, CK.ap[:, kv, :], 512, [])]
                vch = [(V, V.ap[:, t, :], 128, (t - ta) * 128) for t in range(ta, tb)]
                vch += [(CV, CV.ap[:, j, kv * 64:(kv + 1) * 64], 128, nlat + j * 128) for j in range(4)]
                attn_unit(C, W, ui, Q.ap[:, n * 128:(n + 1) * 128], Q, 128, segs, vch, h,
                          S.YA[h * 64:(h + 1) * 64, B0 + n * 128:B0 + (n + 1) * 128], S.rYA)
                ui += 1
        k.barrier()


def phase_attn_sample_C(C, l):
    k, I, S = C.k, C.I, C.S
    B0 = NPT
    with ExitStack() as st:
        W = attn_work(C, st, l, False)
        z = k.sb(st, "zer", [120, 127])
        k.op("dve", lambda e: e.memset(z.ap[:], 0.0), [], [z])
        k.dma("sp", S.E[0], z.ap[:], [z], [S.rE])
        k.dma("sp", S.E[0, :, 48:79], I.na_rpb[l].rearrange("h r c -> (h r) c"), [S.rE], [S.rE])
        MB = k.sb(st, "MB", [64, 120, 64])
        cm = k.sb(st, "colmask", [64, 64])
        k.dma("sp", cm.ap[:], I.colmask[:, :], [], [cm])
        for c in range(64):
            k.dma(None, MB.ap[c:c + 1, :, :], S.E[0:1, :, 63 - c:127 - c], [S.rE], [MB])
        k.op("dve", lambda e: e.scalar_tensor_tensor(out=MB.ap[:], in0=MB.ap[:], scalar=1.0 / SCALE, in1=bc(cm.ap[:].unsqueeze(1), [64, 120, 64]), op0=ALU.mult, op1=ALU.add), [MB, cm], [MB])
        CK = load_cache_kT(C, st, I.cck[l], 8, "CKc")
        CV = k.sb(st, "CVc", [128, 4, 512])
        k.dma(None, CV.ap[:], I.ccv[l].rearrange("(j p) f -> p j f", p=128), [], [CV])
        Q = k.sb(st, "cQ", [64, TS]); K_ = k.sb(st, "cK", [64, TS]); V = k.sb(st, "cV", [64, 32, 64])
        ui = 0
        for h in range(8):
            k.dma(None, Q.ap[:], S.PF[640 + h * 64:640 + (h + 1) * 64, B0:B0 + TS], [S.rPF], [Q])
            k.dma(None, K_.ap[:], S.PF[1152 + h * 64:1152 + (h + 1) * 64, B0:B0 + TS], [S.rPF], [K_])
            k.dma(None, V.ap[:], S.PT[B0:B0 + TS, 2688 + h * 64:2688 + (h + 1) * 64].rearrange("(r c) f -> c r f", c=64), [S.rPT], [V])
            for r in range(32):
                rs = min(max(r - 4, 0), 24)
                dr0 = rs - r + 7
                bias = MB.ap[:, h * 15 + dr0:h * 15 + dr0 + 8, :].rearrange("p a b -> p (a b)")
                segs = [(K_, K_.ap[:, rs * 64:rs * 64 + 512], 512, [(0, 512, MB, bias)]), (CK, CK.ap[:, h, :], 512, [])]
                vch = [(V, V.ap[:, rs + a, :], 64, a * 64) for a in range(8)]
                vch += [(CV, CV.ap[:, j, h * 64:(h + 1) * 64], 128, 512 + j * 128) for j in range(4)]
                attn_unit(C, W, ui, Q.ap[:, r * 64:(r + 1) * 64], Q, 64, segs, vch, None,
                          S.YC[h * 64:(h + 1) * 64, B0 + r * 64:B0 + (r + 1) * 64], S.rYC)
                ui += 1
        k.barrier()


def phase_attn(C, l):
    phase_attn_prompt(C, l)
    phase_attn_sample_A(C, l)
    phase_attn_sample_C(C, l)


SEQS = [(0, 256), (256, 256), (512, 2048)]
TC = 32
DEC_SCALE = -0.6065306597126334


def pbcast(X, l, n):
    return bass.AP(tensor=X.tensor, offset=l * n, ap=[[0, 128], [1, n]])


def v3(ap, h=8):
    return ap.rearrange("p (h j) -> p h j", h=h)


def phase_rwkv_prep(C, l):
    k, I, S = C.k, C.I, C.S
    with ExitStack() as st:
        mu = k.sb(st, "mu_b", [128, 1920]); w0 = k.sb(st, "w0_b", [128, 1024]); a0 = k.sb(st, "a0_b", [128, 1024])
        kkp = k.sb(st, "kkp_b", [128, 512]); ka = k.sb(st, "ka_b", [128, 512]); omk = k.sb(st, "omk_b", [128, 512]); rk = k.sb(st, "rk_b", [128, 512])
        w2 = k.sb(st, "w2", [128, 512]); a2 = k.sb(st, "a2", [128, 512]); g2 = k.sb(st, "g2", [128, 512]); J = k.sb(st, "J", [128, 128])
        e12 = k.sb(st, "e12", [128, 1])
        k.op("dve", lambda e: e.memset(e12.ap[:], 1e-12), [], [e12])
        for (t, X, n) in ((mu, I.rw_mu, 1920), (w0, I.rw_w0, 1024), (a0, I.rw_a0, 1024), (kkp, I.rw_kk, 512), (ka, I.rw_ka, 512), (rk, I.rw_rk, 512)):
            k.dma(None, t.ap[:], pbcast(X, l, n), [], [t])
        for (t, X) in ((w2, I.rw_w2), (a2, I.rw_a2), (g2, I.rw_g2)):
            k.dma(None, t.ap[:], X[l], [], [t])
        k.dma(None, J.ap[:], I.antiid[:, :], [], [J])
        k.op("dve", lambda e: e.tensor_scalar(omk.ap[:], ka.ap[:], -1.0, 1.0, op0=ALU.mult, op1=ALU.add), [ka], [omk])
        pb = k.sb(st, "pb", [128, 1920]); prev = k.sb(st, "prev", [128, 1920]); nxt = k.sb(st, "nxt", [128, 1920])
        lw = k.sb(st, "lw", [128, 384]); lwT = k.sb(st, "lwT", [128, 384])
        az = [k.sb(st, "az%d" % z, [128, 512]) for z in range(2)]
        kk = k.sb(st, "kk", [128, 512]); kkn = k.sb(st, "kkn", [128, 512]); t1 = k.sb(st, "t1", [128, 512]); t2 = k.sb(st, "t2", [128, 512])
        sm = k.sb(st, "rsm", [128, 32])
        opz = [k.sb(st, "opz%d" % z, [128, 8, 6, 64]) for z in range(2)]
        rev = k.sb(st, "rev", [128, 1536]); bg = k.sb(st, "bg", [128, 1024])
        for (s0, T) in SEQS:
            nt = T // 128
            for ti in range(nt):
                t0 = s0 + ti * 128
                k.dma(None, pb.ap[:], S.PT[t0:t0 + 128, 256:2176], [S.rPT], [pb])
                if ti > 0:
                    k.dma(None, prev.ap[:], S.PT[t0 - 1:t0 + 127, 256:2176], [S.rPT], [prev])
                else:
                    k.op("dve", lambda e: e.memset(prev.ap[0:1, :], 0.0), [], [prev])
                    k.dma(None, prev.ap[1:128, :], S.PT[t0:t0 + 127, 256:2176], [S.rPT], [prev])
                if ti < nt - 1:
                    k.dma(None, nxt.ap[:], S.PT[t0 + 1:t0 + 129, 256:2176], [S.rPT], [nxt])
                else:
                    k.op("pool", lambda e: e.memset(nxt.ap[:], 0.0), [], [nxt])
                    k.dma(None, nxt.ap[0:127, :], S.PT[t0 + 1:t0 + 128, 256:2176], [S.rPT], [nxt])
                k.op("pool", lambda e: e.tensor_tensor(out=prev.ap[:], in0=prev.ap[:], in1=nxt.ap[:], op=ALU.add), [prev, nxt], [prev])
                k.op("dve", lambda e: e.scalar_tensor_tensor(out=prev.ap[:], in0=prev.ap[:], scalar=0.5, in1=pb.ap[:], op0=ALU.mult, op1=ALU.subtract), [prev, pb], [prev])
                k.op("dve", lambda e: e.tensor_tensor(out=prev.ap[:], in0=prev.ap[:], in1=mu.ap[:], op=ALU.mult), [prev, mu], [prev])
                k.op("dve", lambda e: e.tensor_tensor(out=pb.ap[:], in0=pb.ap[:], in1=prev.ap[:], op=ALU.add), [pb, prev], [pb])
                r_ = pb.ap[:, 0:512]; k_ = pb.ap[:, 512:1024]; v_ = pb.ap[:, 1024:1536]
                k.op("act", lambda e: e.activation(out=lw.ap[:, 0:128], in_=pb.ap[:, 1536:1664], func=AF.Tanh), [pb], [lw])
                k.op("act", lambda e: e.copy(lw.ap[:, 128:256], pb.ap[:, 1664:1792]), [pb], [lw])
                k.op("act", lambda e: e.activation(out=lw.ap[:, 256:384], in_=pb.ap[:, 1792:1920], func=AF.Sigmoid), [pb], [lw])
                ps = C.PS[0]
                for j in range(3):
                    k.op("pe", lambda e: e.transpose(ps.ap[:, j * 128:(j + 1) * 128], lw.ap[:, j * 128:(j + 1) * 128], C.ident.ap[:]), [lw, C.ident], [ps])
                k.op("dve", lambda e: e.tensor_copy(lwT.ap[:], ps.ap[:, 0:384]), [ps], [lwT])
                for z in range(2):
                    zs = slice(z * 64, (z + 1) * 64)
                    psw = C.PS[1 + z]; psa = C.PS[3 + z]
                    k.op("pe", lambda e: e.matmul(psw.ap[:], lhsT=lwT.ap[zs, 0:128], rhs=w2.ap[zs, :], start=True, stop=True), [lwT, w2], [psw])
                    k.op("pe", lambda e: e.matmul(psa.ap[:], lhsT=lwT.ap[zs, 128:256], rhs=a2.ap[zs, :], start=True, stop=True), [lwT, a2], [psa])
                    dec = v3(t1.ap[:])
                    k.op("dve", lambda e: e.tensor_tensor(out=t1.ap[:], in0=psw.ap[:], in1=w0.ap[:, z * 512:(z + 1) * 512], op=ALU.add), [psw, w0], [t1])
                    k.op("act", lambda e: e.activation(out=t1.ap[:], in_=t1.ap[:], func=AF.Sigmoid), [t1], [t1])
                    k.op("act", lambda e: e.activation(out=opz[z].ap[:, :, 1, :], in_=dec, func=AF.Exp, scale=DEC_SCALE), [t1], [opz[z]])
                    k.op("dve", lambda e: e.tensor_tensor(out=az[z].ap[:], in0=psa.ap[:], in1=a0.ap[:, z * 512:(z + 1) * 512], op=ALU.add), [psa, a0], [az[z]])
                    k.op("act", lambda e: e.activation(out=az[z].ap[:], in_=az[z].ap[:], func=AF.Sigmoid), [az[z]], [az[z]])
                psg = C.PS[5]
                k.op("pe", lambda e: e.matmul(psg.ap[:], lhsT=lwT.ap[:, 256:384], rhs=g2.ap[:, :], start=True, stop=True), [lwT, g2], [psg])
                k.op("act", lambda e: e.copy(bg.ap[:, 512:1024], psg.ap[:]), [psg], [bg])
                k.op("dve", lambda e: e.tensor_tensor(out=kk.ap[:], in0=k_, in1=kkp.ap[:], op=ALU.mult), [pb, kkp], [kk])
                k.op("pool", lambda e: e.tensor_tensor(out=t2.ap[:], in0=kk.ap[:], in1=kk.ap[:], op=ALU.mult), [kk], [t2])
                k.op("dve", lambda e: e.tensor_reduce(out=sm.ap[:, 0:8], in_=v3(t2.ap[:]), axis=AX.X, op=ALU.add), [t2], [sm])
                k.op("act", lambda e: e.activation(out=sm.ap[:, 0:8], in_=sm.ap[:, 0:8], func=AF.Sqrt, bias=e12.ap[:, 0:1], scale=1.0), [sm, e12], [sm])
                k.op("dve", lambda e: e.reciprocal(sm.ap[:, 8:16], sm.ap[:, 0:8]), [sm], [sm])
                k.op("dve", lambda e: e.tensor_tensor(out=v3(kkn.ap[:]), in0=v3(kk.ap[:]), in1=bc(sm.ap[:, 8:16].unsqueeze(2), [128, 8, 64]), op=ALU.mult), [kk, sm], [kkn])
                for z in range(2):
                    k.op("dve", lambda e: e.tensor_tensor(out=t1.ap[:], in0=az[z].ap[:], in1=ka.ap[:], op=ALU.mult), [az[z], ka], [t1])
                    k.op("pool", lambda e: e.tensor_tensor(out=t1.ap[:], in0=t1.ap[:], in1=omk.ap[:], op=ALU.add), [t1, omk], [t1])
                    k.op("dve", lambda e: e.tensor_tensor(out=opz[z].ap[:, :, 3, :], in0=v3(t1.ap[:]), in1=v3(k_), op=ALU.mult), [t1, pb], [opz[z]])
                    k.op("pool", lambda e: e.tensor_tensor(out=opz[z].ap[:, :, 2, :], in0=v3(kkn.ap[:]), in1=v3(az[z].ap[:]), op=ALU.mult), [kkn, az[z]], [opz[z]])
                    k.op("dve", lambda e: e.tensor_scalar(opz[z].ap[:, :, 0, :], v3(kkn.ap[:]), -1.0, None, op0=ALU.mult), [kkn], [opz[z]])
                    k.op("act", lambda e: e.copy(opz[z].ap[:, :, 4, :], v3(r_)), [pb], [opz[z]])
                    k.op("act", lambda e: e.copy(opz[z].ap[:, :, 5, :], v3(v_)), [pb], [opz[z]])
                k.op("dve", lambda e: e.tensor_tensor(out=v3(t1.ap[:]), in0=opz[0].ap[:, :, 3, :], in1=opz[1].ap[:, :, 3, :], op=ALU.add), [opz[0], opz[1]], [t1])
                k.op("dve", lambda e: e.tensor_tensor(out=t1.ap[:], in0=t1.ap[:], in1=r_, op=ALU.mult), [t1, pb], [t1])
                k.op("dve", lambda e: e.tensor_tensor(out=t1.ap[:], in0=t1.ap[:], in1=rk.ap[:], op=ALU.mult), [t1, rk], [t1])
                k.op("dve", lambda e: e.tensor_reduce(out=sm.ap[:, 16:24], in_=v3(t1.ap[:]), axis=AX.X, op=ALU.add), [t1], [sm])
                k.op("dve", lambda e: e.tensor_tensor(out=v3(bg.ap[:, 0:512]), in0=v3(v_), in1=bc(sm.ap[:, 16:24].unsqueeze(2), [128, 8, 64]), op=ALU.mult), [pb, sm], [bg])
                k.dma(None, S.BG[t0:t0 + 128, :], bg.ap[:], [bg], [S.rBG])
                k.dma(None, S.OPS[0, t0:t0 + 128, :], opz[0].ap[:].rearrange("p a b c -> p (a b c)"), [opz[0]], [S.rOPS])
                flat = opz[1].ap[:].rearrange("p a b c -> p (a b c)")
                tr = s0 + (nt - 1 - ti) * 128
                for half in range(2):
                    for j in range(3):
                        c0 = half * 1536 + j * 512
                        psr = C.PS[5 + j] if j < 2 else C.PS[7]
                        k.op("pe", lambda e: e.matmul(psr.ap[:], lhsT=J.ap[:], rhs=flat[:, c0:c0 + 512], start=True, stop=True), [J, opz[1]], [psr])
                        if j % 2 == 0:
                            k.op("dve", lambda e: e.tensor_copy(rev.ap[:, j * 512:(j + 1) * 512], psr.ap[:]), [psr], [rev])
                        else:
                            k.op("act", lambda e: e.copy(rev.ap[:, j * 512:(j + 1) * 512], psr.ap[:]), [psr], [rev])
                    k.dma(None, S.OPS[1, tr:tr + 128, half * 1536:(half + 1) * 1536], rev.ap[:], [rev], [S.rOPS])
        k.barrier()


def scan_group(C, st, l, insts, T, ILO, s_init, s_final):
    k, I, S = C.k, C.I, C.S
    F = ILO * 64
    IC = 64 // ILO
    St = k.sb(st, "scanS", [128, F]); tmp = k.sb(st, "scantmp", [128, F]); sa = k.sb(st, "scansa", [128, ILO])
    OPB = [k.sb(st, "OPB%d" % i, [128, TC, 5, 64]) for i in range(2)]
    VB = [k.sb(st, "VB%d" % i, [128, TC, ILO]) for i in range(2)]
    YC_ = [k.sb(st, "Yc%d" % i, [128, TC, ILO]) for i in range(2)]
    if s_init is not None:
        k.dma("sp", St.ap[:], s_init, [], [St])
    else:
        k.op("dve", lambda e: e.memset(St.ap[:], 0.0), [], [St])
    S3 = St.ap[:].rearrange("p (i j) -> p i j", j=64)
    T3 = tmp.ap[:].rearrange("p (i j) -> p i j", j=64)
    shp = [128, ILO, 64]
    for ch in range(T // TC):
        opb = OPB[ch % 2]; vb = VB[ch % 2]; yc = YC_[ch % 2]
        for (p0, z, h, s0) in insts:
            base = (z * TOK + s0 + ch * TC) * 3072 + h * 384
            src = bass.AP(tensor=S.OPS.tensor, offset=base, ap=[[0, IC], [3072, TC], [1, 320]])
            k.dma(None, opb.ap[p0:p0 + IC, :, :, :].rearrange("p t a b -> p t (a b)"), src, [S.rOPS], [opb])
            srcv = bass.AP(tensor=S.OPS.tensor, offset=base + 320, ap=[[ILO, IC], [3072, TC], [1, ILO]])
            k.dma(None, vb.ap[p0:p0 + IC, :, :], srcv, [S.rOPS], [vb])
        for tc in range(TC):
            def row(kind):
                return bc(opb.ap[:, tc, kind, :].unsqueeze(1), shp)
            k.op("dve", lambda e: e.tensor_tensor(out=T3, in0=S3, in1=row(0), op=ALU.mult), [St, opb], [tmp])
            k.op("dve", lambda e: e.tensor_reduce(out=sa.ap[:], in_=T3, axis=AX.X, op=ALU.add), [tmp], [sa])
            k.op("dve", lambda e: e.tensor_tensor(out=S3, in0=S3, in1=row(1), op=ALU.mult), [St, opb], [St])
            k.op("dve", lambda e: e.tensor_tensor(out=T3, in0=bc(sa.ap[:].unsqueeze(2), shp), in1=row(2), op=ALU.mult), [sa, opb], [tmp])
            k.op("dve", lambda e: e.tensor_tensor(out=S3, in0=S3, in1=T3, op=ALU.add), [St, tmp], [St])
            k.op("dve", lambda e: e.tensor_tensor(out=T3, in0=bc(vb.ap[:, tc, :].unsqueeze(2), shp), in1=row(3), op=ALU.mult), [vb, opb], [tmp])
            k.op("dve", lambda e: e.tensor_tensor(out=S3, in0=S3, in1=T3, op=ALU.add), [St, tmp], [St])
            k.op("dve", lambda e: e.tensor_tensor(out=T3, in0=S3, in1=row(4), op=ALU.mult), [St, opb], [tmp])
            k.op("dve", lambda e: e.tensor_reduce(out=yc.ap[:, tc, :], in_=T3, axis=AX.X, op=ALU.add), [tmp], [yc])
        groups = {}
        for (p0, z, h, s0) in insts:
            groups.setdefault((z, s0), []).append((p0, h))
        for (z, s0), lst in groups.items():
            pa = min(p for p, _ in lst)
            np_ = len(lst) * IC
            dst = bass.AP(tensor=S.YS.tensor, offset=(z * TOK + s0 + ch * TC) * 512, ap=[[ILO, np_], [512, TC], [1, ILO]])
            k.dma(None, dst, yc.ap[pa:pa + np_, :, :], [yc], [S.rYS])
    if s_final is not None:
        for (dst, p0, n) in s_final:
            k.dma("sp", dst, St.ap[p0:p0 + n, :], [St], [])


def phase_rwkv_scan(C, l):
    k, I, S, O = C.k, C.I, C.S, C.O
    with ExitStack() as st:
        insts = []
        for s in range(2):
            for z in range(2):
                for h in range(8):
                    insts.append((s * 64 + z * 32 + h * 4, z, h, s * 256))
        scan_group(C, st, l, insts, 256, 16, None, [(O.nst[s, l], s * 64, 64) for s in range(2)])
        k.barrier()
    with ExitStack() as st:
        insts = []
        for z in range(2):
            for h in range(8):
                insts.append((z * 64 + h * 8, z, h, 512))
        scan_group(C, st, l, insts, 2048, 8, I.st0[l], None)
        k.barrier()


def phase_rwkv_post(C, l):
    k, I, S = C.k, C.I, C.S
    with ExitStack() as st:
        lg = k.sb(st, "lnxg_b", [128, 512]); lb = k.sb(st, "lnxb_b", [128, 512]); J = k.sb(st, "J2", [128, 128])
        gne = k.sb(st, "gne", [128, 1])
        k.op("dve", lambda e: e.memset(gne.ap[:], 64e-5), [], [gne])
        k.dma(None, lg.ap[:], pbcast(I.rw_lnx_g, l, 512), [], [lg])
        k.dma(None, lb.ap[:], pbcast(I.rw_lnx_b, l, 512), [], [lb])
        k.dma(None, J.ap[:], I.antiid[:, :], [], [J])
        yf = [k.sb(st, "yf%d" % i, [128, 512]) for i in range(2)]
        yb = [k.sb(st, "yb%d" % i, [128, 512]) for i in range(2)]
        bg = [k.sb(st, "pbg%d" % i, [128, 1024]) for i in range(2)]
        t1 = k.sb(st, "pt1", [128, 512]); sm = k.sb(st, "psm", [128, 32]); yT = [k.sb(st, "pyT%d" % i, [128, 4, 128]) for i in range(2)]
        cnt = 0
        for (s0, T) in SEQS:
            nt = T // 128
            for ti in range(nt):
                par = cnt % 2; cnt += 1
                t0 = s0 + ti * 128
                tr = s0 + (nt - 1 - ti) * 128
                y = yf[par]; y2 = yb[par]; b = bg[par]
                k.dma(None, y.ap[:], S.YS[0, t0:t0 + 128, :], [S.rYS], [y])
                k.dma(None, y2.ap[:], S.YS[1, tr:tr + 128, :], [S.rYS], [y2])
                k.dma(None, b.ap[:], S.BG[t0:t0 + 128, :], [S.rBG], [b])
                ps = C.PS[par]
                k.op("pe", lambda e: e.matmul(ps.ap[:], lhsT=J.ap[:], rhs=y2.ap[:], start=True, stop=True), [J, y2], [ps])
                k.op("dve", lambda e: e.tensor_tensor(out=y.ap[:], in0=y.ap[:], in1=ps.ap[:], op=ALU.add), [y, ps], [y])
                k.op("dve", lambda e: e.tensor_reduce(out=sm.ap[:, 0:8], in_=v3(y.ap[:]), axis=AX.X, op=ALU.add), [y], [sm])
                k.op("dve", lambda e: e.tensor_scalar(sm.ap[:, 0:8], sm.ap[:, 0:8], 1.0 / 64, None, op0=ALU.mult), [sm], [sm])
                k.op("dve", lambda e: e.tensor_tensor(out=v3(y.ap[:]), in0=v3(y.ap[:]), in1=bc(sm.ap[:, 0:8].unsqueeze(2), [128, 8, 64]), op=ALU.subtract), [y, sm], [y])
                k.op("pool", lambda e: e.tensor_tensor(out=t1.ap[:], in0=y.ap[:], in1=y.ap[:], op=ALU.mult), [y], [t1])
                k.op("dve", lambda e: e.tensor_reduce(out=sm.ap[:, 8:16], in_=v3(t1.ap[:]), axis=AX.X, op=ALU.add), [t1], [sm])
                k.op("act", lambda e: e.activation(out=sm.ap[:, 8:16], in_=sm.ap[:, 8:16], func=AF.Sqrt, bias=gne.ap[:, 0:1], scale=1.0 / 64), [sm, gne], [sm])
                k.op("dve", lambda e: e.reciprocal(sm.ap[:, 16:24], sm.ap[:, 8:16]), [sm], [sm])
                k.op("dve", lambda e: e.tensor_tensor(out=v3(y.ap[:]), in0=v3(y.ap[:]), in1=bc(sm.ap[:, 16:24].unsqueeze(2), [128, 8, 64]), op=ALU.mult), [y, sm], [y])
                k.op("pool", lambda e: e.tensor_tensor(out=y.ap[:], in0=y.ap[:], in1=lg.ap[:], op=ALU.mult), [y, lg], [y])
                k.op("dve", lambda e: e.tensor_tensor(out=y.ap[:], in0=y.ap[:], in1=lb.ap[:], op=ALU.add), [y, lb], [y])
                k.op("dve", lambda e: e.tensor_tensor(out=y.ap[:], in0=y.ap[:], in1=b.ap[:, 0:512], op=ALU.add), [y, b], [y])
                k.op("dve", lambda e: e.tensor_tensor(out=y.ap[:], in0=y.ap[:], in1=b.ap[:, 512:1024], op=ALU.mult), [y, b], [y])
                pst = C.PS[2 + par]
                for c in range(4):
                    k.op("pe", lambda e: e.transpose(pst.ap[:, c * 128:(c + 1) * 128], y.ap[:, c * 128:(c + 1) * 128], C.ident.ap[:]), [y, C.ident], [pst])
                k.op("act", lambda e: e.copy(yT[par].ap[:], pst.ap[:].rearrange("p (c t) -> p c t", c=4)), [pst], [yT[par]])
                k.dma(None, S.YB[:, t0:t0 + 128].rearrange("(c p) t -> p c t", p=128), yT[par].ap[:], [yT[par]], [S.rYB])
        k.barrier()


def phase_rwkv(C, l):
    phase_rwkv_prep(C, l)
    phase_rwkv_scan(C, l)
    phase_rwkv_post(C, l)


def phase_merge(C, l):
    k, I, S = C.k, C.I, C.S
    outs = (I.a_out, I.rw_out, I.na_out)
    Ys = ((S.YA, S.rYA), (S.YB, S.rYB), (S.YC, S.rYC))
    with ExitStack() as st:
        yT = [k.sb(st, "myT%d" % b, [128, 4, 512]) for b in range(3)]
        mg = k.sb(st, "merged", [128, 8, 512])
        wo = [[k.sb(st, "mwo%d_%d" % (b, i), [128, 4, 128]) for i in range(2)] for b in range(3)]
        gt = [[k.sb(st, "mg%d_%d" % (b, i), [128, 512]) for i in range(2)] for b in range(3)]
        tmp = [k.sb(st, "mtmp%d" % i, [128, 512]) for i in range(2)]
        wob = [k.sb(st, "mwob%d" % i, [128, 8, 128]) for i in range(2)]
        cnt = 0
        for tg in range(5):
            g = 0 if tg == 0 else 1
            cols = slice(tg * 512, (tg + 1) * 512)
            for b in range(3):
                k.dma(None, yT[b].ap[:], Ys[b][0][:, cols].rearrange("(c p) t -> p c t", p=128), [Ys[b][1]], [yT[b]])
            for fc in range(8):
                par = cnt % 2; cnt += 1
                for b in range(3):
                    k.dma(None, wo[b][par].ap[:], outs[b][l][:, fc * 128:(fc + 1) * 128].rearrange("(c p) n -> p c n", p=128), [], [wo[b][par]])
                    r0 = 1664 + b * 1024 + fc * 128
                    k.dma(None, gt[b][par].ap[:], S.PF[r0:r0 + 128, cols], [S.rPF], [gt[b][par]])
                for b in range(3):
                    ps = C.PS[(fc * 3 + b) % 6]
                    for c in range(4):
                        k.op("pe", lambda e: e.matmul(ps.ap[:], lhsT=wo[b][par].ap[:, c, :], rhs=yT[b].ap[:, c, :], start=(c == 0), stop=(c == 3)), [wo[b][par], yT[b]], [ps])
                    if b == 0:
                        k.op("dve", lambda e: e.tensor_tensor(out=mg.ap[:, fc, :], in0=ps.ap[:], in1=gt[b][par].ap[:], op=ALU.mult), [ps, gt[b][par]], [mg])
                    else:
                        t = tmp[b % 2]
                        k.op("dve", lambda e: e.tensor_tensor(out=t.ap[:], in0=ps.ap[:], in1=gt[b][par].ap[:], op=ALU.mult), [ps, gt[b][par]], [t])
                        k.op("pool", lambda e: e.tensor_tensor(out=mg.ap[:, fc, :], in0=mg.ap[:, fc, :], in1=t.ap[:], op=ALU.add), [mg, t], [mg])
            for fc2 in range(8):
                w = wob[fc2 % 2]
                k.dma(None, w.ap[:], I.w_o[l][:, fc2 * 128:(fc2 + 1) * 128].rearrange("(c p) n -> p c n", p=128), [], [w])
                ps = C.PS[6 + fc2 % 2]
                for kc in range(8):
                    k.op("pe", lambda e: e.matmul(ps.ap[:], lhsT=w.ap[:, kc, :], rhs=mg.ap[:, kc, :], start=(kc == 0), stop=(kc == 7)), [w, mg], [ps])
                k.op("dve", lambda e: e.scalar_tensor_tensor(out=C.xT.ap[:, fc2, cols], in0=ps.ap[:], scalar=C.modT.ap[:, 16 + fc2, g:g + 1], in1=C.xT.ap[:, fc2, cols], op0=ALU.mult, op1=ALU.add), [ps, C.modT, C.xT], [C.xT])
        k.barrier()


NBUF = 4


def phase_peer(C, l):
    k, I, S = C.k, C.I, C.S
    with ExitStack() as st:
        skn = k.sb(st, "skn", [128, 16, 128]); skT = k.sb(st, "skT", [128, 16, 128])
        k.dma(None, skn.ap[:], I.pe_sk[l].rearrange("c n k -> n c k"), [], [skn])
        for c in range(16):
            ps = C.PS[c % 4]
            k.op("pe", lambda e: e.transpose(ps.ap[:, 0:128], skn.ap[:, c, :], C.ident.ap[:]), [skn, C.ident], [ps])
            k.op("dve", lambda e: e.tensor_copy(skT.ap[:, c, :], ps.ap[:, 0:128]), [ps], [skT])
        h2T = k.sb(st, "h2T", [128, 8, 128]); h2 = k.sb(st, "h2tok", [128, D])
        sq = [k.sb(st, "psq%d" % i, [128, 128]) for i in range(2)]; rstd = k.sb(st, "prstd", [128, 128])
        wq = [k.sb(st, "wq%d" % i, [128, 8, 128]) for i in range(2)]
        qT = k.sb(st, "qT", [128, 16, 128])
        Sc = k.sb(st, "Sc", [128, 16, 128])
        vals = k.sb(st, "vals", [128, 16, 16]); idxu = k.sb(st, "idxu", [128, 16, 16], U32); idxf = k.sb(st, "idxf", [128, 16, 16])
        cand = k.sb(st, "cand", [128, 8, 256]); cand2 = k.sb(st, "cand2", [128, 8, 256]); cande = k.sb(st, "cande", [128, 8, 256])
        sv = k.sb(st, "sv", [128, 8, 16]); gate = k.sb(st, "gate", [128, 8, 16]); sm = k.sb(st, "pesm", [128, 32])
        junk = k.sb(st, "pjunk", [128, 256]); ef = k.sb(st, "ef", [128, 128]); ei = k.sb(st, "ei", [128, 128], I32)
        dots = k.sb(st, "dots", [128, 128]); tg_ = k.sb(st, "pt_g", [128, 128]); wgt = k.sb(st, "wgt", [128, 128])
        UV = [k.sb(st, "UV%d" % i, [128, D]) for i in range(NBUF)]
        junk2 = k.sb(st, "pjunk2", [128, D]); acc = k.sb(st, "pacc", [128, D])
        for tile in range(TOK // 128):
            g = 0 if tile < 4 else 1
            t0 = tile * 128
            cols = slice(t0, t0 + 128)
            psn = C.PS[7]
            for kc in range(8):
                s = sq[kc % 2]
                k.op("act", lambda e: e.activation(out=s.ap[:], in_=C.xT.ap[:, kc, cols], func=AF.Square), [C.xT], [s])
                k.op("pe", lambda e: e.matmul(psn.ap[:, 0:128], lhsT=C.ones.ap[:], rhs=s.ap[:], start=(kc == 0), stop=(kc == 7)), [s, C.ones], [psn])
            k.op("act", lambda e: e.activation(out=rstd.ap[:], in_=psn.ap[:, 0:128], func=AF.Sqrt, scale=1.0 / D, bias=C.eps.ap[:, 0:1]), [psn, C.eps], [rstd])
            k.op("dve", lambda e: e.reciprocal(rstd.ap[:], rstd.ap[:]), [rstd], [rstd])
            for kc in range(8):
                k.op("dve", lambda e: e.tensor_tensor(out=h2T.ap[:, kc, :], in0=C.xT.ap[:, kc, cols], in1=rstd.ap[:], op=ALU.mult), [C.xT, rstd], [h2T])
                k.op("dve", lambda e: e.tensor_scalar(h2T.ap[:, kc, :], h2T.ap[:, kc, :], C.gs2.ap[:, kc, g:g + 1], C.modT.ap[:, 24 + kc, g:g + 1], op0=ALU.mult, op1=ALU.add), [h2T, C.gs2, C.modT], [h2T])
            for half in range(2):
                ps = C.PS[half]
                for j in range(4):
                    kc = half * 4 + j
                    k.op("pe", lambda e: e.transpose(ps.ap[:, j * 128:(j + 1) * 128], h2T.ap[:, kc, :], C.ident.ap[:]), [h2T, C.ident], [ps])
                k.op("act", lambda e: e.copy(h2.ap[:, half * 512:(half + 1) * 512], ps.ap[:]), [ps], [h2])
            for c in range(16):
                w = wq[c % 2]
                k.dma(None, w.ap[:], I.pe_q[l][:, c * 128:(c + 1) * 128].rearrange("(c p) n -> p c n", p=128), [], [w])
                ps = C.PS[2 + c % 2]
                for kc in range(8):
                    k.op("pe", lambda e: e.matmul(ps.ap[:, 0:128], lhsT=w.ap[:, kc, :], rhs=h2T.ap[:, kc, :], start=(kc == 0), stop=(kc == 7)), [w, h2T], [ps])
                k.op("act", lambda e: e.copy(qT.ap[:, c, :], ps.ap[:, 0:128]), [ps], [qT])
            for q4 in range(4):
                ps = C.PS[4 + q4 % 2]
                for j in range(4):
                    c = q4 * 4 + j
                    k.op("pe", lambda e: e.matmul(ps.ap[:, j * 128:(j + 1) * 128], lhsT=qT.ap[:, c, :], rhs=skT.ap[:, c, :], start=True, stop=True), [qT, skT], [ps])
                k.op("dve", lambda e: e.tensor_copy(Sc.ap[:, q4 * 4:q4 * 4 + 4, :], ps.ap[:].rearrange("p (a b) -> p a b", a=4)), [ps], [Sc])
            for c in range(16):
                k.op("dve", lambda e: e.max(out=vals.ap[:, c, 0:8], in_=Sc.ap[:, c, :]), [Sc], [vals])
                k.op("dve", lambda e: e.max_index(out=idxu.ap[:, c, 0:8], in_max=vals.ap[:, c, 0:8], in_values=Sc.ap[:, c, :]), [Sc, vals], [idxu])
                k.op("dve", lambda e: e.match_replace(out=Sc.ap[:, c, :], in_to_replace=vals.ap[:, c, 0:8], in_values=Sc.ap[:, c, :], imm_value=-1e30), [Sc, vals], [Sc])
                k.op("dve", lambda e: e.max(out=vals.ap[:, c, 8:16], in_=Sc.ap[:, c, :]), [Sc], [vals])
                k.op("dve", lambda e: e.max_index(out=idxu.ap[:, c, 8:16], in_max=vals.ap[:, c, 8:16], in_values=Sc.ap[:, c, :]), [Sc, vals], [idxu])
            k.op("dve", lambda e: e.tensor_copy(idxf.ap[:], idxu.ap[:]), [idxu], [idxf])
            v4 = vals.ap[:].rearrange("p (h z) a -> p h z a", z=2)
            i4 = idxf.ap[:].rearrange("p (h z) a -> p h z a", z=2)
            c4 = cand.ap[:].rearrange("p h (a b) -> p h a b", a=16)
            ce4 = cande.ap[:].rearrange("p h (a b) -> p h a b", a=16)
            shp4 = [128, 8, 16, 16]
            shp3 = [128, 16, 16]
            for h in range(8):
                k.op("dve", lambda e: e.tensor_tensor(out=c4[:, h], in0=bc(v4[:, h, 0, :].unsqueeze(2), shp3), in1=bc(v4[:, h, 1, :].unsqueeze(1), shp3), op=ALU.add), [vals], [cand])
                k.op("dve", lambda e: e.scalar_tensor_tensor(out=ce4[:, h], in0=bc(i4[:, h, 0, :].unsqueeze(2), shp3), scalar=128.0, in1=bc(i4[:, h, 1, :].unsqueeze(1), shp3), op0=ALU.mult, op1=ALU.add), [idxf], [cande])
            for h in range(8):
                k.op("dve", lambda e: e.max(out=sv.ap[:, h, 0:8], in_=cand.ap[:, h, :]), [cand], [sv])
                k.op("dve", lambda e: e.match_replace(out=cand2.ap[:, h, :], in_to_replace=sv.ap[:, h, 0:8], in_values=cand.ap[:, h, :], imm_value=-1e30), [cand, sv], [cand2])
                k.op("dve", lambda e: e.max(out=sv.ap[:, h, 8:16], in_=cand2.ap[:, h, :]), [cand2], [sv])
            k.op("dve", lambda e: e.tensor_tensor(out=gate.ap[:], in0=sv.ap[:], in1=bc(sv.ap[:, :, 0:1], [128, 8, 16]), op=ALU.subtract), [sv], [gate])
            k.op("act", lambda e: e.activation(out=gate.ap[:], in_=gate.ap[:], func=AF.Exp), [gate], [gate])
            k.op("dve", lambda e: e.tensor_reduce(out=sm.ap[:, 0:8], in_=gate.ap[:], axis=AX.X, op=ALU.add), [gate], [sm])
            k.op("dve", lambda e: e.reciprocal(sm.ap[:, 8:16], sm.ap[:, 0:8]), [sm], [sm])
            k.op("dve", lambda e: e.tensor_tensor(out=gate.ap[:], in0=gate.ap[:], in1=bc(sm.ap[:, 8:16].unsqueeze(2), [128, 8, 16]), op=ALU.mult), [gate, sm], [gate])
            k.op("pool", lambda e: e.memset(ef.ap[:], 0.0), [], [ef])
            for h in range(8):
                for kk2 in range(16):
                    j = h * 16 + kk2
                    k.op("dve", lambda e: e.scalar_tensor_tensor(out=junk.ap[:], in0=cand.ap[:, h, :], scalar=sv.ap[:, h, kk2:kk2 + 1], in1=cande.ap[:, h, :], op0=ALU.is_equal, op1=ALU.mult, accum_out=ef.ap[:, j:j + 1]), [cand, cande, sv], [junk, ef])
            k.op("dve", lambda e: e.tensor_scalar(ef.ap[:], ef.ap[:], 0.0, 16383.0, op0=ALU.max, op1=ALU.min), [ef], [ef])
            k.op("dve", lambda e: e.tensor_scalar(ef.ap[:], ef.ap[:], float(l * 16384), None, op0=ALU.add), [ef], [ef])
            k.op("dve", lambda e: e.tensor_copy(ei.ap[:], ef.ap[:]), [ef], [ei])
            k.op("pool", lambda e: e.memset(dots.ap[:], 0.0), [], [dots])
            for j in range(128):
                u = UV[j % NBUF]
                k.idma(u.ap[:], I.pe_u[:, :], ei.ap[:, j:j + 1], [ei], [u])
                k.op("dve", lambda e: e.scalar_tensor_tensor(out=junk2.ap[:], in0=u.ap[:], scalar=1.0, in1=h2.ap[:], op0=ALU.mult, op1=ALU.mult, accum_out=dots.ap[:, j:j + 1]), [u, h2], [junk2, dots])
            k.op("dve", lambda e: e.tensor_tensor(out=tg_.ap[:], in0=dots.ap[:], in1=dots.ap[:], op=ALU.mult), [dots], [tg_])
            k.op("dve", lambda e: e.tensor_tensor(out=tg_.ap[:], in0=tg_.ap[:], in1=dots.ap[:], op=ALU.mult), [tg_, dots], [tg_])
            k.op("dve", lambda e: e.scalar_tensor_tensor(out=tg_.ap[:], in0=tg_.ap[:], scalar=0.044715, in1=dots.ap[:], op0=ALU.mult, op1=ALU.add), [tg_, dots], [tg_])
            k.op("act", lambda e: e.activation(out=tg_.ap[:], in_=tg_.ap[:], func=AF.Tanh, scale=0.7978845608028654), [tg_], [tg_])
            k.op("dve", lambda e: e.tensor_scalar(tg_.ap[:], tg_.ap[:], 1.0, 0.5, op0=ALU.add, op1=ALU.mult), [tg_], [tg_])
            k.op("dve", lambda e: e.tensor_tensor(out=tg_.ap[:], in0=tg_.ap[:], in1=dots.ap[:], op=ALU.mult), [tg_, dots], [tg_])
            k.op("dve", lambda e: e.tensor_tensor(out=wgt.ap[:], in0=tg_.ap[:], in1=gate.ap[:].rearrange("p h a -> p (h a)"), op=ALU.mult), [tg_, gate], [wgt])
            k.op("pool", lambda e: e.memset(acc.ap[:], 0.0), [], [acc])
            for j in range(128):
                v = UV[j % NBUF]
                k.idma(v.ap[:], I.pe_v[:, :], ei.ap[:, j:j + 1], [ei], [v])
                k.op("dve", lambda e: e.scalar_tensor_tensor(out=acc.ap[:], in0=v.ap[:], scalar=wgt.ap[:, j:j + 1], in1=acc.ap[:], op0=ALU.mult, op1=ALU.add), [v, wgt, acc], [acc])
            for half in range(2):
                ps = C.PS[half]
                for j in range(4):
                    kc = half * 4 + j
                    k.op("pe", lambda e: e.transpose(ps.ap[:, j * 128:(j + 1) * 128], acc.ap[:, kc * 128:(kc + 1) * 128], C.ident.ap[:]), [acc, C.ident], [ps])
                for j in range(4):
                    kc = half * 4 + j
                    k.op("dve", lambda e: e.scalar_tensor_tensor(out=C.xT.ap[:, kc, cols], in0=ps.ap[:, j * 128:(j + 1) * 128], scalar=C.modT.ap[:, 40 + kc, g:g + 1], in1=C.xT.ap[:, kc, cols], op0=ALU.mult, op1=ALU.add), [ps, C.modT, C.xT], [C.xT])
        k.barrier()


def phase_final(C):
    k, I, O = C.k, C.I, C.O
    with ExitStack() as st:
        lnfT = k.sb(st, "lnfT", [128, 8])
        k.dma("sp", lnfT.ap[:], I.lnf.rearrange("(c p) -> p c", p=128), [], [lnfT], allow_slow_non_contiguous=True)
        sq = [k.sb(st, "fsq%d" % i, [128, 512]) for i in range(2)]
        rstd = k.sb(st, "frstd", [128, 512])
        hT = k.sb(st, "fhT", [128, 8, 512])
        yt = [k.sb(st, "fy%d" % i, [128, D]) for i in range(2)]
        for tg in range(5):
            cols = slice(tg * 512, (tg + 1) * 512)
            ps = C.PS[7]
            for kc in range(8):
                s = sq[kc % 2]
                k.op("act", lambda e: e.activation(out=s.ap[:], in_=C.xT.ap[:, kc, cols], func=AF.Square), [C.xT], [s])
                k.op("pe", lambda e: e.matmul(ps.ap[:], lhsT=C.ones.ap[:], rhs=s.ap[:], start=(kc == 0), stop=(kc == 7)), [s, C.ones], [ps])
            k.op("act", lambda e: e.activation(out=rstd.ap[:], in_=ps.ap[:], func=AF.Sqrt, scale=1.0 / D, bias=C.eps.ap[:, 0:1]), [ps, C.eps], [rstd])
            k.op("dve", lambda e: e.reciprocal(rstd.ap[:], rstd.ap[:]), [rstd], [rstd])
            for kc in range(8):
                k.op("dve", lambda e: e.scalar_tensor_tensor(out=hT.ap[:, kc, :], in0=C.xT.ap[:, kc, cols], scalar=lnfT.ap[:, kc:kc + 1], in1=rstd.ap[:], op0=ALU.mult, op1=ALU.mult), [C.xT, rstd, lnfT], [hT])
            for tt in range(4):
                y = yt[tt % 2]
                for half in range(2):
                    ps2 = C.PS[(tt * 2 + half) % 6]
                    for j in range(4):
                        kc = half * 4 + j
                        k.op("pe", lambda e: e.transpose(ps2.ap[:, j * 128:(j + 1) * 128], hT.ap[:, kc, tt * 128:(tt + 1) * 128], C.ident.ap[:]), [hT, C.ident], [ps2])
                    if half == 0:
                        k.op("dve", lambda e: e.tensor_copy(y.ap[:, 0:512], ps2.ap[:]), [ps2], [y])
                    else:
                        k.op("act", lambda e: e.copy(y.ap[:, 512:1024], ps2.ap[:]), [ps2], [y])
                t0 = tg * 512 + tt * 128
                k.dma(None, O.y[t0:t0 + 128, :], y.ap[:], [y], [])
        k.barrier()


def make_consts():
    c = {}
    c["c_ident"] = np.eye(128, dtype=np.float32)
    c["c_antiid"] = np.ascontiguousarray(np.eye(128, dtype=np.float32)[::-1])
    R = np.zeros((128, 128), np.float32)
    cos = np.zeros((128, TS), np.float32)
    sin = np.zeros((128, TS), np.float32)
    t = np.arange(TS)
    pos = np.stack([t // 64, t % 64], 0).astype(np.float32)
    freqs = (10000.0 ** (-np.arange(16, dtype=np.float32) / 16)).astype(np.float32)
    for hh in range(2):
        for ax in range(2):
            for f in range(16):
                d1 = hh * 64 + ax * 32 + f
                d2 = d1 + 16
                ang = (pos[ax] * freqs[f]).astype(np.float32)
                cos[d1] = np.cos(ang); cos[d2] = np.cos(ang)
                sin[d1] = np.sin(ang); sin[d2] = np.sin(ang)
                R[d2, d1] = -1.0
                R[d1, d2] = 1.0
    c["c_rot"] = R
    c["c_cos"] = cos
    c["c_sin"] = sin
    r = np.arange(128)[:, None]
    cc = np.arange(128)[None, :]
    c["c_mleft"] = np.where(cc >= r, 0.0, NEG).astype(np.float32)
    c["c_mright"] = np.where(cc <= r, 0.0, NEG).astype(np.float32)
    col = np.arange(64)
    cs = np.clip(col - 8, 0, 48)
    ok = (col[None, :] >= cs[:, None]) & (col[None, :] < cs[:, None] + 16)
    c["c_colmask"] = np.where(ok, 0.0, NEG).astype(np.float32)
    b = np.zeros((128, 128), np.float32)
    b[:64, :64] = 1.0
    b[64:, 64:] = 1.0
    c["c_blk64"] = b
    c["c_iota"] = np.tile(np.arange(256, dtype=np.float32)[None, :], (128, 1))
    return c


def make_in_maps(inp):
    f = lambda a: np.ascontiguousarray(np.asarray(a), dtype=np.float32)
    consts = make_consts()
    shared = {
        "ln1_g": f(inp["ln1_g"]), "ln2_g": f(inp["ln2_g"]), "lnf_g": f(inp["lnf_g"]),
        "ada_w": f(inp["ada_w"]), "ada_b": f(inp["ada_b"]), "w_in": f(inp["w_in"]),
        "a_sink": f(inp["a_sink"]), "a_out": f(inp["a_out"]), "rw_mu": f(inp["rw_mu"]),
        "rw_w0": f(inp["rw_w0"]).reshape(L, 1024), "rw_w2": f(inp["rw_w2"]).reshape(L, 128, 512),
        "rw_a0": f(inp["rw_a0"]).reshape(L, 1024), "rw_a2": f(inp["rw_a2"]).reshape(L, 128, 512),
        "rw_g2": f(inp["rw_g2"]), "rw_kk": f(inp["rw_kk"]), "rw_ka": f(inp["rw_ka"]),
        "rw_rk": f(inp["rw_rk"]).reshape(L, 512), "rw_lnx_g": f(inp["rw_lnx_g"]), "rw_lnx_b": f(inp["rw_lnx_b"]),
        "rw_out": f(inp["rw_out"]), "na_rpb": f(inp["na_rpb"]), "na_out": f(inp["na_out"]), "w_o": f(inp["w_o"]),
        "pe_q": f(inp["pe_q"]), "pe_subkeys": f(inp["pe_subkeys"]).reshape(L, 16, 128, 128),
        "pe_u": f(inp["pe_u"]).reshape(L * 16384, D), "pe_v": f(inp["pe_v"]).reshape(L * 16384, D),
    }
    shared.update(consts)
    xp = f(inp["x_prompt"]); xs = f(inp["x_sample"])
    maps = []
    for c in range(8):
        b = c // 4
        m = dict(shared)
        m["xall"] = np.concatenate([xp[2 * c], xp[2 * c + 1], xs[b]], 0)
        m["cvec"] = np.stack([f(inp["c_ctx"]), f(inp["c"])[b]], 0)
        m["cak"] = f(inp["cache_a_k"])[b].reshape(L, 512, 128)
        m["cav"] = f(inp["cache_a_v"])[b].reshape(L, 512, 128)
        m["cck"] = f(inp["cache_c_k"])[b].reshape(L, 512, 512)
        m["ccv"] = f(inp["cache_c_v"])[b].reshape(L, 512, 512)
        m["st0"] = f(inp["state_rwkv"])[b].reshape(L, 128, 512)
        maps.append(m)
    return maps


_CACHE = {}


def kernel(**inputs):
    if "nc" not in _CACHE:
        _CACHE["nc"] = build()[0]
    nc = _CACHE["nc"]
    maps = make_in_maps(inputs)
    res = run_bass_kernel_spmd(nc, maps, core_ids=list(range(8)))
    R = res.results
    y_prompt = np.zeros((16, 256, D), np.float32)
    y_sample = np.zeros((2, TS, D), np.float32)
    nak = np.zeros((16, L, 256, 2, 64), np.float32)
    nav = np.zeros((16, L, 256, 2, 64), np.float32)
    nck = np.zeros((16, L, 256, 8, 64), np.float32)
    ncv = np.zeros((16, L, 256, 8, 64), np.float32)
    nst = np.zeros((16, L, 2, 8, 64, 64), np.float32)
    for c in range(8):
        r = R[c]
        y = np.asarray(r["y"])
        y_prompt[2 * c] = y[0:256]
        y_prompt[2 * c + 1] = y[256:512]
        if c % 4 == 0:
            y_sample[c // 4] = y[512:]
        for s in range(2):
            nak[2 * c + s] = np.asarray(r["nak"])[s].reshape(L, 256, 2, 64)
            nav[2 * c + s] = np.asarray(r["nav"])[s].reshape(L, 256, 2, 64)
            nck[2 * c + s] = np.asarray(r["nck"])[s].reshape(L, 256, 8, 64)
            ncv[2 * c + s] = np.asarray(r["ncv"])[s].reshape(L, 256, 8, 64)
            nst[2 * c + s] = np.asarray(r["nst"])[s].reshape(L, 2, 8, 64, 64)
    return (y_prompt, y_sample, nak, nav, nck, ncv, nst)
```

```python
import numpy as np
from contextlib import ExitStack
import concourse.bass as bass
import concourse.mybir as mybir
from concourse.bass_utils import run_bass_kernel_spmd

F32 = mybir.dt.float32
I32 = mybir.dt.int32
U32 = mybir.dt.uint32
AF = mybir.ActivationFunctionType
ALU = mybir.AluOpType
AX = mybir.AxisListType

NDS = 40
SAME_ENGINE_SYNC = {"pe": False, "dve": True, "act": True, "pool": True, "sp": True}

D = 1024
L = 4
TOK = 2560
NPT = 512
TS = 2048
IN_COLS = 7296
PT_COLS = 3200
PF_ROWS = 4736
SCALE = 0.125
NEG = -30000.0


class Res:
    __slots__ = ("w", "r", "ap", "name")

    def __init__(self, ap=None, name=None):
        self.w = None
        self.r = []
        self.ap = ap
        self.name = name


class KB:
    def __init__(self, nc):
        self.nc = nc
        self.eng = {"pe": nc.tensor, "dve": nc.vector, "act": nc.scalar, "pool": nc.gpsimd, "sp": nc.sync}
        self.esem = {k: nc.alloc_semaphore("es_" + k) for k in self.eng}
        self.ecnt = {k: 0 for k in self.eng}
        self.seen = {k: {} for k in self.eng}
        self.dsems = [nc.alloc_semaphore("ds%d" % i) for i in range(NDS)]
        self.dcnt = [0] * NDS
        self.dnext = 0
        self.nins = 0
        self.uid = 0
        self.rr = 0

    def sb(self, st, name, shape, dt=F32):
        self.uid += 1
        t = st.enter_context(self.nc.sbuf_tensor("%s_%d" % (name, self.uid), list(shape), dt))
        return Res(t.ap(), name)

    def ps(self, name, shape, dt=F32):
        t = self.nc.alloc_psum_tensor(name, list(shape), dt)
        return Res(t.ap(), name)

    def _wait(self, e, ev):
        sem, val = ev
        own = sem is self.esem[e]
        if own and not SAME_ENGINE_SYNC[e]:
            return
        key = id(sem)
        if self.seen[e].get(key, 0) >= val:
            return
        self.eng[e].wait_ge(sem, val)
        self.seen[e][key] = val
        self.nins += 1

    def deps(self, e, reads, writes, ww=()):
        for r in reads:
            if r.w is not None:
                self._wait(e, r.w)
        for w in writes:
            if w.w is not None:
                self._wait(e, w.w)
            for ev in w.r:
                self._wait(e, ev)
        for w in ww:
            if w.w is not None and w.w[0] is not self.esem[e]:
                self._wait(e, w.w)
            for ev in w.r:
                self._wait(e, ev)

    def _record(self, ev, reads, writes):
        for r in reads:
            r.r.append(ev)
            if len(r.r) > 16:
                d = {}
                for s, v in r.r:
                    if id(s) not in d or d[id(s)][1] < v:
                        d[id(s)] = (s, v)
                r.r = list(d.values())
        for w in writes:
            w.w = ev
            w.r = []

    def op(self, e, fn, reads, writes, ww=()):
        self.deps(e, reads, writes, ww)
        ins = fn(self.eng[e])
        self.ecnt[e] += 1
        ins.then_inc(self.esem[e], 1)
        self.nins += 1
        self._record((self.esem[e], self.ecnt[e]), reads, list(writes) + list(ww))

    def _dma_common(self, q, reads, writes, emit):
        kk = self.dnext
        self.dnext = (kk + 1) % NDS
        sem = self.dsems[kk]
        if self.dcnt[kk] > 0:
            self._wait(q, (sem, self.dcnt[kk]))
        self.deps(q, reads, writes)
        ins = emit()
        self.dcnt[kk] += 16
        ins.then_inc(sem, 16)
        self.nins += 1
        self._record((sem, self.dcnt[kk]), reads, writes)

    def dma(self, q, out, in_, reads, writes, **kw):
        if q is None:
            q = ("sp", "act")[self.rr % 2]
            self.rr += 1
        self._dma_common(q, reads, writes, lambda: self.eng[q].dma_start(out=out, in_=in_, **kw))

    def idma(self, out, in_, off_ap, reads, writes):
        self._dma_common("pool", reads, writes, lambda: self.nc.gpsimd.indirect_dma_start(
            out=out, out_offset=None, in_=in_,
            in_offset=bass.IndirectOffsetOnAxis(ap=off_ap, axis=0)))

    def barrier(self):
        engs = ("pe", "dve", "act", "pool", "sp")
        for e in engs:
            for kk in range(NDS):
                if self.dcnt[kk] > 0:
                    self._wait(e, (self.dsems[kk], self.dcnt[kk]))
            for e2 in engs:
                if e2 != e and self.ecnt[e2] > 0:
                    self._wait(e, (self.esem[e2], self.ecnt[e2]))

    def finish(self):
        self.barrier()


class Ctx:
    pass


def bc(ap, shape):
    return ap.to_broadcast(list(shape))


def build(nlayers=L, debug=False, stages=("all",)):
    nc = bass.Bass("TRN2", target_bir_lowering=False)
    k = KB(nc)
    C = Ctx()
    C.k = k
    C.nc = nc
    C.debug = debug

    def din(name, shape, dt=F32):
        return nc.dram_tensor(name, list(shape), dt, kind="ExternalInput").ap()

    def dout(name, shape, dt=F32):
        return nc.dram_tensor(name, list(shape), dt, kind="ExternalOutput").ap()

    def scr(name, shape, dt=F32):
        if debug:
            return nc.dram_tensor(name, list(shape), dt, kind="ExternalOutput").ap()
        return nc.dram_tensor(name, list(shape), dt).ap()

    I = Ctx()
    C.I = I
    I.xall = din("xall", [TOK, D])
    I.cvec = din("cvec", [2, D])
    I.cak = din("cak", [L, 512, 128])
    I.cav = din("cav", [L, 512, 128])
    I.cck = din("cck", [L, 512, 512])
    I.ccv = din("ccv", [L, 512, 512])
    I.st0 = din("st0", [L, 128, 512])
    I.ln1 = din("ln1_g", [L, D])
    I.ln2 = din("ln2_g", [L, D])
    I.lnf = din("lnf_g", [D])
    I.ada_w = din("ada_w", [L, D, 6 * D])
    I.ada_b = din("ada_b", [L, 6 * D])
    I.w_in = din("w_in", [L, D, IN_COLS])
    I.a_sink = din("a_sink", [L, 8])
    I.a_out = din("a_out", [L, 512, D])
    I.rw_mu = din("rw_mu", [L, 1920])
    I.rw_w0 = din("rw_w0", [L, 1024])
    I.rw_w2 = din("rw_w2", [L, 128, 512])
    I.rw_a0 = din("rw_a0", [L, 1024])
    I.rw_a2 = din("rw_a2", [L, 128, 512])
    I.rw_g2 = din("rw_g2", [L, 128, 512])
    I.rw_kk = din("rw_kk", [L, 512])
    I.rw_ka = din("rw_ka", [L, 512])
    I.rw_rk = din("rw_rk", [L, 512])
    I.rw_lnx_g = din("rw_lnx_g", [L, 512])
    I.rw_lnx_b = din("rw_lnx_b", [L, 512])
    I.rw_out = din("rw_out", [L, 512, D])
    I.na_rpb = din("na_rpb", [L, 8, 15, 31])
    I.na_out = din("na_out", [L, 512, D])
    I.w_o = din("w_o", [L, D, D])
    I.pe_q = din("pe_q", [L, D, 2048])
    I.pe_sk = din("pe_subkeys", [L, 16, 128, 128])
    I.pe_u = din("pe_u", [L * 16384, D])
    I.pe_v = din("pe_v", [L * 16384, D])
    I.ident = din("c_ident", [128, 128])
    I.antiid = din("c_antiid", [128, 128])
    I.rot = din("c_rot", [128, 128])
    I.cos = din("c_cos", [128, TS])
    I.sin = din("c_sin", [128, TS])
    I.mleft = din("c_mleft", [128, 128])
    I.mright = din("c_mright", [128, 128])
    I.colmask = din("c_colmask", [64, 64])
    I.blk64 = din("c_blk64", [128, 128])
    I.iota256 = din("c_iota", [128, 256])

    O = Ctx()
    C.O = O
    O.y = dout("y", [TOK, D])
    O.nak = dout("nak", [2, L, 256, 128])
    O.nav = dout("nav", [2, L, 256, 128])
    O.nck = dout("nck", [2, L, 256, 512])
    O.ncv = dout("ncv", [2, L, 256, 512])
    O.nst = dout("nst", [2, L, 64, 1024])

    S = Ctx()
    C.S = S
    S.PT = scr("s_PT", [TOK, PT_COLS])
    S.PF = scr("s_PF", [PF_ROWS, TOK])
    S.YA = scr("s_YA", [512, TOK])
    S.YB = scr("s_YB", [512, TOK])
    S.YC = scr("s_YC", [512, TOK])
    S.QR = scr("s_QR", [640, TS])
    S.E = scr("s_E", [1, 120, 127])
    S.OPS = scr("s_OPS", [2, TOK, 3072])
    S.YS = scr("s_YS", [2, TOK, 512])
    S.BG = scr("s_BG", [TOK, 1024])
    S.rOPS = Res(); S.rYS = Res(); S.rBG = Res()
    S.rPT = Res(); S.rPF = Res(); S.rYA = Res(); S.rYB = Res(); S.rYC = Res(); S.rQR = Res(); S.rE = Res()

    C.PS = [k.ps("psb%d" % i, [128, 512]) for i in range(8)]

    with ExitStack() as gst:
        C.xT = k.sb(gst, "xT", [128, 8, TOK])
        C.ident = k.sb(gst, "ident", [128, 128])
        C.ones = k.sb(gst, "ones", [128, 128])
        C.eps = k.sb(gst, "eps", [128, 1])
        C.scT = k.sb(gst, "scT", [128, 8, 2])
        C.modT = k.sb(gst, "modT", [128, 48, 2])
        C.gs1 = k.sb(gst, "gs1", [128, 8, 2])
        C.gs2 = k.sb(gst, "gs2", [128, 8, 2])
        k.dma("sp", C.ident.ap[:], I.ident[:, :], [], [C.ident])
        k.op("dve", lambda e: e.memset(C.ones.ap[:], 1.0), [], [C.ones])
        k.op("dve", lambda e: e.memset(C.eps.ap[:], 1e-6), [], [C.eps])

        phase_init(C)
        for l in range(nlayers):
            phase_mod(C, l)
            for tg in range(5):
                phase_norm_win(C, l, tg)
            k.barrier()
            phase_kv_out(C, l)
            if "attn" in stages or "all" in stages:
                phase_attn(C, l)
            if "rwkv" in stages or "all" in stages:
                phase_rwkv(C, l)
            if "merge" in stages or "all" in stages:
                phase_merge(C, l)
            if "peer" in stages or "all" in stages:
                phase_peer(C, l)
        phase_final(C)
        k.finish()
    C.nins = k.nins
    return nc, C


def phase_init(C):
    k, I = C.k, C.I
    with ExitStack() as st:
        xt = [k.sb(st, "xin%d" % i, [128, D]) for i in range(2)]
        for t in range(TOK // 128):
            x = xt[t % 2]
            k.dma(None, x.ap[:], I.xall[t * 128:(t + 1) * 128, :], [], [x])
            for half in range(2):
                ps = C.PS[(t * 2 + half) % 8]
                for j in range(4):
                    kc = half * 4 + j
                    k.op("pe", lambda e: e.transpose(ps.ap[:, j * 128:(j + 1) * 128], x.ap[:, kc * 128:(kc + 1) * 128], C.ident.ap[:]), [x, C.ident], [ps])
                eng = ("dve", "act")[half]
                if eng == "dve":
                    k.op("dve", lambda e: e.tensor_copy(C.xT.ap[:, half * 4:half * 4 + 4, t * 128:(t + 1) * 128], ps.ap[:].rearrange("p (a b) -> p a b", a=4)), [ps], [C.xT])
                else:
                    k.op("act", lambda e: e.copy(C.xT.ap[:, half * 4:half * 4 + 4, t * 128:(t + 1) * 128], ps.ap[:].rearrange("p (a b) -> p a b", a=4)), [ps], [C.xT])
        for g in range(2):
            k.dma("sp", C.scT.ap[:, :, g], I.cvec[g].rearrange("(c p) -> p c", p=128), [], [C.scT], allow_slow_non_contiguous=True)
        k.op("act", lambda e: e.activation(out=C.scT.ap[:], in_=C.scT.ap[:], func=AF.Silu), [C.scT], [C.scT])
        k.barrier()


def phase_mod(C, l):
    k, I = C.k, C.I
    with ExitStack() as st:
        wb = [k.sb(st, "adaw%d" % i, [128, 8, 512]) for i in range(2)]
        abT = k.sb(st, "abT", [128, 48])
        lnT = k.sb(st, "lnT", [128, 2, 8])
        k.dma("sp", abT.ap[:], I.ada_b[l].rearrange("(c p) -> p c", p=128), [], [abT], allow_slow_non_contiguous=True)
        k.dma("sp", lnT.ap[:, 0, :], I.ln1[l].rearrange("(c p) -> p c", p=128), [], [lnT], allow_slow_non_contiguous=True)
        k.dma("sp", lnT.ap[:, 1, :], I.ln2[l].rearrange("(c p) -> p c", p=128), [], [lnT], allow_slow_non_contiguous=True)
        ps = C.PS[0]
        for b in range(12):
            w = wb[b % 2]
            k.dma(None, w.ap[:], I.ada_w[l][:, b * 512:(b + 1) * 512].rearrange("(c p) n -> p c n", p=128), [], [w])
            for sub in range(4):
                fc = b * 4 + sub
                for kc in range(8):
                    k.op("pe", lambda e: e.matmul(ps.ap[:, fc * 2:fc * 2 + 2], lhsT=w.ap[:, kc, sub * 128:(sub + 1) * 128], rhs=C.scT.ap[:, kc, :], start=(kc == 0), stop=(kc == 7)), [w, C.scT], [ps])
        k.op("dve", lambda e: e.tensor_tensor(out=C.modT.ap[:], in0=ps.ap[:, 0:96].rearrange("p (a b) -> p a b", b=2), in1=bc(abT.ap[:].unsqueeze(2), [128, 48, 2]), op=ALU.add), [ps, abT], [C.modT])
        for (gs, mi, li) in ((C.gs1, 1, 0), (C.gs2, 4, 1)):
            k.op("dve", lambda e: e.tensor_scalar(gs.ap[:], C.modT.ap[:, mi * 8:mi * 8 + 8, :], 1.0, None, op0=ALU.add), [C.modT], [gs])
            k.op("dve", lambda e: e.tensor_tensor(out=gs.ap[:], in0=gs.ap[:], in1=bc(lnT.ap[:, li, :].unsqueeze(2), [128, 8, 2]), op=ALU.mult), [gs, lnT], [gs])
        k.barrier()


JOBS = [
    (0, 512, "F", 0), (512, 128, "F", 512), (2688, 512, "F", 640), (3200, 512, "F", 1152),
    (4224, 512, "F", 1664), (4736, 512, "F", 2176), (5248, 512, "F", 2688), (5760, 512, "F", 3200),
    (6272, 512, "F", 3712), (6784, 512, "F", 4224),
    (512, 256, "T", 0), (768, 512, "T", 256), (1280, 512, "T", 768), (1792, 512, "T", 1280), (2304, 384, "T", 1792),
    (3200, 512, "T", 2176), (3712, 512, "T", 2688),
]


def norm_group(C, st, tg, gs, shift_idx, hT):
    k = C.k
    g = 0 if tg == 0 else 1
    cols = slice(tg * 512, (tg + 1) * 512)
    sq = [k.sb(st, "sq%d" % i, [128, 512]) for i in range(2)]
    rstd = k.sb(st, "rstd", [128, 512])
    ps = C.PS[7]
    for kc in range(8):
        s = sq[kc % 2]
        k.op("act", lambda e: e.activation(out=s.ap[:], in_=C.xT.ap[:, kc, cols], func=AF.Square), [C.xT], [s])
        k.op("pe", lambda e: e.matmul(ps.ap[:], lhsT=C.ones.ap[:], rhs=s.ap[:], start=(kc == 0), stop=(kc == 7)), [s, C.ones], [ps])
    k.op("act", lambda e: e.activation(out=rstd.ap[:], in_=ps.ap[:], func=AF.Sqrt, scale=1.0 / D, bias=C.eps.ap[:, 0:1]), [ps, C.eps], [rstd])
    k.op("dve", lambda e: e.reciprocal(rstd.ap[:], rstd.ap[:]), [rstd], [rstd])
    for kc in range(8):
        eng = "dve" if kc % 2 == 0 else "pool"
        k.op(eng, lambda e: e.tensor_tensor(out=hT.ap[:, kc, :], in0=C.xT.ap[:, kc, cols], in1=rstd.ap[:], op=ALU.mult), [C.xT, rstd], [hT])
        k.op(eng, lambda e: e.tensor_scalar(hT.ap[:, kc, :], hT.ap[:, kc, :], gs.ap[:, kc, g:g + 1], C.modT.ap[:, shift_idx * 8 + kc, g:g + 1], op0=ALU.mult, op1=ALU.add), [hT, gs, C.modT], [hT])


def phase_norm_win(C, l, tg):
    k, I, S = C.k, C.I, C.S
    with ExitStack() as st:
        hT = k.sb(st, "hT", [128, 8, 512])
        norm_group(C, st, tg, C.gs1, 0, hT)
        wb = [k.sb(st, "winw%d" % i, [128, 8, 512]) for i in range(2)]
        ev = [k.sb(st, "winev%d" % i, [128, 512]) for i in range(4)]
        ei = 0
        pi = 0
        for ji, (c0, n, lay, d0) in enumerate(JOBS):
            w = wb[ji % 2]
            k.dma(None, w.ap[:, :, 0:n], I.w_in[l][:, c0:c0 + n].rearrange("(c p) n -> p c n", p=128), [], [w])
            if lay == "F":
                for sub in range(n // 128):
                    ps = C.PS[pi % 6]; pi += 1
                    for kc in range(8):
                        k.op("pe", lambda e: e.matmul(ps.ap[:], lhsT=w.ap[:, kc, sub * 128:(sub + 1) * 128], rhs=hT.ap[:, kc, :], start=(kc == 0), stop=(kc == 7)), [w, hT], [ps])
                    o = ev[ei % 4]; ei += 1
                    if d0 >= 1664:
                        k.op("act", lambda e: e.activation(out=o.ap[:], in_=ps.ap[:], func=AF.Sigmoid), [ps], [o])
                    elif ei % 2 == 0:
                        k.op("act", lambda e: e.copy(o.ap[:], ps.ap[:]), [ps], [o])
                    else:
                        k.op("dve", lambda e: e.tensor_copy(o.ap[:], ps.ap[:]), [ps], [o])
                    r0 = d0 + sub * 128
                    k.dma(None, S.PF[r0:r0 + 128, tg * 512:(tg + 1) * 512], o.ap[:], [o], [S.rPF])
            else:
                for tt in range(4):
                    ps = C.PS[pi % 6]; pi += 1
                    for kc in range(8):
                        k.op("pe", lambda e: e.matmul(ps.ap[:, 0:n], lhsT=hT.ap[:, kc, tt * 128:(tt + 1) * 128], rhs=w.ap[:, kc, 0:n], start=(kc == 0), stop=(kc == 7)), [w, hT], [ps])
                    o = ev[ei % 4]; ei += 1
                    if ei % 2 == 0:
                        k.op("act", lambda e: e.copy(o.ap[:, 0:n], ps.ap[:, 0:n]), [ps], [o])
                    else:
                        k.op("dve", lambda e: e.tensor_copy(o.ap[:, 0:n], ps.ap[:, 0:n]), [ps], [o])
                    t0 = tg * 512 + tt * 128
                    k.dma(None, S.PT[t0:t0 + 128, d0:d0 + n], o.ap[:, 0:n], [o], [S.rPT])
        k.barrier()


def phase_kv_out(C, l):
    k, S, O = C.k, C.S, C.O
    for s in range(2):
        rows = slice(s * 256, (s + 1) * 256)
        k.dma(None, O.nak[s, l], S.PT[rows, 0:128], [S.rPT], [])
        k.dma(None, O.nav[s, l], S.PT[rows, 128:256], [S.rPT], [])
        k.dma(None, O.nck[s, l], S.PT[rows, 2176:2688], [S.rPT], [])
        k.dma(None, O.ncv[s, l], S.PT[rows, 2688:3200], [S.rPT], [])


def attn_unit(C, W, ui, qT, qres, Mq, segs, vch, sink_h, out_ap, out_res):
    k = C.k
    par = ui % 2
    S_sb = W.S[par]
    sm = W.sm[par]
    tot = sum(s[2] for s in segs)
    off = 0
    k.op("pool", lambda e: e.memset(sm.ap[:], 0.0), [], [sm])
    for si, (kres, kT, n, masks) in enumerate(segs):
        ps = C.PS[par * 2 + si]
        k.op("pe", lambda e: e.matmul(ps.ap[:Mq, 0:n], lhsT=qT, rhs=kT, start=True, stop=True), [qres, kres], [ps])
        covered = []
        for (c0, ncol, mres, map_) in masks:
            k.op("dve", lambda e: e.tensor_tensor(out=S_sb.ap[:Mq, off + c0:off + c0 + ncol], in0=ps.ap[:Mq, c0:c0 + ncol], in1=map_, op=ALU.add), [ps, mres], [S_sb])
            covered.append((c0, c0 + ncol))
        covered.sort()
        pos = 0
        gaps = []
        for (a, b) in covered:
            if a > pos:
                gaps.append((pos, a))
            pos = max(pos, b)
        if pos < n:
            gaps.append((pos, n))
        for gi, (a, b) in enumerate(gaps):
            if not masks:
                k.op("act", lambda e: e.copy(S_sb.ap[:Mq, off + a:off + b], ps.ap[:Mq, a:b]), [ps], [S_sb])
            else:
                k.op("dve", lambda e: e.tensor_copy(S_sb.ap[:Mq, off + a:off + b], ps.ap[:Mq, a:b]), [ps], [S_sb])
        off += n
    k.op("dve", lambda e: e.reduce_max(out=sm.ap[:Mq, 0:1], in_=S_sb.ap[:Mq, 0:tot], axis=AX.X), [S_sb], [sm])
    if sink_h is not None:
        k.op("dve", lambda e: e.tensor_scalar(sm.ap[:Mq, 1:2], sm.ap[:Mq, 0:1], -SCALE, W.nsink.ap[:Mq, sink_h:sink_h + 1], op0=ALU.mult, op1=ALU.min), [sm, W.nsink], [sm])
    else:
        k.op("dve", lambda e: e.tensor_scalar(sm.ap[:Mq, 1:2], sm.ap[:Mq, 0:1], -SCALE, None, op0=ALU.mult), [sm], [sm])
    k.op("act", lambda e: e.activation(out=S_sb.ap[:Mq, 0:tot], in_=S_sb.ap[:Mq, 0:tot], func=AF.Exp, bias=sm.ap[:Mq, 1:2], scale=SCALE, accum_out=sm.ap[:Mq, 2:3]), [S_sb, sm], [S_sb, sm])
    if sink_h is not None:
        k.op("act", lambda e: e.activation(out=sm.ap[:Mq, 3:4], in_=W.sink.ap[:Mq, sink_h:sink_h + 1], func=AF.Exp, bias=sm.ap[:Mq, 1:2], scale=1.0), [sm, W.sink], [sm])
        k.op("dve", lambda e: e.tensor_tensor(out=sm.ap[:Mq, 2:3], in0=sm.ap[:Mq, 2:3], in1=sm.ap[:Mq, 3:4], op=ALU.add), [sm], [sm])
    k.op("dve", lambda e: e.reciprocal(sm.ap[:Mq, 4:5], sm.ap[:Mq, 2:3]), [sm], [sm])
    k.op("dve", lambda e: e.tensor_scalar(S_sb.ap[:Mq, 0:tot], S_sb.ap[:Mq, 0:tot], sm.ap[:Mq, 4:5], None, op0=ALU.mult), [S_sb, sm], [S_sb])
    pso = C.PS[6 + par]
    nch = len(vch)
    for g0 in range(0, nch, 4):
        grp = vch[g0:g0 + 4]
        gi = W.tcount
        W.tcount += 1
        pst = C.PS[4 + gi % 2]
        ptsb = W.PTs[gi % 2]
        maxnk = max(v[2] for v in grp)
        for jj, (vres, vap, nk, c0) in enumerate(grp):
            k.op("pe", lambda e: e.transpose(pst.ap[:nk, jj * 128:jj * 128 + Mq], S_sb.ap[:Mq, c0:c0 + nk], C.ident.ap[:Mq, :Mq]), [S_sb, C.ident], [pst])
        w = len(grp) * 128
        if gi % 2 == 0:
            k.op("dve", lambda e: e.tensor_copy(ptsb.ap[:maxnk, 0:w], pst.ap[:maxnk, 0:w]), [pst], [ptsb])
        else:
            k.op("act", lambda e: e.copy(ptsb.ap[:maxnk, 0:w], pst.ap[:maxnk, 0:w]), [pst], [ptsb])
        for jj, (vres, vap, nk, c0) in enumerate(grp):
            ci = g0 + jj
            k.op("pe", lambda e: e.matmul(pso.ap[:64, 0:Mq], lhsT=vap, rhs=ptsb.ap[:nk, jj * 128:jj * 128 + Mq], start=(ci == 0), stop=(ci == nch - 1)), [vres, ptsb], [pso])
    osb = W.o[par]
    k.op("act", lambda e: e.copy(osb.ap[:64, 0:Mq], pso.ap[:64, 0:Mq]), [pso], [osb])
    k.dma(None, out_ap, osb.ap[:64, 0:Mq], [osb], [out_res])


def attn_work(C, st, l, need_sink):
    k, I = C.k, C.I
    W = Ctx()
    W.S = [k.sb(st, "S_sb%d" % i, [128, 1024]) for i in range(2)]
    W.sm = [k.sb(st, "sm%d" % i, [128, 8]) for i in range(2)]
    W.PTs = [k.sb(st, "PTs%d" % i, [128, 512]) for i in range(2)]
    W.o = [k.sb(st, "osb%d" % i, [64, 128]) for i in range(2)]
    W.tcount = 0
    if need_sink:
        W.sink = k.sb(st, "sink", [128, 8])
        W.nsink = k.sb(st, "nsink", [128, 8])
        k.dma("sp", W.sink.ap[:], I.a_sink[l:l + 1, :].partition_broadcast(128) if False else bass.AP(tensor=I.a_sink.tensor, offset=l * 8, ap=[[0, 128], [1, 8]]), [], [W.sink])
        k.op("dve", lambda e: e.tensor_scalar(W.nsink.ap[:], W.sink.ap[:], -1.0, None, op0=ALU.mult), [W.sink], [W.nsink])
    return W


def phase_attn_prompt(C, l):
    k, I, S = C.k, C.I, C.S
    with ExitStack() as st:
        W = attn_work(C, st, l, True)
        Q = k.sb(st, "pQ", [64, 8, 256])
        K_ = k.sb(st, "pK", [64, 8, 256])
        V = k.sb(st, "pV", [128, 2, 512])
        ui = 0
        for mixer in ("A", "C"):
            for s in range(2):
                base = s * 256
                if mixer == "A":
                    q0, k0, nkv, v0, vw, Y, rY = 0, 512, 2, 128, 128, S.YA, S.rYA
                else:
                    q0, k0, nkv, v0, vw, Y, rY = 640, 1152, 8, 2688, 512, S.YC, S.rYC
                k.dma(None, Q.ap[:], S.PF[q0:q0 + 512, base:base + 256].rearrange("(h d) t -> d h t", d=64), [S.rPF], [Q])
                k.dma(None, K_.ap[:, 0:nkv, :], S.PF[k0:k0 + nkv * 64, base:base + 256].rearrange("(h d) t -> d h t", d=64), [S.rPF], [K_])
                k.dma(None, V.ap[:, :, 0:vw], S.PT[base:base + 256, v0:v0 + vw].rearrange("(t p) f -> p t f", p=128), [S.rPT], [V])
                for h in range(8):
                    kv = h // 4 if mixer == "A" else h
                    for qb in range(2):
                        segs = [(K_, K_.ap[:, kv, :], 256, [])]
                        vch = [(V, V.ap[:, t, kv * 64:(kv + 1) * 64], 128, t * 128) for t in range(2)]
                        attn_unit(C, W, ui, Q.ap[:, h, qb * 128:(qb + 1) * 128], Q, 128, segs, vch,
                                  h if mixer == "A" else None,
                                  Y[h * 64:(h + 1) * 64, base + qb * 128:base + (qb + 1) * 128], rY)
                        ui += 1
        k.barrier()


def load_cache_kT(C, st, src, nh, name):
    k = C.k
    ct = k.sb(st, name + "_tm", [128, 4, nh * 64])
    CK = k.sb(st, name, [64, nh, 512])
    k.dma(None, ct.ap[:], src.rearrange("(j p) f -> p j f", p=128), [], [ct])
    for h in range(nh):
        ps = C.PS[h % 4]
        for j in range(4):
            k.op("pe", lambda e: e.transpose(ps.ap[:64, j * 128:(j + 1) * 128], ct.ap[:, j, h * 64:(h + 1) * 64], C.ident.ap[:]), [ct, C.ident], [ps])
        k.op("dve", lambda e: e.tensor_copy(CK.ap[:, h, :], ps.ap[:64, :]), [ps], [CK])
    return CK


def phase_attn_sample_A(C, l):
    k, I, S = C.k, C.I, C.S
    B0 = NPT
    with ExitStack() as st:
        cos = k.sb(st, "cos", [128, TS]); sin = k.sb(st, "sin", [128, TS]); rot = k.sb(st, "rot", [128, 128])
        k.dma("sp", cos.ap[:], I.cos[:, :], [], [cos])
        k.dma("act", sin.ap[:], I.sin[:, :], [], [sin])
        k.dma("sp", rot.ap[:], I.rot[:, :], [], [rot])
        X = [k.sb(st, "ropeX%d" % i, [128, TS]) for i in range(2)]
        T1 = [k.sb(st, "ropeT%d" % i, [128, 512]) for i in range(2)]
        XR = [k.sb(st, "ropeR%d" % i, [128, 512]) for i in range(2)]
        cnt = 0
        for c in range(5):
            x = X[c % 2]
            k.dma(None, x.ap[:], S.PF[c * 128:(c + 1) * 128, B0:B0 + TS], [S.rPF], [x])
            for tg in range(4):
                cs = slice(tg * 512, (tg + 1) * 512)
                ps = C.PS[cnt % 4]; t1 = T1[cnt % 2]; xr = XR[cnt % 2]; cnt += 1
                k.op("pe", lambda e: e.matmul(ps.ap[:], lhsT=rot.ap[:], rhs=x.ap[:, cs], start=True, stop=True), [rot, x], [ps])
                k.op("pool", lambda e: e.tensor_tensor(out=t1.ap[:], in0=x.ap[:, cs], in1=cos.ap[:, cs], op=ALU.mult), [x, cos], [t1])
                k.op("dve", lambda e: e.tensor_tensor(out=xr.ap[:], in0=ps.ap[:], in1=sin.ap[:, cs], op=ALU.mult), [ps, sin], [xr])
                k.op("dve", lambda e: e.tensor_tensor(out=xr.ap[:], in0=xr.ap[:], in1=t1.ap[:], op=ALU.add), [xr, t1], [xr])
                k.dma(None, S.QR[c * 128:(c + 1) * 128, cs], xr.ap[:], [xr], [S.rQR])
        k.barrier()
    with ExitStack() as st:
        W = attn_work(C, st, l, True)
        ml = k.sb(st, "mleft", [128, 128]); mr = k.sb(st, "mright", [128, 128])
        k.dma("sp", ml.ap[:], I.mleft[:, :], [], [ml])
        k.dma("sp", mr.ap[:], I.mright[:, :], [], [mr])
        CK = load_cache_kT(C, st, I.cak[l], 2, "CKa")
        CV = k.sb(st, "CVa", [128, 4, 128])
        k.dma(None, CV.ap[:], I.cav[l].rearrange("(j p) f -> p j f", p=128), [], [CV])
        Q = k.sb(st, "sQ", [64, TS]); K_ = k.sb(st, "sK", [64, TS]); V = k.sb(st, "sV", [128, 16, 64])
        ui = 0
        for h in range(8):
            kv = h // 4
            if h % 4 == 0:
                k.dma(None, K_.ap[:], S.QR[512 + kv * 64:512 + (kv + 1) * 64, :], [S.rQR], [K_])
                k.dma(None, V.ap[:], S.PT[B0:B0 + TS, 128 + kv * 64:128 + (kv + 1) * 64].rearrange("(t p) f -> p t f", p=128), [S.rPT], [V])
            k.dma(None, Q.ap[:], S.QR[h * 64:(h + 1) * 64, :], [S.rQR], [Q])
            for n in range(16):
                ta = max(0, n - 1); tb = min(16, n + 2)
                nlat = (tb - ta) * 128
                masks = []
                if n > 0:
                    masks.append((0, 128, ml, ml.ap[:, :]))
                if n < 15:
                    masks.append((nlat - 128, 128, mr, mr.ap[:, :]))
                segs = [(K_, K_.ap[:, ta * 128:tb * 128], nlat, masks), (CK, CK.ap[:, kv, :], 512, [])]
                vch = [(V, V.ap[:, t, :], 128, (t - ta) * 128) for t in range(ta, tb)]
                vch += [(CV, CV.ap[:, j, kv * 64:(kv + 1) * 64], 128, nlat + j * 128) for j in range(4)]
                attn_unit(C, W, ui, Q.ap[:, n * 128:(n + 1) * 128], Q, 128, segs, vch, h,
                          S.YA[h * 64:(h + 1) * 64, B0 + n * 128:B0 + (n + 1) * 128], S.rYA)
                ui += 1
        k.barrier()


def phase_attn_sample_C(C, l):
    k, I, S = C.k, C.I, C.S
    B0 = NPT
    with ExitStack() as st:
        W = attn_work(C, st, l, False)
        z = k.sb(st, "zer", [120, 127])
        k.op("dve", lambda e: e.memset(z.ap[:], 0.0), [], [z])
        k.dma("sp", S.E[0], z.ap[:], [z], [S.rE])
        k.dma("sp", S.E[0, :, 48:79], I.na_rpb[l].rearrange("h r c -> (h r) c"), [S.rE], [S.rE])
        MB = k.sb(st, "MB", [64, 120, 64])
        cm = k.sb(st, "colmask", [64, 64])
        k.dma("sp", cm.ap[:], I.colmask[:, :], [], [cm])
        for c in range(64):
            k.dma(None, MB.ap[c:c + 1, :, :], S.E[0:1, :, 63 - c:127 - c], [S.rE], [MB])
        k.op("dve", lambda e: e.scalar_tensor_tensor(out=MB.ap[:], in0=MB.ap[:], scalar=1.0 / SCALE, in1=bc(cm.ap[:].unsqueeze(1), [64, 120, 64]), op0=ALU.mult, op1=ALU.add), [MB, cm], [MB])
        CK = load_cache_kT(C, st, I.cck[l], 8, "CKc")
        CV = k.sb(st, "CVc", [128, 4, 512])
        k.dma(None, CV.ap[:], I.ccv[l].rearrange("(j p) f -> p j f", p=128), [], [CV])
        Q = k.sb(st, "cQ", [64, TS]); K_ = k.sb(st, "cK", [64, TS]); V = k.sb(st, "cV", [64, 32, 64])
        ui = 0
        for h in range(8):
            k.dma(None, Q.ap[:], S.PF[640 + h * 64:640 + (h + 1) * 64, B0:B0 + TS], [S.rPF], [Q])
            k.dma(None, K_.ap[:], S.PF[1152 + h * 64:1152 + (h + 1) * 64, B0:B0 + TS], [S.rPF], [K_])
            k.dma(None, V.ap[:], S.PT[B0:B0 + TS, 2688 + h * 64:2688 + (h + 1) * 64].rearrange("(r c) f -> c r f", c=64), [S.rPT], [V])
            for r in range(32):
                rs = min(max(r - 4, 0), 24)
                dr0 = rs - r + 7
                bias = MB.ap[:, h * 15 + dr0:h * 15 + dr0 + 8, :].rearrange("p a b -> p (a b)")
                segs = [(K_, K_.ap[:, rs * 64:rs * 64 + 512], 512, [(0, 512, MB, bias)]), (CK, CK.ap[:, h, :], 512, [])]
                vch = [(V, V.ap[:, rs + a, :], 64, a * 64) for a in range(8)]
                vch += [(CV, CV.ap[:, j, h * 64:(h + 1) * 64], 128, 512 + j * 128) for j in range(4)]
                attn_unit(C, W, ui, Q.ap[:, r * 64:(r + 1) * 64], Q, 64, segs, vch, None,
                          S.YC[h * 64:(h + 1) * 64, B0 + r * 64:B0 + (r + 1) * 64], S.rYC)
                ui += 1
        k.barrier()


def phase_attn(C, l):
    phase_attn_prompt(C, l)
    phase_attn_sample_A(C, l)
    phase_attn_sample_C(C, l)


SEQS = [(0, 256), (256, 256), (512, 2048)]
TC = 32
DEC_SCALE = -0.6065306597126334


def pbcast(X, l, n):
    return bass.AP(tensor=X.tensor, offset=l * n, ap=[[0, 128], [1, n]])


def v3(ap, h=8):
    return ap.rearrange("p (h j) -> p h j", h=h)


def phase_rwkv_prep(C, l):
    k, I, S = C.k, C.I, C.S
    with ExitStack() as st:
        mu = k.sb(st, "mu_b", [128, 1920]); w0 = k.sb(st, "w0_b", [128, 1024]); a0 = k.sb(st, "a0_b", [128, 1024])
        kkp = k.sb(st, "kkp_b", [128, 512]); ka = k.sb(st, "ka_b", [128, 512]); omk = k.sb(st, "omk_b", [128, 512]); rk = k.sb(st, "rk_b", [128, 512])
        w2 = k.sb(st, "w2", [128, 512]); a2 = k.sb(st, "a2", [128, 512]); g2 = k.sb(st, "g2", [128, 512]); J = k.sb(st, "J", [128, 128])
        e12 = k.sb(st, "e12", [128, 1])
        k.op("dve", lambda e: e.memset(e12.ap[:], 1e-12), [], [e12])
        for (t, X, n) in ((mu, I.rw_mu, 1920), (w0, I.rw_w0, 1024), (a0, I.rw_a0, 1024), (kkp, I.rw_kk, 512), (ka, I.rw_ka, 512), (rk, I.rw_rk, 512)):
            k.dma(None, t.ap[:], pbcast(X, l, n), [], [t])
        for (t, X) in ((w2, I.rw_w2), (a2, I.rw_a2), (g2, I.rw_g2)):
            k.dma(None, t.ap[:], X[l], [], [t])
        k.dma(None, J.ap[:], I.antiid[:, :], [], [J])
        k.op("dve", lambda e: e.tensor_scalar(omk.ap[:], ka.ap[:], -1.0, 1.0, op0=ALU.mult, op1=ALU.add), [ka], [omk])
        pb = k.sb(st, "pb", [128, 1920]); prev = k.sb(st, "prev", [128, 1920]); nxt = k.sb(st, "nxt", [128, 1920])
        lw = k.sb(st, "lw", [128, 384]); lwT = k.sb(st, "lwT", [128, 384])
        az = [k.sb(st, "az%d" % z, [128, 512]) for z in range(2)]
        kk = k.sb(st, "kk", [128, 512]); kkn = k.sb(st, "kkn", [128, 512]); t1 = k.sb(st, "t1", [128, 512]); t2 = k.sb(st, "t2", [128, 512])
        sm = k.sb(st, "rsm", [128, 32])
        opz = [k.sb(st, "opz%d" % z, [128, 8, 6, 64]) for z in range(2)]
        rev = k.sb(st, "rev", [128, 1536]); bg = k.sb(st, "bg", [128, 1024])
        for (s0, T) in SEQS:
            nt = T // 128
            for ti in range(nt):
                t0 = s0 + ti * 128
                k.dma(None, pb.ap[:], S.PT[t0:t0 + 128, 256:2176], [S.rPT], [pb])
                if ti > 0:
                    k.dma(None, prev.ap[:], S.PT[t0 - 1:t0 + 127, 256:2176], [S.rPT], [prev])
                else:
                    k.op("dve", lambda e: e.memset(prev.ap[0:1, :], 0.0), [], [prev])
                    k.dma(None, prev.ap[1:128, :], S.PT[t0:t0 + 127, 256:2176], [S.rPT], [prev])
                if ti < nt - 1:
                    k.dma(None, nxt.ap[:], S.PT[t0 + 1:t0 + 129, 256:2176], [S.rPT], [nxt])
                else:
                    k.op("pool", lambda e: e.memset(nxt.ap[:], 0.0), [], [nxt])
                    k.dma(None, nxt.ap[0:127, :], S.PT[t0 + 1:t0 + 128, 256:2176], [S.rPT], [nxt])
                k.op("pool", lambda e: e.tensor_tensor(out=prev.ap[:], in0=prev.ap[:], in1=nxt.ap[:], op=ALU.add), [prev, nxt], [prev])
                k.op("dve", lambda e: e.scalar_tensor_tensor(out=prev.ap[:], in0=prev.ap[:], scalar=0.5, in1=pb.ap[:], op0=ALU.mult, op1=ALU.subtract), [prev, pb], [prev])
                k.op("dve", lambda e: e.tensor_tensor(out=prev.ap[:], in0=prev.ap[:], in1=mu.ap[:], op=ALU.mult), [prev, mu], [prev])
                k.op("dve", lambda e: e.tensor_tensor(out=pb.ap[:], in0=pb.ap[:], in1=prev.ap[:], op=ALU.add), [pb, prev], [pb])
                r_ = pb.ap[:, 0:512]; k_ = pb.ap[:, 512:1024]; v_ = pb.ap[:, 1024:1536]
                k.op("act", lambda e: e.activation(out=lw.ap[:, 0:128], in_=pb.ap[:, 1536:1664], func=AF.Tanh), [pb], [lw])
                k.op("act", lambda e: e.copy(lw.ap[:, 128:256], pb.ap[:, 1664:1792]), [pb], [lw])
                k.op("act", lambda e: e.activation(out=lw.ap[:, 256:384], in_=pb.ap[:, 1792:1920], func=AF.Sigmoid), [pb], [lw])
                ps = C.PS[0]
                for j in range(3):
                    k.op("pe", lambda e: e.transpose(ps.ap[:, j * 128:(j + 1) * 128], lw.ap[:, j * 128:(j + 1) * 128], C.ident.ap[:]), [lw, C.ident], [ps])
                k.op("dve", lambda e: e.tensor_copy(lwT.ap[:], ps.ap[:, 0:384]), [ps], [lwT])
                for z in range(2):
                    zs = slice(z * 64, (z + 1) * 64)
                    psw = C.PS[1 + z]; psa = C.PS[3 + z]
                    k.op("pe", lambda e: e.matmul(psw.ap[:], lhsT=lwT.ap[zs, 0:128], rhs=w2.ap[zs, :], start=True, stop=True), [lwT, w2], [psw])
                    k.op("pe", lambda e: e.matmul(psa.ap[:], lhsT=lwT.ap[zs, 128:256], rhs=a2.ap[zs, :], start=True, stop=True), [lwT, a2], [psa])
                    dec = v3(t1.ap[:])
                    k.op("dve", lambda e: e.tensor_tensor(out=t1.ap[:], in0=psw.ap[:], in1=w0.ap[:, z * 512:(z + 1) * 512], op=ALU.add), [psw, w0], [t1])
                    k.op("act", lambda e: e.activation(out=t1.ap[:], in_=t1.ap[:], func=AF.Sigmoid), [t1], [t1])
                    k.op("act", lambda e: e.activation(out=opz[z].ap[:, :, 1, :], in_=dec, func=AF.Exp, scale=DEC_SCALE), [t1], [opz[z]])
                    k.op("dve", lambda e: e.tensor_tensor(out=az[z].ap[:], in0=psa.ap[:], in1=a0.ap[:, z * 512:(z + 1) * 512], op=ALU.add), [psa, a0], [az[z]])
                    k.op("act", lambda e: e.activation(out=az[z].ap[:], in_=az[z].ap[:], func=AF.Sigmoid), [az[z]], [az[z]])
                psg = C.PS[5]
                k.op("pe", lambda e: e.matmul(psg.ap[:], lhsT=lwT.ap[:, 256:384], rhs=g2.ap[:, :], start=True, stop=True), [lwT, g2], [psg])
                k.op("act", lambda e: e.copy(bg.ap[:, 512:1024], psg.ap[:]), [psg], [bg])
                k.op("dve", lambda e: e.tensor_tensor(out=kk.ap[:], in0=k_, in1=kkp.ap[:], op=ALU.mult), [pb, kkp], [kk])
                k.op("pool", lambda e: e.tensor_tensor(out=t2.ap[:], in0=kk.ap[:], in1=kk.ap[:], op=ALU.mult), [kk], [t2])
                k.op("dve", lambda e: e.tensor_reduce(out=sm.ap[:, 0:8], in_=v3(t2.ap[:]), axis=AX.X, op=ALU.add), [t2], [sm])
                k.op("act", lambda e: e.activation(out=sm.ap[:, 0:8], in_=sm.ap[:, 0:8], func=AF.Sqrt, bias=e12.ap[:, 0:1], scale=1.0), [sm, e12], [sm])
                k.op("dve", lambda e: e.reciprocal(sm.ap[:, 8:16], sm.ap[:, 0:8]), [sm], [sm])
                k.op("dve", lambda e: e.tensor_tensor(out=v3(kkn.ap[:]), in0=v3(kk.ap[:]), in1=bc(sm.ap[:, 8:16].unsqueeze(2), [128, 8, 64]), op=ALU.mult), [kk, sm], [kkn])
                for z in range(2):
                    k.op("dve", lambda e: e.tensor_tensor(out=t1.ap[:], in0=az[z].ap[:], in1=ka.ap[:], op=ALU.mult), [az[z], ka], [t1])
                    k.op("pool", lambda e: e.tensor_tensor(out=t1.ap[:], in0=t1.ap[:], in1=omk.ap[:], op=ALU.add), [t1, omk], [t1])
                    k.op("dve", lambda e: e.tensor_tensor(out=opz[z].ap[:, :, 3, :], in0=v3(t1.ap[:]), in1=v3(k_), op=ALU.mult), [t1, pb], [opz[z]])
                    k.op("pool", lambda e: e.tensor_tensor(out=opz[z].ap[:, :, 2, :], in0=v3(kkn.ap[:]), in1=v3(az[z].ap[:]), op=ALU.mult), [kkn, az[z]], [opz[z]])
                    k.op("dve", lambda e: e.tensor_scalar(opz[z].ap[:, :, 0, :], v3(kkn.ap[:]), -1.0, None, op0=ALU.mult), [kkn], [opz[z]])
                    k.op("act", lambda e: e.copy(opz[z].ap[:, :, 4, :], v3(r_)), [pb], [opz[z]])
                    k.op("act", lambda e: e.copy(opz[z].ap[:, :, 5, :], v3(v_)), [pb], [opz[z]])
                k.op("dve", lambda e: e.tensor_tensor(out=v3(t1.ap[:]), in0=opz[0].ap[:, :, 3, :], in1=opz[1].ap[:, :, 3, :], op=ALU.add), [opz[0], opz[1]], [t1])
                k.op("dve", lambda e: e.tensor_tensor(out=t1.ap[:], in0=t1.ap[:], in1=r_, op=ALU.mult), [t1, pb], [t1])
                k.op("dve", lambda e: e.tensor_tensor(out=t1.ap[:], in0=t1.ap[:], in1=rk.ap[:], op=ALU.mult), [t1, rk], [t1])
                k.op("dve", lambda e: e.tensor_reduce(out=sm.ap[:, 16:24], in_=v3(t1.ap[:]), axis=AX.X, op=ALU.add), [t1], [sm])
                k.op("dve", lambda e: e.tensor_tensor(out=v3(bg.ap[:, 0:512]), in0=v3(v_), in1=bc(sm.ap[:, 16:24].unsqueeze(2), [128, 8, 64]), op=ALU.mult), [pb, sm], [bg])
                k.dma(None, S.BG[t0:t0 + 128, :], bg.ap[:], [bg], [S.rBG])
                k.dma(None, S.OPS[0, t0:t0 + 128, :], opz[0].ap[:].rearrange("p a b c -> p (a b c)"), [opz[0]], [S.rOPS])
                flat = opz[1].ap[:].rearrange("p a b c -> p (a b c)")
                tr = s0 + (nt - 1 - ti) * 128
                for half in range(2):
                    for j in range(3):
                        c0 = half * 1536 + j * 512
                        psr = C.PS[5 + j] if j < 2 else C.PS[7]
                        k.op("pe", lambda e: e.matmul(psr.ap[:], lhsT=J.ap[:], rhs=flat[:, c0:c0 + 512], start=True, stop=True), [J, opz[1]], [psr])
                        if j % 2 == 0:
                            k.op("dve", lambda e: e.tensor_copy(rev.ap[:, j * 512:(j + 1) * 512], psr.ap[:]), [psr], [rev])
                        else:
                            k.op("act", lambda e: e.copy(rev.ap[:, j * 512:(j + 1) * 512], psr.ap[:]), [psr], [rev])
                    k.dma(None, S.OPS[1, tr:tr + 128, half * 1536:(half + 1) * 1536], rev.ap[:], [rev], [S.rOPS])
        k.barrier()


def scan_group(C, st, l, insts, T, ILO, s_init, s_final, chains):
    k, I, S = C.k, C.I, C.S
    IC = 64 // ILO
    OPB = [k.sb(st, "OPB%d" % i, [128, TC, 5, 64]) for i in range(2)]
    VB = [k.sb(st, "VB%d" % i, [128, TC, ILO]) for i in range(2)]
    CH = []
    for ci, (eng, i0, i1) in enumerate(chains):
        n = i1 - i0
        c = Ctx()
        c.eng, c.i0, c.i1, c.n = eng, i0, i1, n
        c.S = k.sb(st, "scanS%d" % ci, [128, n * 64]); c.tmp = k.sb(st, "scantmp%d" % ci, [128, n * 64]); c.sa = k.sb(st, "scansa%d" % ci, [128, n])
        c.Y = [k.sb(st, "Yc%d_%d" % (ci, i), [128, TC, n]) for i in range(2)]
        if s_init is not None:
            k.dma(None, c.S.ap[:], s_init[:, i0 * 64:i1 * 64], [], [c.S])
        else:
            k.op("pool", lambda e: e.memset(c.S.ap[:], 0.0), [], [c.S])
        c.S3 = c.S.ap[:].rearrange("p (i j) -> p i j", j=64)
        c.T3 = c.tmp.ap[:].rearrange("p (i j) -> p i j", j=64)
        c.shp = [128, n, 64]
        CH.append(c)
    groups = {}
    for (p0, z, h, s0) in insts:
        groups.setdefault((z, s0), []).append((p0, h))
    for ch in range(T // TC):
        opb = OPB[ch % 2]; vb = VB[ch % 2]
        for (p0, z, h, s0) in insts:
            base = (z * TOK + s0 + ch * TC) * 3072 + h * 384
            src = bass.AP(tensor=S.OPS.tensor, offset=base, ap=[[0, IC], [3072, TC], [1, 320]])
            k.dma(None, opb.ap[p0:p0 + IC, :, :, :].rearrange("p t a b -> p t (a b)"), src, [S.rOPS], [opb])
            srcv = bass.AP(tensor=S.OPS.tensor, offset=base + 320, ap=[[ILO, IC], [3072, TC], [1, ILO]])
            k.dma(None, vb.ap[p0:p0 + IC, :, :], srcv, [S.rOPS], [vb])
        for tc in range(TC):
            def row(c, kind):
                return bc(opb.ap[:, tc, kind, :].unsqueeze(1), c.shp)
            for step in range(9):
                for c in CH:
                    yc = c.Y[ch % 2]
                    if step == 0:
                        k.op(c.eng, lambda e: e.tensor_tensor(out=c.T3, in0=c.S3, in1=row(c, 0), op=ALU.mult), [c.S, opb], [c.tmp])
                    elif step == 1:
                        k.op(c.eng, lambda e: e.tensor_reduce(out=c.sa.ap[:], in_=c.T3, axis=AX.X, op=ALU.add), [c.tmp], [c.sa])
                    elif step == 2:
                        k.op(c.eng, lambda e: e.tensor_tensor(out=c.S3, in0=c.S3, in1=row(c, 1), op=ALU.mult), [c.S, opb], [c.S])
                    elif step == 3:
                        k.op(c.eng, lambda e: e.tensor_tensor(out=c.T3, in0=bc(c.sa.ap[:].unsqueeze(2), c.shp), in1=row(c, 2), op=ALU.mult), [c.sa, opb], [c.tmp])
                    elif step == 4:
                        k.op(c.eng, lambda e: e.tensor_tensor(out=c.S3, in0=c.S3, in1=c.T3, op=ALU.add), [c.S, c.tmp], [c.S])
                    elif step == 5:
                        k.op(c.eng, lambda e: e.tensor_tensor(out=c.T3, in0=bc(vb.ap[:, tc, c.i0:c.i1].unsqueeze(2), c.shp), in1=row(c, 3), op=ALU.mult), [vb, opb], [c.tmp])
                    elif step == 6:
                        k.op(c.eng, lambda e: e.tensor_tensor(out=c.S3, in0=c.S3, in1=c.T3, op=ALU.add), [c.S, c.tmp], [c.S])
                    elif step == 7:
                        k.op(c.eng, lambda e: e.tensor_tensor(out=c.T3, in0=c.S3, in1=row(c, 4), op=ALU.mult), [c.S, opb], [c.tmp])
                    else:
                        k.op(c.eng, lambda e: e.tensor_reduce(out=yc.ap[:, tc, :], in_=c.T3, axis=AX.X, op=ALU.add), [c.tmp], [yc])
        for (z, s0), lst in groups.items():
            pa = min(p for p, _ in lst)
            np_ = len(lst) * IC
            for c in CH:
                dst = bass.AP(tensor=S.YS.tensor, offset=(z * TOK + s0 + ch * TC) * 512 + c.i0, ap=[[ILO, np_], [512, TC], [1, c.n]])
                k.dma(None, dst, c.Y[ch % 2].ap[pa:pa + np_, :, :], [c.Y[ch % 2]], [S.rYS])
    if s_final is not None:
        for (dst, p0, n) in s_final:
            for c in CH:
                k.dma(None, dst[:, c.i0 * 64:c.i1 * 64], c.S.ap[p0:p0 + n, :], [c.S], [])


def phase_rwkv_scan(C, l):
    k, I, S, O = C.k, C.I, C.S, C.O
    with ExitStack() as st:
        insts = []
        for s in range(2):
            for z in range(2):
                for h in range(8):
                    insts.append((s * 64 + z * 32 + h * 4, z, h, s * 256))
        scan_group(C, st, l, insts, 256, 16, None, [(O.nst[s, l], s * 64, 64) for s in range(2)], [("dve", 0, 8), ("dve", 8, 16)])
        k.barrier()
    with ExitStack() as st:
        insts = []
        for z in range(2):
            for h in range(8):
                insts.append((z * 64 + h * 8, z, h, 512))
        scan_group(C, st, l, insts, 2048, 8, I.st0[l], None, [("dve", 0, 4), ("dve", 4, 8)])
        k.barrier()


def phase_rwkv_post(C, l):
    k, I, S = C.k, C.I, C.S
    with ExitStack() as st:
        lg = k.sb(st, "lnxg_b", [128, 512]); lb = k.sb(st, "lnxb_b", [128, 512]); J = k.sb(st, "J2", [128, 128])
        gne = k.sb(st, "gne", [128, 1])
        k.op("dve", lambda e: e.memset(gne.ap[:], 64e-5), [], [gne])
        k.dma(None, lg.ap[:], pbcast(I.rw_lnx_g, l, 512), [], [lg])
        k.dma(None, lb.ap[:], pbcast(I.rw_lnx_b, l, 512), [], [lb])
        k.dma(None, J.ap[:], I.antiid[:, :], [], [J])
        yf = [k.sb(st, "yf%d" % i, [128, 512]) for i in range(2)]
        yb = [k.sb(st, "yb%d" % i, [128, 512]) for i in range(2)]
        bg = [k.sb(st, "pbg%d" % i, [128, 1024]) for i in range(2)]
        t1 = k.sb(st, "pt1", [128, 512]); sm = k.sb(st, "psm", [128, 32]); yT = [k.sb(st, "pyT%d" % i, [128, 4, 128]) for i in range(2)]
        cnt = 0
        for (s0, T) in SEQS:
            nt = T // 128
            for ti in range(nt):
                par = cnt % 2; cnt += 1
                t0 = s0 + ti * 128
                tr = s0 + (nt - 1 - ti) * 128
                y = yf[par]; y2 = yb[par]; b = bg[par]
                k.dma(None, y.ap[:], S.YS[0, t0:t0 + 128, :], [S.rYS], [y])
                k.dma(None, y2.ap[:], S.YS[1, tr:tr + 128, :], [S.rYS], [y2])
                k.dma(None, b.ap[:], S.BG[t0:t0 + 128, :], [S.rBG], [b])
                ps = C.PS[par]
                k.op("pe", lambda e: e.matmul(ps.ap[:], lhsT=J.ap[:], rhs=y2.ap[:], start=True, stop=True), [J, y2], [ps])
                k.op("dve", lambda e: e.tensor_tensor(out=y.ap[:], in0=y.ap[:], in1=ps.ap[:], op=ALU.add), [y, ps], [y])
                k.op("dve", lambda e: e.tensor_reduce(out=sm.ap[:, 0:8], in_=v3(y.ap[:]), axis=AX.X, op=ALU.add), [y], [sm])
                k.op("dve", lambda e: e.tensor_scalar(sm.ap[:, 0:8], sm.ap[:, 0:8], 1.0 / 64, None, op0=ALU.mult), [sm], [sm])
                k.op("dve", lambda e: e.tensor_tensor(out=v3(y.ap[:]), in0=v3(y.ap[:]), in1=bc(sm.ap[:, 0:8].unsqueeze(2), [128, 8, 64]), op=ALU.subtract), [y, sm], [y])
                k.op("pool", lambda e: e.tensor_tensor(out=t1.ap[:], in0=y.ap[:], in1=y.ap[:], op=ALU.mult), [y], [t1])
                k.op("dve", lambda e: e.tensor_reduce(out=sm.ap[:, 8:16], in_=v3(t1.ap[:]), axis=AX.X, op=ALU.add), [t1], [sm])
                k.op("act", lambda e: e.activation(out=sm.ap[:, 8:16], in_=sm.ap[:, 8:16], func=AF.Sqrt, bias=gne.ap[:, 0:1], scale=1.0 / 64), [sm, gne], [sm])
                k.op("dve", lambda e: e.reciprocal(sm.ap[:, 16:24], sm.ap[:, 8:16]), [sm], [sm])
                k.op("dve", lambda e: e.tensor_tensor(out=v3(y.ap[:]), in0=v3(y.ap[:]), in1=bc(sm.ap[:, 16:24].unsqueeze(2), [128, 8, 64]), op=ALU.mult), [y, sm], [y])
                k.op("pool", lambda e: e.tensor_tensor(out=y.ap[:], in0=y.ap[:], in1=lg.ap[:], op=ALU.mult), [y, lg], [y])
                k.op("dve", lambda e: e.tensor_tensor(out=y.ap[:], in0=y.ap[:], in1=lb.ap[:], op=ALU.add), [y, lb], [y])
                k.op("dve", lambda e: e.tensor_tensor(out=y.ap[:], in0=y.ap[:], in1=b.ap[:, 0:512], op=ALU.add), [y, b], [y])
                k.op("dve", lambda e: e.tensor_tensor(out=y.ap[:], in0=y.ap[:], in1=b.ap[:, 512:1024], op=ALU.mult), [y, b], [y])
                pst = C.PS[2 + par]
                for c in range(4):
                    k.op("pe", lambda e: e.transpose(pst.ap[:, c * 128:(c + 1) * 128], y.ap[:, c * 128:(c + 1) * 128], C.ident.ap[:]), [y, C.ident], [pst])
                k.op("act", lambda e: e.copy(yT[par].ap[:], pst.ap[:].rearrange("p (c t) -> p c t", c=4)), [pst], [yT[par]])
                k.dma(None, S.YB[:, t0:t0 + 128].rearrange("(c p) t -> p c t", p=128), yT[par].ap[:], [yT[par]], [S.rYB])
        k.barrier()


def phase_rwkv(C, l):
    phase_rwkv_prep(C, l)
    phase_rwkv_scan(C, l)
    phase_rwkv_post(C, l)


def phase_merge(C, l):
    k, I, S = C.k, C.I, C.S
    outs = (I.a_out, I.rw_out, I.na_out)
    Ys = ((S.YA, S.rYA), (S.YB, S.rYB), (S.YC, S.rYC))
    with ExitStack() as st:
        yT = [k.sb(st, "myT%d" % b, [128, 4, 512]) for b in range(3)]
        mg = k.sb(st, "merged", [128, 8, 512])
        wo = [[k.sb(st, "mwo%d_%d" % (b, i), [128, 4, 128]) for i in range(2)] for b in range(3)]
        gt = [[k.sb(st, "mg%d_%d" % (b, i), [128, 512]) for i in range(2)] for b in range(3)]
        tmp = [k.sb(st, "mtmp%d" % i, [128, 512]) for i in range(2)]
        wob = [k.sb(st, "mwob%d" % i, [128, 8, 128]) for i in range(2)]
        cnt = 0
        for tg in range(5):
            g = 0 if tg == 0 else 1
            cols = slice(tg * 512, (tg + 1) * 512)
            for b in range(3):
                k.dma(None, yT[b].ap[:], Ys[b][0][:, cols].rearrange("(c p) t -> p c t", p=128), [Ys[b][1]], [yT[b]])
            for fc in range(8):
                par = cnt % 2; cnt += 1
                for b in range(3):
                    k.dma(None, wo[b][par].ap[:], outs[b][l][:, fc * 128:(fc + 1) * 128].rearrange("(c p) n -> p c n", p=128), [], [wo[b][par]])
                    r0 = 1664 + b * 1024 + fc * 128
                    k.dma(None, gt[b][par].ap[:], S.PF[r0:r0 + 128, cols], [S.rPF], [gt[b][par]])
                for b in range(3):
                    ps = C.PS[(fc * 3 + b) % 6]
                    for c in range(4):
                        k.op("pe", lambda e: e.matmul(ps.ap[:], lhsT=wo[b][par].ap[:, c, :], rhs=yT[b].ap[:, c, :], start=(c == 0), stop=(c == 3)), [wo[b][par], yT[b]], [ps])
                    if b == 0:
                        k.op("dve", lambda e: e.tensor_tensor(out=mg.ap[:, fc, :], in0=ps.ap[:], in1=gt[b][par].ap[:], op=ALU.mult), [ps, gt[b][par]], [mg])
                    else:
                        t = tmp[b % 2]
                        k.op("dve", lambda e: e.tensor_tensor(out=t.ap[:], in0=ps.ap[:], in1=gt[b][par].ap[:], op=ALU.mult), [ps, gt[b][par]], [t])
                        k.op("pool", lambda e: e.tensor_tensor(out=mg.ap[:, fc, :], in0=mg.ap[:, fc, :], in1=t.ap[:], op=ALU.add), [mg, t], [mg])
            for fc2 in range(8):
                w = wob[fc2 % 2]
                k.dma(None, w.ap[:], I.w_o[l][:, fc2 * 128:(fc2 + 1) * 128].rearrange("(c p) n -> p c n", p=128), [], [w])
                ps = C.PS[6 + fc2 % 2]
                for kc in range(8):
                    k.op("pe", lambda e: e.matmul(ps.ap[:], lhsT=w.ap[:, kc, :], rhs=mg.ap[:, kc, :], start=(kc == 0), stop=(kc == 7)), [w, mg], [ps])
                k.op("dve", lambda e: e.scalar_tensor_tensor(out=C.xT.ap[:, fc2, cols], in0=ps.ap[:], scalar=C.modT.ap[:, 16 + fc2, g:g + 1], in1=C.xT.ap[:, fc2, cols], op0=ALU.mult, op1=ALU.add), [ps, C.modT, C.xT], [C.xT])
        k.barrier()


import os
PEER_STOP = int(os.environ.get('PEER_STOP', '0'))
NBUF = 4
NACC = 2


def phase_peer(C, l):
    k, I, S = C.k, C.I, C.S
    with ExitStack() as st:
        skn = k.sb(st, "skn", [128, 16, 128]); skT = k.sb(st, "skT", [128, 16, 128])
        k.dma(None, skn.ap[:], I.pe_sk[l].rearrange("c n k -> n c k"), [], [skn])
        for c in range(16):
            ps = C.PS[c % 4]
            k.op("pe", lambda e: e.transpose(ps.ap[:, 0:128], skn.ap[:, c, :], C.ident.ap[:]), [skn, C.ident], [ps])
            k.op("dve", lambda e: e.tensor_copy(skT.ap[:, c, :], ps.ap[:, 0:128]), [ps], [skT])
        h2T = k.sb(st, "h2T", [128, 8, 128]); h2 = k.sb(st, "h2tok", [128, D])
        sq = [k.sb(st, "psq%d" % i, [128, 128]) for i in range(2)]; rstd = k.sb(st, "prstd", [128, 128])
        wq = [k.sb(st, "wq%d" % i, [128, 8, 128]) for i in range(2)]
        qT = k.sb(st, "qT", [128, 16, 128])
        Sc = [k.sb(st, "Sc%d" % c, [128, 128]) for c in range(16)]
        vals = [k.sb(st, "vals%d" % c, [128, 16]) for c in range(16)]
        idxu = [k.sb(st, "idxu%d" % c, [128, 16], U32) for c in range(16)]
        idxf = [k.sb(st, "idxf%d" % c, [128, 16]) for c in range(16)]
        cand = [k.sb(st, "cand%d" % h, [128, 256]) for h in range(8)]
        cand2 = [k.sb(st, "candb%d" % h, [128, 256]) for h in range(8)]
        cande = [k.sb(st, "cande%d" % h, [128, 256]) for h in range(8)]
        sv = [k.sb(st, "sv%d" % h, [128, 16]) for h in range(8)]
        gate = k.sb(st, "gate", [128, 8, 16]); sm = k.sb(st, "pesm", [128, 32])
        junk = k.sb(st, "pjunk", [128, 256]); ef = k.sb(st, "ef", [128, 128]); ei = k.sb(st, "ei", [128, 128], I32)
        dots = k.sb(st, "dots", [128, 128]); tg_ = k.sb(st, "pt_g", [128, 128]); wgt = k.sb(st, "wgt", [128, 128])
        UV = [k.sb(st, "UV%d" % i, [128, D]) for i in range(NBUF)]
        junk2 = k.sb(st, "pjunk2", [128, D])
        accs = [k.sb(st, "pacc%d" % i, [128, D]) for i in range(NACC)]
        shp3 = [128, 16, 16]
        if l == 0:
            print("PEER sbuf remaining", C.nc.sbuf_bytes_remaining)
        for tile in range(TOK // 128):
            g = 0 if tile < 4 else 1
            t0 = tile * 128
            cols = slice(t0, t0 + 128)
            psn = C.PS[7]
            for kc in range(8):
                s_ = sq[kc % 2]
                k.op("act", lambda e: e.activation(out=s_.ap[:], in_=C.xT.ap[:, kc, cols], func=AF.Square), [C.xT], [s_])
                k.op("pe", lambda e: e.matmul(psn.ap[:, 0:128], lhsT=C.ones.ap[:], rhs=s_.ap[:], start=(kc == 0), stop=(kc == 7)), [s_, C.ones], [psn])
            k.op("act", lambda e: e.activation(out=rstd.ap[:], in_=psn.ap[:, 0:128], func=AF.Sqrt, scale=1.0 / D, bias=C.eps.ap[:, 0:1]), [psn, C.eps], [rstd])
            k.op("dve", lambda e: e.reciprocal(rstd.ap[:], rstd.ap[:]), [rstd], [rstd])
            for kc in range(8):
                k.op("dve", lambda e: e.tensor_tensor(out=h2T.ap[:, kc, :], in0=C.xT.ap[:, kc, cols], in1=rstd.ap[:], op=ALU.mult), [C.xT, rstd], [h2T])
            for kc in range(8):
                k.op("dve", lambda e: e.tensor_scalar(h2T.ap[:, kc, :], h2T.ap[:, kc, :], C.gs2.ap[:, kc, g:g + 1], C.modT.ap[:, 24 + kc, g:g + 1], op0=ALU.mult, op1=ALU.add), [h2T, C.gs2, C.modT], [h2T])
            for half in range(2):
                ps = C.PS[half]
                for j in range(4):
                    kc = half * 4 + j
                    k.op("pe", lambda e: e.transpose(ps.ap[:, j * 128:(j + 1) * 128], h2T.ap[:, kc, :], C.ident.ap[:]), [h2T, C.ident], [ps])
                k.op("act", lambda e: e.copy(h2.ap[:, half * 512:(half + 1) * 512], ps.ap[:]), [ps], [], ww=[h2])
            if PEER_STOP == 4:
                continue
            for c in range(16):
                w = wq[c % 2]
                k.dma(None, w.ap[:], I.pe_q[l][:, c * 128:(c + 1) * 128].rearrange("(c p) n -> p c n", p=128), [], [w])
                ps = C.PS[2 + c % 2]
                for kc in range(8):
                    k.op("pe", lambda e: e.matmul(ps.ap[:, 0:128], lhsT=w.ap[:, kc, :], rhs=h2T.ap[:, kc, :], start=(kc == 0), stop=(kc == 7)), [w, h2T], [ps])
                k.op("act", lambda e: e.copy(qT.ap[:, c, :], ps.ap[:, 0:128]), [ps], [], ww=[qT])
            if PEER_STOP == 5:
                continue
            for q4 in range(4):
                ps = C.PS[4 + q4 % 2]
                for j in range(4):
                    c = q4 * 4 + j
                    k.op("pe", lambda e: e.matmul(ps.ap[:, j * 128:(j + 1) * 128], lhsT=qT.ap[:, c, :], rhs=skT.ap[:, c, :], start=True, stop=True), [qT, skT], [ps])
                for j in range(4):
                    c = q4 * 4 + j
                    if q4 % 2 == 0:
                        k.op("dve", lambda e: e.tensor_copy(Sc[c].ap[:], ps.ap[:, j * 128:(j + 1) * 128]), [ps], [Sc[c]])
                    else:
                        k.op("act", lambda e: e.copy(Sc[c].ap[:], ps.ap[:, j * 128:(j + 1) * 128]), [ps], [Sc[c]])
            if PEER_STOP == 2:
                continue
            for c in range(16):
                k.op("dve", lambda e: e.max(out=vals[c].ap[:, 0:8], in_=Sc[c].ap[:]), [Sc[c]], [vals[c]])
            for c in range(16):
                k.op("dve", lambda e: e.max_index(out=idxu[c].ap[:, 0:8], in_max=vals[c].ap[:, 0:8], in_values=Sc[c].ap[:]), [Sc[c], vals[c]], [idxu[c]])
            for c in range(16):
                k.op("dve", lambda e: e.match_replace(out=Sc[c].ap[:], in_to_replace=vals[c].ap[:, 0:8], in_values=Sc[c].ap[:], imm_value=-1e30), [Sc[c], vals[c]], [Sc[c]])
            for c in range(16):
                k.op("dve", lambda e: e.max(out=vals[c].ap[:, 8:16], in_=Sc[c].ap[:]), [Sc[c]], [vals[c]])
            for c in range(16):
                k.op("dve", lambda e: e.max_index(out=idxu[c].ap[:, 8:16], in_max=vals[c].ap[:, 8:16], in_values=Sc[c].ap[:]), [Sc[c], vals[c]], [idxu[c]])
            for c in range(16):
                k.op("dve", lambda e: e.tensor_copy(idxf[c].ap[:], idxu[c].ap[:]), [idxu[c]], [idxf[c]])
            if PEER_STOP == 3:
                continue
            for h in range(8):
                c4 = cand[h].ap[:].rearrange("p (a b) -> p a b", a=16)
                k.op("dve", lambda e: e.tensor_tensor(out=c4, in0=bc(vals[2 * h].ap[:].unsqueeze(2), shp3), in1=bc(vals[2 * h + 1].ap[:].unsqueeze(1), shp3), op=ALU.add), [vals[2 * h], vals[2 * h + 1]], [cand[h]])
            for h in range(8):
                ce4 = cande[h].ap[:].rearrange("p (a b) -> p a b", a=16)
                k.op("dve", lambda e: e.scalar_tensor_tensor(out=ce4, in0=bc(idxf[2 * h].ap[:].unsqueeze(2), shp3), scalar=128.0, in1=bc(idxf[2 * h + 1].ap[:].unsqueeze(1), shp3), op0=ALU.mult, op1=ALU.add), [idxf[2 * h], idxf[2 * h + 1]], [cande[h]])
            for h in range(8):
                k.op("dve", lambda e: e.max(out=sv[h].ap[:, 0:8], in_=cand[h].ap[:]), [cand[h]], [sv[h]])
            for h in range(8):
                k.op("dve", lambda e: e.match_replace(out=cand2[h].ap[:], in_to_replace=sv[h].ap[:, 0:8], in_values=cand[h].ap[:], imm_value=-1e30), [cand[h], sv[h]], [cand2[h]])
            for h in range(8):
                k.op("dve", lambda e: e.max(out=sv[h].ap[:, 8:16], in_=cand2[h].ap[:]), [cand2[h]], [sv[h]])
            for h in range(8):
                k.op("dve", lambda e: e.tensor_scalar(gate.ap[:, h, :], sv[h].ap[:], sv[h].ap[:, 0:1], None, op0=ALU.subtract), [sv[h]], [], ww=[gate])
            k.op("act", lambda e: e.activation(out=gate.ap[:], in_=gate.ap[:], func=AF.Exp), [gate], [gate])
            k.op("dve", lambda e: e.tensor_reduce(out=sm.ap[:, 0:8], in_=gate.ap[:], axis=AX.X, op=ALU.add), [gate], [sm])
            k.op("dve", lambda e: e.reciprocal(sm.ap[:, 8:16], sm.ap[:, 0:8]), [sm], [sm])
            k.op("dve", lambda e: e.tensor_tensor(out=gate.ap[:], in0=gate.ap[:], in1=bc(sm.ap[:, 8:16].unsqueeze(2), [128, 8, 16]), op=ALU.mult), [gate, sm], [gate])
            k.op("pool", lambda e: e.memset(ef.ap[:], 0.0), [], [ef])
            for h in range(8):
                for kk2 in range(16):
                    j = h * 16 + kk2
                    k.op("dve", lambda e: e.scalar_tensor_tensor(out=junk.ap[:], in0=cand[h].ap[:], scalar=sv[h].ap[:, kk2:kk2 + 1], in1=cande[h].ap[:], op0=ALU.is_equal, op1=ALU.mult, accum_out=ef.ap[:, j:j + 1]), [cand[h], cande[h], sv[h]], [ef], ww=[junk])
            k.op("dve", lambda e: e.tensor_scalar(ef.ap[:], ef.ap[:], 0.0, 16383.0, op0=ALU.max, op1=ALU.min), [ef], [ef])
            k.op("dve", lambda e: e.tensor_scalar(ef.ap[:], ef.ap[:], float(l * 16384), None, op0=ALU.add), [ef], [ef])
            k.op("dve", lambda e: e.tensor_copy(ei.ap[:], ef.ap[:]), [ef], [ei])
            if PEER_STOP == 1:
                continue
            k.op("pool", lambda e: e.memset(dots.ap[:], 0.0), [], [dots])
            for j in range(128):
                u = UV[j % NBUF]
                k.idma(u.ap[:], I.pe_u[:, :], ei.ap[:, j:j + 1], [ei], [u])
                k.op("dve", lambda e: e.scalar_tensor_tensor(out=junk2.ap[:], in0=u.ap[:], scalar=1.0, in1=h2.ap[:], op0=ALU.mult, op1=ALU.mult, accum_out=dots.ap[:, j:j + 1]), [u, h2], [dots], ww=[junk2])
            k.op("dve", lambda e: e.tensor_tensor(out=tg_.ap[:], in0=dots.ap[:], in1=dots.ap[:], op=ALU.mult), [dots], [tg_])
            k.op("dve", lambda e: e.tensor_tensor(out=tg_.ap[:], in0=tg_.ap[:], in1=dots.ap[:], op=ALU.mult), [tg_, dots], [tg_])
            k.op("dve", lambda e: e.scalar_tensor_tensor(out=tg_.ap[:], in0=tg_.ap[:], scalar=0.044715, in1=dots.ap[:], op0=ALU.mult, op1=ALU.add), [tg_, dots], [tg_])
            k.op("act", lambda e: e.activation(out=tg_.ap[:], in_=tg_.ap[:], func=AF.Tanh, scale=0.7978845608028654), [tg_], [tg_])
            k.op("dve", lambda e: e.tensor_scalar(tg_.ap[:], tg_.ap[:], 1.0, 0.5, op0=ALU.add, op1=ALU.mult), [tg_], [tg_])
            k.op("dve", lambda e: e.tensor_tensor(out=tg_.ap[:], in0=tg_.ap[:], in1=dots.ap[:], op=ALU.mult), [tg_, dots], [tg_])
            k.op("dve", lambda e: e.tensor_tensor(out=wgt.ap[:], in0=tg_.ap[:], in1=gate.ap[:].rearrange("p h a -> p (h a)"), op=ALU.mult), [tg_, gate], [wgt])
            for a_ in accs:
                k.op("pool", lambda e: e.memset(a_.ap[:], 0.0), [], [a_])
            for j in range(128):
                v = UV[j % NBUF]
                acc = accs[j % NACC]
                k.idma(v.ap[:], I.pe_v[:, :], ei.ap[:, j:j + 1], [ei], [v])
                k.op("dve", lambda e: e.scalar_tensor_tensor(out=acc.ap[:], in0=v.ap[:], scalar=wgt.ap[:, j:j + 1], in1=acc.ap[:], op0=ALU.mult, op1=ALU.add), [v, wgt, acc], [acc])
            for ai in range(1, NACC):
                k.op("dve", lambda e: e.tensor_tensor(out=accs[0].ap[:], in0=accs[0].ap[:], in1=accs[ai].ap[:], op=ALU.add), [accs[0], accs[ai]], [accs[0]])
            acc = accs[0]
            for half in range(2):
                ps = C.PS[half]
                for j in range(4):
                    kc = half * 4 + j
                    k.op("pe", lambda e: e.transpose(ps.ap[:, j * 128:(j + 1) * 128], acc.ap[:, kc * 128:(kc + 1) * 128], C.ident.ap[:]), [acc, C.ident], [ps])
                for j in range(4):
                    kc = half * 4 + j
                    k.op("dve", lambda e: e.scalar_tensor_tensor(out=C.xT.ap[:, kc, cols], in0=ps.ap[:, j * 128:(j + 1) * 128], scalar=C.modT.ap[:, 40 + kc, g:g + 1], in1=C.xT.ap[:, kc, cols], op0=ALU.mult, op1=ALU.add), [ps, C.modT, C.xT], [], ww=[C.xT])
        k.barrier()


def phase_final(C):
    k, I, O = C.k, C.I, C.O
    with ExitStack() as st:
        lnfT = k.sb(st, "lnfT", [128, 8])
        k.dma("sp", lnfT.ap[:], I.lnf.rearrange("(c p) -> p c", p=128), [], [lnfT], allow_slow_non_contiguous=True)
        sq = [k.sb(st, "fsq%d" % i, [128, 512]) for i in range(2)]
        rstd = k.sb(st, "frstd", [128, 512])
        hT = k.sb(st, "fhT", [128, 8, 512])
        yt = [k.sb(st, "fy%d" % i, [128, D]) for i in range(2)]
        for tg in range(5):
            cols = slice(tg * 512, (tg + 1) * 512)
            ps = C.PS[7]
            for kc in range(8):
                s = sq[kc % 2]
                k.op("act", lambda e: e.activation(out=s.ap[:], in_=C.xT.ap[:, kc, cols], func=AF.Square), [C.xT], [s])
                k.op("pe", lambda e: e.matmul(ps.ap[:], lhsT=C.ones.ap[:], rhs=s.ap[:], start=(kc == 0), stop=(kc == 7)), [s, C.ones], [ps])
            k.op("act", lambda e: e.activation(out=rstd.ap[:], in_=ps.ap[:], func=AF.Sqrt, scale=1.0 / D, bias=C.eps.ap[:, 0:1]), [ps, C.eps], [rstd])
            k.op("dve", lambda e: e.reciprocal(rstd.ap[:], rstd.ap[:]), [rstd], [rstd])
            for kc in range(8):
                k.op("dve", lambda e: e.scalar_tensor_tensor(out=hT.ap[:, kc, :], in0=C.xT.ap[:, kc, cols], scalar=lnfT.ap[:, kc:kc + 1], in1=rstd.ap[:], op0=ALU.mult, op1=ALU.mult), [C.xT, rstd, lnfT], [hT])
            for tt in range(4):
                y = yt[tt % 2]
                for half in range(2):
                    ps2 = C.PS[(tt * 2 + half) % 6]
                    for j in range(4):
                        kc = half * 4 + j
                        k.op("pe", lambda e: e.transpose(ps2.ap[:, j * 128:(j + 1) * 128], hT.ap[:, kc, tt * 128:(tt + 1) * 128], C.ident.ap[:]), [hT, C.ident], [ps2])
                    if half == 0:
                        k.op("dve", lambda e: e.tensor_copy(y.ap[:, 0:512], ps2.ap[:]), [ps2], [y])
                    else:
                        k.op("act", lambda e: e.copy(y.ap[:, 512:1024], ps2.ap[:]), [ps2], [y])
                t0 = tg * 512 + tt * 128
                k.dma(None, O.y[t0:t0 + 128, :], y.ap[:], [y], [])
        k.barrier()


def make_consts():
    c = {}
    c["c_ident"] = np.eye(128, dtype=np.float32)
    c["c_antiid"] = np.ascontiguousarray(np.eye(128, dtype=np.float32)[::-1])
    R = np.zeros((128, 128), np.float32)
    cos = np.zeros((128, TS), np.float32)
    sin = np.zeros((128, TS), np.float32)
    t = np.arange(TS)
    pos = np.stack([t // 64, t % 64], 0).astype(np.float32)
    freqs = (10000.0 ** (-np.arange(16, dtype=np.float32) / 16)).astype(np.float32)
    for hh in range(2):
        for ax in range(2):
            for f in range(16):
                d1 = hh * 64 + ax * 32 + f
                d2 = d1 + 16
                ang = (pos[ax] * freqs[f]).astype(np.float32)
                cos[d1] = np.cos(ang); cos[d2] = np.cos(ang)
                sin[d1] = np.sin(ang); sin[d2] = np.sin(ang)
                R[d2, d1] = -1.0
                R[d1, d2] = 1.0
    c["c_rot"] = R
    c["c_cos"] = cos
    c["c_sin"] = sin
    r = np.arange(128)[:, None]
    cc = np.arange(128)[None, :]
    c["c_mleft"] = np.where(cc >= r, 0.0, NEG).astype(np.float32)
    c["c_mright"] = np.where(cc <= r, 0.0, NEG).astype(np.float32)
    col = np.arange(64)
    cs = np.clip(col - 8, 0, 48)
    ok = (col[None, :] >= cs[:, None]) & (col[None, :] < cs[:, None] + 16)
    c["c_colmask"] = np.where(ok, 0.0, NEG).astype(np.float32)
    b = np.zeros((128, 128), np.float32)
    b[:64, :64] = 1.0
    b[64:, 64:] = 1.0
    c["c_blk64"] = b
    c["c_iota"] = np.tile(np.arange(256, dtype=np.float32)[None, :], (128, 1))
    return c


def make_in_maps(inp):
    f = lambda a: np.ascontiguousarray(np.asarray(a), dtype=np.float32)
    consts = make_consts()
    shared = {
        "ln1_g": f(inp["ln1_g"]), "ln2_g": f(inp["ln2_g"]), "lnf_g": f(inp["lnf_g"]),
        "ada_w": f(inp["ada_w"]), "ada_b": f(inp["ada_b"]), "w_in": f(inp["w_in"]),
        "a_sink": f(inp["a_sink"]), "a_out": f(inp["a_out"]), "rw_mu": f(inp["rw_mu"]),
        "rw_w0": f(inp["rw_w0"]).reshape(L, 1024), "rw_w2": f(inp["rw_w2"]).reshape(L, 128, 512),
        "rw_a0": f(inp["rw_a0"]).reshape(L, 1024), "rw_a2": f(inp["rw_a2"]).reshape(L, 128, 512),
        "rw_g2": f(inp["rw_g2"]), "rw_kk": f(inp["rw_kk"]), "rw_ka": f(inp["rw_ka"]),
        "rw_rk": f(inp["rw_rk"]).reshape(L, 512), "rw_lnx_g": f(inp["rw_lnx_g"]), "rw_lnx_b": f(inp["rw_lnx_b"]),
        "rw_out": f(inp["rw_out"]), "na_rpb": f(inp["na_rpb"]), "na_out": f(inp["na_out"]), "w_o": f(inp["w_o"]),
        "pe_q": f(inp["pe_q"]), "pe_subkeys": f(inp["pe_subkeys"]).reshape(L, 16, 128, 128),
        "pe_u": f(inp["pe_u"]).reshape(L * 16384, D), "pe_v": f(inp["pe_v"]).reshape(L * 16384, D),
    }
    shared.update(consts)
    xp = f(inp["x_prompt"]); xs = f(inp["x_sample"])
    maps = []
    for c in range(8):
        b = c // 4
        m = dict(shared)
        m["xall"] = np.concatenate([xp[2 * c], xp[2 * c + 1], xs[b]], 0)
        m["cvec"] = np.stack([f(inp["c_ctx"]), f(inp["c"])[b]], 0)
        m["cak"] = f(inp["cache_a_k"])[b].reshape(L, 512, 128)
        m["cav"] = f(inp["cache_a_v"])[b].reshape(L, 512, 128)
        m["cck"] = f(inp["cache_c_k"])[b].reshape(L, 512, 512)
        m["ccv"] = f(inp["cache_c_v"])[b].reshape(L, 512, 512)
        m["st0"] = f(inp["state_rwkv"])[b].reshape(L, 128, 512)
        maps.append(m)
    return maps


_CACHE = {}


def kernel(**inputs):
    if "nc" not in _CACHE:
        _CACHE["nc"] = build()[0]
    nc = _CACHE["nc"]
    maps = make_in_maps(inputs)
    res = run_bass_kernel_spmd(nc, maps, core_ids=list(range(8)))
    R = res.results
    y_prompt = np.zeros((16, 256, D), np.float32)
    y_sample = np.zeros((2, TS, D), np.float32)
    nak = np.zeros((16, L, 256, 2, 64), np.float32)
    nav = np.zeros((16, L, 256, 2, 64), np.float32)
    nck = np.zeros((16, L, 256, 8, 64), np.float32)
    ncv = np.zeros((16, L, 256, 8, 64), np.float32)
    nst = np.zeros((16, L, 2, 8, 64, 64), np.float32)
    for c in range(8):
        r = R[c]
        y = np.asarray(r["y"])
        y_prompt[2 * c] = y[0:256]
        y_prompt[2 * c + 1] = y[256:512]
        if c % 4 == 0:
            y_sample[c // 4] = y[512:]
        for s in range(2):
            nak[2 * c + s] = np.asarray(r["nak"])[s].reshape(L, 256, 2, 64)
            nav[2 * c + s] = np.asarray(r["nav"])[s].reshape(L, 256, 2, 64)
            nck[2 * c + s] = np.asarray(r["nck"])[s].reshape(L, 256, 8, 64)
            ncv[2 * c + s] = np.asarray(r["ncv"])[s].reshape(L, 256, 8, 64)
            nst[2 * c + s] = np.asarray(r["nst"])[s].reshape(L, 2, 8, 64, 64)
    return (y_prompt, y_sample, nak, nav, nck, ncv, nst)
```

```python
import os
import numpy as np
from contextlib import ExitStack
import concourse.bass as bass
import concourse.mybir as mybir
from concourse.bass_utils import run_bass_kernel_spmd

F32 = mybir.dt.float32
I32 = mybir.dt.int32
U32 = mybir.dt.uint32
AF = mybir.ActivationFunctionType
ALU = mybir.AluOpType
AX = mybir.AxisListType

NDS = 40
SAME_ENGINE_SYNC = {"pe": False, "dve": True, "act": True, "pool": True, "sp": True}

D = 1024
L = 4
TOK = 2560
NPT = 512
TS = 2048
IN_COLS = 7296
PT_COLS = 3200
PF_ROWS = 4736
SCALE = 0.125
NEG = -30000.0


class Res:
    __slots__ = ("w", "r", "ap", "name")

    def __init__(self, ap=None, name=None):
        self.w = None
        self.r = []
        self.ap = ap
        self.name = name


class KB:
    def __init__(self, nc):
        self.nc = nc
        self.eng = {"pe": nc.tensor, "dve": nc.vector, "act": nc.scalar, "pool": nc.gpsimd, "sp": nc.sync}
        self.esem = {k: nc.alloc_semaphore("es_" + k) for k in self.eng}
        self.ecnt = {k: 0 for k in self.eng}
        self.seen = {k: {} for k in self.eng}
        self.dsems = [nc.alloc_semaphore("ds%d" % i) for i in range(NDS)]
        self.dcnt = [0] * NDS
        self.dnext = 0
        self.nins = 0
        self.uid = 0
        self.rr = 0

    def sb(self, st, name, shape, dt=F32):
        self.uid += 1
        t = st.enter_context(self.nc.sbuf_tensor("%s_%d" % (name, self.uid), list(shape), dt))
        return Res(t.ap(), name)

    def ps(self, name, shape, dt=F32):
        t = self.nc.alloc_psum_tensor(name, list(shape), dt)
        return Res(t.ap(), name)

    def _wait(self, e, ev, mind=0):
        sem, val = ev
        own = sem is self.esem[e]
        if own and not SAME_ENGINE_SYNC[e]:
            return
        if own and mind > 0:
            return
        key = id(sem)
        if self.seen[e].get(key, 0) >= val:
            return
        self.eng[e].wait_ge(sem, val)
        self.seen[e][key] = val
        self.nins += 1

    def deps(self, e, reads, writes, ww=(), mind=0):
        for r in reads:
            if r.w is not None:
                self._wait(e, r.w, mind)
        for w in writes:
            if w.w is not None:
                self._wait(e, w.w, mind)
            for ev in w.r:
                self._wait(e, ev, mind)
        for w in ww:
            if w.w is not None and w.w[0] is not self.esem[e]:
                self._wait(e, w.w)
            for ev in w.r:
                self._wait(e, ev)

    def _record(self, ev, reads, writes):
        for r in reads:
            r.r.append(ev)
            if len(r.r) > 16:
                d = {}
                for s, v in r.r:
                    if id(s) not in d or d[id(s)][1] < v:
                        d[id(s)] = (s, v)
                r.r = list(d.values())
        for w in writes:
            w.w = ev
            w.r = []

    def op(self, e, fn, reads, writes, ww=(), mind=0, inc=True):
        self.deps(e, reads, writes, ww, mind)
        ins = fn(self.eng[e])
        self.nins += 1
        if inc:
            self.ecnt[e] += 1
            ins.then_inc(self.esem[e], 1)
            ev = (self.esem[e], self.ecnt[e])
        else:
            ev = (self.esem[e], self.ecnt[e] + 1)
        self._record(ev, reads, list(writes) + list(ww))

    def _dma_common(self, q, reads, writes, emit):
        kk = self.dnext
        self.dnext = (kk + 1) % NDS
        sem = self.dsems[kk]
        if self.dcnt[kk] > 0:
            self._wait(q, (sem, self.dcnt[kk]))
        self.deps(q, reads, writes)
        ins = emit()
        self.dcnt[kk] += 16
        ins.then_inc(sem, 16)
        self.nins += 1
        self._record((sem, self.dcnt[kk]), reads, writes)

    def dma(self, q, out, in_, reads, writes, **kw):
        if q is None:
            q = ("sp", "act")[self.rr % 2]
            self.rr += 1
        self._dma_common(q, reads, writes, lambda: self.eng[q].dma_start(out=out, in_=in_, **kw))

    def idma(self, out, in_, off_ap, reads, writes):
        self._dma_common("pool", reads, writes, lambda: self.nc.gpsimd.indirect_dma_start(
            out=out, out_offset=None, in_=in_,
            in_offset=bass.IndirectOffsetOnAxis(ap=off_ap, axis=0)))

    def barrier(self):
        engs = ("pe", "dve", "act", "pool", "sp")
        for e in engs:
            for kk in range(NDS):
                if self.dcnt[kk] > 0:
                    self._wait(e, (self.dsems[kk], self.dcnt[kk]))
            for e2 in engs:
                if e2 != e and self.ecnt[e2] > 0:
                    self._wait(e, (self.esem[e2], self.ecnt[e2]))

    def finish(self):
        self.barrier()


class Ctx:
    pass


def bc(ap, shape):
    return ap.to_broadcast(list(shape))


def build(nlayers=L, debug=False, stages=("all",)):
    nc = bass.Bass("TRN2", target_bir_lowering=False)
    k = KB(nc)
    C = Ctx()
    C.k = k
    C.nc = nc
    C.debug = debug

    def din(name, shape, dt=F32):
        return nc.dram_tensor(name, list(shape), dt, kind="ExternalInput").ap()

    def dout(name, shape, dt=F32):
        return nc.dram_tensor(name, list(shape), dt, kind="ExternalOutput").ap()

    def scr(name, shape, dt=F32):
        if debug:
            return nc.dram_tensor(name, list(shape), dt, kind="ExternalOutput").ap()
        return nc.dram_tensor(name, list(shape), dt).ap()

    I = Ctx()
    C.I = I
    I.xall = din("xall", [TOK, D])
    I.cvec = din("cvec", [2, D])
    I.cak = din("cak", [L, 512, 128])
    I.cav = din("cav", [L, 512, 128])
    I.cck = din("cck", [L, 512, 512])
    I.ccv = din("ccv", [L, 512, 512])
    I.st0 = din("st0", [L, 128, 512])
    I.ln1 = din("ln1_g", [L, D])
    I.ln2 = din("ln2_g", [L, D])
    I.lnf = din("lnf_g", [D])
    I.ada_w = din("ada_w", [L, D, 6 * D])
    I.ada_b = din("ada_b", [L, 6 * D])
    I.w_in = din("w_in", [L, D, IN_COLS])
    I.a_sink = din("a_sink", [L, 8])
    I.a_out = din("a_out", [L, 512, D])
    I.rw_mu = din("rw_mu", [L, 1920])
    I.rw_w0 = din("rw_w0", [L, 1024])
    I.rw_w2 = din("rw_w2", [L, 128, 512])
    I.rw_a0 = din("rw_a0", [L, 1024])
    I.rw_a2 = din("rw_a2", [L, 128, 512])
    I.rw_g2 = din("rw_g2", [L, 128, 512])
    I.rw_kk = din("rw_kk", [L, 512])
    I.rw_ka = din("rw_ka", [L, 512])
    I.rw_rk = din("rw_rk", [L, 512])
    I.rw_lnx_g = din("rw_lnx_g", [L, 512])
    I.rw_lnx_b = din("rw_lnx_b", [L, 512])
    I.rw_out = din("rw_out", [L, 512, D])
    I.na_rpb = din("na_rpb", [L, 8, 15, 31])
    I.na_out = din("na_out", [L, 512, D])
    I.w_o = din("w_o", [L, D, D])
    I.pe_q = din("pe_q", [L, D, 2048])
    I.pe_sk = din("pe_subkeys", [L, 16, 128, 128])
    I.pe_u = din("pe_u", [L * 16384, D])
    I.pe_v = din("pe_v", [L * 16384, D])
    I.ident = din("c_ident", [128, 128])
    I.antiid = din("c_antiid", [128, 128])
    I.rot = din("c_rot", [128, 128])
    I.cos = din("c_cos", [128, TS])
    I.sin = din("c_sin", [128, TS])
    I.mleft = din("c_mleft", [128, 128])
    I.mright = din("c_mright", [128, 128])
    I.colmask = din("c_colmask", [64, 64])
    I.blk64 = din("c_blk64", [128, 128])
    I.iota256 = din("c_iota", [128, 256])

    O = Ctx()
    C.O = O
    O.y = dout("y", [TOK, D])
    O.nak = dout("nak", [2, L, 256, 128])
    O.nav = dout("nav", [2, L, 256, 128])
    O.nck = dout("nck", [2, L, 256, 512])
    O.ncv = dout("ncv", [2, L, 256, 512])
    O.nst = dout("nst", [2, L, 64, 1024])

    S = Ctx()
    C.S = S
    S.PT = scr("s_PT", [TOK, PT_COLS])
    S.PF = scr("s_PF", [PF_ROWS, TOK])
    S.YA = scr("s_YA", [512, TOK])
    S.YB = scr("s_YB", [512, TOK])
    S.YC = scr("s_YC", [512, TOK])
    S.QR = scr("s_QR", [640, TS])
    S.E = scr("s_E", [1, 120, 127])
    S.OPS = scr("s_OPS", [2, TOK, 3072])
    S.YS = scr("s_YS", [2, TOK, 512])
    S.BG = scr("s_BG", [TOK, 1024])
    S.rOPS = Res(); S.rYS = Res(); S.rBG = Res()
    S.rPT = Res(); S.rPF = Res(); S.rYA = Res(); S.rYB = Res(); S.rYC = Res(); S.rQR = Res(); S.rE = Res()

    C.PS = [k.ps("psb%d" % i, [128, 512]) for i in range(8)]

    with ExitStack() as gst:
        C.xT = k.sb(gst, "xT", [128, 8, TOK])
        C.ident = k.sb(gst, "ident", [128, 128])
        C.ones = k.sb(gst, "ones", [128, 128])
        C.eps = k.sb(gst, "eps", [128, 1])
        C.scT = k.sb(gst, "scT", [128, 8, 2])
        C.modT = k.sb(gst, "modT", [128, 48, 2])
        C.gs1 = k.sb(gst, "gs1", [128, 8, 2])
        C.gs2 = k.sb(gst, "gs2", [128, 8, 2])
        k.dma("sp", C.ident.ap[:], I.ident[:, :], [], [C.ident])
        k.op("dve", lambda e: e.memset(C.ones.ap[:], 1.0), [], [C.ones])
        k.op("dve", lambda e: e.memset(C.eps.ap[:], 1e-6), [], [C.eps])

        phase_init(C)
        for l in range(nlayers):
            phase_mod(C, l)
            for tg in range(5):
                phase_norm_win(C, l, tg)
            k.barrier()
            phase_kv_out(C, l)
            if "attn" in stages or "all" in stages:
                phase_attn(C, l)
            if "rwkv" in stages or "all" in stages:
                phase_rwkv(C, l)
            if "merge" in stages or "all" in stages:
                phase_merge(C, l)
            if "peer" in stages or "all" in stages:
                phase_peer(C, l)
        phase_final(C)
        k.finish()
    C.nins = k.nins
    return nc, C


def phase_init(C):
    k, I = C.k, C.I
    with ExitStack() as st:
        xt = [k.sb(st, "xin%d" % i, [128, D]) for i in range(2)]
        for t in range(TOK // 128):
            x = xt[t % 2]
            k.dma(None, x.ap[:], I.xall[t * 128:(t + 1) * 128, :], [], [x])
            for half in range(2):
                ps = C.PS[(t * 2 + half) % 8]
                for j in range(4):
                    kc = half * 4 + j
                    k.op("pe", lambda e: e.transpose(ps.ap[:, j * 128:(j + 1) * 128], x.ap[:, kc * 128:(kc + 1) * 128], C.ident.ap[:]), [x, C.ident], [ps])
                eng = ("dve", "act")[half]
                if eng == "dve":
                    k.op("dve", lambda e: e.tensor_copy(C.xT.ap[:, half * 4:half * 4 + 4, t * 128:(t + 1) * 128], ps.ap[:].rearrange("p (a b) -> p a b", a=4)), [ps], [C.xT])
                else:
                    k.op("act", lambda e: e.copy(C.xT.ap[:, half * 4:half * 4 + 4, t * 128:(t + 1) * 128], ps.ap[:].rearrange("p (a b) -> p a b", a=4)), [ps], [C.xT])
        for g in range(2):
            k.dma("sp", C.scT.ap[:, :, g], I.cvec[g].rearrange("(c p) -> p c", p=128), [], [C.scT], allow_slow_non_contiguous=True)
        k.op("act", lambda e: e.activation(out=C.scT.ap[:], in_=C.scT.ap[:], func=AF.Silu), [C.scT], [C.scT])
        k.barrier()


def phase_mod(C, l):
    k, I = C.k, C.I
    with ExitStack() as st:
        wb = [k.sb(st, "adaw%d" % i, [128, 8, 512]) for i in range(2)]
        abT = k.sb(st, "abT", [128, 48])
        lnT = k.sb(st, "lnT", [128, 2, 8])
        k.dma("sp", abT.ap[:], I.ada_b[l].rearrange("(c p) -> p c", p=128), [], [abT], allow_slow_non_contiguous=True)
        k.dma("sp", lnT.ap[:, 0, :], I.ln1[l].rearrange("(c p) -> p c", p=128), [], [lnT], allow_slow_non_contiguous=True)
        k.dma("sp", lnT.ap[:, 1, :], I.ln2[l].rearrange("(c p) -> p c", p=128), [], [lnT], allow_slow_non_contiguous=True)
        ps = C.PS[0]
        for b in range(12):
            w = wb[b % 2]
            k.dma(None, w.ap[:], I.ada_w[l][:, b * 512:(b + 1) * 512].rearrange("(c p) n -> p c n", p=128), [], [w])
            for sub in range(4):
                fc = b * 4 + sub
                for kc in range(8):
                    k.op("pe", lambda e: e.matmul(ps.ap[:, fc * 2:fc * 2 + 2], lhsT=w.ap[:, kc, sub * 128:(sub + 1) * 128], rhs=C.scT.ap[:, kc, :], start=(kc == 0), stop=(kc == 7)), [w, C.scT], [ps])
        k.op("dve", lambda e: e.tensor_tensor(out=C.modT.ap[:], in0=ps.ap[:, 0:96].rearrange("p (a b) -> p a b", b=2), in1=bc(abT.ap[:].unsqueeze(2), [128, 48, 2]), op=ALU.add), [ps, abT], [C.modT])
        for (gs, mi, li) in ((C.gs1, 1, 0), (C.gs2, 4, 1)):
            k.op("dve", lambda e: e.tensor_scalar(gs.ap[:], C.modT.ap[:, mi * 8:mi * 8 + 8, :], 1.0, None, op0=ALU.add), [C.modT], [gs])
            k.op("dve", lambda e: e.tensor_tensor(out=gs.ap[:], in0=gs.ap[:], in1=bc(lnT.ap[:, li, :].unsqueeze(2), [128, 8, 2]), op=ALU.mult), [gs, lnT], [gs])
        k.barrier()


JOBS = [
    (0, 512, "F", 0), (512, 128, "F", 512), (2688, 512, "F", 640), (3200, 512, "F", 1152),
    (4224, 512, "F", 1664), (4736, 512, "F", 2176), (5248, 512, "F", 2688), (5760, 512, "F", 3200),
    (6272, 512, "F", 3712), (6784, 512, "F", 4224),
    (512, 256, "T", 0), (768, 512, "T", 256), (1280, 512, "T", 768), (1792, 512, "T", 1280), (2304, 384, "T", 1792),
    (3200, 512, "T", 2176), (3712, 512, "T", 2688),
]


def norm_group(C, st, tg, gs, shift_idx, hT):
    k = C.k
    g = 0 if tg == 0 else 1
    cols = slice(tg * 512, (tg + 1) * 512)
    sq = [k.sb(st, "sq%d" % i, [128, 512]) for i in range(2)]
    rstd = k.sb(st, "rstd", [128, 512])
    ps = C.PS[7]
    for kc in range(8):
        s = sq[kc % 2]
        k.op("act", lambda e: e.activation(out=s.ap[:], in_=C.xT.ap[:, kc, cols], func=AF.Square), [C.xT], [s])
        k.op("pe", lambda e: e.matmul(ps.ap[:], lhsT=C.ones.ap[:], rhs=s.ap[:], start=(kc == 0), stop=(kc == 7)), [s, C.ones], [ps])
    k.op("act", lambda e: e.activation(out=rstd.ap[:], in_=ps.ap[:], func=AF.Sqrt, scale=1.0 / D, bias=C.eps.ap[:, 0:1]), [ps, C.eps], [rstd])
    k.op("dve", lambda e: e.reciprocal(rstd.ap[:], rstd.ap[:]), [rstd], [rstd])
    for kc in range(8):
        eng = "dve" if kc % 2 == 0 else "pool"
        k.op(eng, lambda e: e.tensor_tensor(out=hT.ap[:, kc, :], in0=C.xT.ap[:, kc, cols], in1=rstd.ap[:], op=ALU.mult), [C.xT, rstd], [hT])
        k.op(eng, lambda e: e.tensor_scalar(hT.ap[:, kc, :], hT.ap[:, kc, :], gs.ap[:, kc, g:g + 1], C.modT.ap[:, shift_idx * 8 + kc, g:g + 1], op0=ALU.mult, op1=ALU.add), [hT, gs, C.modT], [hT])


def phase_norm_win(C, l, tg):
    k, I, S = C.k, C.I, C.S
    with ExitStack() as st:
        hT = k.sb(st, "hT", [128, 8, 512])
        norm_group(C, st, tg, C.gs1, 0, hT)
        wb = [k.sb(st, "winw%d" % i, [128, 8, 512]) for i in range(2)]
        ev = [k.sb(st, "winev%d" % i, [128, 512]) for i in range(4)]
        ei = 0
        pi = 0
        for ji, (c0, n, lay, d0) in enumerate(JOBS):
            w = wb[ji % 2]
            k.dma(None, w.ap[:, :, 0:n], I.w_in[l][:, c0:c0 + n].rearrange("(c p) n -> p c n", p=128), [], [w])
            if lay == "F":
                for sub in range(n // 128):
                    ps = C.PS[pi % 6]; pi += 1
                    for kc in range(8):
                        k.op("pe", lambda e: e.matmul(ps.ap[:], lhsT=w.ap[:, kc, sub * 128:(sub + 1) * 128], rhs=hT.ap[:, kc, :], start=(kc == 0), stop=(kc == 7)), [w, hT], [ps])
                    o = ev[ei % 4]; ei += 1
                    if d0 >= 1664:
                        k.op("act", lambda e: e.activation(out=o.ap[:], in_=ps.ap[:], func=AF.Sigmoid), [ps], [o])
                    elif ei % 2 == 0:
                        k.op("act", lambda e: e.copy(o.ap[:], ps.ap[:]), [ps], [o])
                    else:
                        k.op("dve", lambda e: e.tensor_copy(o.ap[:], ps.ap[:]), [ps], [o])
                    r0 = d0 + sub * 128
                    k.dma(None, S.PF[r0:r0 + 128, tg * 512:(tg + 1) * 512], o.ap[:], [o], [S.rPF])
            else:
                for tt in range(4):
                    ps = C.PS[pi % 6]; pi += 1
                    for kc in range(8):
                        k.op("pe", lambda e: e.matmul(ps.ap[:, 0:n], lhsT=hT.ap[:, kc, tt * 128:(tt + 1) * 128], rhs=w.ap[:, kc, 0:n], start=(kc == 0), stop=(kc == 7)), [w, hT], [ps])
                    o = ev[ei % 4]; ei += 1
                    if ei % 2 == 0:
                        k.op("act", lambda e: e.copy(o.ap[:, 0:n], ps.ap[:, 0:n]), [ps], [o])
                    else:
                        k.op("dve", lambda e: e.tensor_copy(o.ap[:, 0:n], ps.ap[:, 0:n]), [ps], [o])
                    t0 = tg * 512 + tt * 128
                    k.dma(None, S.PT[t0:t0 + 128, d0:d0 + n], o.ap[:, 0:n], [o], [S.rPT])
        k.barrier()


def phase_kv_out(C, l):
    k, S, O = C.k, C.S, C.O
    for s in range(2):
        rows = slice(s * 256, (s + 1) * 256)
        k.dma(None, O.nak[s, l], S.PT[rows, 0:128], [S.rPT], [])
        k.dma(None, O.nav[s, l], S.PT[rows, 128:256], [S.rPT], [])
        k.dma(None, O.nck[s, l], S.PT[rows, 2176:2688], [S.rPT], [])
        k.dma(None, O.ncv[s, l], S.PT[rows, 2688:3200], [S.rPT], [])


def attn_unit(C, W, ui, qT, qres, Mq, segs, vch, sink_h, out_ap, out_res):
    k = C.k
    par = ui % 2
    S_sb = W.S[par]
    sm = W.sm[par]
    tot = sum(s[2] for s in segs)
    off = 0
    k.op("pool", lambda e: e.memset(sm.ap[:], 0.0), [], [sm])
    for si, (kres, kT, n, masks) in enumerate(segs):
        ps = C.PS[par * 2 + si]
        k.op("pe", lambda e: e.matmul(ps.ap[:Mq, 0:n], lhsT=qT, rhs=kT, start=True, stop=True), [qres, kres], [ps])
        covered = []
        for (c0, ncol, mres, map_) in masks:
            k.op("dve", lambda e: e.tensor_tensor(out=S_sb.ap[:Mq, off + c0:off + c0 + ncol], in0=ps.ap[:Mq, c0:c0 + ncol], in1=map_, op=ALU.add), [ps, mres], [S_sb])
            covered.append((c0, c0 + ncol))
        covered.sort()
        pos = 0
        gaps = []
        for (a, b) in covered:
            if a > pos:
                gaps.append((pos, a))
            pos = max(pos, b)
        if pos < n:
            gaps.append((pos, n))
        for gi, (a, b) in enumerate(gaps):
            if not masks:
                k.op("act", lambda e: e.copy(S_sb.ap[:Mq, off + a:off + b], ps.ap[:Mq, a:b]), [ps], [S_sb])
            else:
                k.op("dve", lambda e: e.tensor_copy(S_sb.ap[:Mq, off + a:off + b], ps.ap[:Mq, a:b]), [ps], [S_sb])
        off += n
    k.op("dve", lambda e: e.reduce_max(out=sm.ap[:Mq, 0:1], in_=S_sb.ap[:Mq, 0:tot], axis=AX.X), [S_sb], [sm])
    if sink_h is not None:
        k.op("dve", lambda e: e.tensor_scalar(sm.ap[:Mq, 1:2], sm.ap[:Mq, 0:1], -SCALE, W.nsink.ap[:Mq, sink_h:sink_h + 1], op0=ALU.mult, op1=ALU.min), [sm, W.nsink], [sm])
    else:
        k.op("dve", lambda e: e.tensor_scalar(sm.ap[:Mq, 1:2], sm.ap[:Mq, 0:1], -SCALE, None, op0=ALU.mult), [sm], [sm])
    k.op("act", lambda e: e.activation(out=S_sb.ap[:Mq, 0:tot], in_=S_sb.ap[:Mq, 0:tot], func=AF.Exp, bias=sm.ap[:Mq, 1:2], scale=SCALE, accum_out=sm.ap[:Mq, 2:3]), [S_sb, sm], [S_sb, sm])
    if sink_h is not None:
        k.op("act", lambda e: e.activation(out=sm.ap[:Mq, 3:4], in_=W.sink.ap[:Mq, sink_h:sink_h + 1], func=AF.Exp, bias=sm.ap[:Mq, 1:2], scale=1.0), [sm, W.sink], [sm])
        k.op("dve", lambda e: e.tensor_tensor(out=sm.ap[:Mq, 2:3], in0=sm.ap[:Mq, 2:3], in1=sm.ap[:Mq, 3:4], op=ALU.add), [sm], [sm])
    k.op("dve", lambda e: e.reciprocal(sm.ap[:Mq, 4:5], sm.ap[:Mq, 2:3]), [sm], [sm])
    k.op("dve", lambda e: e.tensor_scalar(S_sb.ap[:Mq, 0:tot], S_sb.ap[:Mq, 0:tot], sm.ap[:Mq, 4:5], None, op0=ALU.mult), [S_sb, sm], [S_sb])
    pso = C.PS[6 + par]
    nch = len(vch)
    for g0 in range(0, nch, 4):
        grp = vch[g0:g0 + 4]
        gi = W.tcount
        W.tcount += 1
        pst = C.PS[4 + gi % 2]
        ptsb = W.PTs[gi % 2]
        maxnk = max(v[2] for v in grp)
        for jj, (vres, vap, nk, c0) in enumerate(grp):
            k.op("pe", lambda e: e.transpose(pst.ap[:nk, jj * 128:jj * 128 + Mq], S_sb.ap[:Mq, c0:c0 + nk], C.ident.ap[:Mq, :Mq]), [S_sb, C.ident], [pst])
        w = len(grp) * 128
        if gi % 2 == 0:
            k.op("dve", lambda e: e.tensor_copy(ptsb.ap[:maxnk, 0:w], pst.ap[:maxnk, 0:w]), [pst], [ptsb])
        else:
            k.op("act", lambda e: e.copy(ptsb.ap[:maxnk, 0:w], pst.ap[:maxnk, 0:w]), [pst], [ptsb])
        for jj, (vres, vap, nk, c0) in enumerate(grp):
            ci = g0 + jj
            k.op("pe", lambda e: e.matmul(pso.ap[:64, 0:Mq], lhsT=vap, rhs=ptsb.ap[:nk, jj * 128:jj * 128 + Mq], start=(ci == 0), stop=(ci == nch - 1)), [vres, ptsb], [pso])
    osb = W.o[par]
    k.op("act", lambda e: e.copy(osb.ap[:64, 0:Mq], pso.ap[:64, 0:Mq]), [pso], [osb])
    k.dma(None, out_ap, osb.ap[:64, 0:Mq], [osb], [out_res])


def attn_work(C, st, l, need_sink):
    k, I = C.k, C.I
    W = Ctx()
    W.S = [k.sb(st, "S_sb%d" % i, [128, 1024]) for i in range(2)]
    W.sm = [k.sb(st, "sm%d" % i, [128, 8]) for i in range(2)]
    W.PTs = [k.sb(st, "PTs%d" % i, [128, 512]) for i in range(2)]
    W.o = [k.sb(st, "osb%d" % i, [64, 128]) for i in range(2)]
    W.tcount = 0
    if need_sink:
        W.sink = k.sb(st, "sink", [128, 8])
        W.nsink = k.sb(st, "nsink", [128, 8])
        k.dma("sp", W.sink.ap[:], I.a_sink[l:l + 1, :].partition_broadcast(128) if False else bass.AP(tensor=I.a_sink.tensor, offset=l * 8, ap=[[0, 128], [1, 8]]), [], [W.sink])
        k.op("dve", lambda e: e.tensor_scalar(W.nsink.ap[:], W.sink.ap[:], -1.0, None, op0=ALU.mult), [W.sink], [W.nsink])
    return W


def phase_attn_prompt(C, l):
    k, I, S = C.k, C.I, C.S
    with ExitStack() as st:
        W = attn_work(C, st, l, True)
        Q = k.sb(st, "pQ", [64, 8, 256])
        K_ = k.sb(st, "pK", [64, 8, 256])
        V = k.sb(st, "pV", [128, 2, 512])
        ui = 0
        for mixer in ("A", "C"):
            for s in range(2):
                base = s * 256
                if mixer == "A":
                    q0, k0, nkv, v0, vw, Y, rY = 0, 512, 2, 128, 128, S.YA, S.rYA
                else:
                    q0, k0, nkv, v0, vw, Y, rY = 640, 1152, 8, 2688, 512, S.YC, S.rYC
                k.dma(None, Q.ap[:], S.PF[q0:q0 + 512, base:base + 256].rearrange("(h d) t -> d h t", d=64), [S.rPF], [Q])
                k.dma(None, K_.ap[:, 0:nkv, :], S.PF[k0:k0 + nkv * 64, base:base + 256].rearrange("(h d) t -> d h t", d=64), [S.rPF], [K_])
                k.dma(None, V.ap[:, :, 0:vw], S.PT[base:base + 256, v0:v0 + vw].rearrange("(t p) f -> p t f", p=128), [S.rPT], [V])
                for h in range(8):
                    kv = h // 4 if mixer == "A" else h
                    for qb in range(2):
                        segs = [(K_, K_.ap[:, kv, :], 256, [])]
                        vch = [(V, V.ap[:, t, kv * 64:(kv + 1) * 64], 128, t * 128) for t in range(2)]
                        attn_unit(C, W, ui, Q.ap[:, h, qb * 128:(qb + 1) * 128], Q, 128, segs, vch,
                                  h if mixer == "A" else None,
                                  Y[h * 64:(h + 1) * 64, base + qb * 128:base + (qb + 1) * 128], rY)
                        ui += 1
        k.barrier()


def load_cache_kT(C, st, src, nh, name):
    k = C.k
    ct = k.sb(st, name + "_tm", [128, 4, nh * 64])
    CK = k.sb(st, name, [64, nh, 512])
    k.dma(None, ct.ap[:], src.rearrange("(j p) f -> p j f", p=128), [], [ct])
    for h in range(nh):
        ps = C.PS[h % 4]
        for j in range(4):
            k.op("pe", lambda e: e.transpose(ps.ap[:64, j * 128:(j + 1) * 128], ct.ap[:, j, h * 64:(h + 1) * 64], C.ident.ap[:]), [ct, C.ident], [ps])
        k.op("dve", lambda e: e.tensor_copy(CK.ap[:, h, :], ps.ap[:64, :]), [ps], [CK])
    return CK


def phase_attn_sample_A(C, l):
    k, I, S = C.k, C.I, C.S
    B0 = NPT
    with ExitStack() as st:
        cos = k.sb(st, "cos", [128, TS]); sin = k.sb(st, "sin", [128, TS]); rot = k.sb(st, "rot", [128, 128])
        k.dma("sp", cos.ap[:], I.cos[:, :], [], [cos])
        k.dma("act", sin.ap[:], I.sin[:, :], [], [sin])
        k.dma("sp", rot.ap[:], I.rot[:, :], [], [rot])
        X = [k.sb(st, "ropeX%d" % i, [128, TS]) for i in range(2)]
        T1 = [k.sb(st, "ropeT%d" % i, [128, 512]) for i in range(2)]
        XR = [k.sb(st, "ropeR%d" % i, [128, 512]) for i in range(2)]
        cnt = 0
        for c in range(5):
            x = X[c % 2]
            k.dma(None, x.ap[:], S.PF[c * 128:(c + 1) * 128, B0:B0 + TS], [S.rPF], [x])
            for tg in range(4):
                cs = slice(tg * 512, (tg + 1) * 512)
                ps = C.PS[cnt % 4]; t1 = T1[cnt % 2]; xr = XR[cnt % 2]; cnt += 1
                k.op("pe", lambda e: e.matmul(ps.ap[:], lhsT=rot.ap[:], rhs=x.ap[:, cs], start=True, stop=True), [rot, x], [ps])
                k.op("pool", lambda e: e.tensor_tensor(out=t1.ap[:], in0=x.ap[:, cs], in1=cos.ap[:, cs], op=ALU.mult), [x, cos], [t1])
                k.op("dve", lambda e: e.tensor_tensor(out=xr.ap[:], in0=ps.ap[:], in1=sin.ap[:, cs], op=ALU.mult), [ps, sin], [xr])
                k.op("dve", lambda e: e.tensor_tensor(out=xr.ap[:], in0=xr.ap[:], in1=t1.ap[:], op=ALU.add), [xr, t1], [xr])
                k.dma(None, S.QR[c * 128:(c + 1) * 128, cs], xr.ap[:], [xr], [S.rQR])
        k.barrier()
    with ExitStack() as st:
        W = attn_work(C, st, l, True)
        ml = k.sb(st, "mleft", [128, 128]); mr = k.sb(st, "mright", [128, 128])
        k.dma("sp", ml.ap[:], I.mleft[:, :], [], [ml])
        k.dma("sp", mr.ap[:], I.mright[:, :], [], [mr])
        CK = load_cache_kT(C, st, I.cak[l], 2, "CKa")
        CV = k.sb(st, "CVa", [128, 4, 128])
        k.dma(None, CV.ap[:], I.cav[l].rearrange("(j p) f -> p j f", p=128), [], [CV])
        Q = k.sb(st, "sQ", [64, TS]); K_ = k.sb(st, "sK", [64, TS]); V = k.sb(st, "sV", [128, 16, 64])
        ui = 0
        for h in range(8):
            kv = h // 4
            if h % 4 == 0:
                k.dma(None, K_.ap[:], S.QR[512 + kv * 64:512 + (kv + 1) * 64, :], [S.rQR], [K_])
                k.dma(None, V.ap[:], S.PT[B0:B0 + TS, 128 + kv * 64:128 + (kv + 1) * 64].rearrange("(t p) f -> p t f", p=128), [S.rPT], [V])
            k.dma(None, Q.ap[:], S.QR[h * 64:(h + 1) * 64, :], [S.rQR], [Q])
            for n in range(16):
                ta = max(0, n - 1); tb = min(16, n + 2)
                nlat = (tb - ta) * 128
                masks = []
                if n > 0:
                    masks.append((0, 128, ml, ml.ap[:, :]))
                if n < 15:
                    masks.append((nlat - 128, 128, mr, mr.ap[:, :]))
                segs = [(K_, K_.ap[:, ta * 128:tb * 128], nlat, masks), (CK, CK.ap[:, kv, :], 512, [])]
                vch = [(V, V.ap[:, t, :], 128, (t - ta) * 128) for t in range(ta, tb)]
                vch += [(CV, CV.ap[:, j, kv * 64:(kv + 1) * 64], 128, nlat + j * 128) for j in range(4)]
                attn_unit(C, W, ui, Q.ap[:, n * 128:(n + 1) * 128], Q, 128, segs, vch, h,
                          S.YA[h * 64:(h + 1) * 64, B0 + n * 128:B0 + (n + 1) * 128], S.rYA)
                ui += 1
        k.barrier()


def phase_attn_sample_C(C, l):
    k, I, S = C.k, C.I, C.S
    B0 = NPT
    with ExitStack() as st:
        W = attn_work(C, st, l, False)
        z = k.sb(st, "zer", [120, 127])
        k.op("dve", lambda e: e.memset(z.ap[:], 0.0), [], [z])
        k.dma("sp", S.E[0], z.ap[:], [z], [S.rE])
        k.dma("sp", S.E[0, :, 48:79], I.na_rpb[l].rearrange("h r c -> (h r) c"), [S.rE], [S.rE])
        MB = k.sb(st, "MB", [64, 120, 64])
        cm = k.sb(st, "colmask", [64, 64])
        k.dma("sp", cm.ap[:], I.colmask[:, :], [], [cm])
        for c in range(64):
            k.dma(None, MB.ap[c:c + 1, :, :], S.E[0:1, :, 63 - c:127 - c], [S.rE], [MB])
        k.op("dve", lambda e: e.scalar_tensor_tensor(out=MB.ap[:], in0=MB.ap[:], scalar=1.0 / SCALE, in1=bc(cm.ap[:].unsqueeze(1), [64, 120, 64]), op0=ALU.mult, op1=ALU.add), [MB, cm], [MB])
        CK = load_cache_kT(C, st, I.cck[l], 8, "CKc")
        CV = k.sb(st, "CVc", [128, 4, 512])
        k.dma(None, CV.ap[:], I.ccv[l].rearrange("(j p) f -> p j f", p=128), [], [CV])
        Q = k.sb(st, "cQ", [64, TS]); K_ = k.sb(st, "cK", [64, TS]); V = k.sb(st, "cV", [64, 32, 64])
        ui = 0
        for h in range(8):
            k.dma(None, Q.ap[:], S.PF[640 + h * 64:640 + (h + 1) * 64, B0:B0 + TS], [S.rPF], [Q])
            k.dma(None, K_.ap[:], S.PF[1152 + h * 64:1152 + (h + 1) * 64, B0:B0 + TS], [S.rPF], [K_])
            k.dma(None, V.ap[:], S.PT[B0:B0 + TS, 2688 + h * 64:2688 + (h + 1) * 64].rearrange("(r c) f -> c r f", c=64), [S.rPT], [V])
            for r in range(32):
                rs = min(max(r - 4, 0), 24)
                dr0 = rs - r + 7
                bias = MB.ap[:, h * 15 + dr0:h * 15 + dr0 + 8, :].rearrange("p a b -> p (a b)")
                segs = [(K_, K_.ap[:, rs * 64:rs * 64 + 512], 512, [(0, 512, MB, bias)]), (CK, CK.ap[:, h, :], 512, [])]
                vch = [(V, V.ap[:, rs + a, :], 64, a * 64) for a in range(8)]
                vch += [(CV, CV.ap[:, j, h * 64:(h + 1) * 64], 128, 512 + j * 128) for j in range(4)]
                attn_unit(C, W, ui, Q.ap[:, r * 64:(r + 1) * 64], Q, 64, segs, vch, None,
                          S.YC[h * 64:(h + 1) * 64, B0 + r * 64:B0 + (r + 1) * 64], S.rYC)
                ui += 1
        k.barrier()


def phase_attn(C, l):
    phase_attn_prompt(C, l)
    phase_attn_sample_A(C, l)
    phase_attn_sample_C(C, l)


SEQS = [(0, 256), (256, 256), (512, 2048)]
TC = 32
DEC_SCALE = -0.6065306597126334


def pbcast(X, l, n):
    return bass.AP(tensor=X.tensor, offset=l * n, ap=[[0, 128], [1, n]])


def v3(ap, h=8):
    return ap.rearrange("p (h j) -> p h j", h=h)


def phase_rwkv_prep(C, l):
    k, I, S = C.k, C.I, C.S
    with ExitStack() as st:
        mu = k.sb(st, "mu_b", [128, 1920]); w0 = k.sb(st, "w0_b", [128, 1024]); a0 = k.sb(st, "a0_b", [128, 1024])
        kkp = k.sb(st, "kkp_b", [128, 512]); ka = k.sb(st, "ka_b", [128, 512]); omk = k.sb(st, "omk_b", [128, 512]); rk = k.sb(st, "rk_b", [128, 512])
        w2 = k.sb(st, "w2", [128, 512]); a2 = k.sb(st, "a2", [128, 512]); g2 = k.sb(st, "g2", [128, 512]); J = k.sb(st, "J", [128, 128])
        e12 = k.sb(st, "e12", [128, 1])
        k.op("dve", lambda e: e.memset(e12.ap[:], 1e-12), [], [e12])
        for (t, X, n) in ((mu, I.rw_mu, 1920), (w0, I.rw_w0, 1024), (a0, I.rw_a0, 1024), (kkp, I.rw_kk, 512), (ka, I.rw_ka, 512), (rk, I.rw_rk, 512)):
            k.dma(None, t.ap[:], pbcast(X, l, n), [], [t])
        for (t, X) in ((w2, I.rw_w2), (a2, I.rw_a2), (g2, I.rw_g2)):
            k.dma(None, t.ap[:], X[l], [], [t])
        k.dma(None, J.ap[:], I.antiid[:, :], [], [J])
        k.op("dve", lambda e: e.tensor_scalar(omk.ap[:], ka.ap[:], -1.0, 1.0, op0=ALU.mult, op1=ALU.add), [ka], [omk])
        pb = k.sb(st, "pb", [128, 1920]); prev = k.sb(st, "prev", [128, 1920]); nxt = k.sb(st, "nxt", [128, 1920])
        lw = k.sb(st, "lw", [128, 384]); lwT = k.sb(st, "lwT", [128, 384])
        az = [k.sb(st, "az%d" % z, [128, 512]) for z in range(2)]
        kk = k.sb(st, "kk", [128, 512]); kkn = k.sb(st, "kkn", [128, 512]); t1 = k.sb(st, "t1", [128, 512]); t2 = k.sb(st, "t2", [128, 512])
        sm = k.sb(st, "rsm", [128, 32])
        opz = [k.sb(st, "opz%d" % z, [128, 8, 6, 64]) for z in range(2)]
        rev = k.sb(st, "rev", [128, 1536]); bg = k.sb(st, "bg", [128, 1024])
        for (s0, T) in SEQS:
            nt = T // 128
            for ti in range(nt):
                t0 = s0 + ti * 128
                k.dma(None, pb.ap[:], S.PT[t0:t0 + 128, 256:2176], [S.rPT], [pb])
                if ti > 0:
                    k.dma(None, prev.ap[:], S.PT[t0 - 1:t0 + 127, 256:2176], [S.rPT], [prev])
                else:
                    k.op("dve", lambda e: e.memset(prev.ap[0:1, :], 0.0), [], [prev])
                    k.dma(None, prev.ap[1:128, :], S.PT[t0:t0 + 127, 256:2176], [S.rPT], [prev])
                if ti < nt - 1:
                    k.dma(None, nxt.ap[:], S.PT[t0 + 1:t0 + 129, 256:2176], [S.rPT], [nxt])
                else:
                    k.op("pool", lambda e: e.memset(nxt.ap[:], 0.0), [], [nxt])
                    k.dma(None, nxt.ap[0:127, :], S.PT[t0 + 1:t0 + 128, 256:2176], [S.rPT], [nxt])
                k.op("pool", lambda e: e.tensor_tensor(out=prev.ap[:], in0=prev.ap[:], in1=nxt.ap[:], op=ALU.add), [prev, nxt], [prev])
                k.op("dve", lambda e: e.scalar_tensor_tensor(out=prev.ap[:], in0=prev.ap[:], scalar=0.5, in1=pb.ap[:], op0=ALU.mult, op1=ALU.subtract), [prev, pb], [prev])
                k.op("dve", lambda e: e.tensor_tensor(out=prev.ap[:], in0=prev.ap[:], in1=mu.ap[:], op=ALU.mult), [prev, mu], [prev])
                k.op("dve", lambda e: e.tensor_tensor(out=pb.ap[:], in0=pb.ap[:], in1=prev.ap[:], op=ALU.add), [pb, prev], [pb])
                r_ = pb.ap[:, 0:512]; k_ = pb.ap[:, 512:1024]; v_ = pb.ap[:, 1024:1536]
                k.op("act", lambda e: e.activation(out=lw.ap[:, 0:128], in_=pb.ap[:, 1536:1664], func=AF.Tanh), [pb], [lw])
                k.op("act", lambda e: e.copy(lw.ap[:, 128:256], pb.ap[:, 1664:1792]), [pb], [lw])
                k.op("act", lambda e: e.activation(out=lw.ap[:, 256:384], in_=pb.ap[:, 1792:1920], func=AF.Sigmoid), [pb], [lw])
                ps = C.PS[0]
                for j in range(3):
                    k.op("pe", lambda e: e.transpose(ps.ap[:, j * 128:(j + 1) * 128], lw.ap[:, j * 128:(j + 1) * 128], C.ident.ap[:]), [lw, C.ident], [ps])
                k.op("dve", lambda e: e.tensor_copy(lwT.ap[:], ps.ap[:, 0:384]), [ps], [lwT])
                for z in range(2):
                    zs = slice(z * 64, (z + 1) * 64)
                    psw = C.PS[1 + z]; psa = C.PS[3 + z]
                    k.op("pe", lambda e: e.matmul(psw.ap[:], lhsT=lwT.ap[zs, 0:128], rhs=w2.ap[zs, :], start=True, stop=True), [lwT, w2], [psw])
                    k.op("pe", lambda e: e.matmul(psa.ap[:], lhsT=lwT.ap[zs, 128:256], rhs=a2.ap[zs, :], start=True, stop=True), [lwT, a2], [psa])
                    dec = v3(t1.ap[:])
                    k.op("dve", lambda e: e.tensor_tensor(out=t1.ap[:], in0=psw.ap[:], in1=w0.ap[:, z * 512:(z + 1) * 512], op=ALU.add), [psw, w0], [t1])
                    k.op("act", lambda e: e.activation(out=t1.ap[:], in_=t1.ap[:], func=AF.Sigmoid), [t1], [t1])
                    k.op("act", lambda e: e.activation(out=opz[z].ap[:, :, 1, :], in_=dec, func=AF.Exp, scale=DEC_SCALE), [t1], [opz[z]])
                    k.op("dve", lambda e: e.tensor_tensor(out=az[z].ap[:], in0=psa.ap[:], in1=a0.ap[:, z * 512:(z + 1) * 512], op=ALU.add), [psa, a0], [az[z]])
                    k.op("act", lambda e: e.activation(out=az[z].ap[:], in_=az[z].ap[:], func=AF.Sigmoid), [az[z]], [az[z]])
                psg = C.PS[5]
                k.op("pe", lambda e: e.matmul(psg.ap[:], lhsT=lwT.ap[:, 256:384], rhs=g2.ap[:, :], start=True, stop=True), [lwT, g2], [psg])
                k.op("act", lambda e: e.copy(bg.ap[:, 512:1024], psg.ap[:]), [psg], [bg])
                k.op("dve", lambda e: e.tensor_tensor(out=kk.ap[:], in0=k_, in1=kkp.ap[:], op=ALU.mult), [pb, kkp], [kk])
                k.op("pool", lambda e: e.tensor_tensor(out=t2.ap[:], in0=kk.ap[:], in1=kk.ap[:], op=ALU.mult), [kk], [t2])
                k.op("dve", lambda e: e.tensor_reduce(out=sm.ap[:, 0:8], in_=v3(t2.ap[:]), axis=AX.X, op=ALU.add), [t2], [sm])
                k.op("act", lambda e: e.activation(out=sm.ap[:, 0:8], in_=sm.ap[:, 0:8], func=AF.Sqrt, bias=e12.ap[:, 0:1], scale=1.0), [sm, e12], [sm])
                k.op("dve", lambda e: e.reciprocal(sm.ap[:, 8:16], sm.ap[:, 0:8]), [sm], [sm])
                k.op("dve", lambda e: e.tensor_tensor(out=v3(kkn.ap[:]), in0=v3(kk.ap[:]), in1=bc(sm.ap[:, 8:16].unsqueeze(2), [128, 8, 64]), op=ALU.mult), [kk, sm], [kkn])
                for z in range(2):
                    k.op("dve", lambda e: e.tensor_tensor(out=t1.ap[:], in0=az[z].ap[:], in1=ka.ap[:], op=ALU.mult), [az[z], ka], [t1])
                    k.op("pool", lambda e: e.tensor_tensor(out=t1.ap[:], in0=t1.ap[:], in1=omk.ap[:], op=ALU.add), [t1, omk], [t1])
                    k.op("dve", lambda e: e.tensor_tensor(out=opz[z].ap[:, :, 3, :], in0=v3(t1.ap[:]), in1=v3(k_), op=ALU.mult), [t1, pb], [opz[z]])
                    k.op("pool", lambda e: e.tensor_tensor(out=opz[z].ap[:, :, 2, :], in0=v3(kkn.ap[:]), in1=v3(az[z].ap[:]), op=ALU.mult), [kkn, az[z]], [opz[z]])
                    k.op("dve", lambda e: e.tensor_scalar(opz[z].ap[:, :, 0, :], v3(kkn.ap[:]), -1.0, None, op0=ALU.mult), [kkn], [opz[z]])
                    k.op("act", lambda e: e.copy(opz[z].ap[:, :, 4, :], v3(r_)), [pb], [opz[z]])
                    k.op("act", lambda e: e.copy(opz[z].ap[:, :, 5, :], v3(v_)), [pb], [opz[z]])
                k.op("dve", lambda e: e.tensor_tensor(out=v3(t1.ap[:]), in0=opz[0].ap[:, :, 3, :], in1=opz[1].ap[:, :, 3, :], op=ALU.add), [opz[0], opz[1]], [t1])
                k.op("dve", lambda e: e.tensor_tensor(out=t1.ap[:], in0=t1.ap[:], in1=r_, op=ALU.mult), [t1, pb], [t1])
                k.op("dve", lambda e: e.tensor_tensor(out=t1.ap[:], in0=t1.ap[:], in1=rk.ap[:], op=ALU.mult), [t1, rk], [t1])
                k.op("dve", lambda e: e.tensor_reduce(out=sm.ap[:, 16:24], in_=v3(t1.ap[:]), axis=AX.X, op=ALU.add), [t1], [sm])
                k.op("dve", lambda e: e.tensor_tensor(out=v3(bg.ap[:, 0:512]), in0=v3(v_), in1=bc(sm.ap[:, 16:24].unsqueeze(2), [128, 8, 64]), op=ALU.mult), [pb, sm], [bg])
                k.dma(None, S.BG[t0:t0 + 128, :], bg.ap[:], [bg], [S.rBG])
                k.dma(None, S.OPS[0, t0:t0 + 128, :], opz[0].ap[:].rearrange("p a b c -> p (a b c)"), [opz[0]], [S.rOPS])
                flat = opz[1].ap[:].rearrange("p a b c -> p (a b c)")
                tr = s0 + (nt - 1 - ti) * 128
                for half in range(2):
                    for j in range(3):
                        c0 = half * 1536 + j * 512
                        psr = C.PS[5 + j] if j < 2 else C.PS[7]
                        k.op("pe", lambda e: e.matmul(psr.ap[:], lhsT=J.ap[:], rhs=flat[:, c0:c0 + 512], start=True, stop=True), [J, opz[1]], [psr])
                        if j % 2 == 0:
                            k.op("dve", lambda e: e.tensor_copy(rev.ap[:, j * 512:(j + 1) * 512], psr.ap[:]), [psr], [rev])
                        else:
                            k.op("act", lambda e: e.copy(rev.ap[:, j * 512:(j + 1) * 512], psr.ap[:]), [psr], [rev])
                    k.dma(None, S.OPS[1, tr:tr + 128, half * 1536:(half + 1) * 1536], rev.ap[:], [rev], [S.rOPS])
        k.barrier()


SCAN_MIND = int(os.environ.get('SCAN_MIND', '0'))


def scan_group(C, st, l, insts, T, ILO, s_init, s_final, chains):
    k, I, S = C.k, C.I, C.S
    IC = 64 // ILO
    OPB = [k.sb(st, "OPB%d" % i, [128, TC, 5, 64]) for i in range(2)]
    VB = [k.sb(st, "VB%d" % i, [128, TC, ILO]) for i in range(2)]
    CH = []
    for ci, (eng, i0, i1) in enumerate(chains):
        n = i1 - i0
        c = Ctx()
        c.eng, c.i0, c.i1, c.n = eng, i0, i1, n
        c.S = k.sb(st, "scanS%d" % ci, [128, n * 64]); c.tmp = k.sb(st, "scantmp%d" % ci, [128, n * 64]); c.sa = k.sb(st, "scansa%d" % ci, [128, n])
        c.Y = [k.sb(st, "Yc%d_%d" % (ci, i), [128, TC, n]) for i in range(2)]
        if s_init is not None:
            k.dma(None, c.S.ap[:], s_init[:, i0 * 64:i1 * 64], [], [c.S])
        else:
            k.op("pool", lambda e: e.memset(c.S.ap[:], 0.0), [], [c.S])
        c.S3 = c.S.ap[:].rearrange("p (i j) -> p i j", j=64)
        c.T3 = c.tmp.ap[:].rearrange("p (i j) -> p i j", j=64)
        c.shp = [128, n, 64]
        CH.append(c)
    groups = {}
    for (p0, z, h, s0) in insts:
        groups.setdefault((z, s0), []).append((p0, h))
    for ch in range(T // TC):
        opb = OPB[ch % 2]; vb = VB[ch % 2]
        for (p0, z, h, s0) in insts:
            base = (z * TOK + s0 + ch * TC) * 3072 + h * 384
            src = bass.AP(tensor=S.OPS.tensor, offset=base, ap=[[0, IC], [3072, TC], [1, 320]])
            k.dma(None, opb.ap[p0:p0 + IC, :, :, :].rearrange("p t a b -> p t (a b)"), src, [S.rOPS], [opb])
            srcv = bass.AP(tensor=S.OPS.tensor, offset=base + 320, ap=[[ILO, IC], [3072, TC], [1, ILO]])
            k.dma(None, vb.ap[p0:p0 + IC, :, :], srcv, [S.rOPS], [vb])
        for tc in range(TC):
            def row(c, kind):
                return bc(opb.ap[:, tc, kind, :].unsqueeze(1), c.shp)
            for step in range(9):
                for c in CH:
                    yc = c.Y[ch % 2]
                    if step == 0:
                        k.op(c.eng, lambda e: e.tensor_tensor(out=c.T3, in0=c.S3, in1=row(c, 0), op=ALU.mult), [c.S, opb], [c.tmp], mind=SCAN_MIND, inc=(SCAN_MIND == 0))
                    elif step == 1:
                        k.op(c.eng, lambda e: e.tensor_reduce(out=c.sa.ap[:], in_=c.T3, axis=AX.X, op=ALU.add), [c.tmp], [c.sa], mind=SCAN_MIND, inc=(SCAN_MIND == 0))
                    elif step == 2:
                        k.op(c.eng, lambda e: e.tensor_tensor(out=c.S3, in0=c.S3, in1=row(c, 1), op=ALU.mult), [c.S, opb], [c.S], mind=SCAN_MIND, inc=(SCAN_MIND == 0))
                    elif step == 3:
                        k.op(c.eng, lambda e: e.tensor_tensor(out=c.T3, in0=bc(c.sa.ap[:].unsqueeze(2), c.shp), in1=row(c, 2), op=ALU.mult), [c.sa, opb], [c.tmp], mind=SCAN_MIND, inc=(SCAN_MIND == 0))
                    elif step == 4:
                        k.op(c.eng, lambda e: e.tensor_tensor(out=c.S3, in0=c.S3, in1=c.T3, op=ALU.add), [c.S, c.tmp], [c.S], mind=SCAN_MIND, inc=(SCAN_MIND == 0))
                    elif step == 5:
                        k.op(c.eng, lambda e: e.tensor_tensor(out=c.T3, in0=bc(vb.ap[:, tc, c.i0:c.i1].unsqueeze(2), c.shp), in1=row(c, 3), op=ALU.mult), [vb, opb], [c.tmp], mind=SCAN_MIND, inc=(SCAN_MIND == 0))
                    elif step == 6:
                        k.op(c.eng, lambda e: e.tensor_tensor(out=c.S3, in0=c.S3, in1=c.T3, op=ALU.add), [c.S, c.tmp], [c.S], mind=SCAN_MIND, inc=(SCAN_MIND == 0))
                    elif step == 7:
                        k.op(c.eng, lambda e: e.tensor_tensor(out=c.T3, in0=c.S3, in1=row(c, 4), op=ALU.mult), [c.S, opb], [c.tmp], mind=SCAN_MIND, inc=(SCAN_MIND == 0))
                    else:
                        k.op(c.eng, lambda e: e.tensor_reduce(out=yc.ap[:, tc, :], in_=c.T3, axis=AX.X, op=ALU.add), [c.tmp], [yc], mind=SCAN_MIND)
        for (z, s0), lst in groups.items():
            pa = min(p for p, _ in lst)
            np_ = len(lst) * IC
            for c in CH:
                dst = bass.AP(tensor=S.YS.tensor, offset=(z * TOK + s0 + ch * TC) * 512 + c.i0, ap=[[ILO, np_], [512, TC], [1, c.n]])
                k.dma(None, dst, c.Y[ch % 2].ap[pa:pa + np_, :, :], [c.Y[ch % 2]], [S.rYS])
    if s_final is not None:
        for (dst, p0, n) in s_final:
            for c in CH:
                k.dma(None, dst[:, c.i0 * 64:c.i1 * 64], c.S.ap[p0:p0 + n, :], [c.S], [])


def phase_rwkv_scan(C, l):
    k, I, S, O = C.k, C.I, C.S, C.O
    with ExitStack() as st:
        insts = []
        for s in range(2):
            for z in range(2):
                for h in range(8):
                    insts.append((s * 64 + z * 32 + h * 4, z, h, s * 256))
        scan_group(C, st, l, insts, 256, 16, None, [(O.nst[s, l], s * 64, 64) for s in range(2)], [("dve", 0, 8), ("dve", 8, 16)])
        k.barrier()
    with ExitStack() as st:
        insts = []
        for z in range(2):
            for h in range(8):
                insts.append((z * 64 + h * 8, z, h, 512))
        scan_group(C, st, l, insts, 2048, 8, I.st0[l], None, [("dve", 0, 4), ("dve", 4, 8)])
        k.barrier()


def phase_rwkv_post(C, l):
    k, I, S = C.k, C.I, C.S
    with ExitStack() as st:
        lg = k.sb(st, "lnxg_b", [128, 512]); lb = k.sb(st, "lnxb_b", [128, 512]); J = k.sb(st, "J2", [128, 128])
        gne = k.sb(st, "gne", [128, 1])
        k.op("dve", lambda e: e.memset(gne.ap[:], 64e-5), [], [gne])
        k.dma(None, lg.ap[:], pbcast(I.rw_lnx_g, l, 512), [], [lg])
        k.dma(None, lb.ap[:], pbcast(I.rw_lnx_b, l, 512), [], [lb])
        k.dma(None, J.ap[:], I.antiid[:, :], [], [J])
        yf = [k.sb(st, "yf%d" % i, [128, 512]) for i in range(2)]
        yb = [k.sb(st, "yb%d" % i, [128, 512]) for i in range(2)]
        bg = [k.sb(st, "pbg%d" % i, [128, 1024]) for i in range(2)]
        t1 = k.sb(st, "pt1", [128, 512]); sm = k.sb(st, "psm", [128, 32]); yT = [k.sb(st, "pyT%d" % i, [128, 4, 128]) for i in range(2)]
        cnt = 0
        for (s0, T) in SEQS:
            nt = T // 128
            for ti in range(nt):
                par = cnt % 2; cnt += 1
                t0 = s0 + ti * 128
                tr = s0 + (nt - 1 - ti) * 128
                y = yf[par]; y2 = yb[par]; b = bg[par]
                k.dma(None, y.ap[:], S.YS[0, t0:t0 + 128, :], [S.rYS], [y])
                k.dma(None, y2.ap[:], S.YS[1, tr:tr + 128, :], [S.rYS], [y2])
                k.dma(None, b.ap[:], S.BG[t0:t0 + 128, :], [S.rBG], [b])
                ps = C.PS[par]
                k.op("pe", lambda e: e.matmul(ps.ap[:], lhsT=J.ap[:], rhs=y2.ap[:], start=True, stop=True), [J, y2], [ps])
                k.op("dve", lambda e: e.tensor_tensor(out=y.ap[:], in0=y.ap[:], in1=ps.ap[:], op=ALU.add), [y, ps], [y])
                k.op("dve", lambda e: e.tensor_reduce(out=sm.ap[:, 0:8], in_=v3(y.ap[:]), axis=AX.X, op=ALU.add), [y], [sm])
                k.op("dve", lambda e: e.tensor_scalar(sm.ap[:, 0:8], sm.ap[:, 0:8], 1.0 / 64, None, op0=ALU.mult), [sm], [sm])
                k.op("dve", lambda e: e.tensor_tensor(out=v3(y.ap[:]), in0=v3(y.ap[:]), in1=bc(sm.ap[:, 0:8].unsqueeze(2), [128, 8, 64]), op=ALU.subtract), [y, sm], [y])
                k.op("pool", lambda e: e.tensor_tensor(out=t1.ap[:], in0=y.ap[:], in1=y.ap[:], op=ALU.mult), [y], [t1])
                k.op("dve", lambda e: e.tensor_reduce(out=sm.ap[:, 8:16], in_=v3(t1.ap[:]), axis=AX.X, op=ALU.add), [t1], [sm])
                k.op("act", lambda e: e.activation(out=sm.ap[:, 8:16], in_=sm.ap[:, 8:16], func=AF.Sqrt, bias=gne.ap[:, 0:1], scale=1.0 / 64), [sm, gne], [sm])
                k.op("dve", lambda e: e.reciprocal(sm.ap[:, 16:24], sm.ap[:, 8:16]), [sm], [sm])
                k.op("dve", lambda e: e.tensor_tensor(out=v3(y.ap[:]), in0=v3(y.ap[:]), in1=bc(sm.ap[:, 16:24].unsqueeze(2), [128, 8, 64]), op=ALU.mult), [y, sm], [y])
                k.op("pool", lambda e: e.tensor_tensor(out=y.ap[:], in0=y.ap[:], in1=lg.ap[:], op=ALU.mult), [y, lg], [y])
                k.op("dve", lambda e: e.tensor_tensor(out=y.ap[:], in0=y.ap[:], in1=lb.ap[:], op=ALU.add), [y, lb], [y])
                k.op("dve", lambda e: e.tensor_tensor(out=y.ap[:], in0=y.ap[:], in1=b.ap[:, 0:512], op=ALU.add), [y, b], [y])
                k.op("dve", lambda e: e.tensor_tensor(out=y.ap[:], in0=y.ap[:], in1=b.ap[:, 512:1024], op=ALU.mult), [y, b], [y])
                pst = C.PS[2 + par]
                for c in range(4):
                    k.op("pe", lambda e: e.transpose(pst.ap[:, c * 128:(c + 1) * 128], y.ap[:, c * 128:(c + 1) * 128], C.ident.ap[:]), [y, C.ident], [pst])
                k.op("act", lambda e: e.copy(yT[par].ap[:], pst.ap[:].rearrange("p (c t) -> p c t", c=4)), [pst], [yT[par]])
                k.dma(None, S.YB[:, t0:t0 + 128].rearrange("(c p) t -> p c t", p=128), yT[par].ap[:], [yT[par]], [S.rYB])
        k.barrier()


def phase_rwkv(C, l):
    phase_rwkv_prep(C, l)
    phase_rwkv_scan(C, l)
    phase_rwkv_post(C, l)


def phase_merge(C, l):
    k, I, S = C.k, C.I, C.S
    outs = (I.a_out, I.rw_out, I.na_out)
    Ys = ((S.YA, S.rYA), (S.YB, S.rYB), (S.YC, S.rYC))
    with ExitStack() as st:
        yT = [k.sb(st, "myT%d" % b, [128, 4, 512]) for b in range(3)]
        mg = k.sb(st, "merged", [128, 8, 512])
        wo = [[k.sb(st, "mwo%d_%d" % (b, i), [128, 4, 128]) for i in range(2)] for b in range(3)]
        gt = [[k.sb(st, "mg%d_%d" % (b, i), [128, 512]) for i in range(2)] for b in range(3)]
        tmp = [k.sb(st, "mtmp%d" % i, [128, 512]) for i in range(2)]
        wob = [k.sb(st, "mwob%d" % i, [128, 8, 128]) for i in range(2)]
        cnt = 0
        for tg in range(5):
            g = 0 if tg == 0 else 1
            cols = slice(tg * 512, (tg + 1) * 512)
            for b in range(3):
                k.dma(None, yT[b].ap[:], Ys[b][0][:, cols].rearrange("(c p) t -> p c t", p=128), [Ys[b][1]], [yT[b]])
            for fc in range(8):
                par = cnt % 2; cnt += 1
                for b in range(3):
                    k.dma(None, wo[b][par].ap[:], outs[b][l][:, fc * 128:(fc + 1) * 128].rearrange("(c p) n -> p c n", p=128), [], [wo[b][par]])
                    r0 = 1664 + b * 1024 + fc * 128
                    k.dma(None, gt[b][par].ap[:], S.PF[r0:r0 + 128, cols], [S.rPF], [gt[b][par]])
                for b in range(3):
                    ps = C.PS[(fc * 3 + b) % 6]
                    for c in range(4):
                        k.op("pe", lambda e: e.matmul(ps.ap[:], lhsT=wo[b][par].ap[:, c, :], rhs=yT[b].ap[:, c, :], start=(c == 0), stop=(c == 3)), [wo[b][par], yT[b]], [ps])
                    if b == 0:
                        k.op("dve", lambda e: e.tensor_tensor(out=mg.ap[:, fc, :], in0=ps.ap[:], in1=gt[b][par].ap[:], op=ALU.mult), [ps, gt[b][par]], [mg])
                    else:
                        t = tmp[b % 2]
                        k.op("dve", lambda e: e.tensor_tensor(out=t.ap[:], in0=ps.ap[:], in1=gt[b][par].ap[:], op=ALU.mult), [ps, gt[b][par]], [t])
                        k.op("pool", lambda e: e.tensor_tensor(out=mg.ap[:, fc, :], in0=mg.ap[:, fc, :], in1=t.ap[:], op=ALU.add), [mg, t], [mg])
            for fc2 in range(8):
                w = wob[fc2 % 2]
                k.dma(None, w.ap[:], I.w_o[l][:, fc2 * 128:(fc2 + 1) * 128].rearrange("(c p) n -> p c n", p=128), [], [w])
                ps = C.PS[6 + fc2 % 2]
                for kc in range(8):
                    k.op("pe", lambda e: e.matmul(ps.ap[:], lhsT=w.ap[:, kc, :], rhs=mg.ap[:, kc, :], start=(kc == 0), stop=(kc == 7)), [w, mg], [ps])
                k.op("dve", lambda e: e.scalar_tensor_tensor(out=C.xT.ap[:, fc2, cols], in0=ps.ap[:], scalar=C.modT.ap[:, 16 + fc2, g:g + 1], in1=C.xT.ap[:, fc2, cols], op0=ALU.mult, op1=ALU.add), [ps, C.modT, C.xT], [C.xT])
        k.barrier()


PEER_STOP = int(os.environ.get('PEER_STOP', '0'))
NBUF = 4
NACC = 2


def phase_peer(C, l):
    k, I, S = C.k, C.I, C.S
    with ExitStack() as st:
        skn = k.sb(st, "skn", [128, 16, 128]); skT = k.sb(st, "skT", [128, 16, 128])
        k.dma(None, skn.ap[:], I.pe_sk[l].rearrange("c n k -> n c k"), [], [skn])
        for c in range(16):
            ps = C.PS[c % 4]
            k.op("pe", lambda e: e.transpose(ps.ap[:, 0:128], skn.ap[:, c, :], C.ident.ap[:]), [skn, C.ident], [ps])
            k.op("dve", lambda e: e.tensor_copy(skT.ap[:, c, :], ps.ap[:, 0:128]), [ps], [skT])
        h2T = k.sb(st, "h2T", [128, 8, 128])
        h2s = [k.sb(st, "h2tok%d" % i, [128, D]) for i in range(2)]
        sq = [k.sb(st, "psq%d" % i, [128, 128]) for i in range(2)]; rstd = k.sb(st, "prstd", [128, 128])
        wq = [k.sb(st, "wq%d" % i, [128, 8, 128]) for i in range(2)]
        qT = k.sb(st, "qT", [128, 16, 128])
        Sc = [k.sb(st, "Sc%d" % c, [128, 128]) for c in range(16)]
        vals = [k.sb(st, "vals%d" % c, [128, 16]) for c in range(16)]
        idxu = [k.sb(st, "idxu%d" % c, [128, 16], U32) for c in range(16)]
        idxf = [k.sb(st, "idxf%d" % c, [128, 16]) for c in range(16)]
        cand = [k.sb(st, "cand%d" % h, [128, 256]) for h in range(8)]
        cand2 = [k.sb(st, "candb%d" % h, [128, 256]) for h in range(8)]
        cande = [k.sb(st, "cande%d" % h, [128, 256]) for h in range(8)]
        sv = [k.sb(st, "sv%d" % h, [128, 16]) for h in range(8)]
        gates = [k.sb(st, "gate%d" % i, [128, 8, 16]) for i in range(2)]; sm = k.sb(st, "pesm", [128, 32])
        junk = k.sb(st, "pjunk", [128, 256]); ef = k.sb(st, "ef", [128, 128])
        eis = [k.sb(st, "ei%d" % i, [128, 128], I32) for i in range(2)]
        dots = k.sb(st, "dots", [128, 128]); tg_ = k.sb(st, "pt_g", [128, 128]); wgt = k.sb(st, "wgt", [128, 128])
        UV = [k.sb(st, "UV%d" % i, [128, D]) for i in range(NBUF)]
        junk2 = k.sb(st, "pjunk2", [128, D])
        accs = [k.sb(st, "pacc%d" % i, [128, D]) for i in range(NACC)]
        shp3 = [128, 16, 16]
        if l == 0:
            print("PEER sbuf remaining", C.nc.sbuf_bytes_remaining)

        def stageA(tile):
            buf = tile % 2
            h2 = h2s[buf]; gate = gates[buf]; ei = eis[buf]
            g = 0 if tile < 4 else 1
            cols = slice(tile * 128, tile * 128 + 128)
            psn = C.PS[7]
            for kc in range(8):
                s_ = sq[kc % 2]
                k.op("act", lambda e: e.activation(out=s_.ap[:], in_=C.xT.ap[:, kc, cols], func=AF.Square), [C.xT], [s_])
                k.op("pe", lambda e: e.matmul(psn.ap[:, 0:128], lhsT=C.ones.ap[:], rhs=s_.ap[:], start=(kc == 0), stop=(kc == 7)), [s_, C.ones], [psn])
                yield
            k.op("act", lambda e: e.activation(out=rstd.ap[:], in_=psn.ap[:, 0:128], func=AF.Sqrt, scale=1.0 / D, bias=C.eps.ap[:, 0:1]), [psn, C.eps], [rstd])
            k.op("dve", lambda e: e.reciprocal(rstd.ap[:], rstd.ap[:]), [rstd], [rstd])
            yield
            for kc in range(8):
                k.op("dve", lambda e: e.tensor_tensor(out=h2T.ap[:, kc, :], in0=C.xT.ap[:, kc, cols], in1=rstd.ap[:], op=ALU.mult), [C.xT, rstd], [h2T])
                yield
            for kc in range(8):
                k.op("dve", lambda e: e.tensor_scalar(h2T.ap[:, kc, :], h2T.ap[:, kc, :], C.gs2.ap[:, kc, g:g + 1], C.modT.ap[:, 24 + kc, g:g + 1], op0=ALU.mult, op1=ALU.add), [h2T, C.gs2, C.modT], [h2T])
                yield
            for half in range(2):
                ps = C.PS[half]
                for j in range(4):
                    kc = half * 4 + j
                    k.op("pe", lambda e: e.transpose(ps.ap[:, j * 128:(j + 1) * 128], h2T.ap[:, kc, :], C.ident.ap[:]), [h2T, C.ident], [ps])
                k.op("act", lambda e: e.copy(h2.ap[:, half * 512:(half + 1) * 512], ps.ap[:]), [ps], [], ww=[h2])
                yield
            for c in range(16):
                w = wq[c % 2]
                k.dma(None, w.ap[:], I.pe_q[l][:, c * 128:(c + 1) * 128].rearrange("(c p) n -> p c n", p=128), [], [w])
                ps = C.PS[2 + c % 2]
                for kc in range(8):
                    k.op("pe", lambda e: e.matmul(ps.ap[:, 0:128], lhsT=w.ap[:, kc, :], rhs=h2T.ap[:, kc, :], start=(kc == 0), stop=(kc == 7)), [w, h2T], [ps])
                k.op("act", lambda e: e.copy(qT.ap[:, c, :], ps.ap[:, 0:128]), [ps], [], ww=[qT])
                yield
            for q4 in range(4):
                ps = C.PS[4 + q4 % 2]
                for j in range(4):
                    c = q4 * 4 + j
                    k.op("pe", lambda e: e.matmul(ps.ap[:, j * 128:(j + 1) * 128], lhsT=qT.ap[:, c, :], rhs=skT.ap[:, c, :], start=True, stop=True), [qT, skT], [ps])
                for j in range(4):
                    c = q4 * 4 + j
                    if q4 % 2 == 0:
                        k.op("dve", lambda e: e.tensor_copy(Sc[c].ap[:], ps.ap[:, j * 128:(j + 1) * 128]), [ps], [Sc[c]])
                    else:
                        k.op("act", lambda e: e.copy(Sc[c].ap[:], ps.ap[:, j * 128:(j + 1) * 128]), [ps], [Sc[c]])
                    yield
            for c in range(16):
                k.op("dve", lambda e: e.max(out=vals[c].ap[:, 0:8], in_=Sc[c].ap[:]), [Sc[c]], [vals[c]])
                yield
            for c in range(16):
                k.op("dve", lambda e: e.max_index(out=idxu[c].ap[:, 0:8], in_max=vals[c].ap[:, 0:8], in_values=Sc[c].ap[:]), [Sc[c], vals[c]], [idxu[c]])
                yield
            for c in range(16):
                k.op("dve", lambda e: e.match_replace(out=Sc[c].ap[:], in_to_replace=vals[c].ap[:, 0:8], in_values=Sc[c].ap[:], imm_value=-1e30), [Sc[c], vals[c]], [Sc[c]])
                yield
            for c in range(16):
                k.op("dve", lambda e: e.max(out=vals[c].ap[:, 8:16], in_=Sc[c].ap[:]), [Sc[c]], [vals[c]])
                yield
            for c in range(16):
                k.op("dve", lambda e: e.max_index(out=idxu[c].ap[:, 8:16], in_max=vals[c].ap[:, 8:16], in_values=Sc[c].ap[:]), [Sc[c], vals[c]], [idxu[c]])
                yield
            for c in range(16):
                k.op("dve", lambda e: e.tensor_copy(idxf[c].ap[:], idxu[c].ap[:]), [idxu[c]], [idxf[c]])
                yield
            for h in range(8):
                c4 = cand[h].ap[:].rearrange("p (a b) -> p a b", a=16)
                k.op("dve", lambda e: e.tensor_tensor(out=c4, in0=bc(vals[2 * h].ap[:].unsqueeze(2), shp3), in1=bc(vals[2 * h + 1].ap[:].unsqueeze(1), shp3), op=ALU.add), [vals[2 * h], vals[2 * h + 1]], [cand[h]])
                yield
            for h in range(8):
                ce4 = cande[h].ap[:].rearrange("p (a b) -> p a b", a=16)
                k.op("dve", lambda e: e.scalar_tensor_tensor(out=ce4, in0=bc(idxf[2 * h].ap[:].unsqueeze(2), shp3), scalar=128.0, in1=bc(idxf[2 * h + 1].ap[:].unsqueeze(1), shp3), op0=ALU.mult, op1=ALU.add), [idxf[2 * h], idxf[2 * h + 1]], [cande[h]])
                yield
            for h in range(8):
                k.op("dve", lambda e: e.max(out=sv[h].ap[:, 0:8], in_=cand[h].ap[:]), [cand[h]], [sv[h]])
                yield
            for h in range(8):
                k.op("dve", lambda e: e.match_replace(out=cand2[h].ap[:], in_to_replace=sv[h].ap[:, 0:8], in_values=cand[h].ap[:], imm_value=-1e30), [cand[h], sv[h]], [cand2[h]])
                yield
            for h in range(8):
                k.op("dve", lambda e: e.max(out=sv[h].ap[:, 8:16], in_=cand2[h].ap[:]), [cand2[h]], [sv[h]])
                yield
            for h in range(8):
                k.op("dve", lambda e: e.tensor_scalar(gate.ap[:, h, :], sv[h].ap[:], sv[h].ap[:, 0:1], None, op0=ALU.subtract), [sv[h]], [], ww=[gate])
                yield
            k.op("act", lambda e: e.activation(out=gate.ap[:], in_=gate.ap[:], func=AF.Exp), [gate], [gate])
            k.op("dve", lambda e: e.tensor_reduce(out=sm.ap[:, 0:8], in_=gate.ap[:], axis=AX.X, op=ALU.add), [gate], [sm])
            yield
            k.op("dve", lambda e: e.reciprocal(sm.ap[:, 8:16], sm.ap[:, 0:8]), [sm], [sm])
            yield
            k.op("dve", lambda e: e.tensor_tensor(out=gate.ap[:], in0=gate.ap[:], in1=bc(sm.ap[:, 8:16].unsqueeze(2), [128, 8, 16]), op=ALU.mult), [gate, sm], [gate])
            yield
            k.op("pool", lambda e: e.memset(ef.ap[:], 0.0), [], [ef])
            for h in range(8):
                for kk2 in range(16):
                    j = h * 16 + kk2
                    k.op("dve", lambda e: e.scalar_tensor_tensor(out=junk.ap[:], in0=cand[h].ap[:], scalar=sv[h].ap[:, kk2:kk2 + 1], in1=cande[h].ap[:], op0=ALU.is_equal, op1=ALU.mult, accum_out=ef.ap[:, j:j + 1]), [cand[h], cande[h], sv[h]], [ef], ww=[junk])
                    yield
            k.op("dve", lambda e: e.tensor_scalar(ef.ap[:], ef.ap[:], 0.0, 16383.0, op0=ALU.max, op1=ALU.min), [ef], [ef])
            yield
            k.op("dve", lambda e: e.tensor_scalar(ef.ap[:], ef.ap[:], float(l * 16384), None, op0=ALU.add), [ef], [ef])
            yield
            k.op("dve", lambda e: e.tensor_copy(ei.ap[:], ef.ap[:]), [ef], [ei])
            yield

        def stageB(tile):
            buf = tile % 2
            h2 = h2s[buf]; gate = gates[buf]; ei = eis[buf]
            g = 0 if tile < 4 else 1
            cols = slice(tile * 128, tile * 128 + 128)
            k.op("pool", lambda e: e.memset(dots.ap[:], 0.0), [], [dots])
            for j in range(128):
                u = UV[j % NBUF]
                k.idma(u.ap[:], I.pe_u[:, :], ei.ap[:, j:j + 1], [ei], [u])
                k.op("dve", lambda e: e.scalar_tensor_tensor(out=junk2.ap[:], in0=u.ap[:], scalar=1.0, in1=h2.ap[:], op0=ALU.mult, op1=ALU.mult, accum_out=dots.ap[:, j:j + 1]), [u, h2], [dots], ww=[junk2])
                yield
            k.op("dve", lambda e: e.tensor_tensor(out=tg_.ap[:], in0=dots.ap[:], in1=dots.ap[:], op=ALU.mult), [dots], [tg_])
            yield
            k.op("dve", lambda e: e.tensor_tensor(out=tg_.ap[:], in0=tg_.ap[:], in1=dots.ap[:], op=ALU.mult), [tg_, dots], [tg_])
            yield
            k.op("dve", lambda e: e.scalar_tensor_tensor(out=tg_.ap[:], in0=tg_.ap[:], scalar=0.044715, in1=dots.ap[:], op0=ALU.mult, op1=ALU.add), [tg_, dots], [tg_])
            yield
            k.op("act", lambda e: e.activation(out=tg_.ap[:], in_=tg_.ap[:], func=AF.Tanh, scale=0.7978845608028654), [tg_], [tg_])
            k.op("dve", lambda e: e.tensor_scalar(tg_.ap[:], tg_.ap[:], 1.0, 0.5, op0=ALU.add, op1=ALU.mult), [tg_], [tg_])
            yield
            k.op("dve", lambda e: e.tensor_tensor(out=tg_.ap[:], in0=tg_.ap[:], in1=dots.ap[:], op=ALU.mult), [tg_, dots], [tg_])
            yield
            k.op("dve", lambda e: e.tensor_tensor(out=wgt.ap[:], in0=tg_.ap[:], in1=gate.ap[:].rearrange("p h a -> p (h a)"), op=ALU.mult), [tg_, gate], [wgt])
            yield
            for a_ in accs:
                k.op("pool", lambda e: e.memset(a_.ap[:], 0.0), [], [a_])
            for j in range(128):
                v = UV[j % NBUF]
                acc = accs[j % NACC]
                k.idma(v.ap[:], I.pe_v[:, :], ei.ap[:, j:j + 1], [ei], [v])
                k.op("dve", lambda e: e.scalar_tensor_tensor(out=acc.ap[:], in0=v.ap[:], scalar=wgt.ap[:, j:j + 1], in1=acc.ap[:], op0=ALU.mult, op1=ALU.add), [v, wgt, acc], [acc])
                yield
            for ai in range(1, NACC):
                k.op("dve", lambda e: e.tensor_tensor(out=accs[0].ap[:], in0=accs[0].ap[:], in1=accs[ai].ap[:], op=ALU.add), [accs[0], accs[ai]], [accs[0]])
                yield
            acc = accs[0]
            ps = C.PS[6]
            for half in range(2):
                for j in range(4):
                    kc = half * 4 + j
                    k.op("pe", lambda e: e.transpose(ps.ap[:, j * 128:(j + 1) * 128], acc.ap[:, kc * 128:(kc + 1) * 128], C.ident.ap[:]), [acc, C.ident], [ps])
                for j in range(4):
                    kc = half * 4 + j
                    k.op("dve", lambda e: e.scalar_tensor_tensor(out=C.xT.ap[:, kc, cols], in0=ps.ap[:, j * 128:(j + 1) * 128], scalar=C.modT.ap[:, 40 + kc, g:g + 1], in1=C.xT.ap[:, kc, cols], op0=ALU.mult, op1=ALU.add), [ps, C.modT, C.xT], [], ww=[C.xT])
                    yield

        ntiles = TOK // 128
        for _ in stageA(0):
            pass
        for t in range(ntiles):
            gb = stageB(t)
            ga = stageA(t + 1) if t + 1 < ntiles else iter(())
            a_live, b_live = True, True
            while a_live or b_live:
                if b_live:
                    try:
                        next(gb)
                    except StopIteration:
                        b_live = False
                if a_live:
                    try:
                        next(ga)
                    except StopIteration:
                        a_live = False
        k.barrier()


def phase_final(C):
    k, I, O = C.k, C.I, C.O
    with ExitStack() as st:
        lnfT = k.sb(st, "lnfT", [128, 8])
        k.dma("sp", lnfT.ap[:], I.lnf.rearrange("(c p) -> p c", p=128), [], [lnfT], allow_slow_non_contiguous=True)
        sq = [k.sb(st, "fsq%d" % i, [128, 512]) for i in range(2)]
        rstd = k.sb(st, "frstd", [128, 512])
        hT = k.sb(st, "fhT", [128, 8, 512])
        yt = [k.sb(st, "fy%d" % i, [128, D]) for i in range(2)]
        for tg in range(5):
            cols = slice(tg * 512, (tg + 1) * 512)
            ps = C.PS[7]
            for kc in range(8):
                s = sq[kc % 2]
                k.op("act", lambda e: e.activation(out=s.ap[:], in_=C.xT.ap[:, kc, cols], func=AF.Square), [C.xT], [s])
                k.op("pe", lambda e: e.matmul(ps.ap[:], lhsT=C.ones.ap[:], rhs=s.ap[:], start=(kc == 0), stop=(kc == 7)), [s, C.ones], [ps])
            k.op("act", lambda e: e.activation(out=rstd.ap[:], in_=ps.ap[:], func=AF.Sqrt, scale=1.0 / D, bias=C.eps.ap[:, 0:1]), [ps, C.eps], [rstd])
            k.op("dve", lambda e: e.reciprocal(rstd.ap[:], rstd.ap[:]), [rstd], [rstd])
            for kc in range(8):
                k.op("dve", lambda e: e.scalar_tensor_tensor(out=hT.ap[:, kc, :], in0=C.xT.ap[:, kc, cols], scalar=lnfT.ap[:, kc:kc + 1], in1=rstd.ap[:], op0=ALU.mult, op1=ALU.mult), [C.xT, rstd, lnfT], [hT])
            for tt in range(4):
                y = yt[tt % 2]
                for half in range(2):
                    ps2 = C.PS[(tt * 2 + half) % 6]
                    for j in range(4):
                        kc = half * 4 + j
                        k.op("pe", lambda e: e.transpose(ps2.ap[:, j * 128:(j + 1) * 128], hT.ap[:, kc, tt * 128:(tt + 1) * 128], C.ident.ap[:]), [hT, C.ident], [ps2])
                    if half == 0:
                        k.op("dve", lambda e: e.tensor_copy(y.ap[:, 0:512], ps2.ap[:]), [ps2], [y])
                    else:
                        k.op("act", lambda e: e.copy(y.ap[:, 512:1024], ps2.ap[:]), [ps2], [y])
                t0 = tg * 512 + tt * 128
                k.dma(None, O.y[t0:t0 + 128, :], y.ap[:], [y], [])
        k.barrier()


def make_consts():
    c = {}
    c["c_ident"] = np.eye(128, dtype=np.float32)
    c["c_antiid"] = np.ascontiguousarray(np.eye(128, dtype=np.float32)[::-1])
    R = np.zeros((128, 128), np.float32)
    cos = np.zeros((128, TS), np.float32)
    sin = np.zeros((128, TS), np.float32)
    t = np.arange(TS)
    pos = np.stack([t // 64, t % 64], 0).astype(np.float32)
    freqs = (10000.0 ** (-np.arange(16, dtype=np.float32) / 16)).astype(np.float32)
    for hh in range(2):
        for ax in range(2):
            for f in range(16):
                d1 = hh * 64 + ax * 32 + f
                d2 = d1 + 16
                ang = (pos[ax] * freqs[f]).astype(np.float32)
                cos[d1] = np.cos(ang); cos[d2] = np.cos(ang)
                sin[d1] = np.sin(ang); sin[d2] = np.sin(ang)
                R[d2, d1] = -1.0
                R[d1, d2] = 1.0
    c["c_rot"] = R
    c["c_cos"] = cos
    c["c_sin"] = sin
    r = np.arange(128)[:, None]
    cc = np.arange(128)[None, :]
    c["c_mleft"] = np.where(cc >= r, 0.0, NEG).astype(np.float32)
    c["c_mright"] = np.where(cc <= r, 0.0, NEG).astype(np.float32)
    col = np.arange(64)
    cs = np.clip(col - 8, 0, 48)
    ok = (col[None, :] >= cs[:, None]) & (col[None, :] < cs[:, None] + 16)
    c["c_colmask"] = np.where(ok, 0.0, NEG).astype(np.float32)
    b = np.zeros((128, 128), np.float32)
    b[:64, :64] = 1.0
    b[64:, 64:] = 1.0
    c["c_blk64"] = b
    c["c_iota"] = np.tile(np.arange(256, dtype=np.float32)[None, :], (128, 1))
    return c


def make_in_maps(inp):
    f = lambda a: np.ascontiguousarray(np.asarray(a), dtype=np.float32)
    consts = make_consts()
    shared = {
        "ln1_g": f(inp["ln1_g"]), "ln2_g": f(inp["ln2_g"]), "lnf_g": f(inp["lnf_g"]),
        "ada_w": f(inp["ada_w"]), "ada_b": f(inp["ada_b"]), "w_in": f(inp["w_in"]),
        "a_sink": f(inp["a_sink"]), "a_out": f(inp["a_out"]), "rw_mu": f(inp["rw_mu"]),
        "rw_w0": f(inp["rw_w0"]).reshape(L, 1024), "rw_w2": f(inp["rw_w2"]).reshape(L, 128, 512),
        "rw_a0": f(inp["rw_a0"]).reshape(L, 1024), "rw_a2": f(inp["rw_a2"]).reshape(L, 128, 512),
        "rw_g2": f(inp["rw_g2"]), "rw_kk": f(inp["rw_kk"]), "rw_ka": f(inp["rw_ka"]),
        "rw_rk": f(inp["rw_rk"]).reshape(L, 512), "rw_lnx_g": f(inp["rw_lnx_g"]), "rw_lnx_b": f(inp["rw_lnx_b"]),
        "rw_out": f(inp["rw_out"]), "na_rpb": f(inp["na_rpb"]), "na_out": f(inp["na_out"]), "w_o": f(inp["w_o"]),
        "pe_q": f(inp["pe_q"]), "pe_subkeys": f(inp["pe_subkeys"]).reshape(L, 16, 128, 128),
        "pe_u": f(inp["pe_u"]).reshape(L * 16384, D), "pe_v": f(inp["pe_v"]).reshape(L * 16384, D),
    }
    shared.update(consts)
    xp = f(inp["x_prompt"]); xs = f(inp["x_sample"])
    maps = []
    for c in range(8):
        b = c // 4
        m = dict(shared)
        m["xall"] = np.concatenate([xp[2 * c], xp[2 * c + 1], xs[b]], 0)
        m["cvec"] = np.stack([f(inp["c_ctx"]), f(inp["c"])[b]], 0)
        m["cak"] = f(inp["cache_a_k"])[b].reshape(L, 512, 128)
        m["cav"] = f(inp["cache_a_v"])[b].reshape(L, 512, 128)
        m["cck"] = f(inp["cache_c_k"])[b].reshape(L, 512, 512)
        m["ccv"] = f(inp["cache_c_v"])[b].reshape(L, 512, 512)
        m["st0"] = f(inp["state_rwkv"])[b].reshape(L, 128, 512)
        maps.append(m)
    return maps


_CACHE = {}


def kernel(**inputs):
    if "nc" not in _CACHE:
        _CACHE["nc"] = build()[0]
    nc = _CACHE["nc"]
    maps = make_in_maps(inputs)
    res = run_bass_kernel_spmd(nc, maps, core_ids=list(range(8)))
    R = res.results
    y_prompt = np.zeros((16, 256, D), np.float32)
    y_sample = np.zeros((2, TS, D), np.float32)
    nak = np.zeros((16, L, 256, 2, 64), np.float32)
    nav = np.zeros((16, L, 256, 2, 64), np.float32)
    nck = np.zeros((16, L, 256, 8, 64), np.float32)
    ncv = np.zeros((16, L, 256, 8, 64), np.float32)
    nst = np.zeros((16, L, 2, 8, 64, 64), np.float32)
    for c in range(8):
        r = R[c]
        y = np.asarray(r["y"])
        y_prompt[2 * c] = y[0:256]
        y_prompt[2 * c + 1] = y[256:512]
        if c % 4 == 0:
            y_sample[c // 4] = y[512:]
        for s in range(2):
            nak[2 * c + s] = np.asarray(r["nak"])[s].reshape(L, 256, 2, 64)
            nav[2 * c + s] = np.asarray(r["nav"])[s].reshape(L, 256, 2, 64)
            nck[2 * c + s] = np.asarray(r["nck"])[s].reshape(L, 256, 8, 64)
            ncv[2 * c + s] = np.asarray(r["ncv"])[s].reshape(L, 256, 8, 64)
            nst[2 * c + s] = np.asarray(r["nst"])[s].reshape(L, 2, 8, 64, 64)
    return (y_prompt, y_sample, nak, nav, nck, ncv, nst)
```

```python
import os
import numpy as np
from contextlib import ExitStack
import concourse.bass as bass
import concourse.mybir as mybir
from concourse.bass_utils import run_bass_kernel_spmd

F32 = mybir.dt.float32
I32 = mybir.dt.int32
U32 = mybir.dt.uint32
AF = mybir.ActivationFunctionType
ALU = mybir.AluOpType
AX = mybir.AxisListType

NDS = 40
SAME_ENGINE_SYNC = {"pe": False, "dve": True, "act": True, "pool": True, "sp": True}

D = 1024
L = 4
TOK = 2560
NPT = 512
TS = 2048
IN_COLS = 7296
PT_COLS = 3200
PF_ROWS = 4736
SCALE = 0.125
NEG = -30000.0


class Res:
    __slots__ = ("w", "r", "ap", "name")

    def __init__(self, ap=None, name=None):
        self.w = None
        self.r = []
        self.ap = ap
        self.name = name


class KB:
    def __init__(self, nc):
        self.nc = nc
        self.eng = {"pe": nc.tensor, "dve": nc.vector, "act": nc.scalar, "pool": nc.gpsimd, "sp": nc.sync}
        self.esem = {k: nc.alloc_semaphore("es_" + k) for k in self.eng}
        self.ecnt = {k: 0 for k in self.eng}
        self.seen = {k: {} for k in self.eng}
        self.dsems = [nc.alloc_semaphore("ds%d" % i) for i in range(NDS)]
        self.dcnt = [0] * NDS
        self.dnext = 0
        self.nins = 0
        self.uid = 0
        self.rr = 0

    def sb(self, st, name, shape, dt=F32):
        self.uid += 1
        t = st.enter_context(self.nc.sbuf_tensor("%s_%d" % (name, self.uid), list(shape), dt))
        return Res(t.ap(), name)

    def ps(self, name, shape, dt=F32):
        t = self.nc.alloc_psum_tensor(name, list(shape), dt)
        return Res(t.ap(), name)

    def _wait(self, e, ev, mind=0):
        sem, val = ev
        own = sem is self.esem[e]
        if own and not SAME_ENGINE_SYNC[e]:
            return
        if own and mind > 0:
            return
        key = id(sem)
        if self.seen[e].get(key, 0) >= val:
            return
        self.eng[e].wait_ge(sem, val)
        self.seen[e][key] = val
        self.nins += 1

    def deps(self, e, reads, writes, ww=(), mind=0):
        for r in reads:
            if r.w is not None:
                self._wait(e, r.w, mind)
        for w in writes:
            if w.w is not None:
                self._wait(e, w.w, mind)
            for ev in w.r:
                self._wait(e, ev, mind)
        for w in ww:
            if w.w is not None and w.w[0] is not self.esem[e]:
                self._wait(e, w.w)
            for ev in w.r:
                self._wait(e, ev)

    def _record(self, ev, reads, writes):
        for r in reads:
            r.r.append(ev)
            if len(r.r) > 16:
                d = {}
                for s, v in r.r:
                    if id(s) not in d or d[id(s)][1] < v:
                        d[id(s)] = (s, v)
                r.r = list(d.values())
        for w in writes:
            w.w = ev
            w.r = []

    def op(self, e, fn, reads, writes, ww=(), mind=0, inc=True):
        self.deps(e, reads, writes, ww, mind)
        ins = fn(self.eng[e])
        self.nins += 1
        if inc:
            self.ecnt[e] += 1
            ins.then_inc(self.esem[e], 1)
            ev = (self.esem[e], self.ecnt[e])
        else:
            ev = (self.esem[e], self.ecnt[e] + 1)
        self._record(ev, reads, list(writes) + list(ww))

    def _dma_common(self, q, reads, writes, emit):
        kk = self.dnext
        self.dnext = (kk + 1) % NDS
        sem = self.dsems[kk]
        if self.dcnt[kk] > 0:
            self._wait(q, (sem, self.dcnt[kk]))
        self.deps(q, reads, writes)
        ins = emit()
        self.dcnt[kk] += 16
        ins.then_inc(sem, 16)
        self.nins += 1
        self._record((sem, self.dcnt[kk]), reads, writes)

    def dma(self, q, out, in_, reads, writes, **kw):
        if q is None:
            q = ("sp", "act")[self.rr % 2]
            self.rr += 1
        self._dma_common(q, reads, writes, lambda: self.eng[q].dma_start(out=out, in_=in_, **kw))

    def idma(self, out, in_, off_ap, reads, writes):
        self._dma_common("pool", reads, writes, lambda: self.nc.gpsimd.indirect_dma_start(
            out=out, out_offset=None, in_=in_,
            in_offset=bass.IndirectOffsetOnAxis(ap=off_ap, axis=0)))

    def barrier(self):
        engs = ("pe", "dve", "act", "pool", "sp")
        for e in engs:
            for kk in range(NDS):
                if self.dcnt[kk] > 0:
                    self._wait(e, (self.dsems[kk], self.dcnt[kk]))
            for e2 in engs:
                if e2 != e and self.ecnt[e2] > 0:
                    self._wait(e, (self.esem[e2], self.ecnt[e2]))

    def finish(self):
        self.barrier()


class Ctx:
    pass


def bc(ap, shape):
    return ap.to_broadcast(list(shape))


def build(nlayers=L, debug=False, stages=("all",)):
    nc = bass.Bass("TRN2", target_bir_lowering=False)
    k = KB(nc)
    C = Ctx()
    C.k = k
    C.nc = nc
    C.debug = debug

    def din(name, shape, dt=F32):
        return nc.dram_tensor(name, list(shape), dt, kind="ExternalInput").ap()

    def dout(name, shape, dt=F32):
        return nc.dram_tensor(name, list(shape), dt, kind="ExternalOutput").ap()

    def scr(name, shape, dt=F32):
        if debug:
            return nc.dram_tensor(name, list(shape), dt, kind="ExternalOutput").ap()
        return nc.dram_tensor(name, list(shape), dt).ap()

    I = Ctx()
    C.I = I
    I.xall = din("xall", [TOK, D])
    I.cvec = din("cvec", [2, D])
    I.cak = din("cak", [L, 512, 128])
    I.cav = din("cav", [L, 512, 128])
    I.cck = din("cck", [L, 512, 512])
    I.ccv = din("ccv", [L, 512, 512])
    I.st0 = din("st0", [L, 128, 512])
    I.ln1 = din("ln1_g", [L, D])
    I.ln2 = din("ln2_g", [L, D])
    I.lnf = din("lnf_g", [D])
    I.ada_w = din("ada_w", [L, D, 6 * D])
    I.ada_b = din("ada_b", [L, 6 * D])
    I.w_in = din("w_in", [L, D, IN_COLS])
    I.a_sink = din("a_sink", [L, 8])
    I.a_out = din("a_out", [L, 512, D])
    I.rw_mu = din("rw_mu", [L, 1920])
    I.rw_w0 = din("rw_w0", [L, 1024])
    I.rw_w2 = din("rw_w2", [L, 128, 512])
    I.rw_a0 = din("rw_a0", [L, 1024])
    I.rw_a2 = din("rw_a2", [L, 128, 512])
    I.rw_g2 = din("rw_g2", [L, 128, 512])
    I.rw_kk = din("rw_kk", [L, 512])
    I.rw_ka = din("rw_ka", [L, 512])
    I.rw_rk = din("rw_rk", [L, 512])
    I.rw_lnx_g = din("rw_lnx_g", [L, 512])
    I.rw_lnx_b = din("rw_lnx_b", [L, 512])
    I.rw_out = din("rw_out", [L, 512, D])
    I.na_rpb = din("na_rpb", [L, 8, 15, 31])
    I.na_out = din("na_out", [L, 512, D])
    I.w_o = din("w_o", [L, D, D])
    I.pe_q = din("pe_q", [L, D, 2048])
    I.pe_sk = din("pe_subkeys", [L, 16, 128, 128])
    I.pe_u = din("pe_u", [L * 16384, D])
    I.pe_v = din("pe_v", [L * 16384, D])
    I.ident = din("c_ident", [128, 128])
    I.antiid = din("c_antiid", [128, 128])
    I.rot = din("c_rot", [128, 128])
    I.cos = din("c_cos", [128, TS])
    I.sin = din("c_sin", [128, TS])
    I.mleft = din("c_mleft", [128, 128])
    I.mright = din("c_mright", [128, 128])
    I.colmask = din("c_colmask", [64, 64])
    I.blk64 = din("c_blk64", [128, 128])
    I.iota256 = din("c_iota", [128, 256])

    O = Ctx()
    C.O = O
    O.y = dout("y", [TOK, D])
    O.nak = dout("nak", [2, L, 256, 128])
    O.nav = dout("nav", [2, L, 256, 128])
    O.nck = dout("nck", [2, L, 256, 512])
    O.ncv = dout("ncv", [2, L, 256, 512])
    O.nst = dout("nst", [2, L, 64, 1024])

    S = Ctx()
    C.S = S
    S.PT = scr("s_PT", [TOK, PT_COLS])
    S.PF = scr("s_PF", [PF_ROWS, TOK])
    S.YA = scr("s_YA", [512, TOK])
    S.YB = scr("s_YB", [512, TOK])
    S.YC = scr("s_YC", [512, TOK])
    S.QR = scr("s_QR", [640, TS])
    S.E = scr("s_E", [1, 120, 127])
    S.OPS = scr("s_OPS", [2, TOK, 3072])
    S.YS = scr("s_YS", [2, TOK, 512])
    S.BG = scr("s_BG", [TOK, 1024])
    S.rOPS = Res(); S.rYS = Res(); S.rBG = Res()
    S.rPT = Res(); S.rPF = Res(); S.rYA = Res(); S.rYB = Res(); S.rYC = Res(); S.rQR = Res(); S.rE = Res()

    C.PS = [k.ps("psb%d" % i, [128, 512]) for i in range(8)]

    with ExitStack() as gst:
        C.xT = k.sb(gst, "xT", [128, 8, TOK])
        C.ident = k.sb(gst, "ident", [128, 128])
        C.ones = k.sb(gst, "ones", [128, 128])
        C.eps = k.sb(gst, "eps", [128, 1])
        C.scT = k.sb(gst, "scT", [128, 8, 2])
        C.modT = k.sb(gst, "modT", [128, 48, 2])
        C.gs1 = k.sb(gst, "gs1", [128, 8, 2])
        C.gs2 = k.sb(gst, "gs2", [128, 8, 2])
        k.dma("sp", C.ident.ap[:], I.ident[:, :], [], [C.ident])
        k.op("dve", lambda e: e.memset(C.ones.ap[:], 1.0), [], [C.ones])
        k.op("dve", lambda e: e.memset(C.eps.ap[:], 1e-6), [], [C.eps])

        phase_init(C)
        for l in range(nlayers):
            phase_mod(C, l)
            for tg in range(5):
                phase_norm_win(C, l, tg)
            k.barrier()
            phase_kv_out(C, l)
            if "attn" in stages or "all" in stages:
                phase_attn(C, l)
            if "rwkv" in stages or "all" in stages:
                phase_rwkv(C, l)
            if "merge" in stages or "all" in stages:
                phase_merge(C, l)
            if "peer" in stages or "all" in stages:
                phase_peer(C, l)
        phase_final(C)
        k.finish()
    C.nins = k.nins
    return nc, C


def phase_init(C):
    k, I = C.k, C.I
    with ExitStack() as st:
        xt = [k.sb(st, "xin%d" % i, [128, D]) for i in range(2)]
        for t in range(TOK // 128):
            x = xt[t % 2]
            k.dma(None, x.ap[:], I.xall[t * 128:(t + 1) * 128, :], [], [x])
            for half in range(2):
                ps = C.PS[(t * 2 + half) % 8]
                for j in range(4):
                    kc = half * 4 + j
                    k.op("pe", lambda e: e.transpose(ps.ap[:, j * 128:(j + 1) * 128], x.ap[:, kc * 128:(kc + 1) * 128], C.ident.ap[:]), [x, C.ident], [ps])
                eng = ("dve", "act")[half]
                if eng == "dve":
                    k.op("dve", lambda e: e.tensor_copy(C.xT.ap[:, half * 4:half * 4 + 4, t * 128:(t + 1) * 128], ps.ap[:].rearrange("p (a b) -> p a b", a=4)), [ps], [C.xT])
                else:
                    k.op("act", lambda e: e.copy(C.xT.ap[:, half * 4:half * 4 + 4, t * 128:(t + 1) * 128], ps.ap[:].rearrange("p (a b) -> p a b", a=4)), [ps], [C.xT])
        for g in range(2):
            k.dma("sp", C.scT.ap[:, :, g], I.cvec[g].rearrange("(c p) -> p c", p=128), [], [C.scT], allow_slow_non_contiguous=True)
        k.op("act", lambda e: e.activation(out=C.scT.ap[:], in_=C.scT.ap[:], func=AF.Silu), [C.scT], [C.scT])
        k.barrier()


def phase_mod(C, l):
    k, I = C.k, C.I
    with ExitStack() as st:
        wb = [k.sb(st, "adaw%d" % i, [128, 8, 512]) for i in range(2)]
        abT = k.sb(st, "abT", [128, 48])
        lnT = k.sb(st, "lnT", [128, 2, 8])
        k.dma("sp", abT.ap[:], I.ada_b[l].rearrange("(c p) -> p c", p=128), [], [abT], allow_slow_non_contiguous=True)
        k.dma("sp", lnT.ap[:, 0, :], I.ln1[l].rearrange("(c p) -> p c", p=128), [], [lnT], allow_slow_non_contiguous=True)
        k.dma("sp", lnT.ap[:, 1, :], I.ln2[l].rearrange("(c p) -> p c", p=128), [], [lnT], allow_slow_non_contiguous=True)
        ps = C.PS[0]
        for b in range(12):
            w = wb[b % 2]
            k.dma(None, w.ap[:], I.ada_w[l][:, b * 512:(b + 1) * 512].rearrange("(c p) n -> p c n", p=128), [], [w])
            for sub in range(4):
                fc = b * 4 + sub
                for kc in range(8):
                    k.op("pe", lambda e: e.matmul(ps.ap[:, fc * 2:fc * 2 + 2], lhsT=w.ap[:, kc, sub * 128:(sub + 1) * 128], rhs=C.scT.ap[:, kc, :], start=(kc == 0), stop=(kc == 7)), [w, C.scT], [ps])
        k.op("dve", lambda e: e.tensor_tensor(out=C.modT.ap[:], in0=ps.ap[:, 0:96].rearrange("p (a b) -> p a b", b=2), in1=bc(abT.ap[:].unsqueeze(2), [128, 48, 2]), op=ALU.add), [ps, abT], [C.modT])
        for (gs, mi, li) in ((C.gs1, 1, 0), (C.gs2, 4, 1)):
            k.op("dve", lambda e: e.tensor_scalar(gs.ap[:], C.modT.ap[:, mi * 8:mi * 8 + 8, :], 1.0, None, op0=ALU.add), [C.modT], [gs])
            k.op("dve", lambda e: e.tensor_tensor(out=gs.ap[:], in0=gs.ap[:], in1=bc(lnT.ap[:, li, :].unsqueeze(2), [128, 8, 2]), op=ALU.mult), [gs, lnT], [gs])
        k.barrier()


JOBS = [
    (0, 512, "F", 0), (512, 128, "F", 512), (2688, 512, "F", 640), (3200, 512, "F", 1152),
    (4224, 512, "F", 1664), (4736, 512, "F", 2176), (5248, 512, "F", 2688), (5760, 512, "F", 3200),
    (6272, 512, "F", 3712), (6784, 512, "F", 4224),
    (512, 256, "T", 0), (768, 512, "T", 256), (1280, 512, "T", 768), (1792, 512, "T", 1280), (2304, 384, "T", 1792),
    (3200, 512, "T", 2176), (3712, 512, "T", 2688),
]


def norm_group(C, st, tg, gs, shift_idx, hT):
    k = C.k
    g = 0 if tg == 0 else 1
    cols = slice(tg * 512, (tg + 1) * 512)
    sq = [k.sb(st, "sq%d" % i, [128, 512]) for i in range(2)]
    rstd = k.sb(st, "rstd", [128, 512])
    ps = C.PS[7]
    for kc in range(8):
        s = sq[kc % 2]
        k.op("act", lambda e: e.activation(out=s.ap[:], in_=C.xT.ap[:, kc, cols], func=AF.Square), [C.xT], [s])
        k.op("pe", lambda e: e.matmul(ps.ap[:], lhsT=C.ones.ap[:], rhs=s.ap[:], start=(kc == 0), stop=(kc == 7)), [s, C.ones], [ps])
    k.op("act", lambda e: e.activation(out=rstd.ap[:], in_=ps.ap[:], func=AF.Sqrt, scale=1.0 / D, bias=C.eps.ap[:, 0:1]), [ps, C.eps], [rstd])
    k.op("dve", lambda e: e.reciprocal(rstd.ap[:], rstd.ap[:]), [rstd], [rstd])
    for kc in range(8):
        eng = "dve" if kc % 2 == 0 else "pool"
        k.op(eng, lambda e: e.tensor_tensor(out=hT.ap[:, kc, :], in0=C.xT.ap[:, kc, cols], in1=rstd.ap[:], op=ALU.mult), [C.xT, rstd], [hT])
        k.op(eng, lambda e: e.tensor_scalar(hT.ap[:, kc, :], hT.ap[:, kc, :], gs.ap[:, kc, g:g + 1], C.modT.ap[:, shift_idx * 8 + kc, g:g + 1], op0=ALU.mult, op1=ALU.add), [hT, gs, C.modT], [hT])


def phase_norm_win(C, l, tg):
    k, I, S = C.k, C.I, C.S
    with ExitStack() as st:
        hT = k.sb(st, "hT", [128, 8, 512])
        norm_group(C, st, tg, C.gs1, 0, hT)
        wb = [k.sb(st, "winw%d" % i, [128, 8, 512]) for i in range(2)]
        ev = [k.sb(st, "winev%d" % i, [128, 512]) for i in range(4)]
        ei = 0
        pi = 0
        for ji, (c0, n, lay, d0) in enumerate(JOBS):
            w = wb[ji % 2]
            k.dma(None, w.ap[:, :, 0:n], I.w_in[l][:, c0:c0 + n].rearrange("(c p) n -> p c n", p=128), [], [w])
            if lay == "F":
                for sub in range(n // 128):
                    ps = C.PS[pi % 6]; pi += 1
                    for kc in range(8):
                        k.op("pe", lambda e: e.matmul(ps.ap[:], lhsT=w.ap[:, kc, sub * 128:(sub + 1) * 128], rhs=hT.ap[:, kc, :], start=(kc == 0), stop=(kc == 7)), [w, hT], [ps])
                    o = ev[ei % 4]; ei += 1
                    if d0 >= 1664:
                        k.op("act", lambda e: e.activation(out=o.ap[:], in_=ps.ap[:], func=AF.Sigmoid), [ps], [o])
                    elif ei % 2 == 0:
                        k.op("act", lambda e: e.copy(o.ap[:], ps.ap[:]), [ps], [o])
                    else:
                        k.op("dve", lambda e: e.tensor_copy(o.ap[:], ps.ap[:]), [ps], [o])
                    r0 = d0 + sub * 128
                    k.dma(None, S.PF[r0:r0 + 128, tg * 512:(tg + 1) * 512], o.ap[:], [o], [S.rPF])
            else:
                for tt in range(4):
                    ps = C.PS[pi % 6]; pi += 1
                    for kc in range(8):
                        k.op("pe", lambda e: e.matmul(ps.ap[:, 0:n], lhsT=hT.ap[:, kc, tt * 128:(tt + 1) * 128], rhs=w.ap[:, kc, 0:n], start=(kc == 0), stop=(kc == 7)), [w, hT], [ps])
                    o = ev[ei % 4]; ei += 1
                    if ei % 2 == 0:
                        k.op("act", lambda e: e.copy(o.ap[:, 0:n], ps.ap[:, 0:n]), [ps], [o])
                    else:
                        k.op("dve", lambda e: e.tensor_copy(o.ap[:, 0:n], ps.ap[:, 0:n]), [ps], [o])
                    t0 = tg * 512 + tt * 128
                    k.dma(None, S.PT[t0:t0 + 128, d0:d0 + n], o.ap[:, 0:n], [o], [S.rPT])
        k.barrier()


def phase_kv_out(C, l):
    k, S, O = C.k, C.S, C.O
    for s in range(2):
        rows = slice(s * 256, (s + 1) * 256)
        k.dma(None, O.nak[s, l], S.PT[rows, 0:128], [S.rPT], [])
        k.dma(None, O.nav[s, l], S.PT[rows, 128:256], [S.rPT], [])
        k.dma(None, O.nck[s, l], S.PT[rows, 2176:2688], [S.rPT], [])
        k.dma(None, O.ncv[s, l], S.PT[rows, 2688:3200], [S.rPT], [])


def attn_unit(C, W, ui, qT, qres, Mq, segs, vch, sink_h, out_ap, out_res):
    k = C.k
    par = ui % 2
    S_sb = W.S[par]
    sm = W.sm[par]
    tot = sum(s[2] for s in segs)
    off = 0
    k.op("pool", lambda e: e.memset(sm.ap[:], 0.0), [], [sm])
    for si, (kres, kT, n, masks) in enumerate(segs):
        ps = C.PS[par * 2 + si]
        k.op("pe", lambda e: e.matmul(ps.ap[:Mq, 0:n], lhsT=qT, rhs=kT, start=True, stop=True), [qres, kres], [ps])
        covered = []
        for (c0, ncol, mres, map_) in masks:
            k.op("dve", lambda e: e.tensor_tensor(out=S_sb.ap[:Mq, off + c0:off + c0 + ncol], in0=ps.ap[:Mq, c0:c0 + ncol], in1=map_, op=ALU.add), [ps, mres], [S_sb])
            covered.append((c0, c0 + ncol))
        covered.sort()
        pos = 0
        gaps = []
        for (a, b) in covered:
            if a > pos:
                gaps.append((pos, a))
            pos = max(pos, b)
        if pos < n:
            gaps.append((pos, n))
        for gi, (a, b) in enumerate(gaps):
            if not masks:
                k.op("act", lambda e: e.copy(S_sb.ap[:Mq, off + a:off + b], ps.ap[:Mq, a:b]), [ps], [S_sb])
            else:
                k.op("dve", lambda e: e.tensor_copy(S_sb.ap[:Mq, off + a:off + b], ps.ap[:Mq, a:b]), [ps], [S_sb])
        off += n
    k.op("dve", lambda e: e.reduce_max(out=sm.ap[:Mq, 0:1], in_=S_sb.ap[:Mq, 0:tot], axis=AX.X), [S_sb], [sm])
    if sink_h is not None:
        k.op("dve", lambda e: e.tensor_scalar(sm.ap[:Mq, 1:2], sm.ap[:Mq, 0:1], -SCALE, W.nsink.ap[:Mq, sink_h:sink_h + 1], op0=ALU.mult, op1=ALU.min), [sm, W.nsink], [sm])
    else:
        k.op("dve", lambda e: e.tensor_scalar(sm.ap[:Mq, 1:2], sm.ap[:Mq, 0:1], -SCALE, None, op0=ALU.mult), [sm], [sm])
    k.op("act", lambda e: e.activation(out=S_sb.ap[:Mq, 0:tot], in_=S_sb.ap[:Mq, 0:tot], func=AF.Exp, bias=sm.ap[:Mq, 1:2], scale=SCALE, accum_out=sm.ap[:Mq, 2:3]), [S_sb, sm], [S_sb, sm])
    if sink_h is not None:
        k.op("act", lambda e: e.activation(out=sm.ap[:Mq, 3:4], in_=W.sink.ap[:Mq, sink_h:sink_h + 1], func=AF.Exp, bias=sm.ap[:Mq, 1:2], scale=1.0), [sm, W.sink], [sm])
        k.op("dve", lambda e: e.tensor_tensor(out=sm.ap[:Mq, 2:3], in0=sm.ap[:Mq, 2:3], in1=sm.ap[:Mq, 3:4], op=ALU.add), [sm], [sm])
    k.op("dve", lambda e: e.reciprocal(sm.ap[:Mq, 4:5], sm.ap[:Mq, 2:3]), [sm], [sm])
    k.op("dve", lambda e: e.tensor_scalar(S_sb.ap[:Mq, 0:tot], S_sb.ap[:Mq, 0:tot], sm.ap[:Mq, 4:5], None, op0=ALU.mult), [S_sb, sm], [S_sb])
    pso = C.PS[6 + par]
    nch = len(vch)
    for g0 in range(0, nch, 4):
        grp = vch[g0:g0 + 4]
        gi = W.tcount
        W.tcount += 1
        pst = C.PS[4 + gi % 2]
        ptsb = W.PTs[gi % 2]
        maxnk = max(v[2] for v in grp)
        for jj, (vres, vap, nk, c0) in enumerate(grp):
            k.op("pe", lambda e: e.transpose(pst.ap[:nk, jj * 128:jj * 128 + Mq], S_sb.ap[:Mq, c0:c0 + nk], C.ident.ap[:Mq, :Mq]), [S_sb, C.ident], [pst])
        w = len(grp) * 128
        if gi % 2 == 0:
            k.op("dve", lambda e: e.tensor_copy(ptsb.ap[:maxnk, 0:w], pst.ap[:maxnk, 0:w]), [pst], [ptsb])
        else:
            k.op("act", lambda e: e.copy(ptsb.ap[:maxnk, 0:w], pst.ap[:maxnk, 0:w]), [pst], [ptsb])
        for jj, (vres, vap, nk, c0) in enumerate(grp):
            ci = g0 + jj
            k.op("pe", lambda e: e.matmul(pso.ap[:64, 0:Mq], lhsT=vap, rhs=ptsb.ap[:nk, jj * 128:jj * 128 + Mq], start=(ci == 0), stop=(ci == nch - 1)), [vres, ptsb], [pso])
    osb = W.o[par]
    k.op("act", lambda e: e.copy(osb.ap[:64, 0:Mq], pso.ap[:64, 0:Mq]), [pso], [osb])
    k.dma(None, out_ap, osb.ap[:64, 0:Mq], [osb], [out_res])


def attn_work(C, st, l, need_sink):
    k, I = C.k, C.I
    W = Ctx()
    W.S = [k.sb(st, "S_sb%d" % i, [128, 1024]) for i in range(2)]
    W.sm = [k.sb(st, "sm%d" % i, [128, 8]) for i in range(2)]
    W.PTs = [k.sb(st, "PTs%d" % i, [128, 512]) for i in range(2)]
    W.o = [k.sb(st, "osb%d" % i, [64, 128]) for i in range(2)]
    W.tcount = 0
    if need_sink:
        W.sink = k.sb(st, "sink", [128, 8])
        W.nsink = k.sb(st, "nsink", [128, 8])
        k.dma("sp", W.sink.ap[:], I.a_sink[l:l + 1, :].partition_broadcast(128) if False else bass.AP(tensor=I.a_sink.tensor, offset=l * 8, ap=[[0, 128], [1, 8]]), [], [W.sink])
        k.op("dve", lambda e: e.tensor_scalar(W.nsink.ap[:], W.sink.ap[:], -1.0, None, op0=ALU.mult), [W.sink], [W.nsink])
    return W


def phase_attn_prompt(C, l):
    k, I, S = C.k, C.I, C.S
    with ExitStack() as st:
        W = attn_work(C, st, l, True)
        Q = k.sb(st, "pQ", [64, 8, 256])
        K_ = k.sb(st, "pK", [64, 8, 256])
        V = k.sb(st, "pV", [128, 2, 512])
        ui = 0
        for mixer in ("A", "C"):
            for s in range(2):
                base = s * 256
                if mixer == "A":
                    q0, k0, nkv, v0, vw, Y, rY = 0, 512, 2, 128, 128, S.YA, S.rYA
                else:
                    q0, k0, nkv, v0, vw, Y, rY = 640, 1152, 8, 2688, 512, S.YC, S.rYC
                k.dma(None, Q.ap[:], S.PF[q0:q0 + 512, base:base + 256].rearrange("(h d) t -> d h t", d=64), [S.rPF], [Q])
                k.dma(None, K_.ap[:, 0:nkv, :], S.PF[k0:k0 + nkv * 64, base:base + 256].rearrange("(h d) t -> d h t", d=64), [S.rPF], [K_])
                k.dma(None, V.ap[:, :, 0:vw], S.PT[base:base + 256, v0:v0 + vw].rearrange("(t p) f -> p t f", p=128), [S.rPT], [V])
                for h in range(8):
                    kv = h // 4 if mixer == "A" else h
                    for qb in range(2):
                        segs = [(K_, K_.ap[:, kv, :], 256, [])]
                        vch = [(V, V.ap[:, t, kv * 64:(kv + 1) * 64], 128, t * 128) for t in range(2)]
                        attn_unit(C, W, ui, Q.ap[:, h, qb * 128:(qb + 1) * 128], Q, 128, segs, vch,
                                  h if mixer == "A" else None,
                                  Y[h * 64:(h + 1) * 64, base + qb * 128:base + (qb + 1) * 128], rY)
                        ui += 1
        k.barrier()


def load_cache_kT(C, st, src, nh, name):
    k = C.k
    ct = k.sb(st, name + "_tm", [128, 4, nh * 64])
    CK = k.sb(st, name, [64, nh, 512])
    k.dma(None, ct.ap[:], src.rearrange("(j p) f -> p j f", p=128), [], [ct])
    for h in range(nh):
        ps = C.PS[h % 4]
        for j in range(4):
            k.op("pe", lambda e: e.transpose(ps.ap[:64, j * 128:(j + 1) * 128], ct.ap[:, j, h * 64:(h + 1) * 64], C.ident.ap[:]), [ct, C.ident], [ps])
        k.op("dve", lambda e: e.tensor_copy(CK.ap[:, h, :], ps.ap[:64, :]), [ps], [CK])
    return CK


def phase_attn_sample_A(C, l):
    k, I, S = C.k, C.I, C.S
    B0 = NPT
    with ExitStack() as st:
        cos = k.sb(st, "cos", [128, TS]); sin = k.sb(st, "sin", [128, TS]); rot = k.sb(st, "rot", [128, 128])
        k.dma("sp", cos.ap[:], I.cos[:, :], [], [cos])
        k.dma("act", sin.ap[:], I.sin[:, :], [], [sin])
        k.dma("sp", rot.ap[:], I.rot[:, :], [], [rot])
        X = [k.sb(st, "ropeX%d" % i, [128, TS]) for i in range(2)]
        T1 = [k.sb(st, "ropeT%d" % i, [128, 512]) for i in range(2)]
        XR = [k.sb(st, "ropeR%d" % i, [128, 512]) for i in range(2)]
        cnt = 0
        for c in range(5):
            x = X[c % 2]
            k.dma(None, x.ap[:], S.PF[c * 128:(c + 1) * 128, B0:B0 + TS], [S.rPF], [x])
            for tg in range(4):
                cs = slice(tg * 512, (tg + 1) * 512)
                ps = C.PS[cnt % 4]; t1 = T1[cnt % 2]; xr = XR[cnt % 2]; cnt += 1
                k.op("pe", lambda e: e.matmul(ps.ap[:], lhsT=rot.ap[:], rhs=x.ap[:, cs], start=True, stop=True), [rot, x], [ps])
                k.op("pool", lambda e: e.tensor_tensor(out=t1.ap[:], in0=x.ap[:, cs], in1=cos.ap[:, cs], op=ALU.mult), [x, cos], [t1])
                k.op("dve", lambda e: e.tensor_tensor(out=xr.ap[:], in0=ps.ap[:], in1=sin.ap[:, cs], op=ALU.mult), [ps, sin], [xr])
                k.op("dve", lambda e: e.tensor_tensor(out=xr.ap[:], in0=xr.ap[:], in1=t1.ap[:], op=ALU.add), [xr, t1], [xr])
                k.dma(None, S.QR[c * 128:(c + 1) * 128, cs], xr.ap[:], [xr], [S.rQR])
        k.barrier()
    with ExitStack() as st:
        W = attn_work(C, st, l, True)
        ml = k.sb(st, "mleft", [128, 128]); mr = k.sb(st, "mright", [128, 128])
        k.dma("sp", ml.ap[:], I.mleft[:, :], [], [ml])
        k.dma("sp", mr.ap[:], I.mright[:, :], [], [mr])
        CK = load_cache_kT(C, st, I.cak[l], 2, "CKa")
        CV = k.sb(st, "CVa", [128, 4, 128])
        k.dma(None, CV.ap[:], I.cav[l].rearrange("(j p) f -> p j f", p=128), [], [CV])
        Q = k.sb(st, "sQ", [64, TS]); K_ = k.sb(st, "sK", [64, TS]); V = k.sb(st, "sV", [128, 16, 64])
        ui = 0
        for h in range(8):
            kv = h // 4
            if h % 4 == 0:
                k.dma(None, K_.ap[:], S.QR[512 + kv * 64:512 + (kv + 1) * 64, :], [S.rQR], [K_])
                k.dma(None, V.ap[:], S.PT[B0:B0 + TS, 128 + kv * 64:128 + (kv + 1) * 64].rearrange("(t p) f -> p t f", p=128), [S.rPT], [V])
            k.dma(None, Q.ap[:], S.QR[h * 64:(h + 1) * 64, :], [S.rQR], [Q])
            for n in range(16):
                ta = max(0, n - 1); tb = min(16, n + 2)
                nlat = (tb - ta) * 128
                masks = []
                if n > 0:
                    masks.append((0, 128, ml, ml.ap[:, :]))
                if n < 15:
                    masks.append((nlat - 128, 128, mr, mr.ap[:, :]))
                segs = [(K_, K_.ap[:, ta * 128:tb * 128], nlat, masks), (CK, CK.ap[:, kv, :], 512, [])]
                vch = [(V, V.ap[:, t, :], 128, (t - ta) * 128) for t in range(ta, tb)]
                vch += [(CV, CV.ap[:, j, kv * 64:(kv + 1) * 64], 128, nlat + j * 128) for j in range(4)]
                attn_unit(C, W, ui, Q.ap[:, n * 128:(n + 1) * 128], Q, 128, segs, vch, h,
                          S.YA[h * 64:(h + 1) * 64, B0 + n * 128:B0 + (n + 1) * 128], S.rYA)
                ui += 1
        k.barrier()


def phase_attn_sample_C(C, l):
    k, I, S = C.k, C.I, C.S
    B0 = NPT
    with ExitStack() as st:
        W = attn_work(C, st, l, False)
        z = k.sb(st, "zer", [120, 127])
        k.op("dve", lambda e: e.memset(z.ap[:], 0.0), [], [z])
        k.dma("sp", S.E[0], z.ap[:], [z], [S.rE])
        k.dma("sp", S.E[0, :, 48:79], I.na_rpb[l].rearrange("h r c -> (h r) c"), [S.rE], [S.rE])
        MB = k.sb(st, "MB", [64, 120, 64])
        cm = k.sb(st, "colmask", [64, 64])
        k.dma("sp", cm.ap[:], I.colmask[:, :], [], [cm])
        for c in range(64):
            k.dma(None, MB.ap[c:c + 1, :, :], S.E[0:1, :, 63 - c:127 - c], [S.rE], [MB])
        k.op("dve", lambda e: e.scalar_tensor_tensor(out=MB.ap[:], in0=MB.ap[:], scalar=1.0 / SCALE, in1=bc(cm.ap[:].unsqueeze(1), [64, 120, 64]), op0=ALU.mult, op1=ALU.add), [MB, cm], [MB])
        CK = load_cache_kT(C, st, I.cck[l], 8, "CKc")
        CV = k.sb(st, "CVc", [128, 4, 512])
        k.dma(None, CV.ap[:], I.ccv[l].rearrange("(j p) f -> p j f", p=128), [], [CV])
        Q = k.sb(st, "cQ", [64, TS]); K_ = k.sb(st, "cK", [64, TS]); V = k.sb(st, "cV", [64, 32, 64])
        ui = 0
        for h in range(8):
            k.dma(None, Q.ap[:], S.PF[640 + h * 64:640 + (h + 1) * 64, B0:B0 + TS], [S.rPF], [Q])
            k.dma(None, K_.ap[:], S.PF[1152 + h * 64:1152 + (h + 1) * 64, B0:B0 + TS], [S.rPF], [K_])
            k.dma(None, V.ap[:], S.PT[B0:B0 + TS, 2688 + h * 64:2688 + (h + 1) * 64].rearrange("(r c) f -> c r f", c=64), [S.rPT], [V])
            for r in range(32):
                rs = min(max(r - 4, 0), 24)
                dr0 = rs - r + 7
                bias = MB.ap[:, h * 15 + dr0:h * 15 + dr0 + 8, :].rearrange("p a b -> p (a b)")
                segs = [(K_, K_.ap[:, rs * 64:rs * 64 + 512], 512, [(0, 512, MB, bias)]), (CK, CK.ap[:, h, :], 512, [])]
                vch = [(V, V.ap[:, rs + a, :], 64, a * 64) for a in range(8)]
                vch += [(CV, CV.ap[:, j, h * 64:(h + 1) * 64], 128, 512 + j * 128) for j in range(4)]
                attn_unit(C, W, ui, Q.ap[:, r * 64:(r + 1) * 64], Q, 64, segs, vch, None,
                          S.YC[h * 64:(h + 1) * 64, B0 + r * 64:B0 + (r + 1) * 64], S.rYC)
                ui += 1
        k.barrier()


def phase_attn(C, l):
    phase_attn_prompt(C, l)
    phase_attn_sample_A(C, l)
    phase_attn_sample_C(C, l)


SEQS = [(0, 256), (256, 256), (512, 2048)]
TC = 32
DEC_SCALE = -0.6065306597126334


def pbcast(X, l, n):
    return bass.AP(tensor=X.tensor, offset=l * n, ap=[[0, 128], [1, n]])


def v3(ap, h=8):
    return ap.rearrange("p (h j) -> p h j", h=h)


def phase_rwkv_prep(C, l):
    k, I, S = C.k, C.I, C.S
    with ExitStack() as st:
        mu = k.sb(st, "mu_b", [128, 1920]); w0 = k.sb(st, "w0_b", [128, 1024]); a0 = k.sb(st, "a0_b", [128, 1024])
        kkp = k.sb(st, "kkp_b", [128, 512]); ka = k.sb(st, "ka_b", [128, 512]); omk = k.sb(st, "omk_b", [128, 512]); rk = k.sb(st, "rk_b", [128, 512])
        w2 = k.sb(st, "w2", [128, 512]); a2 = k.sb(st, "a2", [128, 512]); g2 = k.sb(st, "g2", [128, 512]); J = k.sb(st, "J", [128, 128])
        e12 = k.sb(st, "e12", [128, 1])
        k.op("dve", lambda e: e.memset(e12.ap[:], 1e-12), [], [e12])
        for (t, X, n) in ((mu, I.rw_mu, 1920), (w0, I.rw_w0, 1024), (a0, I.rw_a0, 1024), (kkp, I.rw_kk, 512), (ka, I.rw_ka, 512), (rk, I.rw_rk, 512)):
            k.dma(None, t.ap[:], pbcast(X, l, n), [], [t])
        for (t, X) in ((w2, I.rw_w2), (a2, I.rw_a2), (g2, I.rw_g2)):
            k.dma(None, t.ap[:], X[l], [], [t])
        k.dma(None, J.ap[:], I.antiid[:, :], [], [J])
        k.op("dve", lambda e: e.tensor_scalar(omk.ap[:], ka.ap[:], -1.0, 1.0, op0=ALU.mult, op1=ALU.add), [ka], [omk])
        pb = k.sb(st, "pb", [128, 1920]); prev = k.sb(st, "prev", [128, 1920]); nxt = k.sb(st, "nxt", [128, 1920])
        lw = k.sb(st, "lw", [128, 384]); lwT = k.sb(st, "lwT", [128, 384])
        az = [k.sb(st, "az%d" % z, [128, 512]) for z in range(2)]
        kk = k.sb(st, "kk", [128, 512]); kkn = k.sb(st, "kkn", [128, 512]); t1 = k.sb(st, "t1", [128, 512]); t2 = k.sb(st, "t2", [128, 512])
        sm = k.sb(st, "rsm", [128, 32])
        opz = [k.sb(st, "opz%d" % z, [128, 8, 6, 64]) for z in range(2)]
        rev = k.sb(st, "rev", [128, 1536]); bg = k.sb(st, "bg", [128, 1024])
        for (s0, T) in SEQS:
            nt = T // 128
            for ti in range(nt):
                t0 = s0 + ti * 128
                k.dma(None, pb.ap[:], S.PT[t0:t0 + 128, 256:2176], [S.rPT], [pb])
                if ti > 0:
                    k.dma(None, prev.ap[:], S.PT[t0 - 1:t0 + 127, 256:2176], [S.rPT], [prev])
                else:
                    k.op("dve", lambda e: e.memset(prev.ap[0:1, :], 0.0), [], [prev])
                    k.dma(None, prev.ap[1:128, :], S.PT[t0:t0 + 127, 256:2176], [S.rPT], [prev])
                if ti < nt - 1:
                    k.dma(None, nxt.ap[:], S.PT[t0 + 1:t0 + 129, 256:2176], [S.rPT], [nxt])
                else:
                    k.op("pool", lambda e: e.memset(nxt.ap[:], 0.0), [], [nxt])
                    k.dma(None, nxt.ap[0:127, :], S.PT[t0 + 1:t0 + 128, 256:2176], [S.rPT], [nxt])
                k.op("pool", lambda e: e.tensor_tensor(out=prev.ap[:], in0=prev.ap[:], in1=nxt.ap[:], op=ALU.add), [prev, nxt], [prev])
                k.op("dve", lambda e: e.scalar_tensor_tensor(out=prev.ap[:], in0=prev.ap[:], scalar=0.5, in1=pb.ap[:], op0=ALU.mult, op1=ALU.subtract), [prev, pb], [prev])
                k.op("dve", lambda e: e.tensor_tensor(out=prev.ap[:], in0=prev.ap[:], in1=mu.ap[:], op=ALU.mult), [prev, mu], [prev])
                k.op("dve", lambda e: e.tensor_tensor(out=pb.ap[:], in0=pb.ap[:], in1=prev.ap[:], op=ALU.add), [pb, prev], [pb])
                r_ = pb.ap[:, 0:512]; k_ = pb.ap[:, 512:1024]; v_ = pb.ap[:, 1024:1536]
                k.op("act", lambda e: e.activation(out=lw.ap[:, 0:128], in_=pb.ap[:, 1536:1664], func=AF.Tanh), [pb], [lw])
                k.op("act", lambda e: e.copy(lw.ap[:, 128:256], pb.ap[:, 1664:1792]), [pb], [lw])
                k.op("act", lambda e: e.activation(out=lw.ap[:, 256:384], in_=pb.ap[:, 1792:1920], func=AF.Sigmoid), [pb], [lw])
                ps = C.PS[0]
                for j in range(3):
                    k.op("pe", lambda e: e.transpose(ps.ap[:, j * 128:(j + 1) * 128], lw.ap[:, j * 128:(j + 1) * 128], C.ident.ap[:]), [lw, C.ident], [ps])
                k.op("dve", lambda e: e.tensor_copy(lwT.ap[:], ps.ap[:, 0:384]), [ps], [lwT])
                for z in range(2):
                    zs = slice(z * 64, (z + 1) * 64)
                    psw = C.PS[1 + z]; psa = C.PS[3 + z]
                    k.op("pe", lambda e: e.matmul(psw.ap[:], lhsT=lwT.ap[zs, 0:128], rhs=w2.ap[zs, :], start=True, stop=True), [lwT, w2], [psw])
                    k.op("pe", lambda e: e.matmul(psa.ap[:], lhsT=lwT.ap[zs, 128:256], rhs=a2.ap[zs, :], start=True, stop=True), [lwT, a2], [psa])
                    dec = v3(t1.ap[:])
                    k.op("dve", lambda e: e.tensor_tensor(out=t1.ap[:], in0=psw.ap[:], in1=w0.ap[:, z * 512:(z + 1) * 512], op=ALU.add), [psw, w0], [t1])
                    k.op("act", lambda e: e.activation(out=t1.ap[:], in_=t1.ap[:], func=AF.Sigmoid), [t1], [t1])
                    k.op("act", lambda e: e.activation(out=opz[z].ap[:, :, 1, :], in_=dec, func=AF.Exp, scale=DEC_SCALE), [t1], [opz[z]])
                    k.op("dve", lambda e: e.tensor_tensor(out=az[z].ap[:], in0=psa.ap[:], in1=a0.ap[:, z * 512:(z + 1) * 512], op=ALU.add), [psa, a0], [az[z]])
                    k.op("act", lambda e: e.activation(out=az[z].ap[:], in_=az[z].ap[:], func=AF.Sigmoid), [az[z]], [az[z]])
                psg = C.PS[5]
                k.op("pe", lambda e: e.matmul(psg.ap[:], lhsT=lwT.ap[:, 256:384], rhs=g2.ap[:, :], start=True, stop=True), [lwT, g2], [psg])
                k.op("act", lambda e: e.copy(bg.ap[:, 512:1024], psg.ap[:]), [psg], [bg])
                k.op("dve", lambda e: e.tensor_tensor(out=kk.ap[:], in0=k_, in1=kkp.ap[:], op=ALU.mult), [pb, kkp], [kk])
                k.op("pool", lambda e: e.tensor_tensor(out=t2.ap[:], in0=kk.ap[:], in1=kk.ap[:], op=ALU.mult), [kk], [t2])
                k.op("dve", lambda e: e.tensor_reduce(out=sm.ap[:, 0:8], in_=v3(t2.ap[:]), axis=AX.X, op=ALU.add), [t2], [sm])
                k.op("act", lambda e: e.activation(out=sm.ap[:, 0:8], in_=sm.ap[:, 0:8], func=AF.Sqrt, bias=e12.ap[:, 0:1], scale=1.0), [sm, e12], [sm])
                k.op("dve", lambda e: e.reciprocal(sm.ap[:, 8:16], sm.ap[:, 0:8]), [sm], [sm])
                k.op("dve", lambda e: e.tensor_tensor(out=v3(kkn.ap[:]), in0=v3(kk.ap[:]), in1=bc(sm.ap[:, 8:16].unsqueeze(2), [128, 8, 64]), op=ALU.mult), [kk, sm], [kkn])
                for z in range(2):
                    k.op("dve", lambda e: e.tensor_tensor(out=t1.ap[:], in0=az[z].ap[:], in1=ka.ap[:], op=ALU.mult), [az[z], ka], [t1])
                    k.op("pool", lambda e: e.tensor_tensor(out=t1.ap[:], in0=t1.ap[:], in1=omk.ap[:], op=ALU.add), [t1, omk], [t1])
                    k.op("dve", lambda e: e.tensor_tensor(out=opz[z].ap[:, :, 3, :], in0=v3(t1.ap[:]), in1=v3(k_), op=ALU.mult), [t1, pb], [opz[z]])
                    k.op("pool", lambda e: e.tensor_tensor(out=opz[z].ap[:, :, 2, :], in0=v3(kkn.ap[:]), in1=v3(az[z].ap[:]), op=ALU.mult), [kkn, az[z]], [opz[z]])
                    k.op("dve", lambda e: e.tensor_scalar(opz[z].ap[:, :, 0, :], v3(kkn.ap[:]), -1.0, None, op0=ALU.mult), [kkn], [opz[z]])
                    k.op("act", lambda e: e.copy(opz[z].ap[:, :, 4, :], v3(r_)), [pb], [opz[z]])
                    k.op("act", lambda e: e.copy(opz[z].ap[:, :, 5, :], v3(v_)), [pb], [opz[z]])
                k.op("dve", lambda e: e.tensor_tensor(out=v3(t1.ap[:]), in0=opz[0].ap[:, :, 3, :], in1=opz[1].ap[:, :, 3, :], op=ALU.add), [opz[0], opz[1]], [t1])
                k.op("dve", lambda e: e.tensor_tensor(out=t1.ap[:], in0=t1.ap[:], in1=r_, op=ALU.mult), [t1, pb], [t1])
                k.op("dve", lambda e: e.tensor_tensor(out=t1.ap[:], in0=t1.ap[:], in1=rk.ap[:], op=ALU.mult), [t1, rk], [t1])
                k.op("dve", lambda e: e.tensor_reduce(out=sm.ap[:, 16:24], in_=v3(t1.ap[:]), axis=AX.X, op=ALU.add), [t1], [sm])
                k.op("dve", lambda e: e.tensor_tensor(out=v3(bg.ap[:, 0:512]), in0=v3(v_), in1=bc(sm.ap[:, 16:24].unsqueeze(2), [128, 8, 64]), op=ALU.mult), [pb, sm], [bg])
                k.dma(None, S.BG[t0:t0 + 128, :], bg.ap[:], [bg], [S.rBG])
                k.dma(None, S.OPS[0, t0:t0 + 128, :], opz[0].ap[:].rearrange("p a b c -> p (a b c)"), [opz[0]], [S.rOPS])
                flat = opz[1].ap[:].rearrange("p a b c -> p (a b c)")
                tr = s0 + (nt - 1 - ti) * 128
                for half in range(2):
                    for j in range(3):
                        c0 = half * 1536 + j * 512
                        psr = C.PS[5 + j] if j < 2 else C.PS[7]
                        k.op("pe", lambda e: e.matmul(psr.ap[:], lhsT=J.ap[:], rhs=flat[:, c0:c0 + 512], start=True, stop=True), [J, opz[1]], [psr])
                        if j % 2 == 0:
                            k.op("dve", lambda e: e.tensor_copy(rev.ap[:, j * 512:(j + 1) * 512], psr.ap[:]), [psr], [rev])
                        else:
                            k.op("act", lambda e: e.copy(rev.ap[:, j * 512:(j + 1) * 512], psr.ap[:]), [psr], [rev])
                    k.dma(None, S.OPS[1, tr:tr + 128, half * 1536:(half + 1) * 1536], rev.ap[:], [rev], [S.rOPS])
        k.barrier()


SCAN_MIND = int(os.environ.get('SCAN_MIND', '0'))
SCAN_CHAINS = int(os.environ.get('SCAN_CHAINS', '2'))


def scan_group(C, st, l, insts, T, ILO, s_init, s_final, chains):
    k, I, S = C.k, C.I, C.S
    IC = 64 // ILO
    OPB = [k.sb(st, "OPB%d" % i, [128, TC, 5, 64]) for i in range(2)]
    VB = [k.sb(st, "VB%d" % i, [128, TC, ILO]) for i in range(2)]
    CH = []
    for ci, (eng, i0, i1) in enumerate(chains):
        n = i1 - i0
        c = Ctx()
        c.eng, c.i0, c.i1, c.n = eng, i0, i1, n
        c.S = k.sb(st, "scanS%d" % ci, [128, n * 64]); c.tmp = k.sb(st, "scantmp%d" % ci, [128, n * 64]); c.sa = k.sb(st, "scansa%d" % ci, [128, n])
        c.Y = [k.sb(st, "Yc%d_%d" % (ci, i), [128, TC, n]) for i in range(2)]
        if s_init is not None:
            k.dma(None, c.S.ap[:], s_init[:, i0 * 64:i1 * 64], [], [c.S])
        else:
            k.op("pool", lambda e: e.memset(c.S.ap[:], 0.0), [], [c.S])
        c.S3 = c.S.ap[:].rearrange("p (i j) -> p i j", j=64)
        c.T3 = c.tmp.ap[:].rearrange("p (i j) -> p i j", j=64)
        c.shp = [128, n, 64]
        CH.append(c)
    groups = {}
    for (p0, z, h, s0) in insts:
        groups.setdefault((z, s0), []).append((p0, h))
    for ch in range(T // TC):
        opb = OPB[ch % 2]; vb = VB[ch % 2]
        for (p0, z, h, s0) in insts:
            base = (z * TOK + s0 + ch * TC) * 3072 + h * 384
            src = bass.AP(tensor=S.OPS.tensor, offset=base, ap=[[0, IC], [3072, TC], [1, 320]])
            k.dma(None, opb.ap[p0:p0 + IC, :, :, :].rearrange("p t a b -> p t (a b)"), src, [S.rOPS], [opb])
            srcv = bass.AP(tensor=S.OPS.tensor, offset=base + 320, ap=[[ILO, IC], [3072, TC], [1, ILO]])
            k.dma(None, vb.ap[p0:p0 + IC, :, :], srcv, [S.rOPS], [vb])
        for tc in range(TC):
            def row(c, kind):
                return bc(opb.ap[:, tc, kind, :].unsqueeze(1), c.shp)
            for step in range(9):
                for c in CH:
                    yc = c.Y[ch % 2]
                    if step == 0:
                        k.op(c.eng, lambda e: e.tensor_tensor(out=c.T3, in0=c.S3, in1=row(c, 0), op=ALU.mult), [c.S, opb], [c.tmp], mind=SCAN_MIND, inc=(SCAN_MIND == 0))
                    elif step == 1:
                        k.op(c.eng, lambda e: e.tensor_reduce(out=c.sa.ap[:], in_=c.T3, axis=AX.X, op=ALU.add), [c.tmp], [c.sa], mind=SCAN_MIND, inc=(SCAN_MIND == 0))
                    elif step == 2:
                        k.op(c.eng, lambda e: e.tensor_tensor(out=c.S3, in0=c.S3, in1=row(c, 1), op=ALU.mult), [c.S, opb], [c.S], mind=SCAN_MIND, inc=(SCAN_MIND == 0))
                    elif step == 3:
                        k.op(c.eng, lambda e: e.tensor_tensor(out=c.T3, in0=bc(c.sa.ap[:].unsqueeze(2), c.shp), in1=row(c, 2), op=ALU.mult), [c.sa, opb], [c.tmp], mind=SCAN_MIND, inc=(SCAN_MIND == 0))
                    elif step == 4:
                        k.op(c.eng, lambda e: e.tensor_tensor(out=c.S3, in0=c.S3, in1=c.T3, op=ALU.add), [c.S, c.tmp], [c.S], mind=SCAN_MIND, inc=(SCAN_MIND == 0))
                    elif step == 5:
                        k.op(c.eng, lambda e: e.tensor_tensor(out=c.T3, in0=bc(vb.ap[:, tc, c.i0:c.i1].unsqueeze(2), c.shp), in1=row(c, 3), op=ALU.mult), [vb, opb], [c.tmp], mind=SCAN_MIND, inc=(SCAN_MIND == 0))
                    elif step == 6:
                        k.op(c.eng, lambda e: e.tensor_tensor(out=c.S3, in0=c.S3, in1=c.T3, op=ALU.add), [c.S, c.tmp], [c.S], mind=SCAN_MIND, inc=(SCAN_MIND == 0))
                    elif step == 7:
                        k.op(c.eng, lambda e: e.tensor_tensor(out=c.T3, in0=c.S3, in1=row(c, 4), op=ALU.mult), [c.S, opb], [c.tmp], mind=SCAN_MIND, inc=(SCAN_MIND == 0))
                    else:
                        k.op(c.eng, lambda e: e.tensor_reduce(out=yc.ap[:, tc, :], in_=c.T3, axis=AX.X, op=ALU.add), [c.tmp], [yc], mind=SCAN_MIND)
        for (z, s0), lst in groups.items():
            pa = min(p for p, _ in lst)
            np_ = len(lst) * IC
            for c in CH:
                dst = bass.AP(tensor=S.YS.tensor, offset=(z * TOK + s0 + ch * TC) * 512 + c.i0, ap=[[ILO, np_], [512, TC], [1, c.n]])
                k.dma(None, dst, c.Y[ch % 2].ap[pa:pa + np_, :, :], [c.Y[ch % 2]], [S.rYS])
    if s_final is not None:
        for (dst, p0, n) in s_final:
            for c in CH:
                k.dma(None, dst[:, c.i0 * 64:c.i1 * 64], c.S.ap[p0:p0 + n, :], [c.S], [])


def phase_rwkv_scan(C, l):
    k, I, S, O = C.k, C.I, C.S, C.O
    with ExitStack() as st:
        insts = []
        for s in range(2):
            for z in range(2):
                for h in range(8):
                    insts.append((s * 64 + z * 32 + h * 4, z, h, s * 256))
        scan_group(C, st, l, insts, 256, 16, None, [(O.nst[s, l], s * 64, 64) for s in range(2)], ([("dve", 0, 16)] if SCAN_CHAINS == 1 else [("dve", 0, 8), ("dve", 8, 16)]))
        k.barrier()
    with ExitStack() as st:
        insts = []
        for z in range(2):
            for h in range(8):
                insts.append((z * 64 + h * 8, z, h, 512))
        scan_group(C, st, l, insts, 2048, 8, I.st0[l], None, ([("dve", 0, 8)] if SCAN_CHAINS == 1 else [("dve", 0, 4), ("dve", 4, 8)]))
        k.barrier()


def phase_rwkv_post(C, l):
    k, I, S = C.k, C.I, C.S
    with ExitStack() as st:
        lg = k.sb(st, "lnxg_b", [128, 512]); lb = k.sb(st, "lnxb_b", [128, 512]); J = k.sb(st, "J2", [128, 128])
        gne = k.sb(st, "gne", [128, 1])
        k.op("dve", lambda e: e.memset(gne.ap[:], 64e-5), [], [gne])
        k.dma(None, lg.ap[:], pbcast(I.rw_lnx_g, l, 512), [], [lg])
        k.dma(None, lb.ap[:], pbcast(I.rw_lnx_b, l, 512), [], [lb])
        k.dma(None, J.ap[:], I.antiid[:, :], [], [J])
        yf = [k.sb(st, "yf%d" % i, [128, 512]) for i in range(2)]
        yb = [k.sb(st, "yb%d" % i, [128, 512]) for i in range(2)]
        bg = [k.sb(st, "pbg%d" % i, [128, 1024]) for i in range(2)]
        t1 = k.sb(st, "pt1", [128, 512]); sm = k.sb(st, "psm", [128, 32]); yT = [k.sb(st, "pyT%d" % i, [128, 4, 128]) for i in range(2)]
        cnt = 0
        for (s0, T) in SEQS:
            nt = T // 128
            for ti in range(nt):
                par = cnt % 2; cnt += 1
                t0 = s0 + ti * 128
                tr = s0 + (nt - 1 - ti) * 128
                y = yf[par]; y2 = yb[par]; b = bg[par]
                k.dma(None, y.ap[:], S.YS[0, t0:t0 + 128, :], [S.rYS], [y])
                k.dma(None, y2.ap[:], S.YS[1, tr:tr + 128, :], [S.rYS], [y2])
                k.dma(None, b.ap[:], S.BG[t0:t0 + 128, :], [S.rBG], [b])
                ps = C.PS[par]
                k.op("pe", lambda e: e.matmul(ps.ap[:], lhsT=J.ap[:], rhs=y2.ap[:], start=True, stop=True), [J, y2], [ps])
                k.op("dve", lambda e: e.tensor_tensor(out=y.ap[:], in0=y.ap[:], in1=ps.ap[:], op=ALU.add), [y, ps], [y])
                k.op("dve", lambda e: e.tensor_reduce(out=sm.ap[:, 0:8], in_=v3(y.ap[:]), axis=AX.X, op=ALU.add), [y], [sm])
                k.op("dve", lambda e: e.tensor_scalar(sm.ap[:, 0:8], sm.ap[:, 0:8], 1.0 / 64, None, op0=ALU.mult), [sm], [sm])
                k.op("dve", lambda e: e.tensor_tensor(out=v3(y.ap[:]), in0=v3(y.ap[:]), in1=bc(sm.ap[:, 0:8].unsqueeze(2), [128, 8, 64]), op=ALU.subtract), [y, sm], [y])
                k.op("pool", lambda e: e.tensor_tensor(out=t1.ap[:], in0=y.ap[:], in1=y.ap[:], op=ALU.mult), [y], [t1])
                k.op("dve", lambda e: e.tensor_reduce(out=sm.ap[:, 8:16], in_=v3(t1.ap[:]), axis=AX.X, op=ALU.add), [t1], [sm])
                k.op("act", lambda e: e.activation(out=sm.ap[:, 8:16], in_=sm.ap[:, 8:16], func=AF.Sqrt, bias=gne.ap[:, 0:1], scale=1.0 / 64), [sm, gne], [sm])
                k.op("dve", lambda e: e.reciprocal(sm.ap[:, 16:24], sm.ap[:, 8:16]), [sm], [sm])
                k.op("dve", lambda e: e.tensor_tensor(out=v3(y.ap[:]), in0=v3(y.ap[:]), in1=bc(sm.ap[:, 16:24].unsqueeze(2), [128, 8, 64]), op=ALU.mult), [y, sm], [y])
                k.op("pool", lambda e: e.tensor_tensor(out=y.ap[:], in0=y.ap[:], in1=lg.ap[:], op=ALU.mult), [y, lg], [y])
                k.op("dve", lambda e: e.tensor_tensor(out=y.ap[:], in0=y.ap[:], in1=lb.ap[:], op=ALU.add), [y, lb], [y])
                k.op("dve", lambda e: e.tensor_tensor(out=y.ap[:], in0=y.ap[:], in1=b.ap[:, 0:512], op=ALU.add), [y, b], [y])
                k.op("dve", lambda e: e.tensor_tensor(out=y.ap[:], in0=y.ap[:], in1=b.ap[:, 512:1024], op=ALU.mult), [y, b], [y])
                pst = C.PS[2 + par]
                for c in range(4):
                    k.op("pe", lambda e: e.transpose(pst.ap[:, c * 128:(c + 1) * 128], y.ap[:, c * 128:(c + 1) * 128], C.ident.ap[:]), [y, C.ident], [pst])
                k.op("act", lambda e: e.copy(yT[par].ap[:], pst.ap[:].rearrange("p (c t) -> p c t", c=4)), [pst], [yT[par]])
                k.dma(None, S.YB[:, t0:t0 + 128].rearrange("(c p) t -> p c t", p=128), yT[par].ap[:], [yT[par]], [S.rYB])
        k.barrier()


def phase_rwkv(C, l):
    phase_rwkv_prep(C, l)
    phase_rwkv_scan(C, l)
    phase_rwkv_post(C, l)


def phase_merge(C, l):
    k, I, S = C.k, C.I, C.S
    outs = (I.a_out, I.rw_out, I.na_out)
    Ys = ((S.YA, S.rYA), (S.YB, S.rYB), (S.YC, S.rYC))
    with ExitStack() as st:
        yT = [k.sb(st, "myT%d" % b, [128, 4, 512]) for b in range(3)]
        mg = k.sb(st, "merged", [128, 8, 512])
        wo = [[k.sb(st, "mwo%d_%d" % (b, i), [128, 4, 128]) for i in range(2)] for b in range(3)]
        gt = [[k.sb(st, "mg%d_%d" % (b, i), [128, 512]) for i in range(2)] for b in range(3)]
        tmp = [k.sb(st, "mtmp%d" % i, [128, 512]) for i in range(2)]
        wob = [k.sb(st, "mwob%d" % i, [128, 8, 128]) for i in range(2)]
        cnt = 0
        for tg in range(5):
            g = 0 if tg == 0 else 1
            cols = slice(tg * 512, (tg + 1) * 512)
            for b in range(3):
                k.dma(None, yT[b].ap[:], Ys[b][0][:, cols].rearrange("(c p) t -> p c t", p=128), [Ys[b][1]], [yT[b]])
            for fc in range(8):
                par = cnt % 2; cnt += 1
                for b in range(3):
                    k.dma(None, wo[b][par].ap[:], outs[b][l][:, fc * 128:(fc + 1) * 128].rearrange("(c p) n -> p c n", p=128), [], [wo[b][par]])
                    r0 = 1664 + b * 1024 + fc * 128
                    k.dma(None, gt[b][par].ap[:], S.PF[r0:r0 + 128, cols], [S.rPF], [gt[b][par]])
                for b in range(3):
                    ps = C.PS[(fc * 3 + b) % 6]
                    for c in range(4):
                        k.op("pe", lambda e: e.matmul(ps.ap[:], lhsT=wo[b][par].ap[:, c, :], rhs=yT[b].ap[:, c, :], start=(c == 0), stop=(c == 3)), [wo[b][par], yT[b]], [ps])
                    if b == 0:
                        k.op("dve", lambda e: e.tensor_tensor(out=mg.ap[:, fc, :], in0=ps.ap[:], in1=gt[b][par].ap[:], op=ALU.mult), [ps, gt[b][par]], [mg])
                    else:
                        t = tmp[b % 2]
                        k.op("dve", lambda e: e.tensor_tensor(out=t.ap[:], in0=ps.ap[:], in1=gt[b][par].ap[:], op=ALU.mult), [ps, gt[b][par]], [t])
                        k.op("pool", lambda e: e.tensor_tensor(out=mg.ap[:, fc, :], in0=mg.ap[:, fc, :], in1=t.ap[:], op=ALU.add), [mg, t], [mg])
            for fc2 in range(8):
                w = wob[fc2 % 2]
                k.dma(None, w.ap[:], I.w_o[l][:, fc2 * 128:(fc2 + 1) * 128].rearrange("(c p) n -> p c n", p=128), [], [w])
                ps = C.PS[6 + fc2 % 2]
                for kc in range(8):
                    k.op("pe", lambda e: e.matmul(ps.ap[:], lhsT=w.ap[:, kc, :], rhs=mg.ap[:, kc, :], start=(kc == 0), stop=(kc == 7)), [w, mg], [ps])
                k.op("dve", lambda e: e.scalar_tensor_tensor(out=C.xT.ap[:, fc2, cols], in0=ps.ap[:], scalar=C.modT.ap[:, 16 + fc2, g:g + 1], in1=C.xT.ap[:, fc2, cols], op0=ALU.mult, op1=ALU.add), [ps, C.modT, C.xT], [C.xT])
        k.barrier()


PEER_STOP = int(os.environ.get('PEER_STOP', '0'))
NBUF = 7
NACC = 2


def phase_peer(C, l):
    k, I, S = C.k, C.I, C.S
    with ExitStack() as st:
        skn = k.sb(st, "skn", [128, 16, 128]); skT = k.sb(st, "skT", [128, 16, 128])
        k.dma(None, skn.ap[:], I.pe_sk[l].rearrange("c n k -> n c k"), [], [skn])
        for c in range(16):
            ps = C.PS[c % 4]
            k.op("pe", lambda e: e.transpose(ps.ap[:, 0:128], skn.ap[:, c, :], C.ident.ap[:]), [skn, C.ident], [ps])
            k.op("dve", lambda e: e.tensor_copy(skT.ap[:, c, :], ps.ap[:, 0:128]), [ps], [skT])
        h2T = k.sb(st, "h2T", [128, 8, 128])
        h2s = [k.sb(st, "h2tok%d" % i, [128, D]) for i in range(2)]
        sq = [k.sb(st, "psq%d" % i, [128, 128]) for i in range(2)]; rstd = k.sb(st, "prstd", [128, 128])
        wq = [k.sb(st, "wq%d" % i, [128, 8, 128]) for i in range(2)]
        qT = k.sb(st, "qT", [128, 16, 128])
        Sc = [k.sb(st, "Sc%d" % c, [128, 128]) for c in range(16)]
        vals = [k.sb(st, "vals%d" % c, [128, 16]) for c in range(16)]
        idxu = [k.sb(st, "idxu%d" % c, [128, 16], U32) for c in range(16)]
        idxf = [k.sb(st, "idxf%d" % c, [128, 16]) for c in range(16)]
        cand = [k.sb(st, "cand%d" % h, [128, 256]) for h in range(8)]
        cand2 = [k.sb(st, "candb%d" % h, [128, 256]) for h in range(8)]
        cande = [k.sb(st, "cande%d" % h, [128, 256]) for h in range(8)]
        sv = [k.sb(st, "sv%d" % h, [128, 16]) for h in range(8)]
        gates = [k.sb(st, "gate%d" % i, [128, 8, 16]) for i in range(2)]; sm = k.sb(st, "pesm", [128, 32])
        junk = k.sb(st, "pjunk", [128, 256]); ef = k.sb(st, "ef", [128, 128])
        eis = [k.sb(st, "ei%d" % i, [128, 128], I32) for i in range(2)]
        dots = k.sb(st, "dots", [128, 128]); tg_ = k.sb(st, "pt_g", [128, 128]); wgt = k.sb(st, "wgt", [128, 128])
        UV = [k.sb(st, "UV%d" % i, [128, D]) for i in range(NBUF)]
        accs = [k.sb(st, "pacc%d" % i, [128, D]) for i in range(NACC)]
        shp3 = [128, 16, 16]
        if l == 0:
            print("PEER sbuf remaining", C.nc.sbuf_bytes_remaining)

        def stageA(tile):
            buf = tile % 2
            h2 = h2s[buf]; gate = gates[buf]; ei = eis[buf]
            g = 0 if tile < 4 else 1
            cols = slice(tile * 128, tile * 128 + 128)
            psn = C.PS[7]
            for kc in range(8):
                s_ = sq[kc % 2]
                k.op("act", lambda e: e.activation(out=s_.ap[:], in_=C.xT.ap[:, kc, cols], func=AF.Square), [C.xT], [s_])
                k.op("pe", lambda e: e.matmul(psn.ap[:, 0:128], lhsT=C.ones.ap[:], rhs=s_.ap[:], start=(kc == 0), stop=(kc == 7)), [s_, C.ones], [psn])
                yield
            k.op("act", lambda e: e.activation(out=rstd.ap[:], in_=psn.ap[:, 0:128], func=AF.Sqrt, scale=1.0 / D, bias=C.eps.ap[:, 0:1]), [psn, C.eps], [rstd])
            k.op("dve", lambda e: e.reciprocal(rstd.ap[:], rstd.ap[:]), [rstd], [rstd])
            yield
            for kc in range(8):
                k.op("dve", lambda e: e.tensor_tensor(out=h2T.ap[:, kc, :], in0=C.xT.ap[:, kc, cols], in1=rstd.ap[:], op=ALU.mult), [C.xT, rstd], [h2T])
                yield
            for kc in range(8):
                k.op("dve", lambda e: e.tensor_scalar(h2T.ap[:, kc, :], h2T.ap[:, kc, :], C.gs2.ap[:, kc, g:g + 1], C.modT.ap[:, 24 + kc, g:g + 1], op0=ALU.mult, op1=ALU.add), [h2T, C.gs2, C.modT], [h2T])
                yield
            for half in range(2):
                ps = C.PS[half]
                for j in range(4):
                    kc = half * 4 + j
                    k.op("pe", lambda e: e.transpose(ps.ap[:, j * 128:(j + 1) * 128], h2T.ap[:, kc, :], C.ident.ap[:]), [h2T, C.ident], [ps])
                k.op("act", lambda e: e.copy(h2.ap[:, half * 512:(half + 1) * 512], ps.ap[:]), [ps], [], ww=[h2])
                yield
            for c in range(16):
                w = wq[c % 2]
                k.dma(None, w.ap[:], I.pe_q[l][:, c * 128:(c + 1) * 128].rearrange("(c p) n -> p c n", p=128), [], [w])
                ps = C.PS[2 + c % 2]
                for kc in range(8):
                    k.op("pe", lambda e: e.matmul(ps.ap[:, 0:128], lhsT=w.ap[:, kc, :], rhs=h2T.ap[:, kc, :], start=(kc == 0), stop=(kc == 7)), [w, h2T], [ps])
                k.op("act", lambda e: e.copy(qT.ap[:, c, :], ps.ap[:, 0:128]), [ps], [], ww=[qT])
                yield
            for q4 in range(4):
                ps = C.PS[4 + q4 % 2]
                for j in range(4):
                    c = q4 * 4 + j
                    k.op("pe", lambda e: e.matmul(ps.ap[:, j * 128:(j + 1) * 128], lhsT=qT.ap[:, c, :], rhs=skT.ap[:, c, :], start=True, stop=True), [qT, skT], [ps])
                for j in range(4):
                    c = q4 * 4 + j
                    if q4 % 2 == 0:
                        k.op("dve", lambda e: e.tensor_copy(Sc[c].ap[:], ps.ap[:, j * 128:(j + 1) * 128]), [ps], [Sc[c]])
                    else:
                        k.op("act", lambda e: e.copy(Sc[c].ap[:], ps.ap[:, j * 128:(j + 1) * 128]), [ps], [Sc[c]])
                    yield
            for c in range(16):
                k.op("dve", lambda e: e.max(out=vals[c].ap[:, 0:8], in_=Sc[c].ap[:]), [Sc[c]], [vals[c]])
                yield
            for c in range(16):
                k.op("dve", lambda e: e.max_index(out=idxu[c].ap[:, 0:8], in_max=vals[c].ap[:, 0:8], in_values=Sc[c].ap[:]), [Sc[c], vals[c]], [idxu[c]])
                yield
            for c in range(16):
                k.op("dve", lambda e: e.match_replace(out=Sc[c].ap[:], in_to_replace=vals[c].ap[:, 0:8], in_values=Sc[c].ap[:], imm_value=-1e30), [Sc[c], vals[c]], [Sc[c]])
                yield
            for c in range(16):
                k.op("dve", lambda e: e.max(out=vals[c].ap[:, 8:16], in_=Sc[c].ap[:]), [Sc[c]], [vals[c]])
                yield
            for c in range(16):
                k.op("dve", lambda e: e.max_index(out=idxu[c].ap[:, 8:16], in_max=vals[c].ap[:, 8:16], in_values=Sc[c].ap[:]), [Sc[c], vals[c]], [idxu[c]])
                yield
            for c in range(16):
                k.op("dve", lambda e: e.tensor_copy(idxf[c].ap[:], idxu[c].ap[:]), [idxu[c]], [idxf[c]])
                yield
            for h in range(8):
                c4 = cand[h].ap[:].rearrange("p (a b) -> p a b", a=16)
                k.op("dve", lambda e: e.tensor_tensor(out=c4, in0=bc(vals[2 * h].ap[:].unsqueeze(2), shp3), in1=bc(vals[2 * h + 1].ap[:].unsqueeze(1), shp3), op=ALU.add), [vals[2 * h], vals[2 * h + 1]], [cand[h]])
                yield
            for h in range(8):
                ce4 = cande[h].ap[:].rearrange("p (a b) -> p a b", a=16)
                k.op("dve", lambda e: e.scalar_tensor_tensor(out=ce4, in0=bc(idxf[2 * h].ap[:].unsqueeze(2), shp3), scalar=128.0, in1=bc(idxf[2 * h + 1].ap[:].unsqueeze(1), shp3), op0=ALU.mult, op1=ALU.add), [idxf[2 * h], idxf[2 * h + 1]], [cande[h]])
                yield
            for h in range(8):
                k.op("dve", lambda e: e.max(out=sv[h].ap[:, 0:8], in_=cand[h].ap[:]), [cand[h]], [sv[h]])
                yield
            for h in range(8):
                k.op("dve", lambda e: e.match_replace(out=cand2[h].ap[:], in_to_replace=sv[h].ap[:, 0:8], in_values=cand[h].ap[:], imm_value=-1e30), [cand[h], sv[h]], [cand2[h]])
                yield
            for h in range(8):
                k.op("dve", lambda e: e.max(out=sv[h].ap[:, 8:16], in_=cand2[h].ap[:]), [cand2[h]], [sv[h]])
                yield
            for h in range(8):
                k.op("dve", lambda e: e.tensor_scalar(gate.ap[:, h, :], sv[h].ap[:], sv[h].ap[:, 0:1], None, op0=ALU.subtract), [sv[h]], [], ww=[gate])
                yield
            k.op("act", lambda e: e.activation(out=gate.ap[:], in_=gate.ap[:], func=AF.Exp), [gate], [gate])
            k.op("dve", lambda e: e.tensor_reduce(out=sm.ap[:, 0:8], in_=gate.ap[:], axis=AX.X, op=ALU.add), [gate], [sm])
            yield
            k.op("dve", lambda e: e.reciprocal(sm.ap[:, 8:16], sm.ap[:, 0:8]), [sm], [sm])
            yield
            k.op("dve", lambda e: e.tensor_tensor(out=gate.ap[:], in0=gate.ap[:], in1=bc(sm.ap[:, 8:16].unsqueeze(2), [128, 8, 16]), op=ALU.mult), [gate, sm], [gate])
            yield
            k.op("pool", lambda e: e.memset(ef.ap[:], 0.0), [], [ef])
            for h in range(8):
                for kk2 in range(16):
                    j = h * 16 + kk2
                    k.op("dve", lambda e: e.scalar_tensor_tensor(out=junk.ap[:], in0=cand[h].ap[:], scalar=sv[h].ap[:, kk2:kk2 + 1], in1=cande[h].ap[:], op0=ALU.is_equal, op1=ALU.mult, accum_out=ef.ap[:, j:j + 1]), [cand[h], cande[h], sv[h]], [ef], ww=[junk])
                    yield
            k.op("dve", lambda e: e.tensor_scalar(ef.ap[:], ef.ap[:], 0.0, 16383.0, op0=ALU.max, op1=ALU.min), [ef], [ef])
            yield
            k.op("dve", lambda e: e.tensor_scalar(ef.ap[:], ef.ap[:], float(l * 16384), None, op0=ALU.add), [ef], [ef])
            yield
            k.op("dve", lambda e: e.tensor_copy(ei.ap[:], ef.ap[:]), [ef], [ei])
            yield

        def stageB(tile):
            buf = tile % 2
            h2 = h2s[buf]; gate = gates[buf]; ei = eis[buf]
            g = 0 if tile < 4 else 1
            cols = slice(tile * 128, tile * 128 + 128)
            k.op("pool", lambda e: e.memset(dots.ap[:], 0.0), [], [dots])
            for j in range(128):
                u = UV[j % NBUF]
                k.idma(u.ap[:], I.pe_u[:, :], ei.ap[:, j:j + 1], [ei], [u])
                k.op("dve", lambda e: e.scalar_tensor_tensor(out=u.ap[:], in0=u.ap[:], scalar=1.0, in1=h2.ap[:], op0=ALU.mult, op1=ALU.mult, accum_out=dots.ap[:, j:j + 1]), [h2], [dots, u])
                yield
            k.op("dve", lambda e: e.tensor_tensor(out=tg_.ap[:], in0=dots.ap[:], in1=dots.ap[:], op=ALU.mult), [dots], [tg_])
            yield
            k.op("dve", lambda e: e.tensor_tensor(out=tg_.ap[:], in0=tg_.ap[:], in1=dots.ap[:], op=ALU.mult), [tg_, dots], [tg_])
            yield
            k.op("dve", lambda e: e.scalar_tensor_tensor(out=tg_.ap[:], in0=tg_.ap[:], scalar=0.044715, in1=dots.ap[:], op0=ALU.mult, op1=ALU.add), [tg_, dots], [tg_])
            yield
            k.op("act", lambda e: e.activation(out=tg_.ap[:], in_=tg_.ap[:], func=AF.Tanh, scale=0.7978845608028654), [tg_], [tg_])
            k.op("dve", lambda e: e.tensor_scalar(tg_.ap[:], tg_.ap[:], 1.0, 0.5, op0=ALU.add, op1=ALU.mult), [tg_], [tg_])
            yield
            k.op("dve", lambda e: e.tensor_tensor(out=tg_.ap[:], in0=tg_.ap[:], in1=dots.ap[:], op=ALU.mult), [tg_, dots], [tg_])
            yield
            k.op("dve", lambda e: e.tensor_tensor(out=wgt.ap[:], in0=tg_.ap[:], in1=gate.ap[:].rearrange("p h a -> p (h a)"), op=ALU.mult), [tg_, gate], [wgt])
            yield
            for a_ in accs:
                k.op("pool", lambda e: e.memset(a_.ap[:], 0.0), [], [a_])
            for j in range(128):
                v = UV[j % NBUF]
                acc = accs[j % NACC]
                k.idma(v.ap[:], I.pe_v[:, :], ei.ap[:, j:j + 1], [ei], [v])
                k.op("dve", lambda e: e.scalar_tensor_tensor(out=acc.ap[:], in0=v.ap[:], scalar=wgt.ap[:, j:j + 1], in1=acc.ap[:], op0=ALU.mult, op1=ALU.add), [v, wgt, acc], [acc])
                yield
            for ai in range(1, NACC):
                k.op("dve", lambda e: e.tensor_tensor(out=accs[0].ap[:], in0=accs[0].ap[:], in1=accs[ai].ap[:], op=ALU.add), [accs[0], accs[ai]], [accs[0]])
                yield
            acc = accs[0]
            ps = C.PS[6]
            for half in range(2):
                for j in range(4):
                    kc = half * 4 + j
                    k.op("pe", lambda e: e.transpose(ps.ap[:, j * 128:(j + 1) * 128], acc.ap[:, kc * 128:(kc + 1) * 128], C.ident.ap[:]), [acc, C.ident], [ps])
                for j in range(4):
                    kc = half * 4 + j
                    k.op("dve", lambda e: e.scalar_tensor_tensor(out=C.xT.ap[:, kc, cols], in0=ps.ap[:, j * 128:(j + 1) * 128], scalar=C.modT.ap[:, 40 + kc, g:g + 1], in1=C.xT.ap[:, kc, cols], op0=ALU.mult, op1=ALU.add), [ps, C.modT, C.xT], [], ww=[C.xT])
                    yield

        ntiles = TOK // 128
        for _ in stageA(0):
            pass
        for t in range(ntiles):
            gb = stageB(t)
            ga = stageA(t + 1) if t + 1 < ntiles else iter(())
            a_live, b_live = True, True
            while a_live or b_live:
                if b_live:
                    try:
                        next(gb)
                    except StopIteration:
                        b_live = False
                if a_live:
                    try:
                        next(ga)
                    except StopIteration:
                        a_live = False
        k.barrier()


def phase_final(C):
    k, I, O = C.k, C.I, C.O
    with ExitStack() as st:
        lnfT = k.sb(st, "lnfT", [128, 8])
        k.dma("sp", lnfT.ap[:], I.lnf.rearrange("(c p) -> p c", p=128), [], [lnfT], allow_slow_non_contiguous=True)
        sq = [k.sb(st, "fsq%d" % i, [128, 512]) for i in range(2)]
        rstd = k.sb(st, "frstd", [128, 512])
        hT = k.sb(st, "fhT", [128, 8, 512])
        yt = [k.sb(st, "fy%d" % i, [128, D]) for i in range(2)]
        for tg in range(5):
            cols = slice(tg * 512, (tg + 1) * 512)
            ps = C.PS[7]
            for kc in range(8):
                s = sq[kc % 2]
                k.op("act", lambda e: e.activation(out=s.ap[:], in_=C.xT.ap[:, kc, cols], func=AF.Square), [C.xT], [s])
                k.op("pe", lambda e: e.matmul(ps.ap[:], lhsT=C.ones.ap[:], rhs=s.ap[:], start=(kc == 0), stop=(kc == 7)), [s, C.ones], [ps])
            k.op("act", lambda e: e.activation(out=rstd.ap[:], in_=ps.ap[:], func=AF.Sqrt, scale=1.0 / D, bias=C.eps.ap[:, 0:1]), [ps, C.eps], [rstd])
            k.op("dve", lambda e: e.reciprocal(rstd.ap[:], rstd.ap[:]), [rstd], [rstd])
            for kc in range(8):
                k.op("dve", lambda e: e.scalar_tensor_tensor(out=hT.ap[:, kc, :], in0=C.xT.ap[:, kc, cols], scalar=lnfT.ap[:, kc:kc + 1], in1=rstd.ap[:], op0=ALU.mult, op1=ALU.mult), [C.xT, rstd, lnfT], [hT])
            for tt in range(4):
                y = yt[tt % 2]
                for half in range(2):
                    ps2 = C.PS[(tt * 2 + half) % 6]
                    for j in range(4):
                        kc = half * 4 + j
                        k.op("pe", lambda e: e.transpose(ps2.ap[:, j * 128:(j + 1) * 128], hT.ap[:, kc, tt * 128:(tt + 1) * 128], C.ident.ap[:]), [hT, C.ident], [ps2])
                    if half == 0:
                        k.op("dve", lambda e: e.tensor_copy(y.ap[:, 0:512], ps2.ap[:]), [ps2], [y])
                    else:
                        k.op("act", lambda e: e.copy(y.ap[:, 512:1024], ps2.ap[:]), [ps2], [y])
                t0 = tg * 512 + tt * 128
                k.dma(None, O.y[t0:t0 + 128, :], y.ap[:], [y], [])
        k.barrier()


def make_consts():
    c = {}
    c["c_ident"] = np.eye(128, dtype=np.float32)
    c["c_antiid"] = np.ascontiguousarray(np.eye(128, dtype=np.float32)[::-1])
    R = np.zeros((128, 128), np.float32)
    cos = np.zeros((128, TS), np.float32)
    sin = np.zeros((128, TS), np.float32)
    t = np.arange(TS)
    pos = np.stack([t // 64, t % 64], 0).astype(np.float32)
    freqs = (10000.0 ** (-np.arange(16, dtype=np.float32) / 16)).astype(np.float32)
    for hh in range(2):
        for ax in range(2):
            for f in range(16):
                d1 = hh * 64 + ax * 32 + f
                d2 = d1 + 16
                ang = (pos[ax] * freqs[f]).astype(np.float32)
                cos[d1] = np.cos(ang); cos[d2] = np.cos(ang)
                sin[d1] = np.sin(ang); sin[d2] = np.sin(ang)
                R[d2, d1] = -1.0
                R[d1, d2] = 1.0
    c["c_rot"] = R
    c["c_cos"] = cos
    c["c_sin"] = sin
    r = np.arange(128)[:, None]
    cc = np.arange(128)[None, :]
    c["c_mleft"] = np.where(cc >= r, 0.0, NEG).astype(np.float32)
    c["c_mright"] = np.where(cc <= r, 0.0, NEG).astype(np.float32)
    col = np.arange(64)
    cs = np.clip(col - 8, 0, 48)
    ok = (col[None, :] >= cs[:, None]) & (col[None, :] < cs[:, None] + 16)
    c["c_colmask"] = np.where(ok, 0.0, NEG).astype(np.float32)
    b = np.zeros((128, 128), np.float32)
    b[:64, :64] = 1.0
    b[64:, 64:] = 1.0
    c["c_blk64"] = b
    c["c_iota"] = np.tile(np.arange(256, dtype=np.float32)[None, :], (128, 1))
    return c


def make_in_maps(inp):
    f = lambda a: np.ascontiguousarray(np.asarray(a), dtype=np.float32)
    consts = make_consts()
    shared = {
        "ln1_g": f(inp["ln1_g"]), "ln2_g": f(inp["ln2_g"]), "lnf_g": f(inp["lnf_g"]),
        "ada_w": f(inp["ada_w"]), "ada_b": f(inp["ada_b"]), "w_in": f(inp["w_in"]),
        "a_sink": f(inp["a_sink"]), "a_out": f(inp["a_out"]), "rw_mu": f(inp["rw_mu"]),
        "rw_w0": f(inp["rw_w0"]).reshape(L, 1024), "rw_w2": f(inp["rw_w2"]).reshape(L, 128, 512),
        "rw_a0": f(inp["rw_a0"]).reshape(L, 1024), "rw_a2": f(inp["rw_a2"]).reshape(L, 128, 512),
        "rw_g2": f(inp["rw_g2"]), "rw_kk": f(inp["rw_kk"]), "rw_ka": f(inp["rw_ka"]),
        "rw_rk": f(inp["rw_rk"]).reshape(L, 512), "rw_lnx_g": f(inp["rw_lnx_g"]), "rw_lnx_b": f(inp["rw_lnx_b"]),
        "rw_out": f(inp["rw_out"]), "na_rpb": f(inp["na_rpb"]), "na_out": f(inp["na_out"]), "w_o": f(inp["w_o"]),
        "pe_q": f(inp["pe_q"]), "pe_subkeys": f(inp["pe_subkeys"]).reshape(L, 16, 128, 128),
        "pe_u": f(inp["pe_u"]).reshape(L * 16384, D), "pe_v": f(inp["pe_v"]).reshape(L * 16384, D),
    }
    shared.update(consts)
    xp = f(inp["x_prompt"]); xs = f(inp["x_sample"])
    maps = []
    for c in range(8):
        b = c // 4
        m = dict(shared)
        m["xall"] = np.concatenate([xp[2 * c], xp[2 * c + 1], xs[b]], 0)
        m["cvec"] = np.stack([f(inp["c_ctx"]), f(inp["c"])[b]], 0)
        m["cak"] = f(inp["cache_a_k"])[b].reshape(L, 512, 128)
        m["cav"] = f(inp["cache_a_v"])[b].reshape(L, 512, 128)
        m["cck"] = f(inp["cache_c_k"])[b].reshape(L, 512, 512)
        m["ccv"] = f(inp["cache_c_v"])[b].reshape(L, 512, 512)
        m["st0"] = f(inp["state_rwkv"])[b].reshape(L, 128, 512)
        maps.append(m)
    return maps


_CACHE = {}


def kernel(**inputs):
    if "nc" not in _CACHE:
        _CACHE["nc"] = build()[0]
    nc = _CACHE["nc"]
    maps = make_in_maps(inputs)
    res = run_bass_kernel_spmd(nc, maps, core_ids=list(range(8)))
    R = res.results
    y_prompt = np.zeros((16, 256, D), np.float32)
    y_sample = np.zeros((2, TS, D), np.float32)
    nak = np.zeros((16, L, 256, 2, 64), np.float32)
    nav = np.zeros((16, L, 256, 2, 64), np.float32)
    nck = np.zeros((16, L, 256, 8, 64), np.float32)
    ncv = np.zeros((16, L, 256, 8, 64), np.float32)
    nst = np.zeros((16, L, 2, 8, 64, 64), np.float32)
    for c in range(8):
        r = R[c]
        y = np.asarray(r["y"])
        y_prompt[2 * c] = y[0:256]
        y_prompt[2 * c + 1] = y[256:512]
        if c % 4 == 0:
            y_sample[c // 4] = y[512:]
        for s in range(2):
            nak[2 * c + s] = np.asarray(r["nak"])[s].reshape(L, 256, 2, 64)
            nav[2 * c + s] = np.asarray(r["nav"])[s].reshape(L, 256, 2, 64)
            nck[2 * c + s] = np.asarray(r["nck"])[s].reshape(L, 256, 8, 64)
            ncv[2 * c + s] = np.asarray(r["ncv"])[s].reshape(L, 256, 8, 64)
            nst[2 * c + s] = np.asarray(r["nst"])[s].reshape(L, 2, 8, 64, 64)
    return (y_prompt, y_sample, nak, nav, nck, ncv, nst)
```

```python
import os
import numpy as np
from contextlib import ExitStack
import concourse.bass as bass
import concourse.mybir as mybir
from concourse.bass_utils import run_bass_kernel_spmd

F32 = mybir.dt.float32
I32 = mybir.dt.int32
U32 = mybir.dt.uint32
AF = mybir.ActivationFunctionType
ALU = mybir.AluOpType
AX = mybir.AxisListType

NDS = 40
SAME_ENGINE_SYNC = {"pe": False, "dve": True, "act": True, "pool": True, "sp": True}

D = 1024
L = 4
TOK = 2560
NPT = 512
TS = 2048
IN_COLS = 7296
PT_COLS = 3200
PF_ROWS = 4736
SCALE = 0.125
NEG = -30000.0


class Res:
    __slots__ = ("w", "r", "ap", "name")

    def __init__(self, ap=None, name=None):
        self.w = None
        self.r = []
        self.ap = ap
        self.name = name


class KB:
    def __init__(self, nc):
        self.nc = nc
        self.eng = {"pe": nc.tensor, "dve": nc.vector, "act": nc.scalar, "pool": nc.gpsimd, "sp": nc.sync}
        self.esem = {k: nc.alloc_semaphore("es_" + k) for k in self.eng}
        self.ecnt = {k: 0 for k in self.eng}
        self.seen = {k: {} for k in self.eng}
        self.dsems = [nc.alloc_semaphore("ds%d" % i) for i in range(NDS)]
        self.dcnt = [0] * NDS
        self.dnext = 0
        self.nins = 0
        self.uid = 0
        self.rr = 0

    def sb(self, st, name, shape, dt=F32):
        self.uid += 1
        t = st.enter_context(self.nc.sbuf_tensor("%s_%d" % (name, self.uid), list(shape), dt))
        return Res(t.ap(), name)

    def ps(self, name, shape, dt=F32):
        t = self.nc.alloc_psum_tensor(name, list(shape), dt)
        return Res(t.ap(), name)

    def _wait(self, e, ev, mind=0):
        sem, val = ev
        own = sem is self.esem[e]
        if own and not SAME_ENGINE_SYNC[e]:
            return
        if own and mind > 0:
            return
        key = id(sem)
        if self.seen[e].get(key, 0) >= val:
            return
        self.eng[e].wait_ge(sem, val)
        self.seen[e][key] = val
        self.nins += 1

    def deps(self, e, reads, writes, ww=(), mind=0):
        for r in reads:
            if r.w is not None:
                self._wait(e, r.w, mind)
        for w in writes:
            if w.w is not None:
                self._wait(e, w.w, mind)
            for ev in w.r:
                self._wait(e, ev, mind)
        for w in ww:
            if w.w is not None and w.w[0] is not self.esem[e]:
                self._wait(e, w.w)
            for ev in w.r:
                self._wait(e, ev)

    def _record(self, ev, reads, writes):
        for r in reads:
            r.r.append(ev)
            if len(r.r) > 16:
                d = {}
                for s, v in r.r:
                    if id(s) not in d or d[id(s)][1] < v:
                        d[id(s)] = (s, v)
                r.r = list(d.values())
        for w in writes:
            w.w = ev
            w.r = []

    def op(self, e, fn, reads, writes, ww=(), mind=0, inc=True):
        self.deps(e, reads, writes, ww, mind)
        ins = fn(self.eng[e])
        self.nins += 1
        if inc:
            self.ecnt[e] += 1
            ins.then_inc(self.esem[e], 1)
            ev = (self.esem[e], self.ecnt[e])
        else:
            ev = (self.esem[e], self.ecnt[e] + 1)
        self._record(ev, reads, list(writes) + list(ww))

    def _dma_common(self, q, reads, writes, emit):
        kk = self.dnext
        self.dnext = (kk + 1) % NDS
        sem = self.dsems[kk]
        if self.dcnt[kk] > 0:
            self._wait(q, (sem, self.dcnt[kk]))
        self.deps(q, reads, writes)
        ins = emit()
        self.dcnt[kk] += 16
        ins.then_inc(sem, 16)
        self.nins += 1
        self._record((sem, self.dcnt[kk]), reads, writes)

    def dma(self, q, out, in_, reads, writes, **kw):
        if q is None:
            q = ("sp", "act")[self.rr % 2]
            self.rr += 1
        self._dma_common(q, reads, writes, lambda: self.eng[q].dma_start(out=out, in_=in_, **kw))

    def idma(self, out, in_, off_ap, reads, writes):
        self._dma_common("pool", reads, writes, lambda: self.nc.gpsimd.indirect_dma_start(
            out=out, out_offset=None, in_=in_,
            in_offset=bass.IndirectOffsetOnAxis(ap=off_ap, axis=0)))

    def barrier(self):
        engs = ("pe", "dve", "act", "pool", "sp")
        for e in engs:
            for kk in range(NDS):
                if self.dcnt[kk] > 0:
                    self._wait(e, (self.dsems[kk], self.dcnt[kk]))
            for e2 in engs:
                if e2 != e and self.ecnt[e2] > 0:
                    self._wait(e, (self.esem[e2], self.ecnt[e2]))

    def finish(self):
        self.barrier()


class Ctx:
    pass


def bc(ap, shape):
    return ap.to_broadcast(list(shape))


def build(nlayers=L, debug=False, stages=("all",)):
    nc = bass.Bass("TRN2", target_bir_lowering=False)
    k = KB(nc)
    C = Ctx()
    C.k = k
    C.nc = nc
    C.debug = debug

    def din(name, shape, dt=F32):
        return nc.dram_tensor(name, list(shape), dt, kind="ExternalInput").ap()

    def dout(name, shape, dt=F32):
        return nc.dram_tensor(name, list(shape), dt, kind="ExternalOutput").ap()

    def scr(name, shape, dt=F32):
        if debug:
            return nc.dram_tensor(name, list(shape), dt, kind="ExternalOutput").ap()
        return nc.dram_tensor(name, list(shape), dt).ap()

    I = Ctx()
    C.I = I
    I.xall = din("xall", [TOK, D])
    I.cvec = din("cvec", [2, D])
    I.cak = din("cak", [L, 512, 128])
    I.cav = din("cav", [L, 512, 128])
    I.cck = din("cck", [L, 512, 512])
    I.ccv = din("ccv", [L, 512, 512])
    I.st0 = din("st0", [L, 128, 512])
    I.ln1 = din("ln1_g", [L, D])
    I.ln2 = din("ln2_g", [L, D])
    I.lnf = din("lnf_g", [D])
    I.ada_w = din("ada_w", [L, D, 6 * D])
    I.ada_b = din("ada_b", [L, 6 * D])
    I.w_in = din("w_in", [L, D, IN_COLS])
    I.a_sink = din("a_sink", [L, 8])
    I.a_out = din("a_out", [L, 512, D])
    I.rw_mu = din("rw_mu", [L, 1920])
    I.rw_w0 = din("rw_w0", [L, 1024])
    I.rw_w2 = din("rw_w2", [L, 128, 512])
    I.rw_a0 = din("rw_a0", [L, 1024])
    I.rw_a2 = din("rw_a2", [L, 128, 512])
    I.rw_g2 = din("rw_g2", [L, 128, 512])
    I.rw_kk = din("rw_kk", [L, 512])
    I.rw_ka = din("rw_ka", [L, 512])
    I.rw_rk = din("rw_rk", [L, 512])
    I.rw_lnx_g = din("rw_lnx_g", [L, 512])
    I.rw_lnx_b = din("rw_lnx_b", [L, 512])
    I.rw_out = din("rw_out", [L, 512, D])
    I.na_rpb = din("na_rpb", [L, 8, 15, 31])
    I.na_out = din("na_out", [L, 512, D])
    I.w_o = din("w_o", [L, D, D])
    I.pe_q = din("pe_q", [L, D, 2048])
    I.pe_sk = din("pe_subkeys", [L, 16, 128, 128])
    I.pe_u = din("pe_u", [L * 16384, D])
    I.pe_v = din("pe_v", [L * 16384, D])
    I.ident = din("c_ident", [128, 128])
    I.antiid = din("c_antiid", [128, 128])
    I.rot = din("c_rot", [128, 128])
    I.cos = din("c_cos", [128, TS])
    I.sin = din("c_sin", [128, TS])
    I.mleft = din("c_mleft", [128, 128])
    I.mright = din("c_mright", [128, 128])
    I.colmask = din("c_colmask", [64, 64])
    I.blk64 = din("c_blk64", [128, 128])
    I.iota256 = din("c_iota", [128, 256])

    O = Ctx()
    C.O = O
    O.y = dout("y", [TOK, D])
    O.nak = dout("nak", [2, L, 256, 128])
    O.nav = dout("nav", [2, L, 256, 128])
    O.nck = dout("nck", [2, L, 256, 512])
    O.ncv = dout("ncv", [2, L, 256, 512])
    O.nst = dout("nst", [2, L, 64, 1024])

    S = Ctx()
    C.S = S
    S.PT = scr("s_PT", [TOK, PT_COLS])
    S.PF = scr("s_PF", [PF_ROWS, TOK])
    S.YA = scr("s_YA", [512, TOK])
    S.YB = scr("s_YB", [512, TOK])
    S.YC = scr("s_YC", [512, TOK])
    S.QR = scr("s_QR", [640, TS])
    S.E = scr("s_E", [1, 120, 127])
    S.OPS = scr("s_OPS", [2, TOK, 3072])
    S.YS = scr("s_YS", [2, TOK, 512])
    S.BG = scr("s_BG", [TOK, 1024])
    S.rOPS = Res(); S.rYS = Res(); S.rBG = Res()
    S.rPT = Res(); S.rPF = Res(); S.rYA = Res(); S.rYB = Res(); S.rYC = Res(); S.rQR = Res(); S.rE = Res()

    C.PS = [k.ps("psb%d" % i, [128, 512]) for i in range(8)]

    with ExitStack() as gst:
        C.xT = k.sb(gst, "xT", [128, 8, TOK])
        C.ident = k.sb(gst, "ident", [128, 128])
        C.ones = k.sb(gst, "ones", [128, 128])
        C.eps = k.sb(gst, "eps", [128, 1])
        C.scT = k.sb(gst, "scT", [128, 8, 2])
        C.modT = k.sb(gst, "modT", [128, 48, 2])
        C.gs1 = k.sb(gst, "gs1", [128, 8, 2])
        C.gs2 = k.sb(gst, "gs2", [128, 8, 2])
        k.dma("sp", C.ident.ap[:], I.ident[:, :], [], [C.ident])
        k.op("dve", lambda e: e.memset(C.ones.ap[:], 1.0), [], [C.ones])
        k.op("dve", lambda e: e.memset(C.eps.ap[:], 1e-6), [], [C.eps])

        phase_init(C)
        for l in range(nlayers):
            phase_mod(C, l)
            for tg in range(5):
                phase_norm_win(C, l, tg)
            k.barrier()
            phase_kv_out(C, l)
            if "attn" in stages or "all" in stages:
                phase_attn(C, l)
            if "rwkv" in stages or "all" in stages:
                phase_rwkv(C, l)
            if "merge" in stages or "all" in stages:
                phase_merge(C, l)
            if "peer" in stages or "all" in stages:
                phase_peer(C, l)
        phase_final(C)
        k.finish()
    C.nins = k.nins
    return nc, C


def phase_init(C):
    k, I = C.k, C.I
    with ExitStack() as st:
        xt = [k.sb(st, "xin%d" % i, [128, D]) for i in range(2)]
        for t in range(TOK // 128):
            x = xt[t % 2]
            k.dma(None, x.ap[:], I.xall[t * 128:(t + 1) * 128, :], [], [x])
            for half in range(2):
                ps = C.PS[(t * 2 + half) % 8]
                for j in range(4):
                    kc = half * 4 + j
                    k.op("pe", lambda e: e.transpose(ps.ap[:, j * 128:(j + 1) * 128], x.ap[:, kc * 128:(kc + 1) * 128], C.ident.ap[:]), [x, C.ident], [ps])
                eng = ("dve", "act")[half]
                if eng == "dve":
                    k.op("dve", lambda e: e.tensor_copy(C.xT.ap[:, half * 4:half * 4 + 4, t * 128:(t + 1) * 128], ps.ap[:].rearrange("p (a b) -> p a b", a=4)), [ps], [C.xT])
                else:
                    k.op("act", lambda e: e.copy(C.xT.ap[:, half * 4:half * 4 + 4, t * 128:(t + 1) * 128], ps.ap[:].rearrange("p (a b) -> p a b", a=4)), [ps], [C.xT])
        for g in range(2):
            k.dma("sp", C.scT.ap[:, :, g], I.cvec[g].rearrange("(c p) -> p c", p=128), [], [C.scT], allow_slow_non_contiguous=True)
        k.op("act", lambda e: e.activation(out=C.scT.ap[:], in_=C.scT.ap[:], func=AF.Silu), [C.scT], [C.scT])
        k.barrier()


def phase_mod(C, l):
    k, I = C.k, C.I
    with ExitStack() as st:
        wb = [k.sb(st, "adaw%d" % i, [128, 8, 512]) for i in range(2)]
        abT = k.sb(st, "abT", [128, 48])
        lnT = k.sb(st, "lnT", [128, 2, 8])
        k.dma("sp", abT.ap[:], I.ada_b[l].rearrange("(c p) -> p c", p=128), [], [abT], allow_slow_non_contiguous=True)
        k.dma("sp", lnT.ap[:, 0, :], I.ln1[l].rearrange("(c p) -> p c", p=128), [], [lnT], allow_slow_non_contiguous=True)
        k.dma("sp", lnT.ap[:, 1, :], I.ln2[l].rearrange("(c p) -> p c", p=128), [], [lnT], allow_slow_non_contiguous=True)
        ps = C.PS[0]
        for b in range(12):
            w = wb[b % 2]
            k.dma(None, w.ap[:], I.ada_w[l][:, b * 512:(b + 1) * 512].rearrange("(c p) n -> p c n", p=128), [], [w])
            for sub in range(4):
                fc = b * 4 + sub
                for kc in range(8):
                    k.op("pe", lambda e: e.matmul(ps.ap[:, fc * 2:fc * 2 + 2], lhsT=w.ap[:, kc, sub * 128:(sub + 1) * 128], rhs=C.scT.ap[:, kc, :], start=(kc == 0), stop=(kc == 7)), [w, C.scT], [ps])
        k.op("dve", lambda e: e.tensor_tensor(out=C.modT.ap[:], in0=ps.ap[:, 0:96].rearrange("p (a b) -> p a b", b=2), in1=bc(abT.ap[:].unsqueeze(2), [128, 48, 2]), op=ALU.add), [ps, abT], [C.modT])
        for (gs, mi, li) in ((C.gs1, 1, 0), (C.gs2, 4, 1)):
            k.op("dve", lambda e: e.tensor_scalar(gs.ap[:], C.modT.ap[:, mi * 8:mi * 8 + 8, :], 1.0, None, op0=ALU.add), [C.modT], [gs])
            k.op("dve", lambda e: e.tensor_tensor(out=gs.ap[:], in0=gs.ap[:], in1=bc(lnT.ap[:, li, :].unsqueeze(2), [128, 8, 2]), op=ALU.mult), [gs, lnT], [gs])
        k.barrier()


JOBS = [
    (0, 512, "F", 0), (512, 128, "F", 512), (2688, 512, "F", 640), (3200, 512, "F", 1152),
    (4224, 512, "F", 1664), (4736, 512, "F", 2176), (5248, 512, "F", 2688), (5760, 512, "F", 3200),
    (6272, 512, "F", 3712), (6784, 512, "F", 4224),
    (512, 256, "T", 0), (768, 512, "T", 256), (1280, 512, "T", 768), (1792, 512, "T", 1280), (2304, 384, "T", 1792),
    (3200, 512, "T", 2176), (3712, 512, "T", 2688),
]


def norm_group(C, st, tg, gs, shift_idx, hT):
    k = C.k
    g = 0 if tg == 0 else 1
    cols = slice(tg * 512, (tg + 1) * 512)
    sq = [k.sb(st, "sq%d" % i, [128, 512]) for i in range(2)]
    rstd = k.sb(st, "rstd", [128, 512])
    ps = C.PS[7]
    for kc in range(8):
        s = sq[kc % 2]
        k.op("act", lambda e: e.activation(out=s.ap[:], in_=C.xT.ap[:, kc, cols], func=AF.Square), [C.xT], [s])
        k.op("pe", lambda e: e.matmul(ps.ap[:], lhsT=C.ones.ap[:], rhs=s.ap[:], start=(kc == 0), stop=(kc == 7)), [s, C.ones], [ps])
    k.op("act", lambda e: e.activation(out=rstd.ap[:], in_=ps.ap[:], func=AF.Sqrt, scale=1.0 / D, bias=C.eps.ap[:, 0:1]), [ps, C.eps], [rstd])
    k.op("dve", lambda e: e.reciprocal(rstd.ap[:], rstd.ap[:]), [rstd], [rstd])
    for kc in range(8):
        eng = "dve" if kc % 2 == 0 else "pool"
        k.op(eng, lambda e: e.tensor_tensor(out=hT.ap[:, kc, :], in0=C.xT.ap[:, kc, cols], in1=rstd.ap[:], op=ALU.mult), [C.xT, rstd], [hT])
        k.op(eng, lambda e: e.tensor_scalar(hT.ap[:, kc, :], hT.ap[:, kc, :], gs.ap[:, kc, g:g + 1], C.modT.ap[:, shift_idx * 8 + kc, g:g + 1], op0=ALU.mult, op1=ALU.add), [hT, gs, C.modT], [hT])


def phase_norm_win(C, l, tg):
    k, I, S = C.k, C.I, C.S
    with ExitStack() as st:
        hT = k.sb(st, "hT", [128, 8, 512])
        norm_group(C, st, tg, C.gs1, 0, hT)
        wb = [k.sb(st, "winw%d" % i, [128, 8, 512]) for i in range(2)]
        ev = [k.sb(st, "winev%d" % i, [128, 512]) for i in range(4)]
        ei = 0
        pi = 0
        for ji, (c0, n, lay, d0) in enumerate(JOBS):
            w = wb[ji % 2]
            k.dma(None, w.ap[:, :, 0:n], I.w_in[l][:, c0:c0 + n].rearrange("(c p) n -> p c n", p=128), [], [w])
            if lay == "F":
                for sub in range(n // 128):
                    ps = C.PS[pi % 6]; pi += 1
                    for kc in range(8):
                        k.op("pe", lambda e: e.matmul(ps.ap[:], lhsT=w.ap[:, kc, sub * 128:(sub + 1) * 128], rhs=hT.ap[:, kc, :], start=(kc == 0), stop=(kc == 7)), [w, hT], [ps])
                    o = ev[ei % 4]; ei += 1
                    if d0 >= 1664:
                        k.op("act", lambda e: e.activation(out=o.ap[:], in_=ps.ap[:], func=AF.Sigmoid), [ps], [o])
                    elif ei % 2 == 0:
                        k.op("act", lambda e: e.copy(o.ap[:], ps.ap[:]), [ps], [o])
                    else:
                        k.op("dve", lambda e: e.tensor_copy(o.ap[:], ps.ap[:]), [ps], [o])
                    r0 = d0 + sub * 128
                    k.dma(None, S.PF[r0:r0 + 128, tg * 512:(tg + 1) * 512], o.ap[:], [o], [S.rPF])
            else:
                for tt in range(4):
                    ps = C.PS[pi % 6]; pi += 1
                    for kc in range(8):
                        k.op("pe", lambda e: e.matmul(ps.ap[:, 0:n], lhsT=hT.ap[:, kc, tt * 128:(tt + 1) * 128], rhs=w.ap[:, kc, 0:n], start=(kc == 0), stop=(kc == 7)), [w, hT], [ps])
                    o = ev[ei % 4]; ei += 1
                    if ei % 2 == 0:
                        k.op("act", lambda e: e.copy(o.ap[:, 0:n], ps.ap[:, 0:n]), [ps], [o])
                    else:
                        k.op("dve", lambda e: e.tensor_copy(o.ap[:, 0:n], ps.ap[:, 0:n]), [ps], [o])
                    t0 = tg * 512 + tt * 128
                    k.dma(None, S.PT[t0:t0 + 128, d0:d0 + n], o.ap[:, 0:n], [o], [S.rPT])
        k.barrier()


def phase_kv_out(C, l):
    k, S, O = C.k, C.S, C.O
    for s in range(2):
        rows = slice(s * 256, (s + 1) * 256)
        k.dma(None, O.nak[s, l], S.PT[rows, 0:128], [S.rPT], [])
        k.dma(None, O.nav[s, l], S.PT[rows, 128:256], [S.rPT], [])
        k.dma(None, O.nck[s, l], S.PT[rows, 2176:2688], [S.rPT], [])
        k.dma(None, O.ncv[s, l], S.PT[rows, 2688:3200], [S.rPT], [])


def attn_unit(C, W, ui, qT, qres, Mq, segs, vch, sink_h, out_ap, out_res):
    k = C.k
    par = ui % 2
    S_sb = W.S[par]
    sm = W.sm[par]
    tot = sum(s[2] for s in segs)
    off = 0
    k.op("pool", lambda e: e.memset(sm.ap[:], 0.0), [], [sm])
    for si, (kres, kT, n, masks) in enumerate(segs):
        ps = C.PS[par * 2 + si]
        k.op("pe", lambda e: e.matmul(ps.ap[:Mq, 0:n], lhsT=qT, rhs=kT, start=True, stop=True), [qres, kres], [ps])
        covered = []
        for (c0, ncol, mres, map_) in masks:
            k.op("dve", lambda e: e.tensor_tensor(out=S_sb.ap[:Mq, off + c0:off + c0 + ncol], in0=ps.ap[:Mq, c0:c0 + ncol], in1=map_, op=ALU.add), [ps, mres], [S_sb])
            covered.append((c0, c0 + ncol))
        covered.sort()
        pos = 0
        gaps = []
        for (a, b) in covered:
            if a > pos:
                gaps.append((pos, a))
            pos = max(pos, b)
        if pos < n:
            gaps.append((pos, n))
        for gi, (a, b) in enumerate(gaps):
            if not masks:
                k.op("act", lambda e: e.copy(S_sb.ap[:Mq, off + a:off + b], ps.ap[:Mq, a:b]), [ps], [S_sb])
            else:
                k.op("dve", lambda e: e.tensor_copy(S_sb.ap[:Mq, off + a:off + b], ps.ap[:Mq, a:b]), [ps], [S_sb])
        off += n
    k.op("dve", lambda e: e.reduce_max(out=sm.ap[:Mq, 0:1], in_=S_sb.ap[:Mq, 0:tot], axis=AX.X), [S_sb], [sm])
    if sink_h is not None:
        k.op("dve", lambda e: e.tensor_scalar(sm.ap[:Mq, 1:2], sm.ap[:Mq, 0:1], -SCALE, W.nsink.ap[:Mq, sink_h:sink_h + 1], op0=ALU.mult, op1=ALU.min), [sm, W.nsink], [sm])
    else:
        k.op("dve", lambda e: e.tensor_scalar(sm.ap[:Mq, 1:2], sm.ap[:Mq, 0:1], -SCALE, None, op0=ALU.mult), [sm], [sm])
    k.op("act", lambda e: e.activation(out=S_sb.ap[:Mq, 0:tot], in_=S_sb.ap[:Mq, 0:tot], func=AF.Exp, bias=sm.ap[:Mq, 1:2], scale=SCALE, accum_out=sm.ap[:Mq, 2:3]), [S_sb, sm], [S_sb, sm])
    if sink_h is not None:
        k.op("act", lambda e: e.activation(out=sm.ap[:Mq, 3:4], in_=W.sink.ap[:Mq, sink_h:sink_h + 1], func=AF.Exp, bias=sm.ap[:Mq, 1:2], scale=1.0), [sm, W.sink], [sm])
        k.op("dve", lambda e: e.tensor_tensor(out=sm.ap[:Mq, 2:3], in0=sm.ap[:Mq, 2:3], in1=sm.ap[:Mq, 3:4], op=ALU.add), [sm], [sm])
    k.op("dve", lambda e: e.reciprocal(sm.ap[:Mq, 4:5], sm.ap[:Mq, 2:3]), [sm], [sm])
    k.op("dve", lambda e: e.tensor_scalar(S_sb.ap[:Mq, 0:tot], S_sb.ap[:Mq, 0:tot], sm.ap[:Mq, 4:5], None, op0=ALU.mult), [S_sb, sm], [S_sb])
    pso = C.PS[6 + par]
    nch = len(vch)
    for g0 in range(0, nch, 4):
        grp = vch[g0:g0 + 4]
        gi = W.tcount
        W.tcount += 1
        pst = C.PS[4 + gi % 2]
        ptsb = W.PTs[gi % 2]
        maxnk = max(v[2] for v in grp)
        for jj, (vres, vap, nk, c0) in enumerate(grp):
            k.op("pe", lambda e: e.transpose(pst.ap[:nk, jj * 128:jj * 128 + Mq], S_sb.ap[:Mq, c0:c0 + nk], C.ident.ap[:Mq, :Mq]), [S_sb, C.ident], [pst])
        w = len(grp) * 128
        if gi % 2 == 0:
            k.op("dve", lambda e: e.tensor_copy(ptsb.ap[:maxnk, 0:w], pst.ap[:maxnk, 0:w]), [pst], [ptsb])
        else:
            k.op("act", lambda e: e.copy(ptsb.ap[:maxnk, 0:w], pst.ap[:maxnk, 0:w]), [pst], [ptsb])
        for jj, (vres, vap, nk, c0) in enumerate(grp):
            ci = g0 + jj
            k.op("pe", lambda e: e.matmul(pso.ap[:64, 0:Mq], lhsT=vap, rhs=ptsb.ap[:nk, jj * 128:jj * 128 + Mq], start=(ci == 0), stop=(ci == nch - 1)), [vres, ptsb], [pso])
    osb = W.o[par]
    k.op("act", lambda e: e.copy(osb.ap[:64, 0:Mq], pso.ap[:64, 0:Mq]), [pso], [osb])
    k.dma(None, out_ap, osb.ap[:64, 0:Mq], [osb], [out_res])


def attn_work(C, st, l, need_sink):
    k, I = C.k, C.I
    W = Ctx()
    W.S = [k.sb(st, "S_sb%d" % i, [128, 1024]) for i in range(2)]
    W.sm = [k.sb(st, "sm%d" % i, [128, 8]) for i in range(2)]
    W.PTs = [k.sb(st, "PTs%d" % i, [128, 512]) for i in range(2)]
    W.o = [k.sb(st, "osb%d" % i, [64, 128]) for i in range(2)]
    W.tcount = 0
    if need_sink:
        W.sink = k.sb(st, "sink", [128, 8])
        W.nsink = k.sb(st, "nsink", [128, 8])
        k.dma("sp", W.sink.ap[:], I.a_sink[l:l + 1, :].partition_broadcast(128) if False else bass.AP(tensor=I.a_sink.tensor, offset=l * 8, ap=[[0, 128], [1, 8]]), [], [W.sink])
        k.op("dve", lambda e: e.tensor_scalar(W.nsink.ap[:], W.sink.ap[:], -1.0, None, op0=ALU.mult), [W.sink], [W.nsink])
    return W


def phase_attn_prompt(C, l):
    k, I, S = C.k, C.I, C.S
    with ExitStack() as st:
        W = attn_work(C, st, l, True)
        Q = k.sb(st, "pQ", [64, 8, 256])
        K_ = k.sb(st, "pK", [64, 8, 256])
        V = k.sb(st, "pV", [128, 2, 512])
        ui = 0
        for mixer in ("A", "C"):
            for s in range(2):
                base = s * 256
                if mixer == "A":
                    q0, k0, nkv, v0, vw, Y, rY = 0, 512, 2, 128, 128, S.YA, S.rYA
                else:
                    q0, k0, nkv, v0, vw, Y, rY = 640, 1152, 8, 2688, 512, S.YC, S.rYC
                k.dma(None, Q.ap[:], S.PF[q0:q0 + 512, base:base + 256].rearrange("(h d) t -> d h t", d=64), [S.rPF], [Q])
                k.dma(None, K_.ap[:, 0:nkv, :], S.PF[k0:k0 + nkv * 64, base:base + 256].rearrange("(h d) t -> d h t", d=64), [S.rPF], [K_])
                k.dma(None, V.ap[:, :, 0:vw], S.PT[base:base + 256, v0:v0 + vw].rearrange("(t p) f -> p t f", p=128), [S.rPT], [V])
                for h in range(8):
                    kv = h // 4 if mixer == "A" else h
                    for qb in range(2):
                        segs = [(K_, K_.ap[:, kv, :], 256, [])]
                        vch = [(V, V.ap[:, t, kv * 64:(kv + 1) * 64], 128, t * 128) for t in range(2)]
                        attn_unit(C, W, ui, Q.ap[:, h, qb * 128:(qb + 1) * 128], Q, 128, segs, vch,
                                  h if mixer == "A" else None,
                                  Y[h * 64:(h + 1) * 64, base + qb * 128:base + (qb + 1) * 128], rY)
                        ui += 1
        k.barrier()


def load_cache_kT(C, st, src, nh, name):
    k = C.k
    ct = k.sb(st, name + "_tm", [128, 4, nh * 64])
    CK = k.sb(st, name, [64, nh, 512])
    k.dma(None, ct.ap[:], src.rearrange("(j p) f -> p j f", p=128), [], [ct])
    for h in range(nh):
        ps = C.PS[h % 4]
        for j in range(4):
            k.op("pe", lambda e: e.transpose(ps.ap[:64, j * 128:(j + 1) * 128], ct.ap[:, j, h * 64:(h + 1) * 64], C.ident.ap[:]), [ct, C.ident], [ps])
        k.op("dve", lambda e: e.tensor_copy(CK.ap[:, h, :], ps.ap[:64, :]), [ps], [CK])
    return CK


def phase_attn_sample_A(C, l):
    k, I, S = C.k, C.I, C.S
    B0 = NPT
    with ExitStack() as st:
        cos = k.sb(st, "cos", [128, TS]); sin = k.sb(st, "sin", [128, TS]); rot = k.sb(st, "rot", [128, 128])
        k.dma("sp", cos.ap[:], I.cos[:, :], [], [cos])
        k.dma("act", sin.ap[:], I.sin[:, :], [], [sin])
        k.dma("sp", rot.ap[:], I.rot[:, :], [], [rot])
        X = [k.sb(st, "ropeX%d" % i, [128, TS]) for i in range(2)]
        T1 = [k.sb(st, "ropeT%d" % i, [128, 512]) for i in range(2)]
        XR = [k.sb(st, "ropeR%d" % i, [128, 512]) for i in range(2)]
        cnt = 0
        for c in range(5):
            x = X[c % 2]
            k.dma(None, x.ap[:], S.PF[c * 128:(c + 1) * 128, B0:B0 + TS], [S.rPF], [x])
            for tg in range(4):
                cs = slice(tg * 512, (tg + 1) * 512)
                ps = C.PS[cnt % 4]; t1 = T1[cnt % 2]; xr = XR[cnt % 2]; cnt += 1
                k.op("pe", lambda e: e.matmul(ps.ap[:], lhsT=rot.ap[:], rhs=x.ap[:, cs], start=True, stop=True), [rot, x], [ps])
                k.op("pool", lambda e: e.tensor_tensor(out=t1.ap[:], in0=x.ap[:, cs], in1=cos.ap[:, cs], op=ALU.mult), [x, cos], [t1])
                k.op("dve", lambda e: e.tensor_tensor(out=xr.ap[:], in0=ps.ap[:], in1=sin.ap[:, cs], op=ALU.mult), [ps, sin], [xr])
                k.op("dve", lambda e: e.tensor_tensor(out=xr.ap[:], in0=xr.ap[:], in1=t1.ap[:], op=ALU.add), [xr, t1], [xr])
                k.dma(None, S.QR[c * 128:(c + 1) * 128, cs], xr.ap[:], [xr], [S.rQR])
        k.barrier()
    with ExitStack() as st:
        W = attn_work(C, st, l, True)
        ml = k.sb(st, "mleft", [128, 128]); mr = k.sb(st, "mright", [128, 128])
        k.dma("sp", ml.ap[:], I.mleft[:, :], [], [ml])
        k.dma("sp", mr.ap[:], I.mright[:, :], [], [mr])
        CK = load_cache_kT(C, st, I.cak[l], 2, "CKa")
        CV = k.sb(st, "CVa", [128, 4, 128])
        k.dma(None, CV.ap[:], I.cav[l].rearrange("(j p) f -> p j f", p=128), [], [CV])
        Q = k.sb(st, "sQ", [64, TS]); K_ = k.sb(st, "sK", [64, TS]); V = k.sb(st, "sV", [128, 16, 64])
        ui = 0
        for h in range(8):
            kv = h // 4
            if h % 4 == 0:
                k.dma(None, K_.ap[:], S.QR[512 + kv * 64:512 + (kv + 1) * 64, :], [S.rQR], [K_])
                k.dma(None, V.ap[:], S.PT[B0:B0 + TS, 128 + kv * 64:128 + (kv + 1) * 64].rearrange("(t p) f -> p t f", p=128), [S.rPT], [V])
            k.dma(None, Q.ap[:], S.QR[h * 64:(h + 1) * 64, :], [S.rQR], [Q])
            for n in range(16):
                ta = max(0, n - 1); tb = min(16, n + 2)
                nlat = (tb - ta) * 128
                masks = []
                if n > 0:
                    masks.append((0, 128, ml, ml.ap[:, :]))
                if n < 15:
                    masks.append((nlat - 128, 128, mr, mr.ap[:, :]))
                segs = [(K_, K_.ap[:, ta * 128:tb * 128], nlat, masks), (CK, CK.ap[:, kv, :], 512, [])]
                vch = [(V, V.ap[:, t, :], 128, (t - ta) * 128) for t in range(ta, tb)]
                vch += [(CV, CV.ap[:, j, kv * 64:(kv + 1) * 64], 128, nlat + j * 128) for j in range(4)]
                attn_unit(C, W, ui, Q.ap[:, n * 128:(n + 1) * 128], Q, 128, segs, vch, h,
                          S.YA[h * 64:(h + 1) * 64, B0 + n * 128:B0 + (n + 1) * 128], S.rYA)
                ui += 1
        k.barrier()


def phase_attn_sample_C(C, l):
    k, I, S = C.k, C.I, C.S
    B0 = NPT
    with ExitStack() as st:
        W = attn_work(C, st, l, False)
        z = k.sb(st, "zer", [120, 127])
        k.op("dve", lambda e: e.memset(z.ap[:], 0.0), [], [z])
        k.dma("sp", S.E[0], z.ap[:], [z], [S.rE])
        k.dma("sp", S.E[0, :, 48:79], I.na_rpb[l].rearrange("h r c -> (h r) c"), [S.rE], [S.rE])
        MB = k.sb(st, "MB", [64, 120, 64])
        cm = k.sb(st, "colmask", [64, 64])
        k.dma("sp", cm.ap[:], I.colmask[:, :], [], [cm])
        for c in range(64):
            k.dma(None, MB.ap[c:c + 1, :, :], S.E[0:1, :, 63 - c:127 - c], [S.rE], [MB])
        k.op("dve", lambda e: e.scalar_tensor_tensor(out=MB.ap[:], in0=MB.ap[:], scalar=1.0 / SCALE, in1=bc(cm.ap[:].unsqueeze(1), [64, 120, 64]), op0=ALU.mult, op1=ALU.add), [MB, cm], [MB])
        CK = load_cache_kT(C, st, I.cck[l], 8, "CKc")
        CV = k.sb(st, "CVc", [128, 4, 512])
        k.dma(None, CV.ap[:], I.ccv[l].rearrange("(j p) f -> p j f", p=128), [], [CV])
        Q = k.sb(st, "cQ", [64, TS]); K_ = k.sb(st, "cK", [64, TS]); V = k.sb(st, "cV", [64, 32, 64])
        ui = 0
        for h in range(8):
            k.dma(None, Q.ap[:], S.PF[640 + h * 64:640 + (h + 1) * 64, B0:B0 + TS], [S.rPF], [Q])
            k.dma(None, K_.ap[:], S.PF[1152 + h * 64:1152 + (h + 1) * 64, B0:B0 + TS], [S.rPF], [K_])
            k.dma(None, V.ap[:], S.PT[B0:B0 + TS, 2688 + h * 64:2688 + (h + 1) * 64].rearrange("(r c) f -> c r f", c=64), [S.rPT], [V])
            for r in range(32):
                rs = min(max(r - 4, 0), 24)
                dr0 = rs - r + 7
                bias = MB.ap[:, h * 15 + dr0:h * 15 + dr0 + 8, :].rearrange("p a b -> p (a b)")
                segs = [(K_, K_.ap[:, rs * 64:rs * 64 + 512], 512, [(0, 512, MB, bias)]), (CK, CK.ap[:, h, :], 512, [])]
                vch = [(V, V.ap[:, rs + a, :], 64, a * 64) for a in range(8)]
                vch += [(CV, CV.ap[:, j, h * 64:(h + 1) * 64], 128, 512 + j * 128) for j in range(4)]
                attn_unit(C, W, ui, Q.ap[:, r * 64:(r + 1) * 64], Q, 64, segs, vch, None,
                          S.YC[h * 64:(h + 1) * 64, B0 + r * 64:B0 + (r + 1) * 64], S.rYC)
                ui += 1
        k.barrier()


def phase_attn(C, l):
    phase_attn_prompt(C, l)
    phase_attn_sample_A(C, l)
    phase_attn_sample_C(C, l)


SEQS = [(0, 256), (256, 256), (512, 2048)]
TC = 32
DEC_SCALE = -0.6065306597126334


def pbcast(X, l, n):
    return bass.AP(tensor=X.tensor, offset=l * n, ap=[[0, 128], [1, n]])


def v3(ap, h=8):
    return ap.rearrange("p (h j) -> p h j", h=h)


def phase_rwkv_prep(C, l):
    k, I, S = C.k, C.I, C.S
    with ExitStack() as st:
        mu = k.sb(st, "mu_b", [128, 1920]); w0 = k.sb(st, "w0_b", [128, 1024]); a0 = k.sb(st, "a0_b", [128, 1024])
        kkp = k.sb(st, "kkp_b", [128, 512]); ka = k.sb(st, "ka_b", [128, 512]); omk = k.sb(st, "omk_b", [128, 512]); rk = k.sb(st, "rk_b", [128, 512])
        w2 = k.sb(st, "w2", [128, 512]); a2 = k.sb(st, "a2", [128, 512]); g2 = k.sb(st, "g2", [128, 512]); J = k.sb(st, "J", [128, 128])
        e12 = k.sb(st, "e12", [128, 1])
        k.op("dve", lambda e: e.memset(e12.ap[:], 1e-12), [], [e12])
        for (t, X, n) in ((mu, I.rw_mu, 1920), (w0, I.rw_w0, 1024), (a0, I.rw_a0, 1024), (kkp, I.rw_kk, 512), (ka, I.rw_ka, 512), (rk, I.rw_rk, 512)):
            k.dma(None, t.ap[:], pbcast(X, l, n), [], [t])
        for (t, X) in ((w2, I.rw_w2), (a2, I.rw_a2), (g2, I.rw_g2)):
            k.dma(None, t.ap[:], X[l], [], [t])
        k.dma(None, J.ap[:], I.antiid[:, :], [], [J])
        k.op("dve", lambda e: e.tensor_scalar(omk.ap[:], ka.ap[:], -1.0, 1.0, op0=ALU.mult, op1=ALU.add), [ka], [omk])
        pb = k.sb(st, "pb", [128, 1920]); prev = k.sb(st, "prev", [128, 1920]); nxt = k.sb(st, "nxt", [128, 1920])
        lw = k.sb(st, "lw", [128, 384]); lwT = k.sb(st, "lwT", [128, 384])
        az = [k.sb(st, "az%d" % z, [128, 512]) for z in range(2)]
        kk = k.sb(st, "kk", [128, 512]); kkn = k.sb(st, "kkn", [128, 512]); t1 = k.sb(st, "t1", [128, 512]); t2 = k.sb(st, "t2", [128, 512])
        sm = k.sb(st, "rsm", [128, 32])
        opz = [k.sb(st, "opz%d" % z, [128, 8, 6, 64]) for z in range(2)]
        rev = k.sb(st, "rev", [128, 1536]); bg = k.sb(st, "bg", [128, 1024])
        for (s0, T) in SEQS:
            nt = T // 128
            for ti in range(nt):
                t0 = s0 + ti * 128
                k.dma(None, pb.ap[:], S.PT[t0:t0 + 128, 256:2176], [S.rPT], [pb])
                if ti > 0:
                    k.dma(None, prev.ap[:], S.PT[t0 - 1:t0 + 127, 256:2176], [S.rPT], [prev])
                else:
                    k.op("dve", lambda e: e.memset(prev.ap[0:1, :], 0.0), [], [prev])
                    k.dma(None, prev.ap[1:128, :], S.PT[t0:t0 + 127, 256:2176], [S.rPT], [prev])
                if ti < nt - 1:
                    k.dma(None, nxt.ap[:], S.PT[t0 + 1:t0 + 129, 256:2176], [S.rPT], [nxt])
                else:
                    k.op("pool", lambda e: e.memset(nxt.ap[:], 0.0), [], [nxt])
                    k.dma(None, nxt.ap[0:127, :], S.PT[t0 + 1:t0 + 128, 256:2176], [S.rPT], [nxt])
                k.op("pool", lambda e: e.tensor_tensor(out=prev.ap[:], in0=prev.ap[:], in1=nxt.ap[:], op=ALU.add), [prev, nxt], [prev])
                k.op("dve", lambda e: e.scalar_tensor_tensor(out=prev.ap[:], in0=prev.ap[:], scalar=0.5, in1=pb.ap[:], op0=ALU.mult, op1=ALU.subtract), [prev, pb], [prev])
                k.op("dve", lambda e: e.tensor_tensor(out=prev.ap[:], in0=prev.ap[:], in1=mu.ap[:], op=ALU.mult), [prev, mu], [prev])
                k.op("dve", lambda e: e.tensor_tensor(out=pb.ap[:], in0=pb.ap[:], in1=prev.ap[:], op=ALU.add), [pb, prev], [pb])
                r_ = pb.ap[:, 0:512]; k_ = pb.ap[:, 512:1024]; v_ = pb.ap[:, 1024:1536]
                k.op("act", lambda e: e.activation(out=lw.ap[:, 0:128], in_=pb.ap[:, 1536:1664], func=AF.Tanh), [pb], [lw])
                k.op("act", lambda e: e.copy(lw.ap[:, 128:256], pb.ap[:, 1664:1792]), [pb], [lw])
                k.op("act", lambda e: e.activation(out=lw.ap[:, 256:384], in_=pb.ap[:, 1792:1920], func=AF.Sigmoid), [pb], [lw])
                ps = C.PS[0]
                for j in range(3):
                    k.op("pe", lambda e: e.transpose(ps.ap[:, j * 128:(j + 1) * 128], lw.ap[:, j * 128:(j + 1) * 128], C.ident.ap[:]), [lw, C.ident], [ps])
                k.op("dve", lambda e: e.tensor_copy(lwT.ap[:], ps.ap[:, 0:384]), [ps], [lwT])
                for z in range(2):
                    zs = slice(z * 64, (z + 1) * 64)
                    psw = C.PS[1 + z]; psa = C.PS[3 + z]
                    k.op("pe", lambda e: e.matmul(psw.ap[:], lhsT=lwT.ap[zs, 0:128], rhs=w2.ap[zs, :], start=True, stop=True), [lwT, w2], [psw])
                    k.op("pe", lambda e: e.matmul(psa.ap[:], lhsT=lwT.ap[zs, 128:256], rhs=a2.ap[zs, :], start=True, stop=True), [lwT, a2], [psa])
                    dec = v3(t1.ap[:])
                    k.op("dve", lambda e: e.tensor_tensor(out=t1.ap[:], in0=psw.ap[:], in1=w0.ap[:, z * 512:(z + 1) * 512], op=ALU.add), [psw, w0], [t1])
                    k.op("act", lambda e: e.activation(out=t1.ap[:], in_=t1.ap[:], func=AF.Sigmoid), [t1], [t1])
                    k.op("act", lambda e: e.activation(out=opz[z].ap[:, :, 1, :], in_=dec, func=AF.Exp, scale=DEC_SCALE), [t1], [opz[z]])
                    k.op("dve", lambda e: e.tensor_tensor(out=az[z].ap[:], in0=psa.ap[:], in1=a0.ap[:, z * 512:(z + 1) * 512], op=ALU.add), [psa, a0], [az[z]])
                    k.op("act", lambda e: e.activation(out=az[z].ap[:], in_=az[z].ap[:], func=AF.Sigmoid), [az[z]], [az[z]])
                psg = C.PS[5]
                k.op("pe", lambda e: e.matmul(psg.ap[:], lhsT=lwT.ap[:, 256:384], rhs=g2.ap[:, :], start=True, stop=True), [lwT, g2], [psg])
                k.op("act", lambda e: e.copy(bg.ap[:, 512:1024], psg.ap[:]), [psg], [bg])
                k.op("dve", lambda e: e.tensor_tensor(out=kk.ap[:], in0=k_, in1=kkp.ap[:], op=ALU.mult), [pb, kkp], [kk])
                k.op("pool", lambda e: e.tensor_tensor(out=t2.ap[:], in0=kk.ap[:], in1=kk.ap[:], op=ALU.mult), [kk], [t2])
                k.op("dve", lambda e: e.tensor_reduce(out=sm.ap[:, 0:8], in_=v3(t2.ap[:]), axis=AX.X, op=ALU.add), [t2], [sm])
                k.op("act", lambda e: e.activation(out=sm.ap[:, 0:8], in_=sm.ap[:, 0:8], func=AF.Sqrt, bias=e12.ap[:, 0:1], scale=1.0), [sm, e12], [sm])
                k.op("dve", lambda e: e.reciprocal(sm.ap[:, 8:16], sm.ap[:, 0:8]), [sm], [sm])
                k.op("dve", lambda e: e.tensor_tensor(out=v3(kkn.ap[:]), in0=v3(kk.ap[:]), in1=bc(sm.ap[:, 8:16].unsqueeze(2), [128, 8, 64]), op=ALU.mult), [kk, sm], [kkn])
                for z in range(2):
                    k.op("dve", lambda e: e.tensor_tensor(out=t1.ap[:], in0=az[z].ap[:], in1=ka.ap[:], op=ALU.mult), [az[z], ka], [t1])
                    k.op("pool", lambda e: e.tensor_tensor(out=t1.ap[:], in0=t1.ap[:], in1=omk.ap[:], op=ALU.add), [t1, omk], [t1])
                    k.op("dve", lambda e: e.tensor_tensor(out=opz[z].ap[:, :, 3, :], in0=v3(t1.ap[:]), in1=v3(k_), op=ALU.mult), [t1, pb], [opz[z]])
                    k.op("pool", lambda e: e.tensor_tensor(out=opz[z].ap[:, :, 2, :], in0=v3(kkn.ap[:]), in1=v3(az[z].ap[:]), op=ALU.mult), [kkn, az[z]], [opz[z]])
                    k.op("dve", lambda e: e.tensor_scalar(opz[z].ap[:, :, 0, :], v3(kkn.ap[:]), -1.0, None, op0=ALU.mult), [kkn], [opz[z]])
                    k.op("act", lambda e: e.copy(opz[z].ap[:, :, 4, :], v3(r_)), [pb], [opz[z]])
                    k.op("act", lambda e: e.copy(opz[z].ap[:, :, 5, :], v3(v_)), [pb], [opz[z]])
                k.op("dve", lambda e: e.tensor_tensor(out=v3(t1.ap[:]), in0=opz[0].ap[:, :, 3, :], in1=opz[1].ap[:, :, 3, :], op=ALU.add), [opz[0], opz[1]], [t1])
                k.op("dve", lambda e: e.tensor_tensor(out=t1.ap[:], in0=t1.ap[:], in1=r_, op=ALU.mult), [t1, pb], [t1])
                k.op("dve", lambda e: e.tensor_tensor(out=t1.ap[:], in0=t1.ap[:], in1=rk.ap[:], op=ALU.mult), [t1, rk], [t1])
                k.op("dve", lambda e: e.tensor_reduce(out=sm.ap[:, 16:24], in_=v3(t1.ap[:]), axis=AX.X, op=ALU.add), [t1], [sm])
                k.op("dve", lambda e: e.tensor_tensor(out=v3(bg.ap[:, 0:512]), in0=v3(v_), in1=bc(sm.ap[:, 16:24].unsqueeze(2), [128, 8, 64]), op=ALU.mult), [pb, sm], [bg])
                k.dma(None, S.BG[t0:t0 + 128, :], bg.ap[:], [bg], [S.rBG])
                k.dma(None, S.OPS[0, t0:t0 + 128, :], opz[0].ap[:].rearrange("p a b c -> p (a b c)"), [opz[0]], [S.rOPS])
                flat = opz[1].ap[:].rearrange("p a b c -> p (a b c)")
                tr = s0 + (nt - 1 - ti) * 128
                for half in range(2):
                    for j in range(3):
                        c0 = half * 1536 + j * 512
                        psr = C.PS[5 + j] if j < 2 else C.PS[7]
                        k.op("pe", lambda e: e.matmul(psr.ap[:], lhsT=J.ap[:], rhs=flat[:, c0:c0 + 512], start=True, stop=True), [J, opz[1]], [psr])
                        if j % 2 == 0:
                            k.op("dve", lambda e: e.tensor_copy(rev.ap[:, j * 512:(j + 1) * 512], psr.ap[:]), [psr], [rev])
                        else:
                            k.op("act", lambda e: e.copy(rev.ap[:, j * 512:(j + 1) * 512], psr.ap[:]), [psr], [rev])
                    k.dma(None, S.OPS[1, tr:tr + 128, half * 1536:(half + 1) * 1536], rev.ap[:], [rev], [S.rOPS])
        k.barrier()


SCAN_MIND = int(os.environ.get('SCAN_MIND', '0'))
SCAN_CHAINS = int(os.environ.get('SCAN_CHAINS', '2'))


def scan_group(C, st, l, insts, T, ILO, s_init, s_final, chains):
    k, I, S = C.k, C.I, C.S
    IC = 64 // ILO
    NS = TC + 1
    OPB = [k.sb(st, "OPB%d" % i, [128, TC, 5, 64]) for i in range(2)]
    VB = [k.sb(st, "VB%d" % i, [128, TC, ILO]) for i in range(2)]
    CH = []
    for ci, (eng, i0, i1) in enumerate(chains):
        n = i1 - i0
        c = Ctx()
        c.eng, c.i0, c.i1, c.n = eng, i0, i1, n
        c.S = k.sb(st, "scanS%d" % ci, [128, n * 64]); c.tmp = k.sb(st, "scantmp%d" % ci, [128, n * 64])
        c.T2 = k.sb(st, "scanT2%d" % ci, [128, 2, n * 64])
        c.B = [k.sb(st, "scanB%d_%d" % (ci, i), [128, 2 * NS, n]) for i in range(2)]
        if s_init is not None:
            k.dma(None, c.S.ap[:], s_init[:, i0 * 64:i1 * 64], [], [c.S])
        else:
            k.op("pool", lambda e: e.memset(c.S.ap[:], 0.0), [], [c.S])
        c.S3 = c.S.ap[:].rearrange("p (i j) -> p i j", j=64)
        c.T3 = c.tmp.ap[:].rearrange("p (i j) -> p i j", j=64)
        c.shp = [128, n, 64]
        CH.append(c)
    groups = {}
    for (p0, z, h, s0) in insts:
        groups.setdefault((z, s0), []).append((p0, h))
    nch = T // TC

    def store_y(ch, c, slot0, nslots, trow0):
        Bt = c.B[ch % 2]
        for (z, s0), lst in groups.items():
            pa = min(p for p, _ in lst)
            np_ = len(lst) * IC
            dst = bass.AP(tensor=S.YS.tensor, offset=(z * TOK + s0 + trow0) * 512 + c.i0, ap=[[ILO, np_], [512, nslots], [1, c.n]])
            k.dma(None, dst, Bt.ap[pa:pa + np_, slot0:slot0 + nslots, :], [Bt], [S.rYS])

    for ch in range(nch):
        opb = OPB[ch % 2]; vb = VB[ch % 2]; opb_prev = OPB[(ch + 1) % 2]
        for (p0, z, h, s0) in insts:
            base = (z * TOK + s0 + ch * TC) * 3072 + h * 384
            src = bass.AP(tensor=S.OPS.tensor, offset=base, ap=[[0, IC], [3072, TC], [1, 320]])
            k.dma(None, opb.ap[p0:p0 + IC, :, :, :].rearrange("p t a b -> p t (a b)"), src, [S.rOPS], [opb])
            srcv = bass.AP(tensor=S.OPS.tensor, offset=base + 320, ap=[[ILO, IC], [3072, TC], [1, ILO]])
            k.dma(None, vb.ap[p0:p0 + IC, :, :], srcv, [S.rOPS], [vb])
        for tc in range(TC):
            def row(c, kind):
                return bc(opb.ap[:, tc, kind, :].unsqueeze(1), c.shp)

            def rprev(c):
                if tc > 0:
                    return opb, bc(opb.ap[:, tc - 1, 4, :].unsqueeze(1), c.shp)
                if ch > 0:
                    return opb_prev, bc(opb_prev.ap[:, TC - 1, 4, :].unsqueeze(1), c.shp)
                return opb, bc(opb.ap[:, 0, 4, :].unsqueeze(1), c.shp)
            for step in range(8):
                for c in CH:
                    Bt = c.B[ch % 2]
                    T2a = c.T2.ap[:, 0, :].rearrange("p (i j) -> p i j", j=64)
                    T2b = c.T2.ap[:, 1, :].rearrange("p (i j) -> p i j", j=64)
                    if step == 0:
                        k.op(c.eng, lambda e: e.tensor_tensor(out=T2b, in0=c.S3, in1=row(c, 0), op=ALU.mult), [c.S, opb], [], ww=[c.T2])
                    elif step == 1:
                        rres, rap = rprev(c)
                        k.op(c.eng, lambda e: e.tensor_tensor(out=T2a, in0=c.S3, in1=rap, op=ALU.mult), [c.S, rres], [], ww=[c.T2])
                    elif step == 2:
                        outap = Bt.ap[:, tc::NS, :]
                        k.op(c.eng, lambda e: e.tensor_reduce(out=outap, in_=c.T2.ap[:].rearrange("p a (i j) -> p (a i) j", j=64), axis=AX.X, op=ALU.add), [c.T2], [Bt])
                    elif step == 3:
                        k.op(c.eng, lambda e: e.tensor_tensor(out=c.S3, in0=c.S3, in1=row(c, 1), op=ALU.mult), [c.S, opb], [c.S])
                    elif step == 4:
                        k.op(c.eng, lambda e: e.tensor_tensor(out=c.T3, in0=bc(Bt.ap[:, NS + tc, :].unsqueeze(2), c.shp), in1=row(c, 2), op=ALU.mult), [Bt, opb], [c.tmp])
                    elif step == 5:
                        k.op(c.eng, lambda e: e.tensor_tensor(out=c.S3, in0=c.S3, in1=c.T3, op=ALU.add), [c.S, c.tmp], [c.S])
                    elif step == 6:
                        k.op(c.eng, lambda e: e.tensor_tensor(out=c.T3, in0=bc(vb.ap[:, tc, c.i0:c.i1].unsqueeze(2), c.shp), in1=row(c, 3), op=ALU.mult), [vb, opb], [c.tmp])
                    else:
                        k.op(c.eng, lambda e: e.tensor_tensor(out=c.S3, in0=c.S3, in1=c.T3, op=ALU.add), [c.S, c.tmp], [c.S])
        last = (ch == nch - 1)
        if last:
            for c in CH:
                Bt = c.B[ch % 2]
                k.op(c.eng, lambda e: e.tensor_tensor(out=c.T3, in0=c.S3, in1=bc(opb.ap[:, TC - 1, 4, :].unsqueeze(1), c.shp), op=ALU.mult), [c.S, opb], [c.tmp])
                k.op(c.eng, lambda e: e.tensor_reduce(out=Bt.ap[:, TC, :], in_=c.T3, axis=AX.X, op=ALU.add), [c.tmp], [Bt])
        for c in CH:
            if ch == 0:
                store_y(ch, c, 1, TC - 1 + (1 if last else 0), 0)
            else:
                store_y(ch, c, 0, TC + (1 if last else 0), ch * TC - 1)
    if s_final is not None:
        for (dst, p0, n) in s_final:
            for c in CH:
                k.dma(None, dst[:, c.i0 * 64:c.i1 * 64], c.S.ap[p0:p0 + n, :], [c.S], [])


def phase_rwkv_scan(C, l):
    k, I, S, O = C.k, C.I, C.S, C.O
    with ExitStack() as st:
        insts = []
        for s in range(2):
            for z in range(2):
                for h in range(8):
                    insts.append((s * 64 + z * 32 + h * 4, z, h, s * 256))
        scan_group(C, st, l, insts, 256, 16, None, [(O.nst[s, l], s * 64, 64) for s in range(2)], ([("dve", 0, 16)] if SCAN_CHAINS == 1 else [("dve", 0, 8), ("dve", 8, 16)]))
        k.barrier()
    with ExitStack() as st:
        insts = []
        for z in range(2):
            for h in range(8):
                insts.append((z * 64 + h * 8, z, h, 512))
        scan_group(C, st, l, insts, 2048, 8, I.st0[l], None, ([("dve", 0, 8)] if SCAN_CHAINS == 1 else [("dve", 0, 4), ("dve", 4, 8)]))
        k.barrier()


def phase_rwkv_post(C, l):
    k, I, S = C.k, C.I, C.S
    with ExitStack() as st:
        lg = k.sb(st, "lnxg_b", [128, 512]); lb = k.sb(st, "lnxb_b", [128, 512]); J = k.sb(st, "J2", [128, 128])
        gne = k.sb(st, "gne", [128, 1])
        k.op("dve", lambda e: e.memset(gne.ap[:], 64e-5), [], [gne])
        k.dma(None, lg.ap[:], pbcast(I.rw_lnx_g, l, 512), [], [lg])
        k.dma(None, lb.ap[:], pbcast(I.rw_lnx_b, l, 512), [], [lb])
        k.dma(None, J.ap[:], I.antiid[:, :], [], [J])
        yf = [k.sb(st, "yf%d" % i, [128, 512]) for i in range(2)]
        yb = [k.sb(st, "yb%d" % i, [128, 512]) for i in range(2)]
        bg = [k.sb(st, "pbg%d" % i, [128, 1024]) for i in range(2)]
        t1 = k.sb(st, "pt1", [128, 512]); sm = k.sb(st, "psm", [128, 32]); yT = [k.sb(st, "pyT%d" % i, [128, 4, 128]) for i in range(2)]
        cnt = 0
        for (s0, T) in SEQS:
            nt = T // 128
            for ti in range(nt):
                par = cnt % 2; cnt += 1
                t0 = s0 + ti * 128
                tr = s0 + (nt - 1 - ti) * 128
                y = yf[par]; y2 = yb[par]; b = bg[par]
                k.dma(None, y.ap[:], S.YS[0, t0:t0 + 128, :], [S.rYS], [y])
                k.dma(None, y2.ap[:], S.YS[1, tr:tr + 128, :], [S.rYS], [y2])
                k.dma(None, b.ap[:], S.BG[t0:t0 + 128, :], [S.rBG], [b])
                ps = C.PS[par]
                k.op("pe", lambda e: e.matmul(ps.ap[:], lhsT=J.ap[:], rhs=y2.ap[:], start=True, stop=True), [J, y2], [ps])
                k.op("dve", lambda e: e.tensor_tensor(out=y.ap[:], in0=y.ap[:], in1=ps.ap[:], op=ALU.add), [y, ps], [y])
                k.op("dve", lambda e: e.tensor_reduce(out=sm.ap[:, 0:8], in_=v3(y.ap[:]), axis=AX.X, op=ALU.add), [y], [sm])
                k.op("dve", lambda e: e.tensor_scalar(sm.ap[:, 0:8], sm.ap[:, 0:8], 1.0 / 64, None, op0=ALU.mult), [sm], [sm])
                k.op("dve", lambda e: e.tensor_tensor(out=v3(y.ap[:]), in0=v3(y.ap[:]), in1=bc(sm.ap[:, 0:8].unsqueeze(2), [128, 8, 64]), op=ALU.subtract), [y, sm], [y])
                k.op("pool", lambda e: e.tensor_tensor(out=t1.ap[:], in0=y.ap[:], in1=y.ap[:], op=ALU.mult), [y], [t1])
                k.op("dve", lambda e: e.tensor_reduce(out=sm.ap[:, 8:16], in_=v3(t1.ap[:]), axis=AX.X, op=ALU.add), [t1], [sm])
                k.op("act", lambda e: e.activation(out=sm.ap[:, 8:16], in_=sm.ap[:, 8:16], func=AF.Sqrt, bias=gne.ap[:, 0:1], scale=1.0 / 64), [sm, gne], [sm])
                k.op("dve", lambda e: e.reciprocal(sm.ap[:, 16:24], sm.ap[:, 8:16]), [sm], [sm])
                k.op("dve", lambda e: e.tensor_tensor(out=v3(y.ap[:]), in0=v3(y.ap[:]), in1=bc(sm.ap[:, 16:24].unsqueeze(2), [128, 8, 64]), op=ALU.mult), [y, sm], [y])
                k.op("pool", lambda e: e.tensor_tensor(out=y.ap[:], in0=y.ap[:], in1=lg.ap[:], op=ALU.mult), [y, lg], [y])
                k.op("dve", lambda e: e.tensor_tensor(out=y.ap[:], in0=y.ap[:], in1=lb.ap[:], op=ALU.add), [y, lb], [y])
                k.op("dve", lambda e: e.tensor_tensor(out=y.ap[:], in0=y.ap[:], in1=b.ap[:, 0:512], op=ALU.add), [y, b], [y])
                k.op("dve", lambda e: e.tensor_tensor(out=y.ap[:], in0=y.ap[:], in1=b.ap[:, 512:1024], op=ALU.mult), [y, b], [y])
                pst = C.PS[2 + par]
                for c in range(4):
                    k.op("pe", lambda e: e.transpose(pst.ap[:, c * 128:(c + 1) * 128], y.ap[:, c * 128:(c + 1) * 128], C.ident.ap[:]), [y, C.ident], [pst])
                k.op("act", lambda e: e.copy(yT[par].ap[:], pst.ap[:].rearrange("p (c t) -> p c t", c=4)), [pst], [yT[par]])
                k.dma(None, S.YB[:, t0:t0 + 128].rearrange("(c p) t -> p c t", p=128), yT[par].ap[:], [yT[par]], [S.rYB])
        k.barrier()


def phase_rwkv(C, l):
    phase_rwkv_prep(C, l)
    phase_rwkv_scan(C, l)
    phase_rwkv_post(C, l)


def phase_merge(C, l):
    k, I, S = C.k, C.I, C.S
    outs = (I.a_out, I.rw_out, I.na_out)
    Ys = ((S.YA, S.rYA), (S.YB, S.rYB), (S.YC, S.rYC))
    with ExitStack() as st:
        yT = [k.sb(st, "myT%d" % b, [128, 4, 512]) for b in range(3)]
        mg = k.sb(st, "merged", [128, 8, 512])
        wo = [[k.sb(st, "mwo%d_%d" % (b, i), [128, 4, 128]) for i in range(2)] for b in range(3)]
        gt = [[k.sb(st, "mg%d_%d" % (b, i), [128, 512]) for i in range(2)] for b in range(3)]
        tmp = [k.sb(st, "mtmp%d" % i, [128, 512]) for i in range(2)]
        wob = [k.sb(st, "mwob%d" % i, [128, 8, 128]) for i in range(2)]
        cnt = 0
        for tg in range(5):
            g = 0 if tg == 0 else 1
            cols = slice(tg * 512, (tg + 1) * 512)
            for b in range(3):
                k.dma(None, yT[b].ap[:], Ys[b][0][:, cols].rearrange("(c p) t -> p c t", p=128), [Ys[b][1]], [yT[b]])
            for fc in range(8):
                par = cnt % 2; cnt += 1
                for b in range(3):
                    k.dma(None, wo[b][par].ap[:], outs[b][l][:, fc * 128:(fc + 1) * 128].rearrange("(c p) n -> p c n", p=128), [], [wo[b][par]])
                    r0 = 1664 + b * 1024 + fc * 128
                    k.dma(None, gt[b][par].ap[:], S.PF[r0:r0 + 128, cols], [S.rPF], [gt[b][par]])
                for b in range(3):
                    ps = C.PS[(fc * 3 + b) % 6]
                    for c in range(4):
                        k.op("pe", lambda e: e.matmul(ps.ap[:], lhsT=wo[b][par].ap[:, c, :], rhs=yT[b].ap[:, c, :], start=(c == 0), stop=(c == 3)), [wo[b][par], yT[b]], [ps])
                    if b == 0:
                        k.op("dve", lambda e: e.tensor_tensor(out=mg.ap[:, fc, :], in0=ps.ap[:], in1=gt[b][par].ap[:], op=ALU.mult), [ps, gt[b][par]], [mg])
                    else:
                        t = tmp[b % 2]
                        k.op("dve", lambda e: e.tensor_tensor(out=t.ap[:], in0=ps.ap[:], in1=gt[b][par].ap[:], op=ALU.mult), [ps, gt[b][par]], [t])
                        k.op("pool", lambda e: e.tensor_tensor(out=mg.ap[:, fc, :], in0=mg.ap[:, fc, :], in1=t.ap[:], op=ALU.add), [mg, t], [mg])
            for fc2 in range(8):
                w = wob[fc2 % 2]
                k.dma(None, w.ap[:], I.w_o[l][:, fc2 * 128:(fc2 + 1) * 128].rearrange("(c p) n -> p c n", p=128), [], [w])
                ps = C.PS[6 + fc2 % 2]
                for kc in range(8):
                    k.op("pe", lambda e: e.matmul(ps.ap[:], lhsT=w.ap[:, kc, :], rhs=mg.ap[:, kc, :], start=(kc == 0), stop=(kc == 7)), [w, mg], [ps])
                k.op("dve", lambda e: e.scalar_tensor_tensor(out=C.xT.ap[:, fc2, cols], in0=ps.ap[:], scalar=C.modT.ap[:, 16 + fc2, g:g + 1], in1=C.xT.ap[:, fc2, cols], op0=ALU.mult, op1=ALU.add), [ps, C.modT, C.xT], [C.xT])
        k.barrier()


PEER_STOP = int(os.environ.get('PEER_STOP', '0'))
NBUF = 7
NACC = 2


def phase_peer(C, l):
    k, I, S = C.k, C.I, C.S
    with ExitStack() as st:
        skn = k.sb(st, "skn", [128, 16, 128]); skT = k.sb(st, "skT", [128, 16, 128])
        k.dma(None, skn.ap[:], I.pe_sk[l].rearrange("c n k -> n c k"), [], [skn])
        for c in range(16):
            ps = C.PS[c % 4]
            k.op("pe", lambda e: e.transpose(ps.ap[:, 0:128], skn.ap[:, c, :], C.ident.ap[:]), [skn, C.ident], [ps])
            k.op("dve", lambda e: e.tensor_copy(skT.ap[:, c, :], ps.ap[:, 0:128]), [ps], [skT])
        h2T = k.sb(st, "h2T", [128, 8, 128])
        h2s = [k.sb(st, "h2tok%d" % i, [128, D]) for i in range(2)]
        sq = [k.sb(st, "psq%d" % i, [128, 128]) for i in range(2)]; rstd = k.sb(st, "prstd", [128, 128])
        wq = [k.sb(st, "wq%d" % i, [128, 8, 128]) for i in range(2)]
        qT = k.sb(st, "qT", [128, 16, 128])
        Sc = [k.sb(st, "Sc%d" % c, [128, 128]) for c in range(16)]
        vals = [k.sb(st, "vals%d" % c, [128, 16]) for c in range(16)]
        idxu = [k.sb(st, "idxu%d" % c, [128, 16], U32) for c in range(16)]
        idxf = [k.sb(st, "idxf%d" % c, [128, 16]) for c in range(16)]
        cand = [k.sb(st, "cand%d" % h, [128, 256]) for h in range(8)]
        cand2 = [k.sb(st, "candb%d" % h, [128, 256]) for h in range(8)]
        cande = [k.sb(st, "cande%d" % h, [128, 256]) for h in range(8)]
        sv = [k.sb(st, "sv%d" % h, [128, 16]) for h in range(8)]
        gates = [k.sb(st, "gate%d" % i, [128, 8, 16]) for i in range(2)]; sm = k.sb(st, "pesm", [128, 32])
        junk = k.sb(st, "pjunk", [128, 256]); ef = k.sb(st, "ef", [128, 128])
        eis = [k.sb(st, "ei%d" % i, [128, 128], I32) for i in range(2)]
        dots = k.sb(st, "dots", [128, 128]); tg_ = k.sb(st, "pt_g", [128, 128]); wgt = k.sb(st, "wgt", [128, 128])
        UV = [k.sb(st, "UV%d" % i, [128, D]) for i in range(NBUF)]
        accs = [k.sb(st, "pacc%d" % i, [128, D]) for i in range(NACC)]
        shp3 = [128, 16, 16]
        if l == 0:
            print("PEER sbuf remaining", C.nc.sbuf_bytes_remaining)

        def stageA(tile):
            buf = tile % 2
            h2 = h2s[buf]; gate = gates[buf]; ei = eis[buf]
            g = 0 if tile < 4 else 1
            cols = slice(tile * 128, tile * 128 + 128)
            psn = C.PS[7]
            for kc in range(8):
                s_ = sq[kc % 2]
                k.op("act", lambda e: e.activation(out=s_.ap[:], in_=C.xT.ap[:, kc, cols], func=AF.Square), [C.xT], [s_])
                k.op("pe", lambda e: e.matmul(psn.ap[:, 0:128], lhsT=C.ones.ap[:], rhs=s_.ap[:], start=(kc == 0), stop=(kc == 7)), [s_, C.ones], [psn])
                yield
            k.op("act", lambda e: e.activation(out=rstd.ap[:], in_=psn.ap[:, 0:128], func=AF.Sqrt, scale=1.0 / D, bias=C.eps.ap[:, 0:1]), [psn, C.eps], [rstd])
            k.op("dve", lambda e: e.reciprocal(rstd.ap[:], rstd.ap[:]), [rstd], [rstd])
            yield
            for kc in range(8):
                k.op("dve", lambda e: e.tensor_tensor(out=h2T.ap[:, kc, :], in0=C.xT.ap[:, kc, cols], in1=rstd.ap[:], op=ALU.mult), [C.xT, rstd], [h2T])
                yield
            for kc in range(8):
                k.op("dve", lambda e: e.tensor_scalar(h2T.ap[:, kc, :], h2T.ap[:, kc, :], C.gs2.ap[:, kc, g:g + 1], C.modT.ap[:, 24 + kc, g:g + 1], op0=ALU.mult, op1=ALU.add), [h2T, C.gs2, C.modT], [h2T])
                yield
            for half in range(2):
                ps = C.PS[half]
                for j in range(4):
                    kc = half * 4 + j
                    k.op("pe", lambda e: e.transpose(ps.ap[:, j * 128:(j + 1) * 128], h2T.ap[:, kc, :], C.ident.ap[:]), [h2T, C.ident], [ps])
                k.op("act", lambda e: e.copy(h2.ap[:, half * 512:(half + 1) * 512], ps.ap[:]), [ps], [], ww=[h2])
                yield
            for c in range(16):
                w = wq[c % 2]
                k.dma(None, w.ap[:], I.pe_q[l][:, c * 128:(c + 1) * 128].rearrange("(c p) n -> p c n", p=128), [], [w])
                ps = C.PS[2 + c % 2]
                for kc in range(8):
                    k.op("pe", lambda e: e.matmul(ps.ap[:, 0:128], lhsT=w.ap[:, kc, :], rhs=h2T.ap[:, kc, :], start=(kc == 0), stop=(kc == 7)), [w, h2T], [ps])
                k.op("act", lambda e: e.copy(qT.ap[:, c, :], ps.ap[:, 0:128]), [ps], [], ww=[qT])
                yield
            for q4 in range(4):
                ps = C.PS[4 + q4 % 2]
                for j in range(4):
                    c = q4 * 4 + j
                    k.op("pe", lambda e: e.matmul(ps.ap[:, j * 128:(j + 1) * 128], lhsT=qT.ap[:, c, :], rhs=skT.ap[:, c, :], start=True, stop=True), [qT, skT], [ps])
                for j in range(4):
                    c = q4 * 4 + j
                    if q4 % 2 == 0:
                        k.op("dve", lambda e: e.tensor_copy(Sc[c].ap[:], ps.ap[:, j * 128:(j + 1) * 128]), [ps], [Sc[c]])
                    else:
                        k.op("act", lambda e: e.copy(Sc[c].ap[:], ps.ap[:, j * 128:(j + 1) * 128]), [ps], [Sc[c]])
                    yield
            for c in range(16):
                k.op("dve", lambda e: e.max(out=vals[c].ap[:, 0:8], in_=Sc[c].ap[:]), [Sc[c]], [vals[c]])
                yield
            for c in range(16):
                k.op("dve", lambda e: e.max_index(out=idxu[c].ap[:, 0:8], in_max=vals[c].ap[:, 0:8], in_values=Sc[c].ap[:]), [Sc[c], vals[c]], [idxu[c]])
                yield
            for c in range(16):
                k.op("dve", lambda e: e.match_replace(out=Sc[c].ap[:], in_to_replace=vals[c].ap[:, 0:8], in_values=Sc[c].ap[:], imm_value=-1e30), [Sc[c], vals[c]], [Sc[c]])
                yield
            for c in range(16):
                k.op("dve", lambda e: e.max(out=vals[c].ap[:, 8:16], in_=Sc[c].ap[:]), [Sc[c]], [vals[c]])
                yield
            for c in range(16):
                k.op("dve", lambda e: e.max_index(out=idxu[c].ap[:, 8:16], in_max=vals[c].ap[:, 8:16], in_values=Sc[c].ap[:]), [Sc[c], vals[c]], [idxu[c]])
                yield
            for c in range(16):
                k.op("dve", lambda e: e.tensor_copy(idxf[c].ap[:], idxu[c].ap[:]), [idxu[c]], [idxf[c]])
                yield
            for h in range(8):
                c4 = cand[h].ap[:].rearrange("p (a b) -> p a b", a=16)
                k.op("dve", lambda e: e.tensor_tensor(out=c4, in0=bc(vals[2 * h].ap[:].unsqueeze(2), shp3), in1=bc(vals[2 * h + 1].ap[:].unsqueeze(1), shp3), op=ALU.add), [vals[2 * h], vals[2 * h + 1]], [cand[h]])
                yield
            for h in range(8):
                ce4 = cande[h].ap[:].rearrange("p (a b) -> p a b", a=16)
                k.op("dve", lambda e: e.scalar_tensor_tensor(out=ce4, in0=bc(idxf[2 * h].ap[:].unsqueeze(2), shp3), scalar=128.0, in1=bc(idxf[2 * h + 1].ap[:].unsqueeze(1), shp3), op0=ALU.mult, op1=ALU.add), [idxf[2 * h], idxf[2 * h + 1]], [cande[h]])
                yield
            for h in range(8):
                k.op("dve", lambda e: e.max(out=sv[h].ap[:, 0:8], in_=cand[h].ap[:]), [cand[h]], [sv[h]])
                yield
            for h in range(8):
                k.op("dve", lambda e: e.match_replace(out=cand2[h].ap[:], in_to_replace=sv[h].ap[:, 0:8], in_values=cand[h].ap[:], imm_value=-1e30), [cand[h], sv[h]], [cand2[h]])
                yield
            for h in range(8):
                k.op("dve", lambda e: e.max(out=sv[h].ap[:, 8:16], in_=cand2[h].ap[:]), [cand2[h]], [sv[h]])
                yield
            for h in range(8):
                k.op("dve", lambda e: e.tensor_scalar(gate.ap[:, h, :], sv[h].ap[:], sv[h].ap[:, 0:1], None, op0=ALU.subtract), [sv[h]], [], ww=[gate])
                yield
            k.op("act", lambda e: e.activation(out=gate.ap[:], in_=gate.ap[:], func=AF.Exp), [gate], [gate])
            k.op("dve", lambda e: e.tensor_reduce(out=sm.ap[:, 0:8], in_=gate.ap[:], axis=AX.X, op=ALU.add), [gate], [sm])
            yield
            k.op("dve", lambda e: e.reciprocal(sm.ap[:, 8:16], sm.ap[:, 0:8]), [sm], [sm])
            yield
            k.op("dve", lambda e: e.tensor_tensor(out=gate.ap[:], in0=gate.ap[:], in1=bc(sm.ap[:, 8:16].unsqueeze(2), [128, 8, 16]), op=ALU.mult), [gate, sm], [gate])
            yield
            k.op("pool", lambda e: e.memset(ef.ap[:], 0.0), [], [ef])
            for h in range(8):
                for kk2 in range(16):
                    j = h * 16 + kk2
                    k.op("dve", lambda e: e.scalar_tensor_tensor(out=junk.ap[:], in0=cand[h].ap[:], scalar=sv[h].ap[:, kk2:kk2 + 1], in1=cande[h].ap[:], op0=ALU.is_equal, op1=ALU.mult, accum_out=ef.ap[:, j:j + 1]), [cand[h], cande[h], sv[h]], [ef], ww=[junk])
                    yield
            k.op("dve", lambda e: e.tensor_scalar(ef.ap[:], ef.ap[:], 0.0, 16383.0, op0=ALU.max, op1=ALU.min), [ef], [ef])
            yield
            k.op("dve", lambda e: e.tensor_scalar(ef.ap[:], ef.ap[:], float(l * 16384), None, op0=ALU.add), [ef], [ef])
            yield
            k.op("dve", lambda e: e.tensor_copy(ei.ap[:], ef.ap[:]), [ef], [ei])
            yield

        def stageB(tile):
            buf = tile % 2
            h2 = h2s[buf]; gate = gates[buf]; ei = eis[buf]
            g = 0 if tile < 4 else 1
            cols = slice(tile * 128, tile * 128 + 128)
            k.op("pool", lambda e: e.memset(dots.ap[:], 0.0), [], [dots])
            for j in range(128):
                u = UV[j % NBUF]
                k.idma(u.ap[:], I.pe_u[:, :], ei.ap[:, j:j + 1], [ei], [u])
                k.op("dve", lambda e: e.scalar_tensor_tensor(out=u.ap[:], in0=u.ap[:], scalar=1.0, in1=h2.ap[:], op0=ALU.mult, op1=ALU.mult, accum_out=dots.ap[:, j:j + 1]), [h2], [dots, u])
                yield
            k.op("dve", lambda e: e.tensor_tensor(out=tg_.ap[:], in0=dots.ap[:], in1=dots.ap[:], op=ALU.mult), [dots], [tg_])
            yield
            k.op("dve", lambda e: e.tensor_tensor(out=tg_.ap[:], in0=tg_.ap[:], in1=dots.ap[:], op=ALU.mult), [tg_, dots], [tg_])
            yield
            k.op("dve", lambda e: e.scalar_tensor_tensor(out=tg_.ap[:], in0=tg_.ap[:], scalar=0.044715, in1=dots.ap[:], op0=ALU.mult, op1=ALU.add), [tg_, dots], [tg_])
            yield
            k.op("act", lambda e: e.activation(out=tg_.ap[:], in_=tg_.ap[:], func=AF.Tanh, scale=0.7978845608028654), [tg_], [tg_])
            k.op("dve", lambda e: e.tensor_scalar(tg_.ap[:], tg_.ap[:], 1.0, 0.5, op0=ALU.add, op1=ALU.mult), [tg_], [tg_])
            yield
            k.op("dve", lambda e: e.tensor_tensor(out=tg_.ap[:], in0=tg_.ap[:], in1=dots.ap[:], op=ALU.mult), [tg_, dots], [tg_])
            yield
            k.op("dve", lambda e: e.tensor_tensor(out=wgt.ap[:], in0=tg_.ap[:], in1=gate.ap[:].rearrange("p h a -> p (h a)"), op=ALU.mult), [tg_, gate], [wgt])
            yield
            for a_ in accs:
                k.op("pool", lambda e: e.memset(a_.ap[:], 0.0), [], [a_])
            for j in range(128):
                v = UV[j % NBUF]
                acc = accs[j % NACC]
                k.idma(v.ap[:], I.pe_v[:, :], ei.ap[:, j:j + 1], [ei], [v])
                k.op("dve", lambda e: e.scalar_tensor_tensor(out=acc.ap[:], in0=v.ap[:], scalar=wgt.ap[:, j:j + 1], in1=acc.ap[:], op0=ALU.mult, op1=ALU.add), [v, wgt, acc], [acc])
                yield
            for ai in range(1, NACC):
                k.op("dve", lambda e: e.tensor_tensor(out=accs[0].ap[:], in0=accs[0].ap[:], in1=accs[ai].ap[:], op=ALU.add), [accs[0], accs[ai]], [accs[0]])
                yield
            acc = accs[0]
            ps = C.PS[6]
            for half in range(2):
                for j in range(4):
                    kc = half * 4 + j
                    k.op("pe", lambda e: e.transpose(ps.ap[:, j * 128:(j + 1) * 128], acc.ap[:, kc * 128:(kc + 1) * 128], C.ident.ap[:]), [acc, C.ident], [ps])
                for j in range(4):
                    kc = half * 4 + j
                    k.op("dve", lambda e: e.scalar_tensor_tensor(out=C.xT.ap[:, kc, cols], in0=ps.ap[:, j * 128:(j + 1) * 128], scalar=C.modT.ap[:, 40 + kc, g:g + 1], in1=C.xT.ap[:, kc, cols], op0=ALU.mult, op1=ALU.add), [ps, C.modT, C.xT], [], ww=[C.xT])
                    yield

        ntiles = TOK // 128
        for _ in stageA(0):
            pass
        for t in range(ntiles):
            gb = stageB(t)
            ga = stageA(t + 1) if t + 1 < ntiles else iter(())
            a_live, b_live = True, True
            while a_live or b_live:
                if b_live:
                    try:
                        next(gb)
                    except StopIteration:
                        b_live = False
                if a_live:
                    try:
                        next(ga)
                    except StopIteration:
                        a_live = False
        k.barrier()


def phase_final(C):
    k, I, O = C.k, C.I, C.O
    with ExitStack() as st:
        lnfT = k.sb(st, "lnfT", [128, 8])
        k.dma("sp", lnfT.ap[:], I.lnf.rearrange("(c p) -> p c", p=128), [], [lnfT], allow_slow_non_contiguous=True)
        sq = [k.sb(st, "fsq%d" % i, [128, 512]) for i in range(2)]
        rstd = k.sb(st, "frstd", [128, 512])
        hT = k.sb(st, "fhT", [128, 8, 512])
        yt = [k.sb(st, "fy%d" % i, [128, D]) for i in range(2)]
        for tg in range(5):
            cols = slice(tg * 512, (tg + 1) * 512)
            ps = C.PS[7]
            for kc in range(8):
                s = sq[kc % 2]
                k.op("act", lambda e: e.activation(out=s.ap[:], in_=C.xT.ap[:, kc, cols], func=AF.Square), [C.xT], [s])
                k.op("pe", lambda e: e.matmul(ps.ap[:], lhsT=C.ones.ap[:], rhs=s.ap[:], start=(kc == 0), stop=(kc == 7)), [s, C.ones], [ps])
            k.op("act", lambda e: e.activation(out=rstd.ap[:], in_=ps.ap[:], func=AF.Sqrt, scale=1.0 / D, bias=C.eps.ap[:, 0:1]), [ps, C.eps], [rstd])
            k.op("dve", lambda e: e.reciprocal(rstd.ap[:], rstd.ap[:]), [rstd], [rstd])
            for kc in range(8):
                k.op("dve", lambda e: e.scalar_tensor_tensor(out=hT.ap[:, kc, :], in0=C.xT.ap[:, kc, cols], scalar=lnfT.ap[:, kc:kc + 1], in1=rstd.ap[:], op0=ALU.mult, op1=ALU.mult), [C.xT, rstd, lnfT], [hT])
            for tt in range(4):
                y = yt[tt % 2]
                for half in range(2):
                    ps2 = C.PS[(tt * 2 + half) % 6]
                    for j in range(4):
                        kc = half * 4 + j
                        k.op("pe", lambda e: e.transpose(ps2.ap[:, j * 128:(j + 1) * 128], hT.ap[:, kc, tt * 128:(tt + 1) * 128], C.ident.ap[:]), [hT, C.ident], [ps2])
                    if half == 0:
                        k.op("dve", lambda e: e.tensor_copy(y.ap[:, 0:512], ps2.ap[:]), [ps2], [y])
                    else:
                        k.op("act", lambda e: e.copy(y.ap[:, 512:1024], ps2.ap[:]), [ps2], [y])
                t0 = tg * 512 + tt * 128
                k.dma(None, O.y[t0:t0 + 128, :], y.ap[:], [y], [])
        k.barrier()


def make_consts():
    c = {}
    c["c_ident"] = np.eye(128, dtype=np.float32)
    c["c_antiid"] = np.ascontiguousarray(np.eye(128, dtype=np.float32)[::-1])
    R = np.zeros((128, 128), np.float32)
    cos = np.zeros((128, TS), np.float32)
    sin = np.zeros((128, TS), np.float32)
    t = np.arange(TS)
    pos = np.stack([t // 64, t % 64], 0).astype(np.float32)
    freqs = (10000.0 ** (-np.arange(16, dtype=np.float32) / 16)).astype(np.float32)
    for hh in range(2):
        for ax in range(2):
            for f in range(16):
                d1 = hh * 64 + ax * 32 + f
                d2 = d1 + 16
                ang = (pos[ax] * freqs[f]).astype(np.float32)
                cos[d1] = np.cos(ang); cos[d2] = np.cos(ang)
                sin[d1] = np.sin(ang); sin[d2] = np.sin(ang)
                R[d2, d1] = -1.0
                R[d1, d2] = 1.0
    c["c_rot"] = R
    c["c_cos"] = cos
    c["c_sin"] = sin
    r = np.arange(128)[:, None]
    cc = np.arange(128)[None, :]
    c["c_mleft"] = np.where(cc >= r, 0.0, NEG).astype(np.float32)
    c["c_mright"] = np.where(cc <= r, 0.0, NEG).astype(np.float32)
    col = np.arange(64)
    cs = np.clip(col - 8, 0, 48)
    ok = (col[None, :] >= cs[:, None]) & (col[None, :] < cs[:, None] + 16)
    c["c_colmask"] = np.where(ok, 0.0, NEG).astype(np.float32)
    b = np.zeros((128, 128), np.float32)
    b[:64, :64] = 1.0
    b[64:, 64:] = 1.0
    c["c_blk64"] = b
    c["c_iota"] = np.tile(np.arange(256, dtype=np.float32)[None, :], (128, 1))
    return c


def make_in_maps(inp):
    f = lambda a: np.ascontiguousarray(np.asarray(a), dtype=np.float32)
    consts = make_consts()
    shared = {
        "ln1_g": f(inp["ln1_g"]), "ln2_g": f(inp["ln2_g"]), "lnf_g": f(inp["lnf_g"]),
        "ada_w": f(inp["ada_w"]), "ada_b": f(inp["ada_b"]), "w_in": f(inp["w_in"]),
        "a_sink": f(inp["a_sink"]), "a_out": f(inp["a_out"]), "rw_mu": f(inp["rw_mu"]),
        "rw_w0": f(inp["rw_w0"]).reshape(L, 1024), "rw_w2": f(inp["rw_w2"]).reshape(L, 128, 512),
        "rw_a0": f(inp["rw_a0"]).reshape(L, 1024), "rw_a2": f(inp["rw_a2"]).reshape(L, 128, 512),
        "rw_g2": f(inp["rw_g2"]), "rw_kk": f(inp["rw_kk"]), "rw_ka": f(inp["rw_ka"]),
        "rw_rk": f(inp["rw_rk"]).reshape(L, 512), "rw_lnx_g": f(inp["rw_lnx_g"]), "rw_lnx_b": f(inp["rw_lnx_b"]),
        "rw_out": f(inp["rw_out"]), "na_rpb": f(inp["na_rpb"]), "na_out": f(inp["na_out"]), "w_o": f(inp["w_o"]),
        "pe_q": f(inp["pe_q"]), "pe_subkeys": f(inp["pe_subkeys"]).reshape(L, 16, 128, 128),
        "pe_u": f(inp["pe_u"]).reshape(L * 16384, D), "pe_v": f(inp["pe_v"]).reshape(L * 16384, D),
    }
    shared.update(consts)
    xp = f(inp["x_prompt"]); xs = f(inp["x_sample"])
    maps = []
    for c in range(8):
        b = c // 4
        m = dict(shared)
        m["xall"] = np.concatenate([xp[2 * c], xp[2 * c + 1], xs[b]], 0)
        m["cvec"] = np.stack([f(inp["c_ctx"]), f(inp["c"])[b]], 0)
        m["cak"] = f(inp["cache_a_k"])[b].reshape(L, 512, 128)
        m["cav"] = f(inp["cache_a_v"])[b].reshape(L, 512, 128)
        m["cck"] = f(inp["cache_c_k"])[b].reshape(L, 512, 512)
        m["ccv"] = f(inp["cache_c_v"])[b].reshape(L, 512, 512)
        m["st0"] = f(inp["state_rwkv"])[b].reshape(L, 128, 512)
        maps.append(m)
    return maps


_CACHE = {}


def kernel(**inputs):
    if "nc" not in _CACHE:
        _CACHE["nc"] = build()[0]
    nc = _CACHE["nc"]
    maps = make_in_maps(inputs)
    res = run_bass_kernel_spmd(nc, maps, core_ids=list(range(8)))
    R = res.results
    y_prompt = np.zeros((16, 256, D), np.float32)
    y_sample = np.zeros((2, TS, D), np.float32)
    nak = np.zeros((16, L, 256, 2, 64), np.float32)
    nav = np.zeros((16, L, 256, 2, 64), np.float32)
    nck = np.zeros((16, L, 256, 8, 64), np.float32)
    ncv = np.zeros((16, L, 256, 8, 64), np.float32)
    nst = np.zeros((16, L, 2, 8, 64, 64), np.float32)
    for c in range(8):
        r = R[c]
        y = np.asarray(r["y"])
        y_prompt[2 * c] = y[0:256]
        y_prompt[2 * c + 1] = y[256:512]
        if c % 4 == 0:
            y_sample[c // 4] = y[512:]
        for s in range(2):
            nak[2 * c + s] = np.asarray(r["nak"])[s].reshape(L, 256, 2, 64)
            nav[2 * c + s] = np.asarray(r["nav"])[s].reshape(L, 256, 2, 64)
            nck[2 * c + s] = np.asarray(r["nck"])[s].reshape(L, 256, 8, 64)
            ncv[2 * c + s] = np.asarray(r["ncv"])[s].reshape(L, 256, 8, 64)
            nst[2 * c + s] = np.asarray(r["nst"])[s].reshape(L, 2, 8, 64, 64)
    return (y_prompt, y_sample, nak, nav, nck, ncv, nst)
```

```python
import os
import numpy as np
from contextlib import ExitStack
import concourse.bass as bass
import concourse.mybir as mybir
from concourse.bass_utils import run_bass_kernel_spmd

F32 = mybir.dt.float32
I32 = mybir.dt.int32
U32 = mybir.dt.uint32
AF = mybir.ActivationFunctionType
ALU = mybir.AluOpType
AX = mybir.AxisListType

NDS = 40
SAME_ENGINE_SYNC = {"pe": False, "dve": True, "act": True, "pool": True, "sp": True}

D = 1024
L = 4
TOK = 2560
NPT = 512
TS = 2048
IN_COLS = 7296
PT_COLS = 3200
PF_ROWS = 4736
SCALE = 0.125
NEG = -30000.0


class Res:
    __slots__ = ("w", "r", "ap", "name")

    def __init__(self, ap=None, name=None):
        self.w = None
        self.r = []
        self.ap = ap
        self.name = name


class KB:
    def __init__(self, nc):
        self.nc = nc
        self.eng = {"pe": nc.tensor, "dve": nc.vector, "act": nc.scalar, "pool": nc.gpsimd, "sp": nc.sync}
        self.esem = {k: nc.alloc_semaphore("es_" + k) for k in self.eng}
        self.ecnt = {k: 0 for k in self.eng}
        self.seen = {k: {} for k in self.eng}
        self.dsems = [nc.alloc_semaphore("ds%d" % i) for i in range(NDS)]
        self.dcnt = [0] * NDS
        self.dnext = 0
        self.nins = 0
        self.uid = 0
        self.rr = 0

    def sb(self, st, name, shape, dt=F32):
        self.uid += 1
        t = st.enter_context(self.nc.sbuf_tensor("%s_%d" % (name, self.uid), list(shape), dt))
        return Res(t.ap(), name)

    def ps(self, name, shape, dt=F32):
        t = self.nc.alloc_psum_tensor(name, list(shape), dt)
        return Res(t.ap(), name)

    def _wait(self, e, ev, mind=0):
        sem, val = ev
        own = sem is self.esem[e]
        if own and not SAME_ENGINE_SYNC[e]:
            return
        if own and mind > 0:
            return
        key = id(sem)
        if self.seen[e].get(key, 0) >= val:
            return
        self.eng[e].wait_ge(sem, val)
        self.seen[e][key] = val
        self.nins += 1

    def deps(self, e, reads, writes, ww=(), mind=0):
        for r in reads:
            if r.w is not None:
                self._wait(e, r.w, mind)
        for w in writes:
            if w.w is not None:
                self._wait(e, w.w, mind)
            for ev in w.r:
                self._wait(e, ev, mind)
        for w in ww:
            if w.w is not None and w.w[0] is not self.esem[e]:
                self._wait(e, w.w)
            for ev in w.r:
                self._wait(e, ev)

    def _record(self, ev, reads, writes):
        for r in reads:
            r.r.append(ev)
            if len(r.r) > 16:
                d = {}
                for s, v in r.r:
                    if id(s) not in d or d[id(s)][1] < v:
                        d[id(s)] = (s, v)
                r.r = list(d.values())
        for w in writes:
            w.w = ev
            w.r = []

    def op(self, e, fn, reads, writes, ww=(), mind=0, inc=True):
        self.deps(e, reads, writes, ww, mind)
        ins = fn(self.eng[e])
        self.nins += 1
        if inc:
            self.ecnt[e] += 1
            ins.then_inc(self.esem[e], 1)
            ev = (self.esem[e], self.ecnt[e])
        else:
            ev = (self.esem[e], self.ecnt[e] + 1)
        self._record(ev, reads, list(writes) + list(ww))

    def _dma_common(self, q, reads, writes, emit):
        kk = self.dnext
        self.dnext = (kk + 1) % NDS
        sem = self.dsems[kk]
        if self.dcnt[kk] > 0:
            self._wait(q, (sem, self.dcnt[kk]))
        self.deps(q, reads, writes)
        ins = emit()
        self.dcnt[kk] += 16
        ins.then_inc(sem, 16)
        self.nins += 1
        self._record((sem, self.dcnt[kk]), reads, writes)

    def dma(self, q, out, in_, reads, writes, **kw):
        if q is None:
            q = ("sp", "act")[self.rr % 2]
            self.rr += 1
        self._dma_common(q, reads, writes, lambda: self.eng[q].dma_start(out=out, in_=in_, **kw))

    def idma(self, out, in_, off_ap, reads, writes):
        self._dma_common("pool", reads, writes, lambda: self.nc.gpsimd.indirect_dma_start(
            out=out, out_offset=None, in_=in_,
            in_offset=bass.IndirectOffsetOnAxis(ap=off_ap, axis=0)))

    def barrier(self):
        engs = ("pe", "dve", "act", "pool", "sp")
        for e in engs:
            for kk in range(NDS):
                if self.dcnt[kk] > 0:
                    self._wait(e, (self.dsems[kk], self.dcnt[kk]))
            for e2 in engs:
                if e2 != e and self.ecnt[e2] > 0:
                    self._wait(e, (self.esem[e2], self.ecnt[e2]))

    def finish(self):
        self.barrier()


class Ctx:
    pass


def bc(ap, shape):
    return ap.to_broadcast(list(shape))


def build(nlayers=L, debug=False, stages=("all",)):
    nc = bass.Bass("TRN2", target_bir_lowering=False)
    k = KB(nc)
    C = Ctx()
    C.k = k
    C.nc = nc
    C.debug = debug

    def din(name, shape, dt=F32):
        return nc.dram_tensor(name, list(shape), dt, kind="ExternalInput").ap()

    def dout(name, shape, dt=F32):
        return nc.dram_tensor(name, list(shape), dt, kind="ExternalOutput").ap()

    def scr(name, shape, dt=F32):
        if debug:
            return nc.dram_tensor(name, list(shape), dt, kind="ExternalOutput").ap()
        return nc.dram_tensor(name, list(shape), dt).ap()

    I = Ctx()
    C.I = I
    I.xall = din("xall", [TOK, D])
    I.cvec = din("cvec", [2, D])
    I.cak = din("cak", [L, 512, 128])
    I.cav = din("cav", [L, 512, 128])
    I.cck = din("cck", [L, 512, 512])
    I.ccv = din("ccv", [L, 512, 512])
    I.st0 = din("st0", [L, 128, 512])
    I.ln1 = din("ln1_g", [L, D])
    I.ln2 = din("ln2_g", [L, D])
    I.lnf = din("lnf_g", [D])
    I.ada_w = din("ada_w", [L, D, 6 * D])
    I.ada_b = din("ada_b", [L, 6 * D])
    I.w_in = din("w_in", [L, D, IN_COLS])
    I.a_sink = din("a_sink", [L, 8])
    I.a_out = din("a_out", [L, 512, D])
    I.rw_mu = din("rw_mu", [L, 1920])
    I.rw_w0 = din("rw_w0", [L, 1024])
    I.rw_w2 = din("rw_w2", [L, 128, 512])
    I.rw_a0 = din("rw_a0", [L, 1024])
    I.rw_a2 = din("rw_a2", [L, 128, 512])
    I.rw_g2 = din("rw_g2", [L, 128, 512])
    I.rw_kk = din("rw_kk", [L, 512])
    I.rw_ka = din("rw_ka", [L, 512])
    I.rw_rk = din("rw_rk", [L, 512])
    I.rw_lnx_g = din("rw_lnx_g", [L, 512])
    I.rw_lnx_b = din("rw_lnx_b", [L, 512])
    I.rw_out = din("rw_out", [L, 512, D])
    I.na_rpb = din("na_rpb", [L, 8, 15, 31])
    I.na_out = din("na_out", [L, 512, D])
    I.w_o = din("w_o", [L, D, D])
    I.pe_q = din("pe_q", [L, D, 2048])
    I.pe_sk = din("pe_subkeys", [L, 16, 128, 128])
    I.pe_u = din("pe_u", [L * 16384, D])
    I.pe_v = din("pe_v", [L * 16384, D])
    I.ident = din("c_ident", [128, 128])
    I.antiid = din("c_antiid", [128, 128])
    I.rot = din("c_rot", [128, 128])
    I.cos = din("c_cos", [128, TS])
    I.sin = din("c_sin", [128, TS])
    I.mleft = din("c_mleft", [128, 128])
    I.mright = din("c_mright", [128, 128])
    I.colmask = din("c_colmask", [64, 64])
    I.blk64 = din("c_blk64", [128, 128])
    I.iota256 = din("c_iota", [128, 256])

    O = Ctx()
    C.O = O
    O.y = dout("y", [TOK, D])
    O.nak = dout("nak", [2, L, 256, 128])
    O.nav = dout("nav", [2, L, 256, 128])
    O.nck = dout("nck", [2, L, 256, 512])
    O.ncv = dout("ncv", [2, L, 256, 512])
    O.nst = dout("nst", [2, L, 64, 1024])

    S = Ctx()
    C.S = S
    S.PT = scr("s_PT", [TOK, PT_COLS])
    S.PF = scr("s_PF", [PF_ROWS, TOK])
    S.YA = scr("s_YA", [512, TOK])
    S.YB = scr("s_YB", [512, TOK])
    S.YC = scr("s_YC", [512, TOK])
    S.QR = scr("s_QR", [640, TS])
    S.E = scr("s_E", [1, 120, 127])
    S.OPS = scr("s_OPS", [2, TOK, 3072])
    S.YS = scr("s_YS", [2, TOK, 512])
    S.BG = scr("s_BG", [TOK, 1024])
    S.rOPS = Res(); S.rYS = Res(); S.rBG = Res()
    S.rPT = Res(); S.rPF = Res(); S.rYA = Res(); S.rYB = Res(); S.rYC = Res(); S.rQR = Res(); S.rE = Res()

    C.PS = [k.ps("psb%d" % i, [128, 512]) for i in range(8)]

    with ExitStack() as gst:
        C.xT = k.sb(gst, "xT", [128, 8, TOK])
        C.ident = k.sb(gst, "ident", [128, 128])
        C.ones = k.sb(gst, "ones", [128, 128])
        C.eps = k.sb(gst, "eps", [128, 1])
        C.scT = k.sb(gst, "scT", [128, 8, 2])
        C.modT = k.sb(gst, "modT", [128, 48, 2])
        C.gs1 = k.sb(gst, "gs1", [128, 8, 2])
        C.gs2 = k.sb(gst, "gs2", [128, 8, 2])
        k.dma("sp", C.ident.ap[:], I.ident[:, :], [], [C.ident])
        k.op("dve", lambda e: e.memset(C.ones.ap[:], 1.0), [], [C.ones])
        k.op("dve", lambda e: e.memset(C.eps.ap[:], 1e-6), [], [C.eps])

        phase_init(C)
        for l in range(nlayers):
            phase_mod(C, l)
            for tg in range(5):
                phase_norm_win(C, l, tg)
            k.barrier()
            phase_kv_out(C, l)
            if "attn" in stages or "all" in stages:
                phase_attn(C, l)
            if "rwkv" in stages or "all" in stages:
                phase_rwkv(C, l)
            if "merge" in stages or "all" in stages:
                phase_merge(C, l)
            if "peer" in stages or "all" in stages:
                phase_peer(C, l)
        phase_final(C)
        k.finish()
    C.nins = k.nins
    return nc, C


def phase_init(C):
    k, I = C.k, C.I
    with ExitStack() as st:
        xt = [k.sb(st, "xin%d" % i, [128, D]) for i in range(2)]
        for t in range(TOK // 128):
            x = xt[t % 2]
            k.dma(None, x.ap[:], I.xall[t * 128:(t + 1) * 128, :], [], [x])
            for half in range(2):
                ps = C.PS[(t * 2 + half) % 8]
                for j in range(4):
                    kc = half * 4 + j
                    k.op("pe", lambda e: e.transpose(ps.ap[:, j * 128:(j + 1) * 128], x.ap[:, kc * 128:(kc + 1) * 128], C.ident.ap[:]), [x, C.ident], [ps])
                eng = ("dve", "act")[half]
                if eng == "dve":
                    k.op("dve", lambda e: e.tensor_copy(C.xT.ap[:, half * 4:half * 4 + 4, t * 128:(t + 1) * 128], ps.ap[:].rearrange("p (a b) -> p a b", a=4)), [ps], [C.xT])
                else:
                    k.op("act", lambda e: e.copy(C.xT.ap[:, half * 4:half * 4 + 4, t * 128:(t + 1) * 128], ps.ap[:].rearrange("p (a b) -> p a b", a=4)), [ps], [C.xT])
        for g in range(2):
            k.dma("sp", C.scT.ap[:, :, g], I.cvec[g].rearrange("(c p) -> p c", p=128), [], [C.scT], allow_slow_non_contiguous=True)
        k.op("act", lambda e: e.activation(out=C.scT.ap[:], in_=C.scT.ap[:], func=AF.Silu), [C.scT], [C.scT])
        k.barrier()


def phase_mod(C, l):
    k, I = C.k, C.I
    with ExitStack() as st:
        wb = [k.sb(st, "adaw%d" % i, [128, 8, 512]) for i in range(2)]
        abT = k.sb(st, "abT", [128, 48])
        lnT = k.sb(st, "lnT", [128, 2, 8])
        k.dma("sp", abT.ap[:], I.ada_b[l].rearrange("(c p) -> p c", p=128), [], [abT], allow_slow_non_contiguous=True)
        k.dma("sp", lnT.ap[:, 0, :], I.ln1[l].rearrange("(c p) -> p c", p=128), [], [lnT], allow_slow_non_contiguous=True)
        k.dma("sp", lnT.ap[:, 1, :], I.ln2[l].rearrange("(c p) -> p c", p=128), [], [lnT], allow_slow_non_contiguous=True)
        ps = C.PS[0]
        for b in range(12):
            w = wb[b % 2]
            k.dma(None, w.ap[:], I.ada_w[l][:, b * 512:(b + 1) * 512].rearrange("(c p) n -> p c n", p=128), [], [w])
            for sub in range(4):
                fc = b * 4 + sub
                for kc in range(8):
                    k.op("pe", lambda e: e.matmul(ps.ap[:, fc * 2:fc * 2 + 2], lhsT=w.ap[:, kc, sub * 128:(sub + 1) * 128], rhs=C.scT.ap[:, kc, :], start=(kc == 0), stop=(kc == 7)), [w, C.scT], [ps])
        k.op("dve", lambda e: e.tensor_tensor(out=C.modT.ap[:], in0=ps.ap[:, 0:96].rearrange("p (a b) -> p a b", b=2), in1=bc(abT.ap[:].unsqueeze(2), [128, 48, 2]), op=ALU.add), [ps, abT], [C.modT])
        for (gs, mi, li) in ((C.gs1, 1, 0), (C.gs2, 4, 1)):
            k.op("dve", lambda e: e.tensor_scalar(gs.ap[:], C.modT.ap[:, mi * 8:mi * 8 + 8, :], 1.0, None, op0=ALU.add), [C.modT], [gs])
            k.op("dve", lambda e: e.tensor_tensor(out=gs.ap[:], in0=gs.ap[:], in1=bc(lnT.ap[:, li, :].unsqueeze(2), [128, 8, 2]), op=ALU.mult), [gs, lnT], [gs])
        k.barrier()


JOBS = [
    (0, 512, "F", 0), (512, 128, "F", 512), (2688, 512, "F", 640), (3200, 512, "F", 1152),
    (4224, 512, "F", 1664), (4736, 512, "F", 2176), (5248, 512, "F", 2688), (5760, 512, "F", 3200),
    (6272, 512, "F", 3712), (6784, 512, "F", 4224),
    (512, 256, "T", 0), (768, 512, "T", 256), (1280, 512, "T", 768), (1792, 512, "T", 1280), (2304, 384, "T", 1792),
    (3200, 512, "T", 2176), (3712, 512, "T", 2688),
]


def norm_group(C, st, tg, gs, shift_idx, hT):
    k = C.k
    g = 0 if tg == 0 else 1
    cols = slice(tg * 512, (tg + 1) * 512)
    sq = [k.sb(st, "sq%d" % i, [128, 512]) for i in range(2)]
    rstd = k.sb(st, "rstd", [128, 512])
    ps = C.PS[7]
    for kc in range(8):
        s = sq[kc % 2]
        k.op("act", lambda e: e.activation(out=s.ap[:], in_=C.xT.ap[:, kc, cols], func=AF.Square), [C.xT], [s])
        k.op("pe", lambda e: e.matmul(ps.ap[:], lhsT=C.ones.ap[:], rhs=s.ap[:], start=(kc == 0), stop=(kc == 7)), [s, C.ones], [ps])
    k.op("act", lambda e: e.activation(out=rstd.ap[:], in_=ps.ap[:], func=AF.Sqrt, scale=1.0 / D, bias=C.eps.ap[:, 0:1]), [ps, C.eps], [rstd])
    k.op("dve", lambda e: e.reciprocal(rstd.ap[:], rstd.ap[:]), [rstd], [rstd])
    for kc in range(8):
        eng = "dve" if kc % 2 == 0 else "pool"
        k.op(eng, lambda e: e.tensor_tensor(out=hT.ap[:, kc, :], in0=C.xT.ap[:, kc, cols], in1=rstd.ap[:], op=ALU.mult), [C.xT, rstd], [hT])
        k.op(eng, lambda e: e.tensor_scalar(hT.ap[:, kc, :], hT.ap[:, kc, :], gs.ap[:, kc, g:g + 1], C.modT.ap[:, shift_idx * 8 + kc, g:g + 1], op0=ALU.mult, op1=ALU.add), [hT, gs, C.modT], [hT])


def phase_norm_win(C, l, tg):
    k, I, S = C.k, C.I, C.S
    with ExitStack() as st:
        hT = k.sb(st, "hT", [128, 8, 512])
        norm_group(C, st, tg, C.gs1, 0, hT)
        wb = [k.sb(st, "winw%d" % i, [128, 8, 512]) for i in range(2)]
        ev = [k.sb(st, "winev%d" % i, [128, 512]) for i in range(4)]
        ei = 0
        pi = 0
        for ji, (c0, n, lay, d0) in enumerate(JOBS):
            w = wb[ji % 2]
            k.dma(None, w.ap[:, :, 0:n], I.w_in[l][:, c0:c0 + n].rearrange("(c p) n -> p c n", p=128), [], [w])
            if lay == "F":
                for sub in range(n // 128):
                    ps = C.PS[pi % 6]; pi += 1
                    for kc in range(8):
                        k.op("pe", lambda e: e.matmul(ps.ap[:], lhsT=w.ap[:, kc, sub * 128:(sub + 1) * 128], rhs=hT.ap[:, kc, :], start=(kc == 0), stop=(kc == 7)), [w, hT], [ps])
                    o = ev[ei % 4]; ei += 1
                    if d0 >= 1664:
                        k.op("act", lambda e: e.activation(out=o.ap[:], in_=ps.ap[:], func=AF.Sigmoid), [ps], [o])
                    elif ei % 2 == 0:
                        k.op("act", lambda e: e.copy(o.ap[:], ps.ap[:]), [ps], [o])
                    else:
                        k.op("dve", lambda e: e.tensor_copy(o.ap[:], ps.ap[:]), [ps], [o])
                    r0 = d0 + sub * 128
                    k.dma(None, S.PF[r0:r0 + 128, tg * 512:(tg + 1) * 512], o.ap[:], [o], [S.rPF])
            else:
                for tt in range(4):
                    ps = C.PS[pi % 6]; pi += 1
                    for kc in range(8):
                        k.op("pe", lambda e: e.matmul(ps.ap[:, 0:n], lhsT=hT.ap[:, kc, tt * 128:(tt + 1) * 128], rhs=w.ap[:, kc, 0:n], start=(kc == 0), stop=(kc == 7)), [w, hT], [ps])
                    o = ev[ei % 4]; ei += 1
                    if ei % 2 == 0:
                        k.op("act", lambda e: e.copy(o.ap[:, 0:n], ps.ap[:, 0:n]), [ps], [o])
                    else:
                        k.op("dve", lambda e: e.tensor_copy(o.ap[:, 0:n], ps.ap[:, 0:n]), [ps], [o])
                    t0 = tg * 512 + tt * 128
                    k.dma(None, S.PT[t0:t0 + 128, d0:d0 + n], o.ap[:, 0:n], [o], [S.rPT])
        k.barrier()


def phase_kv_out(C, l):
    k, S, O = C.k, C.S, C.O
    for s in range(2):
        rows = slice(s * 256, (s + 1) * 256)
        k.dma(None, O.nak[s, l], S.PT[rows, 0:128], [S.rPT], [])
        k.dma(None, O.nav[s, l], S.PT[rows, 128:256], [S.rPT], [])
        k.dma(None, O.nck[s, l], S.PT[rows, 2176:2688], [S.rPT], [])
        k.dma(None, O.ncv[s, l], S.PT[rows, 2688:3200], [S.rPT], [])


def attn_unit(C, W, ui, qT, qres, Mq, segs, vch, sink_h, out_ap, out_res):
    k = C.k
    par = ui % 2
    S_sb = W.S[par]
    sm = W.sm[par]
    tot = sum(s[2] for s in segs)
    off = 0
    k.op("pool", lambda e: e.memset(sm.ap[:], 0.0), [], [sm])
    for si, (kres, kT, n, masks) in enumerate(segs):
        ps = C.PS[par * 2 + si]
        k.op("pe", lambda e: e.matmul(ps.ap[:Mq, 0:n], lhsT=qT, rhs=kT, start=True, stop=True), [qres, kres], [ps])
        covered = []
        for (c0, ncol, mres, map_) in masks:
            k.op("dve", lambda e: e.tensor_tensor(out=S_sb.ap[:Mq, off + c0:off + c0 + ncol], in0=ps.ap[:Mq, c0:c0 + ncol], in1=map_, op=ALU.add), [ps, mres], [S_sb])
            covered.append((c0, c0 + ncol))
        covered.sort()
        pos = 0
        gaps = []
        for (a, b) in covered:
            if a > pos:
                gaps.append((pos, a))
            pos = max(pos, b)
        if pos < n:
            gaps.append((pos, n))
        for gi, (a, b) in enumerate(gaps):
            if not masks:
                k.op("act", lambda e: e.copy(S_sb.ap[:Mq, off + a:off + b], ps.ap[:Mq, a:b]), [ps], [S_sb])
            else:
                k.op("dve", lambda e: e.tensor_copy(S_sb.ap[:Mq, off + a:off + b], ps.ap[:Mq, a:b]), [ps], [S_sb])
        off += n
    k.op("dve", lambda e: e.reduce_max(out=sm.ap[:Mq, 0:1], in_=S_sb.ap[:Mq, 0:tot], axis=AX.X), [S_sb], [sm])
    if sink_h is not None:
        k.op("dve", lambda e: e.tensor_scalar(sm.ap[:Mq, 1:2], sm.ap[:Mq, 0:1], -SCALE, W.nsink.ap[:Mq, sink_h:sink_h + 1], op0=ALU.mult, op1=ALU.min), [sm, W.nsink], [sm])
    else:
        k.op("dve", lambda e: e.tensor_scalar(sm.ap[:Mq, 1:2], sm.ap[:Mq, 0:1], -SCALE, None, op0=ALU.mult), [sm], [sm])
    k.op("act", lambda e: e.activation(out=S_sb.ap[:Mq, 0:tot], in_=S_sb.ap[:Mq, 0:tot], func=AF.Exp, bias=sm.ap[:Mq, 1:2], scale=SCALE, accum_out=sm.ap[:Mq, 2:3]), [S_sb, sm], [S_sb, sm])
    if sink_h is not None:
        k.op("act", lambda e: e.activation(out=sm.ap[:Mq, 3:4], in_=W.sink.ap[:Mq, sink_h:sink_h + 1], func=AF.Exp, bias=sm.ap[:Mq, 1:2], scale=1.0), [sm, W.sink], [sm])
        k.op("dve", lambda e: e.tensor_tensor(out=sm.ap[:Mq, 2:3], in0=sm.ap[:Mq, 2:3], in1=sm.ap[:Mq, 3:4], op=ALU.add), [sm], [sm])
    k.op("dve", lambda e: e.reciprocal(sm.ap[:Mq, 4:5], sm.ap[:Mq, 2:3]), [sm], [sm])
    k.op("dve", lambda e: e.tensor_scalar(S_sb.ap[:Mq, 0:tot], S_sb.ap[:Mq, 0:tot], sm.ap[:Mq, 4:5], None, op0=ALU.mult), [S_sb, sm], [S_sb])
    pso = C.PS[6 + par]
    nch = len(vch)
    for g0 in range(0, nch, 4):
        grp = vch[g0:g0 + 4]
        gi = W.tcount
        W.tcount += 1
        pst = C.PS[4 + gi % 2]
        ptsb = W.PTs[gi % 2]
        maxnk = max(v[2] for v in grp)
        for jj, (vres, vap, nk, c0) in enumerate(grp):
            k.op("pe", lambda e: e.transpose(pst.ap[:nk, jj * 128:jj * 128 + Mq], S_sb.ap[:Mq, c0:c0 + nk], C.ident.ap[:Mq, :Mq]), [S_sb, C.ident], [pst])
        w = len(grp) * 128
        if gi % 2 == 0:
            k.op("dve", lambda e: e.tensor_copy(ptsb.ap[:maxnk, 0:w], pst.ap[:maxnk, 0:w]), [pst], [ptsb])
        else:
            k.op("act", lambda e: e.copy(ptsb.ap[:maxnk, 0:w], pst.ap[:maxnk, 0:w]), [pst], [ptsb])
        for jj, (vres, vap, nk, c0) in enumerate(grp):
            ci = g0 + jj
            k.op("pe", lambda e: e.matmul(pso.ap[:64, 0:Mq], lhsT=vap, rhs=ptsb.ap[:nk, jj * 128:jj * 128 + Mq], start=(ci == 0), stop=(ci == nch - 1)), [vres, ptsb], [pso])
    osb = W.o[par]
    k.op("act", lambda e: e.copy(osb.ap[:64, 0:Mq], pso.ap[:64, 0:Mq]), [pso], [osb])
    k.dma(None, out_ap, osb.ap[:64, 0:Mq], [osb], [out_res])


def attn_work(C, st, l, need_sink):
    k, I = C.k, C.I
    W = Ctx()
    W.S = [k.sb(st, "S_sb%d" % i, [128, 1024]) for i in range(2)]
    W.sm = [k.sb(st, "sm%d" % i, [128, 8]) for i in range(2)]
    W.PTs = [k.sb(st, "PTs%d" % i, [128, 512]) for i in range(2)]
    W.o = [k.sb(st, "osb%d" % i, [64, 128]) for i in range(2)]
    W.tcount = 0
    if need_sink:
        W.sink = k.sb(st, "sink", [128, 8])
        W.nsink = k.sb(st, "nsink", [128, 8])
        k.dma("sp", W.sink.ap[:], I.a_sink[l:l + 1, :].partition_broadcast(128) if False else bass.AP(tensor=I.a_sink.tensor, offset=l * 8, ap=[[0, 128], [1, 8]]), [], [W.sink])
        k.op("dve", lambda e: e.tensor_scalar(W.nsink.ap[:], W.sink.ap[:], -1.0, None, op0=ALU.mult), [W.sink], [W.nsink])
    return W


def phase_attn_prompt(C, l):
    k, I, S = C.k, C.I, C.S
    with ExitStack() as st:
        W = attn_work(C, st, l, True)
        Q = k.sb(st, "pQ", [64, 8, 256])
        K_ = k.sb(st, "pK", [64, 8, 256])
        V = k.sb(st, "pV", [128, 2, 512])
        ui = 0
        for mixer in ("A", "C"):
            for s in range(2):
                base = s * 256
                if mixer == "A":
                    q0, k0, nkv, v0, vw, Y, rY = 0, 512, 2, 128, 128, S.YA, S.rYA
                else:
                    q0, k0, nkv, v0, vw, Y, rY = 640, 1152, 8, 2688, 512, S.YC, S.rYC
                k.dma(None, Q.ap[:], S.PF[q0:q0 + 512, base:base + 256].rearrange("(h d) t -> d h t", d=64), [S.rPF], [Q])
                k.dma(None, K_.ap[:, 0:nkv, :], S.PF[k0:k0 + nkv * 64, base:base + 256].rearrange("(h d) t -> d h t", d=64), [S.rPF], [K_])
                k.dma(None, V.ap[:, :, 0:vw], S.PT[base:base + 256, v0:v0 + vw].rearrange("(t p) f -> p t f", p=128), [S.rPT], [V])
                for h in range(8):
                    kv = h // 4 if mixer == "A" else h
                    for qb in range(2):
                        segs = [(K_, K_.ap[:, kv, :], 256, [])]
                        vch = [(V, V.ap[:, t, kv * 64:(kv + 1) * 64], 128, t * 128) for t in range(2)]
                        attn_unit(C, W, ui, Q.ap[:, h, qb * 128:(qb + 1) * 128], Q, 128, segs, vch,
                                  h if mixer == "A" else None,
                                  Y[h * 64:(h + 1) * 64, base + qb * 128:base + (qb + 1) * 128], rY)
                        ui += 1
        k.barrier()


def load_cache_kT(C, st, src, nh, name):
    k = C.k
    ct = k.sb(st, name + "_tm", [128, 4, nh * 64])
    CK = k.sb(st, name, [64, nh, 512])
    k.dma(None, ct.ap[:], src.rearrange("(j p) f -> p j f", p=128), [], [ct])
    for h in range(nh):
        ps = C.PS[h % 4]
        for j in range(4):
            k.op("pe", lambda e: e.transpose(ps.ap[:64, j * 128:(j + 1) * 128], ct.ap[:, j, h * 64:(h + 1) * 64], C.ident.ap[:]), [ct, C.ident], [ps])
        k.op("dve", lambda e: e.tensor_copy(CK.ap[:, h, :], ps.ap[:64, :]), [ps], [CK])
    return CK


def phase_attn_sample_A(C, l):
    k, I, S = C.k, C.I, C.S
    B0 = NPT
    with ExitStack() as st:
        cos = k.sb(st, "cos", [128, TS]); sin = k.sb(st, "sin", [128, TS]); rot = k.sb(st, "rot", [128, 128])
        k.dma("sp", cos.ap[:], I.cos[:, :], [], [cos])
        k.dma("act", sin.ap[:], I.sin[:, :], [], [sin])
        k.dma("sp", rot.ap[:], I.rot[:, :], [], [rot])
        X = [k.sb(st, "ropeX%d" % i, [128, TS]) for i in range(2)]
        T1 = [k.sb(st, "ropeT%d" % i, [128, 512]) for i in range(2)]
        XR = [k.sb(st, "ropeR%d" % i, [128, 512]) for i in range(2)]
        cnt = 0
        for c in range(5):
            x = X[c % 2]
            k.dma(None, x.ap[:], S.PF[c * 128:(c + 1) * 128, B0:B0 + TS], [S.rPF], [x])
            for tg in range(4):
                cs = slice(tg * 512, (tg + 1) * 512)
                ps = C.PS[cnt % 4]; t1 = T1[cnt % 2]; xr = XR[cnt % 2]; cnt += 1
                k.op("pe", lambda e: e.matmul(ps.ap[:], lhsT=rot.ap[:], rhs=x.ap[:, cs], start=True, stop=True), [rot, x], [ps])
                k.op("pool", lambda e: e.tensor_tensor(out=t1.ap[:], in0=x.ap[:, cs], in1=cos.ap[:, cs], op=ALU.mult), [x, cos], [t1])
                k.op("dve", lambda e: e.tensor_tensor(out=xr.ap[:], in0=ps.ap[:], in1=sin.ap[:, cs], op=ALU.mult), [ps, sin], [xr])
                k.op("dve", lambda e: e.tensor_tensor(out=xr.ap[:], in0=xr.ap[:], in1=t1.ap[:], op=ALU.add), [xr, t1], [xr])
                k.dma(None, S.QR[c * 128:(c + 1) * 128, cs], xr.ap[:], [xr], [S.rQR])
        k.barrier()
    with ExitStack() as st:
        W = attn_work(C, st, l, True)
        ml = k.sb(st, "mleft", [128, 128]); mr = k.sb(st, "mright", [128, 128])
        k.dma("sp", ml.ap[:], I.mleft[:, :], [], [ml])
        k.dma("sp", mr.ap[:], I.mright[:, :], [], [mr])
        CK = load_cache_kT(C, st, I.cak[l], 2, "CKa")
        CV = k.sb(st, "CVa", [128, 4, 128])
        k.dma(None, CV.ap[:], I.cav[l].rearrange("(j p) f -> p j f", p=128), [], [CV])
        Q = k.sb(st, "sQ", [64, TS]); K_ = k.sb(st, "sK", [64, TS]); V = k.sb(st, "sV", [128, 16, 64])
        ui = 0
        for h in range(8):
            kv = h // 4
            if h % 4 == 0:
                k.dma(None, K_.ap[:], S.QR[512 + kv * 64:512 + (kv + 1) * 64, :], [S.rQR], [K_])
                k.dma(None, V.ap[:], S.PT[B0:B0 + TS, 128 + kv * 64:128 + (kv + 1) * 64].rearrange("(t p) f -> p t f", p=128), [S.rPT], [V])
            k.dma(None, Q.ap[:], S.QR[h * 64:(h + 1) * 64, :], [S.rQR], [Q])
            for n in range(16):
                ta = max(0, n - 1); tb = min(16, n + 2)
                nlat = (tb - ta) * 128
                masks = []
                if n > 0:
                    masks.append((0, 128, ml, ml.ap[:, :]))
                if n < 15:
                    masks.append((nlat - 128, 128, mr, mr.ap[:, :]))
                segs = [(K_, K_.ap[:, ta * 128:tb * 128], nlat, masks), (CK, CK.ap[:, kv, :], 512, [])]
                vch = [(V, V.ap[:, t, :], 128, (t - ta) * 128) for t in range(ta, tb)]
                vch += [(CV, CV.ap[:, j, kv * 64:(kv + 1) * 64], 128, nlat + j * 128) for j in range(4)]
                attn_unit(C, W, ui, Q.ap[:, n * 128:(n + 1) * 128], Q, 128, segs, vch, h,
                          S.YA[h * 64:(h + 1) * 64, B0 + n * 128:B0 + (n + 1) * 128], S.rYA)
                ui += 1
        k.barrier()


def phase_attn_sample_C(C, l):
    k, I, S = C.k, C.I, C.S
    B0 = NPT
    with ExitStack() as st:
        W = attn_work(C, st, l, False)
        z = k.sb(st, "zer", [120, 127])
        k.op("dve", lambda e: e.memset(z.ap[:], 0.0), [], [z])
        k.dma("sp", S.E[0], z.ap[:], [z], [S.rE])
        k.dma("sp", S.E[0, :, 48:79], I.na_rpb[l].rearrange("h r c -> (h r) c"), [S.rE], [S.rE])
        MB = k.sb(st, "MB", [64, 120, 64])
        cm = k.sb(st, "colmask", [64, 64])
        k.dma("sp", cm.ap[:], I.colmask[:, :], [], [cm])
        for c in range(64):
            k.dma(None, MB.ap[c:c + 1, :, :], S.E[0:1, :, 63 - c:127 - c], [S.rE], [MB])
        k.op("dve", lambda e: e.scalar_tensor_tensor(out=MB.ap[:], in0=MB.ap[:], scalar=1.0 / SCALE, in1=bc(cm.ap[:].unsqueeze(1), [64, 120, 64]), op0=ALU.mult, op1=ALU.add), [MB, cm], [MB])
        CK = load_cache_kT(C, st, I.cck[l], 8, "CKc")
        CV = k.sb(st, "CVc", [128, 4, 512])
        k.dma(None, CV.ap[:], I.ccv[l].rearrange("(j p) f -> p j f", p=128), [], [CV])
        Q = k.sb(st, "cQ", [64, TS]); K_ = k.sb(st, "cK", [64, TS]); V = k.sb(st, "cV", [64, 32, 64])
        ui = 0
        for h in range(8):
            k.dma(None, Q.ap[:], S.PF[640 + h * 64:640 + (h + 1) * 64, B0:B0 + TS], [S.rPF], [Q])
            k.dma(None, K_.ap[:], S.PF[1152 + h * 64:1152 + (h + 1) * 64, B0:B0 + TS], [S.rPF], [K_])
            k.dma(None, V.ap[:], S.PT[B0:B0 + TS, 2688 + h * 64:2688 + (h + 1) * 64].rearrange("(r c) f -> c r f", c=64), [S.rPT], [V])
            for r in range(32):
                rs = min(max(r - 4, 0), 24)
                dr0 = rs - r + 7
                bias = MB.ap[:, h * 15 + dr0:h * 15 + dr0 + 8, :].rearrange("p a b -> p (a b)")
                segs = [(K_, K_.ap[:, rs * 64:rs * 64 + 512], 512, [(0, 512, MB, bias)]), (CK, CK.ap[:, h, :], 512, [])]
                vch = [(V, V.ap[:, rs + a, :], 64, a * 64) for a in range(8)]
                vch += [(CV, CV.ap[:, j, h * 64:(h + 1) * 64], 128, 512 + j * 128) for j in range(4)]
                attn_unit(C, W, ui, Q.ap[:, r * 64:(r + 1) * 64], Q, 64, segs, vch, None,
                          S.YC[h * 64:(h + 1) * 64, B0 + r * 64:B0 + (r + 1) * 64], S.rYC)
                ui += 1
        k.barrier()


def phase_attn(C, l):
    phase_attn_prompt(C, l)
    phase_attn_sample_A(C, l)
    phase_attn_sample_C(C, l)


SEQS = [(0, 256), (256, 256), (512, 2048)]
TC = 32
DEC_SCALE = -0.6065306597126334


def pbcast(X, l, n):
    return bass.AP(tensor=X.tensor, offset=l * n, ap=[[0, 128], [1, n]])


def v3(ap, h=8):
    return ap.rearrange("p (h j) -> p h j", h=h)


def phase_rwkv_prep(C, l):
    k, I, S = C.k, C.I, C.S
    with ExitStack() as st:
        mu = k.sb(st, "mu_b", [128, 1920]); w0 = k.sb(st, "w0_b", [128, 1024]); a0 = k.sb(st, "a0_b", [128, 1024])
        kkp = k.sb(st, "kkp_b", [128, 512]); ka = k.sb(st, "ka_b", [128, 512]); omk = k.sb(st, "omk_b", [128, 512]); rk = k.sb(st, "rk_b", [128, 512])
        w2 = k.sb(st, "w2", [128, 512]); a2 = k.sb(st, "a2", [128, 512]); g2 = k.sb(st, "g2", [128, 512]); J = k.sb(st, "J", [128, 128])
        e12 = k.sb(st, "e12", [128, 1])
        k.op("dve", lambda e: e.memset(e12.ap[:], 1e-12), [], [e12])
        for (t, X, n) in ((mu, I.rw_mu, 1920), (w0, I.rw_w0, 1024), (a0, I.rw_a0, 1024), (kkp, I.rw_kk, 512), (ka, I.rw_ka, 512), (rk, I.rw_rk, 512)):
            k.dma(None, t.ap[:], pbcast(X, l, n), [], [t])
        for (t, X) in ((w2, I.rw_w2), (a2, I.rw_a2), (g2, I.rw_g2)):
            k.dma(None, t.ap[:], X[l], [], [t])
        k.dma(None, J.ap[:], I.antiid[:, :], [], [J])
        k.op("dve", lambda e: e.tensor_scalar(omk.ap[:], ka.ap[:], -1.0, 1.0, op0=ALU.mult, op1=ALU.add), [ka], [omk])
        pb = k.sb(st, "pb", [128, 1920]); prev = k.sb(st, "prev", [128, 1920]); nxt = k.sb(st, "nxt", [128, 1920])
        lw = k.sb(st, "lw", [128, 384]); lwT = k.sb(st, "lwT", [128, 384])
        az = [k.sb(st, "az%d" % z, [128, 512]) for z in range(2)]
        kk = k.sb(st, "kk", [128, 512]); kkn = k.sb(st, "kkn", [128, 512]); t1 = k.sb(st, "t1", [128, 512]); t2 = k.sb(st, "t2", [128, 512])
        sm = k.sb(st, "rsm", [128, 32])
        opz = [k.sb(st, "opz%d" % z, [128, 8, 6, 64]) for z in range(2)]
        rev = k.sb(st, "rev", [128, 1536]); bg = k.sb(st, "bg", [128, 1024])
        for (s0, T) in SEQS:
            nt = T // 128
            for ti in range(nt):
                t0 = s0 + ti * 128
                k.dma(None, pb.ap[:], S.PT[t0:t0 + 128, 256:2176], [S.rPT], [pb])
                if ti > 0:
                    k.dma(None, prev.ap[:], S.PT[t0 - 1:t0 + 127, 256:2176], [S.rPT], [prev])
                else:
                    k.op("dve", lambda e: e.memset(prev.ap[0:1, :], 0.0), [], [prev])
                    k.dma(None, prev.ap[1:128, :], S.PT[t0:t0 + 127, 256:2176], [S.rPT], [prev])
                if ti < nt - 1:
                    k.dma(None, nxt.ap[:], S.PT[t0 + 1:t0 + 129, 256:2176], [S.rPT], [nxt])
                else:
                    k.op("pool", lambda e: e.memset(nxt.ap[:], 0.0), [], [nxt])
                    k.dma(None, nxt.ap[0:127, :], S.PT[t0 + 1:t0 + 128, 256:2176], [S.rPT], [nxt])
                k.op("pool", lambda e: e.tensor_tensor(out=prev.ap[:], in0=prev.ap[:], in1=nxt.ap[:], op=ALU.add), [prev, nxt], [prev])
                k.op("dve", lambda e: e.scalar_tensor_tensor(out=prev.ap[:], in0=prev.ap[:], scalar=0.5, in1=pb.ap[:], op0=ALU.mult, op1=ALU.subtract), [prev, pb], [prev])
                k.op("dve", lambda e: e.tensor_tensor(out=prev.ap[:], in0=prev.ap[:], in1=mu.ap[:], op=ALU.mult), [prev, mu], [prev])
                k.op("dve", lambda e: e.tensor_tensor(out=pb.ap[:], in0=pb.ap[:], in1=prev.ap[:], op=ALU.add), [pb, prev], [pb])
                r_ = pb.ap[:, 0:512]; k_ = pb.ap[:, 512:1024]; v_ = pb.ap[:, 1024:1536]
                k.op("act", lambda e: e.activation(out=lw.ap[:, 0:128], in_=pb.ap[:, 1536:1664], func=AF.Tanh), [pb], [lw])
                k.op("act", lambda e: e.copy(lw.ap[:, 128:256], pb.ap[:, 1664:1792]), [pb], [lw])
                k.op("act", lambda e: e.activation(out=lw.ap[:, 256:384], in_=pb.ap[:, 1792:1920], func=AF.Sigmoid), [pb], [lw])
                ps = C.PS[0]
                for j in range(3):
                    k.op("pe", lambda e: e.transpose(ps.ap[:, j * 128:(j + 1) * 128], lw.ap[:, j * 128:(j + 1) * 128], C.ident.ap[:]), [lw, C.ident], [ps])
                k.op("dve", lambda e: e.tensor_copy(lwT.ap[:], ps.ap[:, 0:384]), [ps], [lwT])
                for z in range(2):
                    zs = slice(z * 64, (z + 1) * 64)
                    psw = C.PS[1 + z]; psa = C.PS[3 + z]
                    k.op("pe", lambda e: e.matmul(psw.ap[:], lhsT=lwT.ap[zs, 0:128], rhs=w2.ap[zs, :], start=True, stop=True), [lwT, w2], [psw])
                    k.op("pe", lambda e: e.matmul(psa.ap[:], lhsT=lwT.ap[zs, 128:256], rhs=a2.ap[zs, :], start=True, stop=True), [lwT, a2], [psa])
                    dec = v3(t1.ap[:])
                    k.op("dve", lambda e: e.tensor_tensor(out=t1.ap[:], in0=psw.ap[:], in1=w0.ap[:, z * 512:(z + 1) * 512], op=ALU.add), [psw, w0], [t1])
                    k.op("act", lambda e: e.activation(out=t1.ap[:], in_=t1.ap[:], func=AF.Sigmoid), [t1], [t1])
                    k.op("act", lambda e: e.activation(out=opz[z].ap[:, :, 1, :], in_=dec, func=AF.Exp, scale=DEC_SCALE), [t1], [opz[z]])
                    k.op("dve", lambda e: e.tensor_tensor(out=az[z].ap[:], in0=psa.ap[:], in1=a0.ap[:, z * 512:(z + 1) * 512], op=ALU.add), [psa, a0], [az[z]])
                    k.op("act", lambda e: e.activation(out=az[z].ap[:], in_=az[z].ap[:], func=AF.Sigmoid), [az[z]], [az[z]])
                psg = C.PS[5]
                k.op("pe", lambda e: e.matmul(psg.ap[:], lhsT=lwT.ap[:, 256:384], rhs=g2.ap[:, :], start=True, stop=True), [lwT, g2], [psg])
                k.op("act", lambda e: e.copy(bg.ap[:, 512:1024], psg.ap[:]), [psg], [bg])
                k.op("dve", lambda e: e.tensor_tensor(out=kk.ap[:], in0=k_, in1=kkp.ap[:], op=ALU.mult), [pb, kkp], [kk])
                k.op("pool", lambda e: e.tensor_tensor(out=t2.ap[:], in0=kk.ap[:], in1=kk.ap[:], op=ALU.mult), [kk], [t2])
                k.op("dve", lambda e: e.tensor_reduce(out=sm.ap[:, 0:8], in_=v3(t2.ap[:]), axis=AX.X, op=ALU.add), [t2], [sm])
                k.op("act", lambda e: e.activation(out=sm.ap[:, 0:8], in_=sm.ap[:, 0:8], func=AF.Sqrt, bias=e12.ap[:, 0:1], scale=1.0), [sm, e12], [sm])
                k.op("dve", lambda e: e.reciprocal(sm.ap[:, 8:16], sm.ap[:, 0:8]), [sm], [sm])
                k.op("dve", lambda e: e.tensor_tensor(out=v3(kkn.ap[:]), in0=v3(kk.ap[:]), in1=bc(sm.ap[:, 8:16].unsqueeze(2), [128, 8, 64]), op=ALU.mult), [kk, sm], [kkn])
                for z in range(2):
                    k.op("dve", lambda e: e.tensor_tensor(out=t1.ap[:], in0=az[z].ap[:], in1=ka.ap[:], op=ALU.mult), [az[z], ka], [t1])
                    k.op("pool", lambda e: e.tensor_tensor(out=t1.ap[:], in0=t1.ap[:], in1=omk.ap[:], op=ALU.add), [t1, omk], [t1])
                    k.op("dve", lambda e: e.tensor_tensor(out=opz[z].ap[:, :, 3, :], in0=v3(t1.ap[:]), in1=v3(k_), op=ALU.mult), [t1, pb], [opz[z]])
                    k.op("pool", lambda e: e.tensor_tensor(out=opz[z].ap[:, :, 2, :], in0=v3(kkn.ap[:]), in1=v3(az[z].ap[:]), op=ALU.mult), [kkn, az[z]], [opz[z]])
                    k.op("dve", lambda e: e.tensor_scalar(opz[z].ap[:, :, 0, :], v3(kkn.ap[:]), -1.0, None, op0=ALU.mult), [kkn], [opz[z]])
                    k.op("act", lambda e: e.copy(opz[z].ap[:, :, 4, :], v3(r_)), [pb], [opz[z]])
                    k.op("act", lambda e: e.copy(opz[z].ap[:, :, 5, :], v3(v_)), [pb], [opz[z]])
                k.op("dve", lambda e: e.tensor_tensor(out=v3(t1.ap[:]), in0=opz[0].ap[:, :, 3, :], in1=opz[1].ap[:, :, 3, :], op=ALU.add), [opz[0], opz[1]], [t1])
                k.op("dve", lambda e: e.tensor_tensor(out=t1.ap[:], in0=t1.ap[:], in1=r_, op=ALU.mult), [t1, pb], [t1])
                k.op("dve", lambda e: e.tensor_tensor(out=t1.ap[:], in0=t1.ap[:], in1=rk.ap[:], op=ALU.mult), [t1, rk], [t1])
                k.op("dve", lambda e: e.tensor_reduce(out=sm.ap[:, 16:24], in_=v3(t1.ap[:]), axis=AX.X, op=ALU.add), [t1], [sm])
                k.op("dve", lambda e: e.tensor_tensor(out=v3(bg.ap[:, 0:512]), in0=v3(v_), in1=bc(sm.ap[:, 16:24].unsqueeze(2), [128, 8, 64]), op=ALU.mult), [pb, sm], [bg])
                k.dma(None, S.BG[t0:t0 + 128, :], bg.ap[:], [bg], [S.rBG])
                k.dma(None, S.OPS[0, t0:t0 + 128, :], opz[0].ap[:].rearrange("p a b c -> p (a b c)"), [opz[0]], [S.rOPS])
                flat = opz[1].ap[:].rearrange("p a b c -> p (a b c)")
                tr = s0 + (nt - 1 - ti) * 128
                for half in range(2):
                    for j in range(3):
                        c0 = half * 1536 + j * 512
                        psr = C.PS[5 + j] if j < 2 else C.PS[7]
                        k.op("pe", lambda e: e.matmul(psr.ap[:], lhsT=J.ap[:], rhs=flat[:, c0:c0 + 512], start=True, stop=True), [J, opz[1]], [psr])
                        if j % 2 == 0:
                            k.op("dve", lambda e: e.tensor_copy(rev.ap[:, j * 512:(j + 1) * 512], psr.ap[:]), [psr], [rev])
                        else:
                            k.op("act", lambda e: e.copy(rev.ap[:, j * 512:(j + 1) * 512], psr.ap[:]), [psr], [rev])
                    k.dma(None, S.OPS[1, tr:tr + 128, half * 1536:(half + 1) * 1536], rev.ap[:], [rev], [S.rOPS])
        k.barrier()


SCAN_MIND = int(os.environ.get('SCAN_MIND', '0'))
SCAN_CHAINS = int(os.environ.get('SCAN_CHAINS', '2'))


def scan_group(C, st, l, insts, T, ILO, s_init, s_final, chains, ring=4):
    k, I, S = C.k, C.I, C.S
    IC = 64 // ILO
    NS = TC + 1
    OPB = [k.sb(st, "OPB%d" % i, [128, TC, 5, 64]) for i in range(2)]
    VB = [k.sb(st, "VB%d" % i, [128, TC, ILO]) for i in range(2)]
    VKr = [k.sb(st, "VKr%d" % i, [128, ILO * 64]) for i in range(ring)]
    CH = []
    for ci, (eng, i0, i1) in enumerate(chains):
        n = i1 - i0
        c = Ctx()
        c.eng, c.i0, c.i1, c.n = eng, i0, i1, n
        c.S = k.sb(st, "scanS%d" % ci, [128, n * 64]); c.tmp = k.sb(st, "scantmp%d" % ci, [128, n * 64])
        c.T2 = k.sb(st, "scanT2%d" % ci, [128, 2, n * 64])
        c.B = [k.sb(st, "scanB%d_%d" % (ci, i), [128, 2 * NS, n]) for i in range(2)]
        if s_init is not None:
            k.dma(None, c.S.ap[:], s_init[:, i0 * 64:i1 * 64], [], [c.S])
        else:
            k.op("pool", lambda e: e.memset(c.S.ap[:], 0.0), [], [c.S])
        c.S3 = c.S.ap[:].rearrange("p (i j) -> p i j", j=64)
        c.T3 = c.tmp.ap[:].rearrange("p (i j) -> p i j", j=64)
        c.shp = [128, n, 64]
        CH.append(c)
    groups = {}
    for (p0, z, h, s0) in insts:
        groups.setdefault((z, s0), []).append((p0, h))
    nch = T // TC

    def store_y(ch, c, slot0, nslots, trow0):
        Bt = c.B[ch % 2]
        for (z, s0), lst in groups.items():
            pa = min(p for p, _ in lst)
            np_ = len(lst) * IC
            dst = bass.AP(tensor=S.YS.tensor, offset=(z * TOK + s0 + trow0) * 512 + c.i0, ap=[[ILO, np_], [512, nslots], [1, c.n]])
            k.dma(None, dst, Bt.ap[pa:pa + np_, slot0:slot0 + nslots, :], [Bt], [S.rYS])

    for ch in range(nch):
        opb = OPB[ch % 2]; vb = VB[ch % 2]; opb_prev = OPB[(ch + 1) % 2]
        for (p0, z, h, s0) in insts:
            base = (z * TOK + s0 + ch * TC) * 3072 + h * 384
            src = bass.AP(tensor=S.OPS.tensor, offset=base, ap=[[0, IC], [3072, TC], [1, 320]])
            k.dma(None, opb.ap[p0:p0 + IC, :, :, :].rearrange("p t a b -> p t (a b)"), src, [S.rOPS], [opb])
            srcv = bass.AP(tensor=S.OPS.tensor, offset=base + 320, ap=[[ILO, IC], [3072, TC], [1, ILO]])
            k.dma(None, vb.ap[p0:p0 + IC, :, :], srcv, [S.rOPS], [vb])
        for tc in range(TC):
            def row(c, kind):
                return bc(opb.ap[:, tc, kind, :].unsqueeze(1), c.shp)

            def rprev(c):
                if tc > 0:
                    return opb, bc(opb.ap[:, tc - 1, 4, :].unsqueeze(1), c.shp)
                if ch > 0:
                    return opb_prev, bc(opb_prev.ap[:, TC - 1, 4, :].unsqueeze(1), c.shp)
                return opb, bc(opb.ap[:, 0, 4, :].unsqueeze(1), c.shp)
            vk = VKr[(ch * TC + tc) % ring]
            for i_ in range(ILO):
                k.op("act", lambda e: e.activation(out=vk.ap[:, i_ * 64:(i_ + 1) * 64], in_=opb.ap[:, tc, 3, :], func=AF.Copy, scale=vb.ap[:, tc, i_:i_ + 1]), [vb, opb], [vk])
            for step in range(7):
                for c in CH:
                    Bt = c.B[ch % 2]
                    T2a = c.T2.ap[:, 0, :].rearrange("p (i j) -> p i j", j=64)
                    T2b = c.T2.ap[:, 1, :].rearrange("p (i j) -> p i j", j=64)
                    if step == 0:
                        k.op(c.eng, lambda e: e.tensor_tensor(out=T2b, in0=c.S3, in1=row(c, 0), op=ALU.mult), [c.S, opb], [], ww=[c.T2])
                    elif step == 1:
                        rres, rap = rprev(c)
                        k.op(c.eng, lambda e: e.tensor_tensor(out=T2a, in0=c.S3, in1=rap, op=ALU.mult), [c.S, rres], [], ww=[c.T2])
                    elif step == 2:
                        outap = Bt.ap[:, tc::NS, :]
                        k.op(c.eng, lambda e: e.tensor_reduce(out=outap, in_=c.T2.ap[:].rearrange("p a (i j) -> p (a i) j", j=64), axis=AX.X, op=ALU.add), [c.T2], [Bt])
                    elif step == 3:
                        k.op(c.eng, lambda e: e.tensor_tensor(out=c.S3, in0=c.S3, in1=row(c, 1), op=ALU.mult), [c.S, opb], [c.S])
                    elif step == 4:
                        k.op(c.eng, lambda e: e.tensor_tensor(out=c.T3, in0=bc(Bt.ap[:, NS + tc, :].unsqueeze(2), c.shp), in1=row(c, 2), op=ALU.mult), [Bt, opb], [c.tmp])
                    elif step == 5:
                        k.op(c.eng, lambda e: e.tensor_tensor(out=c.S3, in0=c.S3, in1=c.T3, op=ALU.add), [c.S, c.tmp], [c.S])
                    else:
                        k.op(c.eng, lambda e: e.tensor_tensor(out=c.S.ap[:], in0=c.S.ap[:], in1=vk.ap[:, c.i0 * 64:c.i1 * 64], op=ALU.add), [c.S, vk], [c.S])
        last = (ch == nch - 1)
        if last:
            for c in CH:
                Bt = c.B[ch % 2]
                k.op(c.eng, lambda e: e.tensor_tensor(out=c.T3, in0=c.S3, in1=bc(opb.ap[:, TC - 1, 4, :].unsqueeze(1), c.shp), op=ALU.mult), [c.S, opb], [c.tmp])
                k.op(c.eng, lambda e: e.tensor_reduce(out=Bt.ap[:, TC, :], in_=c.T3, axis=AX.X, op=ALU.add), [c.tmp], [Bt])
        for c in CH:
            if ch == 0:
                store_y(ch, c, 1, TC - 1 + (1 if last else 0), 0)
            else:
                store_y(ch, c, 0, TC + (1 if last else 0), ch * TC - 1)
    if s_final is not None:
        for (dst, p0, n) in s_final:
            for c in CH:
                k.dma(None, dst[:, c.i0 * 64:c.i1 * 64], c.S.ap[p0:p0 + n, :], [c.S], [])


def phase_rwkv_scan(C, l):
    k, I, S, O = C.k, C.I, C.S, C.O
    with ExitStack() as st:
        insts = []
        for s in range(2):
            for z in range(2):
                for h in range(8):
                    insts.append((s * 64 + z * 32 + h * 4, z, h, s * 256))
        scan_group(C, st, l, insts, 256, 16, None, [(O.nst[s, l], s * 64, 64) for s in range(2)], ([("dve", 0, 16)] if SCAN_CHAINS == 1 else [("dve", 0, 8), ("dve", 8, 16)]), ring=3)
        k.barrier()
    with ExitStack() as st:
        insts = []
        for z in range(2):
            for h in range(8):
                insts.append((z * 64 + h * 8, z, h, 512))
        scan_group(C, st, l, insts, 2048, 8, I.st0[l], None, ([("dve", 0, 8)] if SCAN_CHAINS == 1 else [("dve", 0, 4), ("dve", 4, 8)]), ring=8)
        k.barrier()


def phase_rwkv_post(C, l):
    k, I, S = C.k, C.I, C.S
    with ExitStack() as st:
        lg = k.sb(st, "lnxg_b", [128, 512]); lb = k.sb(st, "lnxb_b", [128, 512]); J = k.sb(st, "J2", [128, 128])
        gne = k.sb(st, "gne", [128, 1])
        k.op("dve", lambda e: e.memset(gne.ap[:], 64e-5), [], [gne])
        k.dma(None, lg.ap[:], pbcast(I.rw_lnx_g, l, 512), [], [lg])
        k.dma(None, lb.ap[:], pbcast(I.rw_lnx_b, l, 512), [], [lb])
        k.dma(None, J.ap[:], I.antiid[:, :], [], [J])
        yf = [k.sb(st, "yf%d" % i, [128, 512]) for i in range(2)]
        yb = [k.sb(st, "yb%d" % i, [128, 512]) for i in range(2)]
        bg = [k.sb(st, "pbg%d" % i, [128, 1024]) for i in range(2)]
        t1 = k.sb(st, "pt1", [128, 512]); sm = k.sb(st, "psm", [128, 32]); yT = [k.sb(st, "pyT%d" % i, [128, 4, 128]) for i in range(2)]
        cnt = 0
        for (s0, T) in SEQS:
            nt = T // 128
            for ti in range(nt):
                par = cnt % 2; cnt += 1
                t0 = s0 + ti * 128
                tr = s0 + (nt - 1 - ti) * 128
                y = yf[par]; y2 = yb[par]; b = bg[par]
                k.dma(None, y.ap[:], S.YS[0, t0:t0 + 128, :], [S.rYS], [y])
                k.dma(None, y2.ap[:], S.YS[1, tr:tr + 128, :], [S.rYS], [y2])
                k.dma(None, b.ap[:], S.BG[t0:t0 + 128, :], [S.rBG], [b])
                ps = C.PS[par]
                k.op("pe", lambda e: e.matmul(ps.ap[:], lhsT=J.ap[:], rhs=y2.ap[:], start=True, stop=True), [J, y2], [ps])
                k.op("dve", lambda e: e.tensor_tensor(out=y.ap[:], in0=y.ap[:], in1=ps.ap[:], op=ALU.add), [y, ps], [y])
                k.op("dve", lambda e: e.tensor_reduce(out=sm.ap[:, 0:8], in_=v3(y.ap[:]), axis=AX.X, op=ALU.add), [y], [sm])
                k.op("dve", lambda e: e.tensor_scalar(sm.ap[:, 0:8], sm.ap[:, 0:8], 1.0 / 64, None, op0=ALU.mult), [sm], [sm])
                k.op("dve", lambda e: e.tensor_tensor(out=v3(y.ap[:]), in0=v3(y.ap[:]), in1=bc(sm.ap[:, 0:8].unsqueeze(2), [128, 8, 64]), op=ALU.subtract), [y, sm], [y])
                k.op("pool", lambda e: e.tensor_tensor(out=t1.ap[:], in0=y.ap[:], in1=y.ap[:], op=ALU.mult), [y], [t1])
                k.op("dve", lambda e: e.tensor_reduce(out=sm.ap[:, 8:16], in_=v3(t1.ap[:]), axis=AX.X, op=ALU.add), [t1], [sm])
                k.op("act", lambda e: e.activation(out=sm.ap[:, 8:16], in_=sm.ap[:, 8:16], func=AF.Sqrt, bias=gne.ap[:, 0:1], scale=1.0 / 64), [sm, gne], [sm])
                k.op("dve", lambda e: e.reciprocal(sm.ap[:, 16:24], sm.ap[:, 8:16]), [sm], [sm])
                k.op("dve", lambda e: e.tensor_tensor(out=v3(y.ap[:]), in0=v3(y.ap[:]), in1=bc(sm.ap[:, 16:24].unsqueeze(2), [128, 8, 64]), op=ALU.mult), [y, sm], [y])
                k.op("pool", lambda e: e.tensor_tensor(out=y.ap[:], in0=y.ap[:], in1=lg.ap[:], op=ALU.mult), [y, lg], [y])
                k.op("dve", lambda e: e.tensor_tensor(out=y.ap[:], in0=y.ap[:], in1=lb.ap[:], op=ALU.add), [y, lb], [y])
                k.op("dve", lambda e: e.tensor_tensor(out=y.ap[:], in0=y.ap[:], in1=b.ap[:, 0:512], op=ALU.add), [y, b], [y])
                k.op("dve", lambda e: e.tensor_tensor(out=y.ap[:], in0=y.ap[:], in1=b.ap[:, 512:1024], op=ALU.mult), [y, b], [y])
                pst = C.PS[2 + par]
                for c in range(4):
                    k.op("pe", lambda e: e.transpose(pst.ap[:, c * 128:(c + 1) * 128], y.ap[:, c * 128:(c + 1) * 128], C.ident.ap[:]), [y, C.ident], [pst])
                k.op("act", lambda e: e.copy(yT[par].ap[:], pst.ap[:].rearrange("p (c t) -> p c t", c=4)), [pst], [yT[par]])
                k.dma(None, S.YB[:, t0:t0 + 128].rearrange("(c p) t -> p c t", p=128), yT[par].ap[:], [yT[par]], [S.rYB])
        k.barrier()


def phase_rwkv(C, l):
    phase_rwkv_prep(C, l)
    phase_rwkv_scan(C, l)
    phase_rwkv_post(C, l)


def phase_merge(C, l):
    k, I, S = C.k, C.I, C.S
    outs = (I.a_out, I.rw_out, I.na_out)
    Ys = ((S.YA, S.rYA), (S.YB, S.rYB), (S.YC, S.rYC))
    with ExitStack() as st:
        yT = [k.sb(st, "myT%d" % b, [128, 4, 512]) for b in range(3)]
        mg = k.sb(st, "merged", [128, 8, 512])
        wo = [[k.sb(st, "mwo%d_%d" % (b, i), [128, 4, 128]) for i in range(2)] for b in range(3)]
        gt = [[k.sb(st, "mg%d_%d" % (b, i), [128, 512]) for i in range(2)] for b in range(3)]
        tmp = [k.sb(st, "mtmp%d" % i, [128, 512]) for i in range(2)]
        wob = [k.sb(st, "mwob%d" % i, [128, 8, 128]) for i in range(2)]
        cnt = 0
        for tg in range(5):
            g = 0 if tg == 0 else 1
            cols = slice(tg * 512, (tg + 1) * 512)
            for b in range(3):
                k.dma(None, yT[b].ap[:], Ys[b][0][:, cols].rearrange("(c p) t -> p c t", p=128), [Ys[b][1]], [yT[b]])
            for fc in range(8):
                par = cnt % 2; cnt += 1
                for b in range(3):
                    k.dma(None, wo[b][par].ap[:], outs[b][l][:, fc * 128:(fc + 1) * 128].rearrange("(c p) n -> p c n", p=128), [], [wo[b][par]])
                    r0 = 1664 + b * 1024 + fc * 128
                    k.dma(None, gt[b][par].ap[:], S.PF[r0:r0 + 128, cols], [S.rPF], [gt[b][par]])
                for b in range(3):
                    ps = C.PS[(fc * 3 + b) % 6]
                    for c in range(4):
                        k.op("pe", lambda e: e.matmul(ps.ap[:], lhsT=wo[b][par].ap[:, c, :], rhs=yT[b].ap[:, c, :], start=(c == 0), stop=(c == 3)), [wo[b][par], yT[b]], [ps])
                    if b == 0:
                        k.op("dve", lambda e: e.tensor_tensor(out=mg.ap[:, fc, :], in0=ps.ap[:], in1=gt[b][par].ap[:], op=ALU.mult), [ps, gt[b][par]], [mg])
                    else:
                        t = tmp[b % 2]
                        k.op("dve", lambda e: e.tensor_tensor(out=t.ap[:], in0=ps.ap[:], in1=gt[b][par].ap[:], op=ALU.mult), [ps, gt[b][par]], [t])
                        k.op("pool", lambda e: e.tensor_tensor(out=mg.ap[:, fc, :], in0=mg.ap[:, fc, :], in1=t.ap[:], op=ALU.add), [mg, t], [mg])
            for fc2 in range(8):
                w = wob[fc2 % 2]
                k.dma(None, w.ap[:], I.w_o[l][:, fc2 * 128:(fc2 + 1) * 128].rearrange("(c p) n -> p c n", p=128), [], [w])
                ps = C.PS[6 + fc2 % 2]
                for kc in range(8):
                    k.op("pe", lambda e: e.matmul(ps.ap[:], lhsT=w.ap[:, kc, :], rhs=mg.ap[:, kc, :], start=(kc == 0), stop=(kc == 7)), [w, mg], [ps])
                k.op("dve", lambda e: e.scalar_tensor_tensor(out=C.xT.ap[:, fc2, cols], in0=ps.ap[:], scalar=C.modT.ap[:, 16 + fc2, g:g + 1], in1=C.xT.ap[:, fc2, cols], op0=ALU.mult, op1=ALU.add), [ps, C.modT, C.xT], [C.xT])
        k.barrier()


PEER_STOP = int(os.environ.get('PEER_STOP', '0'))
NBUF = 7
NACC = 2


def phase_peer(C, l):
    k, I, S = C.k, C.I, C.S
    with ExitStack() as st:
        skn = k.sb(st, "skn", [128, 16, 128]); skT = k.sb(st, "skT", [128, 16, 128])
        k.dma(None, skn.ap[:], I.pe_sk[l].rearrange("c n k -> n c k"), [], [skn])
        for c in range(16):
            ps = C.PS[c % 4]
            k.op("pe", lambda e: e.transpose(ps.ap[:, 0:128], skn.ap[:, c, :], C.ident.ap[:]), [skn, C.ident], [ps])
            k.op("dve", lambda e: e.tensor_copy(skT.ap[:, c, :], ps.ap[:, 0:128]), [ps], [skT])
        h2T = k.sb(st, "h2T", [128, 8, 128])
        h2s = [k.sb(st, "h2tok%d" % i, [128, D]) for i in range(2)]
        sq = [k.sb(st, "psq%d" % i, [128, 128]) for i in range(2)]; rstd = k.sb(st, "prstd", [128, 128])
        wq = [k.sb(st, "wq%d" % i, [128, 8, 128]) for i in range(2)]
        qT = k.sb(st, "qT", [128, 16, 128])
        Sc = [k.sb(st, "Sc%d" % c, [128, 128]) for c in range(16)]
        vals = [k.sb(st, "vals%d" % c, [128, 16]) for c in range(16)]
        idxu = [k.sb(st, "idxu%d" % c, [128, 16], U32) for c in range(16)]
        idxf = [k.sb(st, "idxf%d" % c, [128, 16]) for c in range(16)]
        cand = [k.sb(st, "cand%d" % h, [128, 256]) for h in range(8)]
        cand2 = [k.sb(st, "candb%d" % h, [128, 256]) for h in range(8)]
        cande = [k.sb(st, "cande%d" % h, [128, 256]) for h in range(8)]
        sv = [k.sb(st, "sv%d" % h, [128, 16]) for h in range(8)]
        gates = [k.sb(st, "gate%d" % i, [128, 8, 16]) for i in range(2)]; sm = k.sb(st, "pesm", [128, 32])
        junk = k.sb(st, "pjunk", [128, 256]); ef = k.sb(st, "ef", [128, 128])
        eis = [k.sb(st, "ei%d" % i, [128, 128], I32) for i in range(2)]
        dots = k.sb(st, "dots", [128, 128]); tg_ = k.sb(st, "pt_g", [128, 128]); wgt = k.sb(st, "wgt", [128, 128])
        UV = [k.sb(st, "UV%d" % i, [128, D]) for i in range(NBUF)]
        accs = [k.sb(st, "pacc%d" % i, [128, D]) for i in range(NACC)]
        shp3 = [128, 16, 16]
        if l == 0:
            print("PEER sbuf remaining", C.nc.sbuf_bytes_remaining)

        def stageA(tile):
            buf = tile % 2
            h2 = h2s[buf]; gate = gates[buf]; ei = eis[buf]
            g = 0 if tile < 4 else 1
            cols = slice(tile * 128, tile * 128 + 128)
            psn = C.PS[7]
            for kc in range(8):
                s_ = sq[kc % 2]
                k.op("act", lambda e: e.activation(out=s_.ap[:], in_=C.xT.ap[:, kc, cols], func=AF.Square), [C.xT], [s_])
                k.op("pe", lambda e: e.matmul(psn.ap[:, 0:128], lhsT=C.ones.ap[:], rhs=s_.ap[:], start=(kc == 0), stop=(kc == 7)), [s_, C.ones], [psn])
                yield
            k.op("act", lambda e: e.activation(out=rstd.ap[:], in_=psn.ap[:, 0:128], func=AF.Sqrt, scale=1.0 / D, bias=C.eps.ap[:, 0:1]), [psn, C.eps], [rstd])
            k.op("dve", lambda e: e.reciprocal(rstd.ap[:], rstd.ap[:]), [rstd], [rstd])
            yield
            for kc in range(8):
                k.op("dve", lambda e: e.tensor_tensor(out=h2T.ap[:, kc, :], in0=C.xT.ap[:, kc, cols], in1=rstd.ap[:], op=ALU.mult), [C.xT, rstd], [h2T])
                yield
            for kc in range(8):
                k.op("dve", lambda e: e.tensor_scalar(h2T.ap[:, kc, :], h2T.ap[:, kc, :], C.gs2.ap[:, kc, g:g + 1], C.modT.ap[:, 24 + kc, g:g + 1], op0=ALU.mult, op1=ALU.add), [h2T, C.gs2, C.modT], [h2T])
                yield
            for half in range(2):
                ps = C.PS[half]
                for j in range(4):
                    kc = half * 4 + j
                    k.op("pe", lambda e: e.transpose(ps.ap[:, j * 128:(j + 1) * 128], h2T.ap[:, kc, :], C.ident.ap[:]), [h2T, C.ident], [ps])
                k.op("act", lambda e: e.copy(h2.ap[:, half * 512:(half + 1) * 512], ps.ap[:]), [ps], [], ww=[h2])
                yield
            for c in range(16):
                w = wq[c % 2]
                k.dma(None, w.ap[:], I.pe_q[l][:, c * 128:(c + 1) * 128].rearrange("(c p) n -> p c n", p=128), [], [w])
                ps = C.PS[2 + c % 2]
                for kc in range(8):
                    k.op("pe", lambda e: e.matmul(ps.ap[:, 0:128], lhsT=w.ap[:, kc, :], rhs=h2T.ap[:, kc, :], start=(kc == 0), stop=(kc == 7)), [w, h2T], [ps])
                k.op("act", lambda e: e.copy(qT.ap[:, c, :], ps.ap[:, 0:128]), [ps], [], ww=[qT])
                yield
            for q4 in range(4):
                ps = C.PS[4 + q4 % 2]
                for j in range(4):
                    c = q4 * 4 + j
                    k.op("pe", lambda e: e.matmul(ps.ap[:, j * 128:(j + 1) * 128], lhsT=qT.ap[:, c, :], rhs=skT.ap[:, c, :], start=True, stop=True), [qT, skT], [ps])
                for j in range(4):
                    c = q4 * 4 + j
                    if q4 % 2 == 0:
                        k.op("dve", lambda e: e.tensor_copy(Sc[c].ap[:], ps.ap[:, j * 128:(j + 1) * 128]), [ps], [Sc[c]])
                    else:
                        k.op("act", lambda e: e.copy(Sc[c].ap[:], ps.ap[:, j * 128:(j + 1) * 128]), [ps], [Sc[c]])
                    yield
            for c in range(16):
                k.op("dve", lambda e: e.max(out=vals[c].ap[:, 0:8], in_=Sc[c].ap[:]), [Sc[c]], [vals[c]])
                yield
            for c in range(16):
                k.op("dve", lambda e: e.max_index(out=idxu[c].ap[:, 0:8], in_max=vals[c].ap[:, 0:8], in_values=Sc[c].ap[:]), [Sc[c], vals[c]], [idxu[c]])
                yield
            for c in range(16):
                k.op("dve", lambda e: e.match_replace(out=Sc[c].ap[:], in_to_replace=vals[c].ap[:, 0:8], in_values=Sc[c].ap[:], imm_value=-1e30), [Sc[c], vals[c]], [Sc[c]])
                yield
            for c in range(16):
                k.op("dve", lambda e: e.max(out=vals[c].ap[:, 8:16], in_=Sc[c].ap[:]), [Sc[c]], [vals[c]])
                yield
            for c in range(16):
                k.op("dve", lambda e: e.max_index(out=idxu[c].ap[:, 8:16], in_max=vals[c].ap[:, 8:16], in_values=Sc[c].ap[:]), [Sc[c], vals[c]], [idxu[c]])
                yield
            for c in range(16):
                k.op("dve", lambda e: e.tensor_copy(idxf[c].ap[:], idxu[c].ap[:]), [idxu[c]], [idxf[c]])
                yield
            for h in range(8):
                c4 = cand[h].ap[:].rearrange("p (a b) -> p a b", a=16)
                k.op("dve", lambda e: e.tensor_tensor(out=c4, in0=bc(vals[2 * h].ap[:].unsqueeze(2), shp3), in1=bc(vals[2 * h + 1].ap[:].unsqueeze(1), shp3), op=ALU.add), [vals[2 * h], vals[2 * h + 1]], [cand[h]])
                yield
            for h in range(8):
                ce4 = cande[h].ap[:].rearrange("p (a b) -> p a b", a=16)
                k.op("dve", lambda e: e.scalar_tensor_tensor(out=ce4, in0=bc(idxf[2 * h].ap[:].unsqueeze(2), shp3), scalar=128.0, in1=bc(idxf[2 * h + 1].ap[:].unsqueeze(1), shp3), op0=ALU.mult, op1=ALU.add), [idxf[2 * h], idxf[2 * h + 1]], [cande[h]])
                yield
            for h in range(8):
                k.op("dve", lambda e: e.max(out=sv[h].ap[:, 0:8], in_=cand[h].ap[:]), [cand[h]], [sv[h]])
                yield
            for h in range(8):
                k.op("dve", lambda e: e.match_replace(out=cand2[h].ap[:], in_to_replace=sv[h].ap[:, 0:8], in_values=cand[h].ap[:], imm_value=-1e30), [cand[h], sv[h]], [cand2[h]])
                yield
            for h in range(8):
                k.op("dve", lambda e: e.max(out=sv[h].ap[:, 8:16], in_=cand2[h].ap[:]), [cand2[h]], [sv[h]])
                yield
            for h in range(8):
                k.op("dve", lambda e: e.tensor_scalar(gate.ap[:, h, :], sv[h].ap[:], sv[h].ap[:, 0:1], None, op0=ALU.subtract), [sv[h]], [], ww=[gate])
                yield
            k.op("act", lambda e: e.activation(out=gate.ap[:], in_=gate.ap[:], func=AF.Exp), [gate], [gate])
            k.op("dve", lambda e: e.tensor_reduce(out=sm.ap[:, 0:8], in_=gate.ap[:], axis=AX.X, op=ALU.add), [gate], [sm])
            yield
            k.op("dve", lambda e: e.reciprocal(sm.ap[:, 8:16], sm.ap[:, 0:8]), [sm], [sm])
            yield
            k.op("dve", lambda e: e.tensor_tensor(out=gate.ap[:], in0=gate.ap[:], in1=bc(sm.ap[:, 8:16].unsqueeze(2), [128, 8, 16]), op=ALU.mult), [gate, sm], [gate])
            yield
            k.op("pool", lambda e: e.memset(ef.ap[:], 0.0), [], [ef])
            for h in range(8):
                for kk2 in range(16):
                    j = h * 16 + kk2
                    k.op("dve", lambda e: e.scalar_tensor_tensor(out=junk.ap[:], in0=cand[h].ap[:], scalar=sv[h].ap[:, kk2:kk2 + 1], in1=cande[h].ap[:], op0=ALU.is_equal, op1=ALU.mult, accum_out=ef.ap[:, j:j + 1]), [cand[h], cande[h], sv[h]], [ef], ww=[junk])
                    yield
            k.op("dve", lambda e: e.tensor_scalar(ef.ap[:], ef.ap[:], 0.0, 16383.0, op0=ALU.max, op1=ALU.min), [ef], [ef])
            yield
            k.op("dve", lambda e: e.tensor_scalar(ef.ap[:], ef.ap[:], float(l * 16384), None, op0=ALU.add), [ef], [ef])
            yield
            k.op("dve", lambda e: e.tensor_copy(ei.ap[:], ef.ap[:]), [ef], [ei])
            yield

        def stageB(tile):
            buf = tile % 2
            h2 = h2s[buf]; gate = gates[buf]; ei = eis[buf]
            g = 0 if tile < 4 else 1
            cols = slice(tile * 128, tile * 128 + 128)
            k.op("pool", lambda e: e.memset(dots.ap[:], 0.0), [], [dots])
            for j in range(128):
                u = UV[j % NBUF]
                k.idma(u.ap[:], I.pe_u[:, :], ei.ap[:, j:j + 1], [ei], [u])
                k.op("dve", lambda e: e.scalar_tensor_tensor(out=u.ap[:], in0=u.ap[:], scalar=1.0, in1=h2.ap[:], op0=ALU.mult, op1=ALU.mult, accum_out=dots.ap[:, j:j + 1]), [h2], [dots, u])
                yield
            k.op("dve", lambda e: e.tensor_tensor(out=tg_.ap[:], in0=dots.ap[:], in1=dots.ap[:], op=ALU.mult), [dots], [tg_])
            yield
            k.op("dve", lambda e: e.tensor_tensor(out=tg_.ap[:], in0=tg_.ap[:], in1=dots.ap[:], op=ALU.mult), [tg_, dots], [tg_])
            yield
            k.op("dve", lambda e: e.scalar_tensor_tensor(out=tg_.ap[:], in0=tg_.ap[:], scalar=0.044715, in1=dots.ap[:], op0=ALU.mult, op1=ALU.add), [tg_, dots], [tg_])
            yield
            k.op("act", lambda e: e.activation(out=tg_.ap[:], in_=tg_.ap[:], func=AF.Tanh, scale=0.7978845608028654), [tg_], [tg_])
            k.op("dve", lambda e: e.tensor_scalar(tg_.ap[:], tg_.ap[:], 1.0, 0.5, op0=ALU.add, op1=ALU.mult), [tg_], [tg_])
            yield
            k.op("dve", lambda e: e.tensor_tensor(out=tg_.ap[:], in0=tg_.ap[:], in1=dots.ap[:], op=ALU.mult), [tg_, dots], [tg_])
            yield
            k.op("dve", lambda e: e.tensor_tensor(out=wgt.ap[:], in0=tg_.ap[:], in1=gate.ap[:].rearrange("p h a -> p (h a)"), op=ALU.mult), [tg_, gate], [wgt])
            yield
            for a_ in accs:
                k.op("pool", lambda e: e.memset(a_.ap[:], 0.0), [], [a_])
            for j in range(128):
                v = UV[j % NBUF]
                acc = accs[j % NACC]
                k.idma(v.ap[:], I.pe_v[:, :], ei.ap[:, j:j + 1], [ei], [v])
                k.op("dve", lambda e: e.scalar_tensor_tensor(out=acc.ap[:], in0=v.ap[:], scalar=wgt.ap[:, j:j + 1], in1=acc.ap[:], op0=ALU.mult, op1=ALU.add), [v, wgt, acc], [acc])
                yield
            for ai in range(1, NACC):
                k.op("dve", lambda e: e.tensor_tensor(out=accs[0].ap[:], in0=accs[0].ap[:], in1=accs[ai].ap[:], op=ALU.add), [accs[0], accs[ai]], [accs[0]])
                yield
            acc = accs[0]
            ps = C.PS[6]
            for half in range(2):
                for j in range(4):
                    kc = half * 4 + j
                    k.op("pe", lambda e: e.transpose(ps.ap[:, j * 128:(j + 1) * 128], acc.ap[:, kc * 128:(kc + 1) * 128], C.ident.ap[:]), [acc, C.ident], [ps])
                for j in range(4):
                    kc = half * 4 + j
                    k.op("dve", lambda e: e.scalar_tensor_tensor(out=C.xT.ap[:, kc, cols], in0=ps.ap[:, j * 128:(j + 1) * 128], scalar=C.modT.ap[:, 40 + kc, g:g + 1], in1=C.xT.ap[:, kc, cols], op0=ALU.mult, op1=ALU.add), [ps, C.modT, C.xT], [], ww=[C.xT])
                    yield

        ntiles = TOK // 128
        for _ in stageA(0):
            pass
        for t in range(ntiles):
            gb = stageB(t)
            ga = stageA(t + 1) if t + 1 < ntiles else iter(())
            a_live, b_live = True, True
            while a_live or b_live:
                if b_live:
                    try:
                        next(gb)
                    except StopIteration:
                        b_live = False
                if a_live:
                    try:
                        next(ga)
                    except StopIteration:
                        a_live = False
        k.barrier()


def phase_final(C):
    k, I, O = C.k, C.I, C.O
    with ExitStack() as st:
        lnfT = k.sb(st, "lnfT", [128, 8])
        k.dma("sp", lnfT.ap[:], I.lnf.rearrange("(c p) -> p c", p=128), [], [lnfT], allow_slow_non_contiguous=True)
        sq = [k.sb(st, "fsq%d" % i, [128, 512]) for i in range(2)]
        rstd = k.sb(st, "frstd", [128, 512])
        hT = k.sb(st, "fhT", [128, 8, 512])
        yt = [k.sb(st, "fy%d" % i, [128, D]) for i in range(2)]
        for tg in range(5):
            cols = slice(tg * 512, (tg + 1) * 512)
            ps = C.PS[7]
            for kc in range(8):
                s = sq[kc % 2]
                k.op("act", lambda e: e.activation(out=s.ap[:], in_=C.xT.ap[:, kc, cols], func=AF.Square), [C.xT], [s])
                k.op("pe", lambda e: e.matmul(ps.ap[:], lhsT=C.ones.ap[:], rhs=s.ap[:], start=(kc == 0), stop=(kc == 7)), [s, C.ones], [ps])
            k.op("act", lambda e: e.activation(out=rstd.ap[:], in_=ps.ap[:], func=AF.Sqrt, scale=1.0 / D, bias=C.eps.ap[:, 0:1]), [ps, C.eps], [rstd])
            k.op("dve", lambda e: e.reciprocal(rstd.ap[:], rstd.ap[:]), [rstd], [rstd])
            for kc in range(8):
                k.op("dve", lambda e: e.scalar_tensor_tensor(out=hT.ap[:, kc, :], in0=C.xT.ap[:, kc, cols], scalar=lnfT.ap[:, kc:kc + 1], in1=rstd.ap[:], op0=ALU.mult, op1=ALU.mult), [C.xT, rstd, lnfT], [hT])
            for tt in range(4):
                y = yt[tt % 2]
                for half in range(2):
                    ps2 = C.PS[(tt * 2 + half) % 6]
                    for j in range(4):
                        kc = half * 4 + j
                        k.op("pe", lambda e: e.transpose(ps2.ap[:, j * 128:(j + 1) * 128], hT.ap[:, kc, tt * 128:(tt + 1) * 128], C.ident.ap[:]), [hT, C.ident], [ps2])
                    if half == 0:
                        k.op("dve", lambda e: e.tensor_copy(y.ap[:, 0:512], ps2.ap[:]), [ps2], [y])
                    else:
                        k.op("act", lambda e: e.copy(y.ap[:, 512:1024], ps2.ap[:]), [ps2], [y])
                t0 = tg * 512 + tt * 128
                k.dma(None, O.y[t0:t0 + 128, :], y.ap[:], [y], [])
        k.barrier()


def make_consts():
    c = {}
    c["c_ident"] = np.eye(128, dtype=np.float32)
    c["c_antiid"] = np.ascontiguousarray(np.eye(128, dtype=np.float32)[::-1])
    R = np.zeros((128, 128), np.float32)
    cos = np.zeros((128, TS), np.float32)
    sin = np.zeros((128, TS), np.float32)
    t = np.arange(TS)
    pos = np.stack([t // 64, t % 64], 0).astype(np.float32)
    freqs = (10000.0 ** (-np.arange(16, dtype=np.float32) / 16)).astype(np.float32)
    for hh in range(2):
        for ax in range(2):
            for f in range(16):
                d1 = hh * 64 + ax * 32 + f
                d2 = d1 + 16
                ang = (pos[ax] * freqs[f]).astype(np.float32)
                cos[d1] = np.cos(ang); cos[d2] = np.cos(ang)
                sin[d1] = np.sin(ang); sin[d2] = np.sin(ang)
                R[d2, d1] = -1.0
                R[d1, d2] = 1.0
    c["c_rot"] = R
    c["c_cos"] = cos
    c["c_sin"] = sin
    r = np.arange(128)[:, None]
    cc = np.arange(128)[None, :]
    c["c_mleft"] = np.where(cc >= r, 0.0, NEG).astype(np.float32)
    c["c_mright"] = np.where(cc <= r, 0.0, NEG).astype(np.float32)
    col = np.arange(64)
    cs = np.clip(col - 8, 0, 48)
    ok = (col[None, :] >= cs[:, None]) & (col[None, :] < cs[:, None] + 16)
    c["c_colmask"] = np.where(ok, 0.0, NEG).astype(np.float32)
    b = np.zeros((128, 128), np.float32)
    b[:64, :64] = 1.0
    b[64:, 64:] = 1.0
    c["c_blk64"] = b
    c["c_iota"] = np.tile(np.arange(256, dtype=np.float32)[None, :], (128, 1))
    return c


def make_in_maps(inp):
    f = lambda a: np.ascontiguousarray(np.asarray(a), dtype=np.float32)
    consts = make_consts()
    shared = {
        "ln1_g": f(inp["ln1_g"]), "ln2_g": f(inp["ln2_g"]), "lnf_g": f(inp["lnf_g"]),
        "ada_w": f(inp["ada_w"]), "ada_b": f(inp["ada_b"]), "w_in": f(inp["w_in"]),
        "a_sink": f(inp["a_sink"]), "a_out": f(inp["a_out"]), "rw_mu": f(inp["rw_mu"]),
        "rw_w0": f(inp["rw_w0"]).reshape(L, 1024), "rw_w2": f(inp["rw_w2"]).reshape(L, 128, 512),
        "rw_a0": f(inp["rw_a0"]).reshape(L, 1024), "rw_a2": f(inp["rw_a2"]).reshape(L, 128, 512),
        "rw_g2": f(inp["rw_g2"]), "rw_kk": f(inp["rw_kk"]), "rw_ka": f(inp["rw_ka"]),
        "rw_rk": f(inp["rw_rk"]).reshape(L, 512), "rw_lnx_g": f(inp["rw_lnx_g"]), "rw_lnx_b": f(inp["rw_lnx_b"]),
        "rw_out": f(inp["rw_out"]), "na_rpb": f(inp["na_rpb"]), "na_out": f(inp["na_out"]), "w_o": f(inp["w_o"]),
        "pe_q": f(inp["pe_q"]), "pe_subkeys": f(inp["pe_subkeys"]).reshape(L, 16, 128, 128),
        "pe_u": f(inp["pe_u"]).reshape(L * 16384, D), "pe_v": f(inp["pe_v"]).reshape(L * 16384, D),
    }
    shared.update(consts)
    xp = f(inp["x_prompt"]); xs = f(inp["x_sample"])
    maps = []
    for c in range(8):
        b = c // 4
        m = dict(shared)
        m["xall"] = np.concatenate([xp[2 * c], xp[2 * c + 1], xs[b]], 0)
        m["cvec"] = np.stack([f(inp["c_ctx"]), f(inp["c"])[b]], 0)
        m["cak"] = f(inp["cache_a_k"])[b].reshape(L, 512, 128)
        m["cav"] = f(inp["cache_a_v"])[b].reshape(L, 512, 128)
        m["cck"] = f(inp["cache_c_k"])[b].reshape(L, 512, 512)
        m["ccv"] = f(inp["cache_c_v"])[b].reshape(L, 512, 512)
        m["st0"] = f(inp["state_rwkv"])[b].reshape(L, 128, 512)
        maps.append(m)
    return maps


_CACHE = {}


def kernel(**inputs):
    if "nc" not in _CACHE:
        _CACHE["nc"] = build()[0]
    nc = _CACHE["nc"]
    maps = make_in_maps(inputs)
    res = run_bass_kernel_spmd(nc, maps, core_ids=list(range(8)))
    R = res.results
    y_prompt = np.zeros((16, 256, D), np.float32)
    y_sample = np.zeros((2, TS, D), np.float32)
    nak = np.zeros((16, L, 256, 2, 64), np.float32)
    nav = np.zeros((16, L, 256, 2, 64), np.float32)
    nck = np.zeros((16, L, 256, 8, 64), np.float32)
    ncv = np.zeros((16, L, 256, 8, 64), np.float32)
    nst = np.zeros((16, L, 2, 8, 64, 64), np.float32)
    for c in range(8):
        r = R[c]
        y = np.asarray(r["y"])
        y_prompt[2 * c] = y[0:256]
        y_prompt[2 * c + 1] = y[256:512]
        if c % 4 == 0:
            y_sample[c // 4] = y[512:]
        for s in range(2):
            nak[2 * c + s] = np.asarray(r["nak"])[s].reshape(L, 256, 2, 64)
            nav[2 * c + s] = np.asarray(r["nav"])[s].reshape(L, 256, 2, 64)
            nck[2 * c + s] = np.asarray(r["nck"])[s].reshape(L, 256, 8, 64)
            ncv[2 * c + s] = np.asarray(r["ncv"])[s].reshape(L, 256, 8, 64)
            nst[2 * c + s] = np.asarray(r["nst"])[s].reshape(L, 2, 8, 64, 64)
    return (y_prompt, y_sample, nak, nav, nck, ncv, nst)
```
